# Optimizing a Trainium2 kernel written in Bass

```python
import math
import jax, jax.numpy as jnp
from jax import lax
import numpy as np

D_MODEL = 2048
BATCH = 4
SEQ = 2048
DEPTH = 2
DEC_BATCH = 8
DEC_SEQ = 4
PAST_LEN = 16384
PAGE_SIZE = 128

POOL_WIDTH = D_MODEL // 2
POOL_WINDOWS = (2, 4, 8, 16)
N_POOL_GROUPS = len(POOL_WINDOWS)
POOL_GROUP = POOL_WIDTH // N_POOL_GROUPS
POOL_BUF = max(POOL_WINDOWS) - 1
ATTN_WIDTH = D_MODEL // 2
N_HEADS = 8
HEAD_DIM = ATTN_WIDTH // N_HEADS
N_IDX_HEADS = 16
IDX_DIM = 64
IDX_SCALE = (N_IDX_HEADS * IDX_DIM) ** -0.5
TOPK_MAX = 256
Q_BLOCK = 128
ROPE_THETA = 10000.0
LN_EPS = 1e-5
ALPHA = (2 * DEPTH) ** 0.25
BETA = (8 * DEPTH) ** -0.25
IN_SIZES = (POOL_WIDTH, POOL_WIDTH, ATTN_WIDTH, ATTN_WIDTH, ATTN_WIDTH, ATTN_WIDTH,
            N_IDX_HEADS * IDX_DIM, IDX_DIM, N_IDX_HEADS, D_MODEL, D_MODEL)
IN_WIDTH = sum(IN_SIZES)

kernel_name = 'hybrid_pool_dsa_decoder_step'


def layer_norm(x, g, b):
    xf = x.astype(jnp.float32)
    mu = jnp.mean(xf, axis=-1, keepdims=True)
    var = jnp.mean(jnp.square(xf - mu), axis=-1, keepdims=True)
    out = (xf - mu) * lax.rsqrt(var + LN_EPS) * g.astype(jnp.float32) + b.astype(jnp.float32)
    return out.astype(x.dtype)


def rope(x, pos):
    half = x.shape[-1] // 2
    inv = ROPE_THETA ** (-jnp.arange(half, dtype=jnp.float32) / half)
    ang = pos.astype(jnp.float32)[:, None] * inv[None, :]
    cos = jnp.cos(ang)[:, None, :]
    sin = jnp.sin(ang)[:, None, :]
    xf = x.astype(jnp.float32)
    x1, x2 = xf[..., :half], xf[..., half:]
    return jnp.concatenate([x1 * cos - x2 * sin, x2 * cos + x1 * sin], axis=-1).astype(x.dtype)


def project(x, w_in):
    h = jnp.einsum('btd,dn->btn', x, w_in)
    cuts, c = [], 0
    for s in IN_SIZES[:-1]:
        c += s
        cuts.append(c)
    return jnp.split(h, cuts, axis=-1)


def make_heads(q, k, v, iq, ik, pos):
    B, T = q.shape[:2]
    q = rope(q.reshape(B, T, N_HEADS, HEAD_DIM), pos)
    k = rope(k.reshape(B, T, N_HEADS, HEAD_DIM), pos)
    v = v.reshape(B, T, N_HEADS, HEAD_DIM)
    iq = rope(iq.reshape(B, T, N_IDX_HEADS, IDX_DIM), pos)
    ik = rope(ik[:, :, None, :], pos)[:, :, 0]
    return q, k, v, iq, ik


def pool_mixer(ext, pos, w_mix, scale):
    B = ext.shape[0]
    T = pos.shape[0]
    ef = ext.astype(jnp.float32)
    cs = jnp.concatenate([jnp.zeros((B, 1, POOL_WIDTH), jnp.float32), jnp.cumsum(ef, axis=1)], axis=1)
    end = cs[:, POOL_BUF + 1:]
    means = []
    for g, w in enumerate(POOL_WINDOWS):
        c0, c1 = g * POOL_GROUP, (g + 1) * POOL_GROUP
        start = cs[:, POOL_BUF + 1 - w: POOL_BUF + 1 - w + T, c0:c1]
        cnt = jnp.minimum(pos + 1, w).astype(jnp.float32)[None, :, None]
        means.append((end[:, :, c0:c1] - start) / cnt)
    d = (jnp.concatenate(means, axis=-1) - ef[:, POOL_BUF:]).astype(ext.dtype)
    d = d.reshape(B, T, N_POOL_GROUPS, POOL_GROUP)
    mixed = jnp.einsum('btgc,gce->btge', d, w_mix).reshape(B, T, POOL_WIDTH)
    return mixed * scale


def indexer_scores(iq, ik, iw):
    logits = jnp.einsum('...qhi,...si->...qhs', iq.astype(jnp.float32), ik.astype(jnp.float32))
    return jnp.einsum('...qhs,...qh->...qs', jax.nn.relu(logits), iw.astype(jnp.float32) * IDX_SCALE)


def attend(q, kg, vg, valid):
    s = jnp.einsum('...hd,...khd->...hk', q.astype(jnp.float32), kg.astype(jnp.float32)) * (HEAD_DIM ** -0.5)
    s = jnp.where(valid[..., None, :], s, -jnp.inf)
    p = jax.nn.softmax(s, axis=-1)
    return jnp.einsum('...hk,...khd->...hd', p.astype(vg.dtype), vg)


def prompt_sparse_attention(q, k, v, iq, ik, iw):
    B, T = q.shape[:2]
    nb = T // Q_BLOCK
    k_top = min(TOPK_MAX, T // 4)
    key_pos = jnp.arange(T, dtype=jnp.int32)

    def blk(a):
        return a.reshape((B * nb, Q_BLOCK) + a.shape[2:])

    b_idx = jnp.repeat(jnp.arange(B, dtype=jnp.int32), nb)
    t0 = jnp.tile(jnp.arange(nb, dtype=jnp.int32) * Q_BLOCK, B)

    def one(args):
        qb, iqb, iwb, b, s0 = args
        qpos = s0 + jnp.arange(Q_BLOCK, dtype=jnp.int32)
        sc = indexer_scores(iqb, ik[b], iwb)
        sc = jnp.where(key_pos[None, :] <= qpos[:, None], sc, -jnp.inf)
        _, idx = lax.top_k(sc, k_top)
        valid = idx <= qpos[:, None]
        return attend(qb, k[b, idx], v[b, idx], valid)

    out = lax.map(one, (blk(q), blk(iq), blk(iw), b_idx, t0))
    return out.reshape(B, T, ATTN_WIDTH)


def sample_sparse_attention(q, k_new, v_new, iq, ik_new, iw, cache_k, cache_v, cache_ik, page_table):
    DB, n = q.shape[:2]
    n_pages = page_table.shape[1]
    past = n_pages * PAGE_SIZE
    L = past + n
    k_top = min(TOPK_MAX, L // 4)
    ik_past = cache_ik[page_table].reshape(DB, past, IDX_DIM).astype(ik_new.dtype)
    ik_all = jnp.concatenate([ik_past, ik_new], axis=1)
    qpos = past + jnp.arange(n, dtype=jnp.int32)
    sc = indexer_scores(iq, ik_all, iw)
    sc = jnp.where(jnp.arange(L, dtype=jnp.int32)[None, None, :] <= qpos[None, :, None], sc, -jnp.inf)
    _, idx = lax.top_k(sc, k_top)
    valid = idx <= qpos[None, :, None]
    bi = jnp.arange(DB, dtype=jnp.int32)[:, None, None]
    pidx = jnp.minimum(idx, past - 1)
    phys = page_table[bi, pidx // PAGE_SIZE]
    off = pidx % PAGE_SIZE
    nidx = jnp.clip(idx - past, 0, n - 1)
    is_past = (idx < past)[..., None, None]
    kg = jnp.where(is_past, cache_k[phys, off].astype(k_new.dtype), k_new[bi, nidx])
    vg = jnp.where(is_past, cache_v[phys, off].astype(v_new.dtype), v_new[bi, nidx])
    return attend(q, kg, vg, valid).reshape(DB, n, ATTN_WIDTH)


def merge(x, pool_o, attn_o, zp, za, gp, ga, w_pool_out, w_attn_out, w_o, g, b):
    p = jnp.einsum('btc,cd->btd', pool_o * jax.nn.silu(zp), w_pool_out)
    a = jnp.einsum('btc,cd->btd', attn_o * jax.nn.silu(za), w_attn_out)
    m = jax.nn.sigmoid(gp) * p + jax.nn.sigmoid(ga) * a
    y = jnp.einsum('btd,de->bte', m, w_o)
    return layer_norm(ALPHA * x + y, g, b)


def setup_inputs(seed: int = 0) -> dict:
    key = jax.random.key(seed)
    ks = jax.random.split(key, 20)
    f32 = jnp.float32
    n_pages = PAST_LEN // PAGE_SIZE
    n_pool = (DEC_BATCH * n_pages * 5) // 4
    nrm = jax.random.normal
    perm = jax.random.permutation(ks[6], n_pool)
    page_table = perm[:DEC_BATCH * n_pages].reshape(DEC_BATCH, n_pages).astype(jnp.int32)
    return {
        'x_prompt': nrm(ks[0], (BATCH, SEQ, D_MODEL), f32),
        'x_sample': nrm(ks[1], (DEC_BATCH, DEC_SEQ, D_MODEL), f32),
        'cache_k': nrm(ks[2], (DEPTH, n_pool, PAGE_SIZE, N_HEADS, HEAD_DIM), f32),
        'cache_v': nrm(ks[3], (DEPTH, n_pool, PAGE_SIZE, N_HEADS, HEAD_DIM), f32),
        'cache_idx_k': nrm(ks[4], (DEPTH, n_pool, PAGE_SIZE, IDX_DIM), f32),
        'state_pool': nrm(ks[5], (DEPTH, DEC_BATCH, POOL_BUF, POOL_WIDTH), f32),
        'page_table': page_table,
        'ln_emb_g': 1.0 + 0.05 * nrm(ks[7], (D_MODEL,), f32),
        'ln_emb_b': 0.02 * nrm(ks[8], (D_MODEL,), f32),
        'w_in': nrm(ks[9], (DEPTH, D_MODEL, IN_WIDTH), f32) * D_MODEL ** -0.5,
        'w_pool_mix': nrm(ks[10], (DEPTH, N_POOL_GROUPS, POOL_GROUP, POOL_GROUP), f32) * POOL_GROUP ** -0.5,
        'pool_scale': 1.0 + 0.1 * nrm(ks[11], (DEPTH, POOL_WIDTH), f32),
        'w_pool_out': nrm(ks[12], (DEPTH, POOL_WIDTH, D_MODEL), f32) * (POOL_WIDTH ** -0.5 * BETA),
        'w_attn_out': nrm(ks[13], (DEPTH, ATTN_WIDTH, D_MODEL), f32) * (ATTN_WIDTH ** -0.5 * BETA),
        'w_o': nrm(ks[14], (DEPTH, D_MODEL, D_MODEL), f32) * (D_MODEL ** -0.5 * BETA),
        'ln_g': 1.0 + 0.05 * nrm(ks[15], (DEPTH, D_MODEL), f32),
        'ln_b': 0.02 * nrm(ks[16], (DEPTH, D_MODEL), f32),
    }


def reference(x_prompt, x_sample, cache_k, cache_v, cache_idx_k, state_pool, page_table,
              ln_emb_g, ln_emb_b, w_in, w_pool_mix, pool_scale, w_pool_out, w_attn_out,
              w_o, ln_g, ln_b):
    B, T = x_prompt.shape[:2]
    DB, n = x_sample.shape[:2]
    n_pages = page_table.shape[1]
    pos_p = jnp.arange(T, dtype=jnp.int32)
    pos_s = n_pages * PAGE_SIZE + jnp.arange(n, dtype=jnp.int32)
    xp = layer_norm(x_prompt, ln_emb_g, ln_emb_b)
    xs = layer_norm(x_sample, ln_emb_g, ln_emb_b)
    kp, vp, ikp, plp = [], [], [], []
    kss, vss, iks, pls = [], [], [], []
    for l in range(DEPTH):
        up, zp, q, k, v, za, iq, ik, iw, gp, ga = project(xp, w_in[l])
        q, k, v, iq, ik = make_heads(q, k, v, iq, ik, pos_p)
        ext = jnp.concatenate([jnp.zeros((B, POOL_BUF, POOL_WIDTH), up.dtype), up], axis=1)
        pool_o = pool_mixer(ext, pos_p, w_pool_mix[l], pool_scale[l])
        attn_o = prompt_sparse_attention(q, k, v, iq, ik, iw)
        kp.append(k)
        vp.append(v)
        ikp.append(ik)
        plp.append(ext[:, -POOL_BUF:])
        xp = merge(xp, pool_o, attn_o, zp, za, gp, ga, w_pool_out[l], w_attn_out[l], w_o[l], ln_g[l], ln_b[l])
        us, zps, qs, ks_, vs, zas, iqs, iks_, iws, gps, gas = project(xs, w_in[l])
        qs, ks_, vs, iqs, iks_ = make_heads(qs, ks_, vs, iqs, iks_, pos_s)
        ext_s = jnp.concatenate([state_pool[l].astype(us.dtype), us], axis=1)
        pool_os = pool_mixer(ext_s, pos_s, w_pool_mix[l], pool_scale[l])
        attn_os = sample_sparse_attention(qs, ks_, vs, iqs, iks_, iws, cache_k[l], cache_v[l],
                                          cache_idx_k[l], page_table)
        kss.append(ks_)
        vss.append(vs)
        iks.append(iks_)
        pls.append(ext_s[:, -POOL_BUF:])
        xs = merge(xs, pool_os, attn_os, zps, zas, gps, gas, w_pool_out[l], w_attn_out[l], w_o[l], ln_g[l], ln_b[l])
    return (xp, xs, jnp.stack(kp), jnp.stack(vp), jnp.stack(ikp), jnp.stack(plp),
            jnp.stack(kss), jnp.stack(vss), jnp.stack(iks), jnp.stack(pls))
```

```python
import os as _os
import itertools
import numpy as np
from contextlib import ExitStack
import concourse.bass as bass
import concourse.mybir as mybir
from concourse.bass_utils import run_bass_kernel_spmd

F32 = mybir.dt.float32
BF16 = mybir.dt.bfloat16
I32 = mybir.dt.int32
U32 = mybir.dt.uint32
ALU = mybir.AluOpType
AF = mybir.ActivationFunctionType
AX = mybir.AxisListType

ENGS = ("pe", "act", "dve", "pool", "sp")

D = 2048
SEQ = 2048
NT = 16
HT = 8
DEPTH = 2
PW = 1024
INW = 11344
C_UP, C_ZP, C_Q, C_K, C_V, C_ZA, C_IQ, C_IK, C_IW, C_GP, C_GA = (
    0, 1024, 2048, 3072, 4096, 5120, 6144, 7168, 7232, 7248, 9296)
IDX_SCALE = float((16 * 64) ** -0.5)
ALPHA = float((2 * DEPTH) ** 0.25)
LN_EPS = 1e-5
NEG = -1.0e30
NBIS = 30
SM_SCALE = float(128 ** -0.5)
PAST = 16384
NPG = 128


class Op:
    __slots__ = ("eng", "fn", "deps", "is_dma", "sem_idx", "count", "signal", "idx")

    def __init__(self, eng, fn, is_dma):
        self.eng = eng
        self.fn = fn
        self.deps = set()
        self.is_dma = is_dma
        self.sem_idx = None
        self.count = None
        self.signal = False
        self.idx = None


class Prog:
    N_DMA_SEMS = 14

    def __init__(self, nc):
        self.nc = nc
        self.ops = []
        self.last_writer = {}
        self.readers = {}
        self.dma_slot_last = {}
        self.dma_rr = {e: 0 for e in ENGS}
        self.final_ops = []
        self.after = None
        self.last_on = {}

    def barrier(self, fn):
        deps = set(self.last_on.values()) | set(self.dma_slot_last.values())
        o = Op("pool", fn, False)
        o.idx = len(self.ops)
        self.ops.append(o)
        o.deps = set(d for d in deps if d is not None)
        if self.after is not None:
            o.deps.add(self.after)
        self.after = o
        self.last_on["pool"] = o
        return o

    def _track(self, op, reads, writes):
        pr = [r for r in reads if isinstance(r, tuple) and r[0] == "pb"]
        if pr:
            reads = [r for r in reads if not (isinstance(r, tuple) and r[0] == "pb")]
            writes = list(writes) + pr
        for r in reads:
            w = self.last_writer.get(r)
            if w is not None:
                op.deps.add(w)
        for w_ in writes:
            w = self.last_writer.get(w_)
            if w is not None:
                op.deps.add(w)
            for rd in self.readers.get(w_, ()):
                op.deps.add(rd)
        for r in reads:
            self.readers.setdefault(r, []).append(op)
        for w_ in writes:
            self.last_writer[w_] = op
            self.readers[w_] = []
        op.deps.discard(op)

    def op(self, eng, fn, reads=(), writes=()):
        o = Op(eng, fn, False)
        o.idx = len(self.ops)
        self.ops.append(o)
        self._track(o, reads, writes)
        if self.after is not None:
            o.deps.add(self.after)
        self.last_on[eng] = o
        return o

    def dma(self, eng, fn, reads=(), writes=(), final=False):
        o = Op(eng, fn, True)
        o.idx = len(self.ops)
        self.ops.append(o)
        slot = self.dma_rr[eng] % self.N_DMA_SEMS
        self.dma_rr[eng] += 1
        o.sem_idx = slot
        prev = self.dma_slot_last.get((eng, slot))
        if prev is not None:
            o.deps.add(prev)
        self.dma_slot_last[(eng, slot)] = o
        self._track(o, reads, writes)
        if self.after is not None:
            o.deps.add(self.after)
        self.last_on[eng] = o
        if final:
            self.final_ops.append(o)
        return o

    def emit(self):
        nc = self.nc
        for o in self.ops:
            for d in o.deps:
                if d.is_dma or not (d.eng == "pe" and o.eng == "pe"):
                    d.signal = True
            if o.is_dma:
                o.signal = True
        per = {e: [] for e in ENGS}
        for o in self.ops:
            per[o.eng].append(o)
        for e in ENGS:
            c = 0
            for o in per[e]:
                if (not o.is_dma) and o.signal:
                    c += 1
                    o.count = c
        dma_cnt = {}
        for o in self.ops:
            if o.is_dma:
                k = (o.eng, o.sem_idx)
                dma_cnt[k] = dma_cnt.get(k, 0) + 16
                o.count = dma_cnt[k]
        with ExitStack() as es:
            esem = {e: es.enter_context(nc.semaphore("s_" + e)) for e in ENGS}
            dsem = {}
            for e in ENGS:
                for s in range(min(self.N_DMA_SEMS, self.dma_rr[e])):
                    dsem[(e, s)] = es.enter_context(nc.semaphore("d_%s_%d" % (e, s)))
            block = es.enter_context(nc.Block())

            def semof(d):
                return dsem[(d.eng, d.sem_idx)] if d.is_dma else esem[d.eng]

            def body(ename):
                def f(eng):
                    waited = {}
                    for o in per[ename]:
                        for d in sorted(o.deps, key=lambda x: x.idx):
                            if (not d.is_dma) and d.eng == "pe" and ename == "pe":
                                continue
                            key = (d.eng, d.sem_idx) if d.is_dma else d.eng
                            if waited.get(key, 0) >= d.count:
                                continue
                            eng.wait_ge(semof(d), d.count)
                            waited[key] = d.count
                        ins = o.fn(eng)
                        if o.signal:
                            ins.then_inc(semof(o), 16 if o.is_dma else 1)
                    if ename == "sp":
                        for k_, v_ in dma_cnt.items():
                            eng.wait_ge(dsem[k_], v_)
                return f

            block.tensor(body("pe"))
            block.scalar(body("act"))
            block.vector(body("dve"))
            block.gpsimd(body("pool"))
            block.sync(body("sp"))


def _rope_tab(pos, half):
    inv = (np.float32(10000.0) ** (-np.arange(half, dtype=np.float32) / np.float32(half))).astype(np.float32)
    ang = pos.astype(np.float32)[:, None] * inv[None, :]
    return np.cos(ang).astype(np.float32), np.sin(ang).astype(np.float32)


def _consts():
    c = {}
    pos_p = np.arange(SEQ)
    pos_s = PAST + np.arange(4)
    c["rc128"], c["rs128"] = _rope_tab(pos_p, 64)
    c["rc64"], c["rs64"] = _rope_tab(pos_p, 32)
    c["src128"], c["srs128"] = _rope_tab(pos_s, 64)
    c["src64"], c["srs64"] = _rope_tab(pos_s, 32)
    bands = np.zeros((12, 128, 128), np.float32)
    sb_st = np.zeros((4, 15, 4), np.float32)
    sb_new = np.zeros((4, 4, 4), np.float32)
    for g, w in enumerate((2, 4, 8, 16)):
        for t in range(128):
            for s in range(max(0, t - w + 1), t + 1):
                bands[g, s, t] += 1.0 / w
                bands[4 + g, s, t] += 1.0 / min(t + 1, w)
            bands[g, t, t] -= 1.0
            bands[4 + g, t, t] -= 1.0
            for j in range(t + 1, w):
                bands[8 + g, 128 + t - j, t] += 1.0 / w
        for t in range(4):
            for j in range(w):
                r = 15 + t - j
                if r >= 15:
                    sb_new[g, r - 15, t] += 1.0 / w
                else:
                    sb_st[g, r, t] += 1.0 / w
            sb_new[g, t, t] -= 1.0
    c["bands"] = bands
    c["sb_st"] = sb_st
    c["sb_new"] = sb_new
    cm = np.zeros((128, 128), np.float32)
    cm[np.triu_indices(128, 1)] = NEG
    c["causal"] = cm
    c["pow2"] = (0.5 ** np.arange(1, 33, dtype=np.float32)).astype(np.float32)[None, :]
    esel = np.zeros((64, 32, 128), np.float32)
    for q in range(4):
        for hh in range(16):
            for ch in range(32):
                esel[q * 16 + hh, ch, q * 32 + ch] = 1.0
    c["esel"] = esel.reshape(64, 4096)
    negm = np.full((128, 4), NEG, np.float32)
    for q in range(4):
        negm[q * 32, 0:q + 1] = 0.0
    c["negm"] = negm
    bo = np.zeros((128, 128), np.float32)
    for q in range(4):
        bo[q * 32:(q + 1) * 32, q * 32:(q + 1) * 32] = 1.0
    c["bones"] = bo
    dm = np.zeros((32, 8), np.float32)
    for hh in range(8):
        dm[hh * 4:(hh + 1) * 4, hh] = 1.0
    c["dmaskc"] = dm
    return c


def build(NPOOL=1280, with_sample=True, stop=None):
    nc = bass.Bass("TRN2", target_bir_lowering=False)
    P = Prog(nc)

    def din(name, shape, dt=F32):
        return nc.dram_tensor(name, list(shape), dt, kind="ExternalInput").ap()

    def dout(name, shape, dt=F32):
        return nc.dram_tensor(name, list(shape), dt, kind="ExternalOutput").ap()

    def dscr(name, shape, dt=F32):
        return nc.dram_tensor(name, list(shape), dt, kind="Internal").ap()

    xp = din("xp", [SEQ, D])
    xs_in = din("xs", [4, D])
    ck = din("ck", [DEPTH * NPOOL * 128, 1024])
    cv = din("cv", [DEPTH * NPOOL * 128, 1024])
    cik = din("cik", [DEPTH * NPOOL, 8192])
    stp = din("stp", [DEPTH, 15, PW])
    pt = din("pt", [128, 1], I32)
    lneg = din("lneg", [1, D])
    lneb = din("lneb", [1, D])
    w_in = din("w_in", [DEPTH, D, INW])
    w_mix = din("w_mix", [DEPTH, 4, 256, 256])
    pscale = din("pscale", [DEPTH, PW])
    w_po = din("w_po", [DEPTH, PW, D])
    w_ao = din("w_ao", [DEPTH, PW, D])
    w_o = din("w_o", [DEPTH, D, D])
    lng = din("lng", [DEPTH, D])
    lnb = din("lnb", [DEPTH, D])
    rc128 = din("rc128", [SEQ, 64]); rs128 = din("rs128", [SEQ, 64])
    rc64 = din("rc64", [SEQ, 32]); rs64 = din("rs64", [SEQ, 32])
    src128 = din("src128", [4, 64]); srs128 = din("srs128", [4, 64])
    src64 = din("src64", [4, 32]); srs64 = din("srs64", [4, 32])
    bands_d = din("bands", [12, 128, 128])
    sb_st_d = din("sb_st", [4, 15, 4])
    sb_new_d = din("sb_new", [4, 4, 4])
    causal_d = din("causal", [128, 128])
    pow2_d = din("pow2", [1, 32])
    esel_d = din("esel", [64, 4096])
    negm_d = din("negm", [128, 4])
    bones_d = din("bones", [128, 128])
    dmask_d = din("dmaskc", [32, 8])

    y_p = dout("y_p", [SEQ, D]); y_s = dout("y_s", [4, D])
    k_p = dout("k_p", [DEPTH, SEQ, 1024]); v_p = dout("v_p", [DEPTH, SEQ, 1024])
    ik_p = dout("ik_p", [DEPTH, SEQ, 64]); pl_p = dout("pl_p", [DEPTH, 15, PW])
    k_s = dout("k_s", [DEPTH, 4, 1024]); v_s = dout("v_s", [DEPTH, 4, 1024])
    ik_s = dout("ik_s", [DEPTH, 4, 64]); pl_s = dout("pl_s", [DEPTH, 15, PW])

    dbg = stop is not None or _os.environ.get("DBG")
    if dbg:
        dbg_ag = dout("dbg_ag", [SEQ, 1024], BF16)
        dbg_pg = dout("dbg_pg", [SEQ, 1024], BF16)
    Xs = dscr("Xs", [2, SEQ, D])
    xTs = dscr("xTs", [2, 128, NT, 16, 128], BF16)
    kTs = dscr("kTs", [128, NT, 8, 128], BF16)
    Vs = dscr("Vs", [SEQ, 1024], BF16)
    ikTs = dscr("ikTs", [128, SEQ], BF16)

    es = ExitStack()
    with es:
        def sb(name, shape, dt):
            return es.enter_context(nc.sbuf_tensor(name, list(shape), dt))

        def ps(name, shape, dt):
            return es.enter_context(nc.psum_tensor(name, list(shape), dt))

        R0 = sb("R0", [128, 16384], BF16)
        R1 = sb("R1", [128, 16384], BF16)
        R2 = sb("R2", [128, 24576], BF16)
        RW = sb("RW", [128, 24576], BF16)
        xT = R0[:, :].rearrange("p (t k n) -> p t k n", t=HT, k=16)
        mT = xT
        up_sb = R1[:, 0:8192].rearrange("p (t n) -> p t n", t=HT)
        zp_sb = R1[:, 8192:16384].rearrange("p (t n) -> p t n", t=HT)
        m_sb = R1[:, :].rearrange("p (t n) -> p t n", t=HT)
        qT = R2[:, 0:8192].rearrange("p (t h n) -> p t h n", t=HT, h=8)
        AgT = qT
        iqT = R2[:, 8192:16384].rearrange("p (t h n) -> p t h n", t=HT, h=8)
        PgT = iqT
        za_sb = R2[:, 16384:24576].rearrange("p (t n) -> p t n", t=HT)

        ikT2 = sb("ikT2", [128, SEQ], BF16)
        ident = sb("ident", [128, 128], BF16)
        identf = sb("identf", [128, 128], F32)
        bands = sb("bands_sb", [128, 12, 128], BF16)
        causal = sb("causal_sb", [128, 128], F32)
        pow2 = sb("pow2_sb", [128, 32], F32)
        cos128 = sb("cos128", [128, HT, 64], F32); sin128 = sb("sin128", [128, HT, 64], F32)
        cos64 = sb("cos64", [128, HT, 32], F32); sin64 = sb("sin64", [128, HT, 32], F32)
        iw_sb = sb("iw_sb", [128, HT, 16], F32)
        up_last = sb("up_last", [128, 1024], BF16)
        ones_bf = sb("ones_bf", [128, 1], BF16)
        stat = sb("stat", [128, 4, 6], F32)
        mv = sb("mv", [128, 2], F32)
        rstd = sb("rstd", [128, 1], F32)
        eps_t = sb("eps_t", [128, 1], F32)
        xt_st = [RW[:, i * 4096:(i + 1) * 4096].bitcast(F32) for i in range(2)]
        xbf_st = sb("xbf_st", [128, D], BF16)
        ropeA = sb("ropeA", [128, 512], F32)
        ropeB = sb("ropeB", [128, 512], F32)
        st_f32 = [sb("st_f32_%d" % i, [128, 512], F32) for i in range(2)]
        st_bf = [sb("st_bf%d" % i, [128, 1024], BF16) for i in range(2)]
        ik_st = sb("ik_st", [128, 64], F32)
        ik2_bf = sb("ik2_bf", [128, 128], BF16)
        tr_st = [sb("tr_st%d" % i, [128, 8, 128], BF16) for i in range(2)]
        gb_bc = R1[:, 8192:16384].bitcast(F32).rearrange("p (a n) -> p a n", a=2)
        GBKEYS = ["gb"] + [("zp", t_) for t_ in range(HT)] + [("m", t_) for t_ in range(4, 8)]

        pb = [ps("pb%d" % i, [128, 512], F32) for i in range(8)]

        def pbf(i):
            return pb[i][:, :].bitcast(BF16)

        P.op("pool", lambda e: e.memset(identf[:], 0.0), writes=["identf"])
        P.op("pool", lambda e: e.affine_select(identf[:], identf[:], [[1, 128]], ALU.not_equal, 1.0,
                                               base=0, channel_multiplier=-1), reads=["identf"], writes=["identf"])
        P.op("dve", lambda e: e.tensor_copy(ident[:], identf[:]), reads=["identf"], writes=["ident"])
        P.op("dve", lambda e: e.memset(ones_bf[:], 1.0), writes=["ones"])
        P.op("dve", lambda e: e.memset(eps_t[:], LN_EPS), writes=["eps"])
        P.dma("pool", lambda e: e.dma_start(out=bands[:], in_=bands_d.rearrange("g s t -> s g t")), writes=["bands"])
        P.dma("sp", lambda e: e.dma_start(out=causal[:], in_=causal_d), writes=["causal"])
        P.dma("sp", lambda e: e.dma_start(out=pow2[:], in_=pow2_d.partition_broadcast(128)), writes=["pow2"])

        rr = {"pb": 0, "st": 0, "tr": 0, "xt": 0, "w": 0}
        fdummy = sb("fdummy", [128, 1], F32)
        P2KEYS = ["wmix", "psc", ("dT", 0), ("dT", 1), ("pgf", 0), ("pgf", 1)]

        def fence(old, new):
            P.op("pool", lambda e: e.memset(fdummy[:], 0.0), writes=list(old) + list(new))

        def layer_norm_tile(src_key, x_ap, g_key, out_key):
            for j in range(4):
                P.op("dve", lambda e, j=j: e.bn_stats(stat[:, j, :], x_ap[:, j * 512:(j + 1) * 512]),
                     reads=[src_key], writes=[("stat", j)])
            P.op("dve", lambda e: e.bn_aggr(mv[:], stat[:].rearrange("p a b -> p (a b)")),
                 reads=[("stat", j) for j in range(4)], writes=["mv"])
            P.op("act", lambda e: e.activation(rstd[:], mv[:, 1:2], AF.Sqrt, bias=eps_t[:]), reads=["mv", "eps"], writes=["rstd"])
            P.op("dve", lambda e: e.reciprocal(rstd[:], rstd[:]), reads=["rstd"], writes=["rstd"])
            P.op("dve", lambda e: e.tensor_scalar(x_ap, x_ap, mv[:, 0:1], rstd[:], ALU.subtract, ALU.mult),
                 reads=[src_key, "mv", "rstd"], writes=[src_key])
            P.op("pool", lambda e: e.tensor_tensor(x_ap, x_ap, gb_bc[:, 0, :], ALU.mult), reads=[src_key] + GBKEYS, writes=[src_key])
            P.op("dve", lambda e: e.tensor_tensor(x_ap, x_ap, gb_bc[:, 1, :], ALU.add), reads=[src_key] + GBKEYS, writes=[out_key, src_key])

        def to_xT_scratch(x_ap, x_key, par, gt, np_=128):
            P.op("act", lambda e: e.copy(xbf_st[:np_, :], x_ap), reads=[x_key], writes=["xbf"])
            for g in range(2):
                bk = 2 + g
                for j in range(8):
                    kt = g * 8 + j
                    P.op("pe", lambda e, kt=kt, j=j, bk=bk: e.transpose(pbf(bk)[:, j * 128:j * 128 + np_], xbf_st[:np_, kt * 128:(kt + 1) * 128], ident[:np_, :np_]),
                         reads=["xbf", "ident"], writes=[("pb", bk)])
                tr = tr_st[g]
                P.op("act" if g == 0 else "dve",
                     (lambda e, tr=tr, bk=bk: e.copy(tr[:], pbf(bk).rearrange("p (j n) -> p j n", j=8))) if g == 0 else
                     (lambda e, tr=tr, bk=bk: e.tensor_copy(tr[:], pbf(bk).rearrange("p (j n) -> p j n", j=8))),
                     reads=[("pb", bk)], writes=[("tr", g)])
                P.dma("sp", lambda e, tr=tr, g=g: e.dma_start(out=xTs[par, :, gt, g * 8:(g + 1) * 8, :], in_=tr[:]),
                      reads=[("tr", g)], writes=[("xTs", par, gt)])

        P.dma("sp", lambda e: e.dma_start(out=gb_bc[:, 0, :], in_=lneg.partition_broadcast(128)), writes=GBKEYS)
        P.dma("sp", lambda e: e.dma_start(out=gb_bc[:, 1, :], in_=lneb.partition_broadcast(128)), writes=GBKEYS)
        for gt in range(NT):
            xt = xt_st[gt % 2]
            xk = ("xt", gt % 2)
            P.dma("sp", lambda e, xt=xt, gt=gt: e.dma_start(out=xt[:], in_=xp[gt * 128:(gt + 1) * 128, :]), writes=[xk])
            layer_norm_tile(xk, xt[:], "gb", xk)
            P.dma("sp", lambda e, xt=xt, gt=gt: e.dma_start(out=Xs[0, gt * 128:(gt + 1) * 128, :], in_=xt[:]),
                  reads=[xk], writes=[("Xs", 0, gt)])
            to_xT_scratch(xt[:], xk, 0, gt)

        def rope_ps(bk, nh, half, cos_ap, sin_ap, out_ap, np_=128):
            n = nh * 2 * half
            src = pb[bk][:np_, 0:n].rearrange("p (h two f) -> p h two f", h=nh, two=2)
            cb = cos_ap.rearrange("p (a b f) -> p a b f", a=1, b=1).to_broadcast([np_, nh, 2, half])
            sbb = sin_ap.rearrange("p (a b f) -> p a b f", a=1, b=1).to_broadcast([np_, nh, 2, half])
            A = ropeA[:np_, 0:n].rearrange("p (h two f) -> p h two f", h=nh, two=2)
            B = ropeB[:np_, 0:n].rearrange("p (h two f) -> p h two f", h=nh, two=2)
            o = out_ap.rearrange("p (h two f) -> p h two f", h=nh, two=2)
            P.op("dve", lambda e: e.tensor_tensor(A, src, cb, ALU.mult), reads=[("pb", bk), "ropetab"], writes=["ropeA"])
            P.op("dve", lambda e: e.tensor_tensor(B, src, sbb, ALU.mult), reads=[("pb", bk), "ropetab"], writes=["ropeB"])
            return A, B, o

        def rope_finish(A, B, o, okey):
            P.op("pool", lambda e: e.tensor_tensor(o[:, :, 0, :], A[:, :, 0, :], B[:, :, 1, :], ALU.subtract),
                 reads=["ropeA", "ropeB"], writes=[okey])
            P.op("pool", lambda e: e.tensor_tensor(o[:, :, 1, :], A[:, :, 1, :], B[:, :, 0, :], ALU.add),
                 reads=["ropeA", "ropeB"], writes=[okey, "ropeA", "ropeB"])

        def transposes8(src_bf, src_key, dst_ap, dst_key, bk, evac_eng="act"):
            for j in range(8):
                P.op("pe", lambda e, j=j: e.transpose(pbf(bk)[:, j * 128:(j + 1) * 128], src_bf[:, j * 128:(j + 1) * 128], ident[:]),
                     reads=[src_key, "ident"], writes=[("pb", bk)])
            if evac_eng == "act":
                P.op("act", lambda e: e.copy(dst_ap, pbf(bk).rearrange("p (j n) -> p j n", j=8)), reads=[("pb", bk)], writes=[dst_key])
            else:
                P.op("dve", lambda e: e.tensor_copy(dst_ap, pbf(bk).rearrange("p (j n) -> p j n", j=8)), reads=[("pb", bk)], writes=[dst_key])

        def transposes4(src_bf, src_key, dst_ap, dst_key, bk, evac_eng="act"):
            for j in range(4):
                P.op("pe", lambda e, j=j: e.transpose(pbf(bk)[:, j * 128:(j + 1) * 128], src_bf[:, j * 128:(j + 1) * 128], ident[:]),
                     reads=[src_key, "ident"], writes=[("pb", bk)])
            if evac_eng == "act":
                P.op("act", lambda e: e.copy(dst_ap, pbf(bk)[:, 0:512].rearrange("p (j n) -> p j n", j=4)), reads=[("pb", bk)], writes=[dst_key])
            else:
                P.op("dve", lambda e: e.tensor_copy(dst_ap, pbf(bk)[:, 0:512].rearrange("p (j n) -> p j n", j=4)), reads=[("pb", bk)], writes=[dst_key])

        P1_BLOCKS = []
        for kind, c0 in (("up", C_UP), ("zp", C_ZP), ("q", C_Q), ("k", C_K), ("v", C_V), ("za", C_ZA), ("iq", C_IQ)):
            P1_BLOCKS.append((kind, c0, 512, 0))
            P1_BLOCKS.append((kind, c0 + 512, 512, 1))
        P1_BLOCKS.append(("ikiw", C_IK, 80, 0))
        if _os.environ.get("P1KINDS"):
            P1_BLOCKS = [b for b in P1_BLOCKS if b[0] in _os.environ["P1KINDS"].split(",")]

        wbuf = [RW[:, i * 8192:(i + 1) * 8192].rearrange("p (k n) -> p k n", k=16) for i in range(3)]

        def load_w(dram_ap_2d, krows, ncols, key):
            i = rr["w"] % 3
            rr["w"] += 1
            buf = wbuf[i]
            kt = krows // 128
            P.dma("pool", lambda e: e.dma_start(out=buf[:, 0:kt, 0:ncols], in_=dram_ap_2d.rearrange("(k p) n -> p k n", p=128)),
                  writes=[("wbuf", i)])
            return buf, ("wbuf", i)

        fence([("xt", 0), ("xt", 1)], [("wbuf", i_) for i_ in range(3)])
        def pass_body(l, h):
            if stop == "P0":
                return
            par = l % 2
            last = (l == DEPTH - 1)
            if True:
                P.dma("sp", lambda e, h=h, par=par: e.dma_start(out=xT[:], in_=xTs[par, :, h * HT:(h + 1) * HT, :, :]),
                      reads=[("xTs", par, h * HT + t) for t in range(HT)], writes=[("xT", t) for t in range(HT)])
                for nm, tab, src, hf in (("cos128", cos128, rc128, 64), ("sin128", sin128, rs128, 64),
                                         ("cos64", cos64, rc64, 32), ("sin64", sin64, rs64, 32)):
                    P.dma("sp", lambda e, tab=tab, src=src, h=h: e.dma_start(
                        out=tab[:], in_=src[h * 1024:(h + 1) * 1024, :].rearrange("(t p) f -> p t f", p=128)),
                        writes=["ropetab"])
                if h == 1:
                    P.dma("sp", lambda e: e.dma_start(out=ikT2[:, 0:1024], in_=ikTs[:, 0:1024]),
                          reads=[("ikTs", c) for c in range(HT)], writes=[("ikT2", c) for c in range(HT)])

                for (kind, c0, ncols, sub) in P1_BLOCKS:
                    wb, wkey = load_w(w_in[l, :, c0:c0 + ncols], D, ncols, None)
                    for t in range(HT):
                        gt = h * HT + t
                        bk = rr["pb"] % 2
                        rr["pb"] += 1
                        for kt in range(16):
                            P.op("pe", lambda e, t=t, kt=kt, bk=bk, wb=wb, ncols=ncols: e.matmul(
                                pb[bk][:, 0:ncols], lhsT=xT[:, t, kt, :], rhs=wb[:, kt, 0:ncols], start=(kt == 0), stop=(kt == 15)),
                                reads=[("xT", t), wkey], writes=[("pb", bk)])
                        pk = ("pb", bk)
                        cs = slice(sub * 512, sub * 512 + 512)
                        if kind == "up":
                            P.op("act", lambda e, t=t, bk=bk, cs=cs: e.copy(up_sb[:, t, cs], pb[bk][:, :]), reads=[pk], writes=[("up", t)])
                            if h == 1 and t == HT - 1:
                                sf = st_f32[sub]
                                P.op("dve", lambda e, bk=bk, sf=sf: e.tensor_copy(sf[:], pb[bk][:, :]), reads=[pk], writes=[("stf", sub)])
                                P.dma("sp", lambda e, sf=sf, cs=cs: e.dma_start(out=pl_p[l, :, cs], in_=sf[113:128, :]),
                                      reads=[("stf", sub)], final=True)
                        elif kind == "zp":
                            P.op("act", lambda e, t=t, bk=bk, cs=cs: e.activation(zp_sb[:, t, cs], pb[bk][:, :], AF.Silu), reads=[pk], writes=[("zp", t)])
                        elif kind == "za":
                            P.op("act", lambda e, t=t, bk=bk, cs=cs: e.activation(za_sb[:, t, cs], pb[bk][:, :], AF.Silu), reads=[pk], writes=[("za", t)])
                        elif kind == "q":
                            A, B, o = rope_ps(bk, 4, 64, cos128[:, t, :], sin128[:, t, :], st_bf[0][:, cs])
                            rope_finish(A, B, o, ("stbf", 0))
                            transposes4(st_bf[0][:, cs], ("stbf", 0), qT[:, t, sub * 4:(sub + 1) * 4, :], ("qT", t), 2)
                        elif kind == "iq":
                            A, B, o = rope_ps(bk, 8, 32, cos64[:, t, :], sin64[:, t, :], st_bf[1][:, cs])
                            rope_finish(A, B, o, ("stbf", 1))
                            transposes4(st_bf[1][:, cs], ("stbf", 1), iqT[:, t, sub * 4:(sub + 1) * 4, :], ("iqT", t), 3, evac_eng="dve")
                        elif kind == "k":
                            sf = st_f32[sub]
                            A, B, o = rope_ps(bk, 4, 64, cos128[:, t, :], sin128[:, t, :], sf[:, :])
                            rope_finish(A, B, o, ("stf", sub))
                            P.dma("sp", lambda e, sf=sf, gt=gt, cs=cs: e.dma_start(out=k_p[l, gt * 128:(gt + 1) * 128, cs], in_=sf[:]),
                                  reads=[("stf", sub)], final=True)
                            P.op("act", lambda e, sf=sf, cs=cs: e.copy(st_bf[0][:, cs], sf[:]), reads=[("stf", sub)], writes=[("stbf", 0)])
                            tr = tr_st[0]
                            transposes4(st_bf[0][:, cs], ("stbf", 0), tr[:, 0:4, :], ("tr", 0), 2)
                            P.dma("sp", lambda e, tr=tr, gt=gt, sub=sub: e.dma_start(out=kTs[:, gt, sub * 4:(sub + 1) * 4, :], in_=tr[:, 0:4, :]),
                                  reads=[("tr", 0)], writes=[("kTs", gt)])
                        elif kind == "v":
                            sf = st_f32[sub]
                            P.op("act", lambda e, bk=bk, sf=sf: e.copy(sf[:], pb[bk][:, :]), reads=[pk], writes=[("stf", sub)])
                            if not _os.environ.get("SKIPVP"):
                                P.dma("sp", lambda e, sf=sf, gt=gt, cs=cs: e.dma_start(out=v_p[l, gt * 128:(gt + 1) * 128, cs], in_=sf[:]),
                                      reads=[("stf", sub)], final=True)
                            P.op("dve", lambda e, bk=bk, cs=cs: e.tensor_copy(st_bf[1][:, cs], pb[bk][:, :]), reads=[pk], writes=[("stbf", 1)])
                            P.dma("sp", lambda e, gt=gt, cs=cs: e.dma_start(out=Vs[gt * 128:(gt + 1) * 128, cs], in_=st_bf[1][:, cs]),
                                  reads=[("stbf", 1)], writes=[("Vs", gt)])
                        elif kind == "ikiw":
                            A, B, o = rope_ps(bk, 1, 32, cos64[:, t, :], sin64[:, t, :], ik_st[:, :])
                            rope_finish(A, B, o, "ikst")
                            P.dma("sp", lambda e, gt=gt: e.dma_start(out=ik_p[l, gt * 128:(gt + 1) * 128, :], in_=ik_st[:]),
                                  reads=["ikst"], final=True)
                            P.op("act", lambda e: e.copy(ik2_bf[:, 0:64], ik_st[:]), reads=["ikst"], writes=["ik2"])
                            P.op("act", lambda e: e.copy(ik2_bf[:, 64:128], ik_st[:]), reads=["ikst"], writes=["ik2"])
                            P.op("pe", lambda e: e.transpose(pbf(3)[:, 0:128], ik2_bf[:], ident[:]), reads=["ik2", "ident"], writes=[("pb", 3)])
                            P.op("act", lambda e, gt=gt: e.copy(ikT2[:, gt * 128:(gt + 1) * 128], pbf(3)[:, 0:128]),
                                 reads=[("pb", 3)], writes=[("ikT2", gt)])
                            if h == 0:
                                P.dma("sp", lambda e, gt=gt: e.dma_start(out=ikTs[:, gt * 128:(gt + 1) * 128], in_=ikT2[:, gt * 128:(gt + 1) * 128]),
                                      reads=[("ikT2", gt)], writes=[("ikTs", gt)])
                            P.op("dve", lambda e, t=t, bk=bk: e.tensor_scalar(iw_sb[:, t, :], pb[bk][:, 64:80], IDX_SCALE, None, ALU.mult),
                                 reads=[pk], writes=[("iw", t)])

                if stop == "P1":
                    return
                sc = RW[:, 0:4096].bitcast(F32)
                junk = RW[:, 4096:6144]
                maskb = RW[:, 6144:8192]
                maskT = RW[:, 8192:10240].rearrange("p (c n) -> p c n", c=NT)
                Rb = [RW[:, 10240 + i * 512:10240 + (i + 1) * 512] for i in range(4)]
                diagW = RW[:, 12288:14336].rearrange("p (h n) -> p h n", h=16)
                Eb = [RW[:, 14336 + i * 1024:14336 + (i + 1) * 1024].rearrange("p (h n) -> p h n", h=8) for i in range(2)]
                PTb = [RW[:, 16384 + i * 1024:16384 + (i + 1) * 1024].rearrange("p (h n) -> p h n", h=8) for i in range(2)]
                kTc = [RW[:, 18432 + i * 1024:18432 + (i + 1) * 1024].rearrange("p (h n) -> p h n", h=8) for i in range(2)]
                Vc = [RW[:, 20480 + i * 1024:20480 + (i + 1) * 1024].rearrange("p (h n) -> p h n", h=8) for i in range(2)]
                smallf = RW[:, 22528:22528 + 256].bitcast(F32)
                lo = smallf[:, 0:1]; mid = smallf[:, 1:2]; cnt = smallf[:, 2:3]; wtab = smallf[:, 8:8 + 32]
                rmax = smallf[:, 3:4]; rden = smallf[:, 48:56]
                gei = RW[:, 22528 + 256:22528 + 258].bitcast(I32)
                On = RW[:, 23040:24064].bitcast(F32)
                ag_bf = st_bf[0]
                wk = ["wk_sc", "wk_junk", "wk_mask", "wk_maskT", "wk_diag", "wk_small"]
                P3KEYS = (wk + ["wk_On"] + [("Rb", i_) for i_ in range(4)] + [("Eb", i_, j_) for i_ in range(2) for j_ in range(2)]
                          + [("PT", i_) for i_ in range(2)] + [("kTc", i_) for i_ in range(2)] + [("Vc", i_) for i_ in range(2)])
                WKEYS = [("wbuf", i_) for i_ in range(3)]
                fence(WKEYS + P2KEYS, P3KEYS)
                rrR = 0
                for t in range(HT):
                    i = h * HT + t
                    L = (i + 1) * 128
                    for hh in range(16):
                        P.op("dve", lambda e, hh=hh, t=t: e.tensor_scalar(diagW[:, hh, :], ident[:], iw_sb[:, t, hh:hh + 1], None, ALU.mult),
                             reads=["ident", ("iw", t)], writes=["wk_diag"])
                    nkb = (L + 511) // 512
                    for kb in range(nkb):
                        k0 = kb * 512
                        kn = min(512, L - k0)
                        for hh in range(16):
                            g, half = hh // 2, hh % 2
                            bk = rr["pb"] % 2
                            rr["pb"] += 1
                            P.op("pe", lambda e, t=t, g=g, half=half, bk=bk, k0=k0, kn=kn: e.matmul(
                                pb[bk][:, 0:kn], lhsT=iqT[half * 64:(half + 1) * 64, t, g, :], rhs=ikT2[half * 64:(half + 1) * 64, k0:k0 + kn],
                                start=True, stop=True),
                                reads=[("iqT", t)] + [("ikT2", c) for c in range(k0 // 128, (k0 + kn) // 128)], writes=[("pb", bk)])
                            rb = Rb[rrR % 4]
                            rkey = ("Rb", rrR % 4)
                            rrR += 1
                            if hh % 2 == 0:
                                P.op("act", lambda e, rb=rb, bk=bk, kn=kn: e.activation(rb[:, 0:kn], pb[bk][:, 0:kn], AF.Relu),
                                     reads=[("pb", bk)], writes=[rkey])
                            else:
                                P.op("dve", lambda e, rb=rb, bk=bk, kn=kn: e.tensor_scalar(rb[:, 0:kn], pb[bk][:, 0:kn], 0.0, None, ALU.max),
                                     reads=[("pb", bk)], writes=[rkey])
                            P.op("pe", lambda e, hh=hh, rb=rb, kn=kn: e.matmul(pb[2][:, 0:kn], lhsT=diagW[:, hh, :], rhs=rb[:, 0:kn],
                                                                           start=(hh == 0), stop=(hh == 15)),
                                 reads=["wk_diag", rkey], writes=[("pb", 2)])
                        P.op("act", lambda e, k0=k0, kn=kn: e.copy(sc[:, k0:k0 + kn], pb[2][:, 0:kn]), reads=[("pb", 2)], writes=["wk_sc"])
                    P.op("dve", lambda e, L=L: e.tensor_tensor(sc[:, L - 128:L], sc[:, L - 128:L], causal[:], ALU.add),
                         reads=["wk_sc", "causal"], writes=["wk_sc"])
                    if i >= 2:
                        Lf = L - 128
                        P.op("dve", lambda e, L=L: e.tensor_reduce(rmax, sc[:, 0:L], AX.X, ALU.max), reads=["wk_sc"], writes=["wk_small"])
                        P.op("dve", lambda e, Lf=Lf: e.tensor_reduce(lo, sc[:, 0:Lf], AX.X, ALU.min), reads=["wk_sc"], writes=["wk_small"])
                        P.op("dve", lambda e: e.tensor_tensor(rmax, rmax, lo, ALU.subtract), reads=["wk_small"], writes=["wk_small"])
                        P.op("dve", lambda e: e.tensor_scalar(wtab, pow2[:, 0:32], rmax, None, ALU.mult), reads=["wk_small", "pow2"], writes=["wk_small"])
                        for n in range(NBIS):
                            P.op("dve", lambda e, n=n: e.tensor_tensor(mid, lo, wtab[:, n:n + 1], ALU.add), reads=["wk_small"], writes=["wk_small"])
                            P.op("dve", lambda e, L=L: e.tensor_scalar(junk[:, 0:L], sc[:, 0:L], mid, None, ALU.is_ge, ALU.add, accum_out=cnt),
                                 reads=["wk_sc", "wk_small"], writes=["wk_junk", "wk_small"])
                            P.op("dve", lambda e: e.tensor_scalar(gei, cnt, 255.5, None, ALU.is_ge), reads=["wk_small"], writes=["wk_small"])
                            P.op("dve", lambda e: e.copy_predicated(lo, gei, mid), reads=["wk_small"], writes=["wk_small"])
                        P.op("dve", lambda e, L=L: e.tensor_scalar(maskb[:, 0:L], sc[:, 0:L], lo, None, ALU.is_ge),
                             reads=["wk_sc", "wk_small"], writes=["wk_mask"])
                    else:
                        P.op("dve", lambda e, L=L: e.tensor_scalar(maskb[:, 0:L], sc[:, 0:L], -1.0e29, None, ALU.is_ge),
                             reads=["wk_sc"], writes=["wk_mask"])
                    for c0 in range(0, i + 1, 8):
                        cn = min(8, i + 1 - c0)
                        for j in range(cn):
                            c = c0 + j
                            P.op("pe", lambda e, c=c, j=j: e.transpose(pbf(3)[:, j * 128:(j + 1) * 128], maskb[:, c * 128:(c + 1) * 128], ident[:]),
                                 reads=["wk_mask", "ident"], writes=[("pb", 3)])
                        P.op("act", lambda e, c0=c0, cn=cn: e.copy(maskT[:, c0:c0 + cn, :], pbf(3)[:, 0:cn * 128].rearrange("p (c n) -> p c n", c=cn)),
                             reads=[("pb", 3)], writes=["wk_maskT"])
                    for c in range(i + 1):
                        kb_ = kTc[c % 2]; vb_ = Vc[c % 2]
                        kkey = ("kTc", c % 2); vkey = ("Vc", c % 2)
                        P.dma("sp", lambda e, kb_=kb_, c=c: e.dma_start(out=kb_[:], in_=kTs[:, c, :, :]),
                              reads=[("kTs", c)], writes=[kkey])
                        P.dma("sp", lambda e, vb_=vb_, c=c: e.dma_start(out=vb_[:], in_=Vs[c * 128:(c + 1) * 128, :].rearrange("p (h n) -> p h n", h=8)),
                              reads=[("Vs", c)], writes=[vkey])
                        for hd in range(8):
                            P.op("pe", lambda e, hd=hd, kb_=kb_, t=t: e.matmul(pb[4 + hd // 4][:, (hd % 4) * 128:(hd % 4 + 1) * 128],
                                                                            lhsT=kb_[:, hd, :], rhs=qT[:, t, hd, :], start=True, stop=True),
                                 reads=[kkey, ("qT", t)], writes=[("pb", 4 + hd // 4)])
                        eb = Eb[c % 2]; ptb = PTb[c % 2]
                        for hf in range(2):
                            P.op("act", lambda e, eb=eb, hf=hf: e.activation(eb[:, hf * 4:(hf + 1) * 4, :],
                                                                            pb[4 + hf][:, :].rearrange("p (h n) -> p h n", h=4), AF.Exp, scale=SM_SCALE),
                                 reads=[("pb", 4 + hf)], writes=[("Eb", c % 2, hf)])
                        P.op("pool", lambda e, eb=eb, ptb=ptb, c=c: e.tensor_tensor(
                            ptb[:], eb[:], maskT[:, c:c + 1, :].to_broadcast([128, 8, 128]), ALU.mult),
                            reads=[("Eb", c % 2, 0), ("Eb", c % 2, 1), "wk_maskT"], writes=[("PT", c % 2)])
                        for hd in range(8):
                            P.op("pe", lambda e, hd=hd, ptb=ptb, vb_=vb_, c=c, i=i: e.matmul(
                                pb[6 + hd // 4][:, (hd % 4) * 128:(hd % 4 + 1) * 128], lhsT=ptb[:, hd, :], rhs=vb_[:, hd, :],
                                start=(c == 0 and hd % 4 == 0), stop=(c == i), skip_group_check=True),
                                reads=[("PT", c % 2), vkey], writes=[("pb", 6 + hd // 4)])
                            P.op("pe", lambda e, hd=hd, ptb=ptb, c=c, i=i: e.matmul(
                                pb[3][:, 256 + hd:256 + hd + 1], lhsT=ptb[:, hd, :], rhs=ones_bf[:, 0:1],
                                start=(c == 0 and hd == 0), stop=(c == i), skip_group_check=True),
                                reads=[("PT", c % 2), "ones"], writes=[("pb", 3)])
                    P.op("dve", lambda e: e.reciprocal(rden, pb[3][:, 256:264]), reads=[("pb", 3)], writes=["wk_small"])
                    for hf in range(2):
                        P.op("dve", lambda e, hf=hf: e.tensor_tensor(
                            On.rearrange("p (h n) -> p h n", h=4), pb[6 + hf][:, :].rearrange("p (h n) -> p h n", h=4),
                            rden[:, hf * 4:(hf + 1) * 4].rearrange("p (h o) -> p h o", o=1).to_broadcast([128, 4, 128]), ALU.mult),
                            reads=[("pb", 6 + hf), "wk_small"], writes=["wk_On"])
                        P.op("pool", lambda e, hf=hf, t=t: e.tensor_tensor(ag_bf[:, hf * 512:(hf + 1) * 512], On, za_sb[:, t, hf * 512:(hf + 1) * 512], ALU.mult),
                             reads=["wk_On", ("za", t)], writes=[("stbf", 0), "wk_On"])
                    if dbg and l == 0:
                        P.dma("sp", lambda e, i=i: e.dma_start(out=dbg_ag[i * 128:(i + 1) * 128, :], in_=ag_bf[:]), reads=[("stbf", 0)])
                    transposes8(ag_bf, ("stbf", 0), AgT[:, t, :, :], ("qT", t), 0)

                if stop == "P3":
                    return
                wmix_sb = RW[:, 0:2048].rearrange("p (g c e) -> p g c e", g=4, c=2)
                psc_bc = RW[:, 2048:4096].bitcast(F32)
                dT_sb = RW[:, 4096:5120].rearrange("p (c n) -> p c n", c=8)
                pg_f = RW[:, 5120:7168].bitcast(F32)
                pg_bf = st_bf[1]
                guard = []
                fence(P3KEYS + WKEYS, P2KEYS)
                P.dma("pool", lambda e, l=l: e.dma_start(out=wmix_sb, in_=w_mix[l].rearrange("g (c p) e -> p g c e", p=128)),
                      reads=guard, writes=["wmix"] + guard)
                P.dma("sp", lambda e, l=l: e.dma_start(out=psc_bc, in_=pscale[l:l + 1, :].partition_broadcast(128)),
                      reads=guard, writes=["psc"] + guard)
                for t in range(HT):
                    gt = h * HT + t
                    for ct in range(8):
                        g = ct // 2
                        bk = ct // 4
                        first = (gt == 0)
                        bidx = (4 + g) if first else g
                        P.op("pe", lambda e, t=t, ct=ct, bk=bk, bidx=bidx, first=first: e.matmul(
                            pb[bk][:, (ct % 4) * 128:(ct % 4 + 1) * 128], lhsT=up_sb[:, t, ct * 128:(ct + 1) * 128], rhs=bands[:, bidx, :],
                            start=True, stop=first), reads=[("up", t), "bands"], writes=[("pb", bk)])
                        if not first:
                            if t > 0:
                                prev = up_sb[:, t - 1, ct * 128:(ct + 1) * 128]; pkey = ("up", t - 1)
                            else:
                                prev = up_last[:, ct * 128:(ct + 1) * 128]; pkey = "up_last"
                            P.op("pe", lambda e, ct=ct, bk=bk, g=g, prev=prev: e.matmul(
                                pb[bk][:, (ct % 4) * 128:(ct % 4 + 1) * 128], lhsT=prev, rhs=bands[:, 8 + g, :], start=False, stop=True),
                                reads=[pkey, "bands"], writes=[("pb", bk)])
                    for bk in range(2):
                        P.op("act", lambda e, bk=bk: e.copy(dT_sb[:, bk * 4:(bk + 1) * 4, :], pb[bk][:, :].rearrange("p (c n) -> p c n", c=4)),
                             reads=[("pb", bk)] + guard, writes=[("dT", bk)] + guard)
                    for g in range(4):
                        bk = 2 + g // 2
                        for c in range(2):
                            P.op("pe", lambda e, g=g, c=c, bk=bk: e.matmul(pb[bk][:, (g % 2) * 256:(g % 2 + 1) * 256], lhsT=dT_sb[:, g * 2 + c, :],
                                                                        rhs=wmix_sb[:, g, c, :], start=(c == 0), stop=(c == 1)),
                                 reads=[("dT", (g * 2 + c) // 4), "wmix"], writes=[("pb", bk)])
                    for hf in range(2):
                        P.op("dve", lambda e, hf=hf: e.tensor_tensor(pg_f[:, hf * 512:(hf + 1) * 512], pb[2 + hf][:, :], psc_bc[:, hf * 512:(hf + 1) * 512], ALU.mult),
                             reads=[("pb", 2 + hf), "psc"] + guard, writes=[("pgf", hf)] + guard)
                        P.op("pool", lambda e, hf=hf, t=t: e.tensor_tensor(pg_bf[:, hf * 512:(hf + 1) * 512], pg_f[:, hf * 512:(hf + 1) * 512],
                                                                          zp_sb[:, t, hf * 512:(hf + 1) * 512], ALU.mult),
                             reads=[("pgf", hf), ("zp", t)], writes=[("stbf", 1)])
                    if dbg and l == 0:
                        P.dma("sp", lambda e, gt=gt: e.dma_start(out=dbg_pg[gt * 128:(gt + 1) * 128, :], in_=pg_bf[:]), reads=[("stbf", 1)])
                    transposes8(pg_bf, ("stbf", 1), PgT[:, t, :, :], ("iqT", t), 4, evac_eng="dve")
                if h == 0:
                    P.op("pool", lambda e: e.tensor_copy(up_last[:], up_sb[:, HT - 1, :]), reads=[("up", HT - 1)], writes=["up_last"])

                if stop == "P2":
                    return
                fence(P3KEYS + P2KEYS, WKEYS)
                sgb = st_f32
                prodb = [ropeA, ropeB]

                def m_keys(t):
                    if t < 4:
                        return [("m", t), ("up", 2 * t), ("up", 2 * t + 1)]
                    return [("m", t), ("zp", 2 * (t - 4)), ("zp", 2 * (t - 4) + 1)]

                for jb in range(8):
                    j0 = jb * 256
                    i1 = rr["w"] % 3; rr["w"] += 1
                    i2 = rr["w"] % 3; rr["w"] += 1
                    bA = wbuf[i1]; bB = wbuf[i2]
                    kA = ("wbuf", i1); kB = ("wbuf", i2)
                    P.dma("pool", lambda e, bA=bA, j0=j0, l=l: e.dma_start(out=bA[:, 0:8, 0:256], in_=w_po[l, :, j0:j0 + 256].rearrange("(k p) n -> p k n", p=128)), writes=[kA])
                    P.dma("pool", lambda e, bA=bA, j0=j0, l=l: e.dma_start(out=bA[:, 8:16, 0:256], in_=w_ao[l, :, j0:j0 + 256].rearrange("(k p) n -> p k n", p=128)), writes=[kA])
                    P.dma("pool", lambda e, bB=bB, j0=j0, l=l: e.dma_start(out=bB[:, 0:16, 0:256], in_=w_in[l, :, C_GP + j0:C_GP + j0 + 256].rearrange("(k p) n -> p k n", p=128)), writes=[kB])
                    P.dma("pool", lambda e, bB=bB, j0=j0, l=l: e.dma_start(out=bB[:, 0:16, 256:512], in_=w_in[l, :, C_GA + j0:C_GA + j0 + 256].rearrange("(k p) n -> p k n", p=128)), writes=[kB])
                    for t in range(HT):
                        pr = (t % 2) * 2
                        for kt in range(8):
                            P.op("pe", lambda e, bA=bA, t=t, kt=kt, pr=pr: e.matmul(pb[pr][:, 0:256], lhsT=PgT[:, t, kt, :], rhs=bA[:, kt, 0:256], start=(kt == 0), stop=(kt == 7)),
                                 reads=[("iqT", t), kA], writes=[("pb", pr)])
                        for kt in range(8):
                            P.op("pe", lambda e, bA=bA, t=t, kt=kt, pr=pr: e.matmul(pb[pr][:, 256:512], lhsT=AgT[:, t, kt, :], rhs=bA[:, 8 + kt, 0:256], start=(kt == 0), stop=(kt == 7)),
                                 reads=[("qT", t), kA], writes=[("pb", pr)])
                        for kt in range(16):
                            P.op("pe", lambda e, bB=bB, t=t, kt=kt, pr=pr: e.matmul(pb[pr + 1][:, 0:256], lhsT=xT[:, t, kt, :], rhs=bB[:, kt, 0:256], start=(kt == 0), stop=(kt == 15)),
                                 reads=[("xT", t), kB], writes=[("pb", pr + 1)])
                        for kt in range(16):
                            P.op("pe", lambda e, bB=bB, t=t, kt=kt, pr=pr: e.matmul(pb[pr + 1][:, 256:512], lhsT=xT[:, t, kt, :], rhs=bB[:, kt, 256:512], start=(kt == 0), stop=(kt == 15)),
                                 reads=[("xT", t), kB], writes=[("pb", pr + 1)])
                        sg = sgb[t % 2]; prod = prodb[t % 2]
                        pkey = "ropeA" if t % 2 == 0 else "ropeB"
                        P.op("act", lambda e, sg=sg, pr=pr: e.activation(sg[:], pb[pr + 1][:, :], AF.Sigmoid), reads=[("pb", pr + 1)], writes=[("stf", t % 2)])
                        P.op("dve", lambda e, sg=sg, prod=prod, pr=pr: e.tensor_tensor(prod[:], pb[pr][:, :], sg[:], ALU.mult),
                             reads=[("pb", pr), ("stf", t % 2)], writes=[pkey])
                        P.op("pool", lambda e, prod=prod, t=t, j0=j0: e.tensor_tensor(m_sb[:, t, j0:j0 + 256], prod[:, 0:256], prod[:, 256:512], ALU.add),
                             reads=[pkey], writes=m_keys(t))

                if stop == "P4":
                    return
                for t in range(HT):
                    for g in range(2):
                        bk = 4 + g
                        for j in range(8):
                            kt = g * 8 + j
                            P.op("pe", lambda e, t=t, kt=kt, j=j, bk=bk: e.transpose(pbf(bk)[:, j * 128:(j + 1) * 128], m_sb[:, t, kt * 128:(kt + 1) * 128], ident[:]),
                                 reads=[("m", t), "ident"], writes=[("pb", bk)])
                        if g == 0:
                            P.op("act", lambda e, t=t, bk=bk: e.copy(mT[:, t, 0:8, :], pbf(bk).rearrange("p (j n) -> p j n", j=8)), reads=[("pb", bk)], writes=[("xT", t)])
                        else:
                            P.op("dve", lambda e, t=t, bk=bk: e.tensor_copy(mT[:, t, 8:16, :], pbf(bk).rearrange("p (j n) -> p j n", j=8)), reads=[("pb", bk)], writes=[("xT", t)])

                def xres_ap(j):
                    if j < 6:
                        return R2[:, j * 4096:(j + 1) * 4096].bitcast(F32)
                    return R1[:, (j - 6) * 4096:(j - 5) * 4096].bitcast(F32)

                def xres_keys(j):
                    ks = [("xres", j)]
                    if j < 2:
                        ks += [("qT", tt) for tt in range(j * 4, j * 4 + 4)]
                    elif j < 4:
                        ks += [("iqT", tt) for tt in range((j - 2) * 4, (j - 2) * 4 + 4)]
                    elif j < 6:
                        ks += [("za", tt) for tt in range((j - 4) * 4, (j - 4) * 4 + 4)]
                    else:
                        ks += [("up", tt) for tt in range((j - 6) * 4, (j - 6) * 4 + 4)] + [("m", 2 * (j - 6)), ("m", 2 * (j - 6) + 1)]
                    return ks

                P.dma("sp", lambda e, l=l: e.dma_start(out=gb_bc[:, 0, :], in_=lng[l:l + 1, :].partition_broadcast(128)), writes=GBKEYS)
                P.dma("sp", lambda e, l=l: e.dma_start(out=gb_bc[:, 1, :], in_=lnb[l:l + 1, :].partition_broadcast(128)), writes=GBKEYS)
                for t in range(HT):
                    gt = h * HT + t
                    xa = xres_ap(t)
                    P.dma("sp", lambda e, xa=xa, gt=gt, par=par: e.dma_start(out=xa, in_=Xs[par, gt * 128:(gt + 1) * 128, :]),
                          reads=[("Xs", par, gt)], writes=xres_keys(t))
                for jb in range(4):
                    j0 = jb * 512
                    wo, kwo = load_w(w_o[l, :, j0:j0 + 512], D, 512, None)
                    for t in range(HT):
                        bk = rr["pb"] % 2
                        rr["pb"] += 1
                        for kt in range(16):
                            P.op("pe", lambda e, wo=wo, t=t, kt=kt, bk=bk: e.matmul(pb[bk][:, :], lhsT=mT[:, t, kt, :], rhs=wo[:, kt, :], start=(kt == 0), stop=(kt == 15)),
                                 reads=[("xT", t), kwo], writes=[("pb", bk)])
                        xa = xres_ap(t)
                        P.op("dve", lambda e, xa=xa, bk=bk, j0=j0: e.scalar_tensor_tensor(xa[:, j0:j0 + 512], xa[:, j0:j0 + 512], ALPHA, pb[bk][:, :], ALU.mult, ALU.add),
                             reads=[("pb", bk), ("xres", t)], writes=[("xres", t)])
                for t in range(HT):
                    gt = h * HT + t
                    xa = xres_ap(t)
                    layer_norm_tile(("xres", t), xa, "gb", ("xres", t))
                    if last:
                        P.dma("sp", lambda e, xa=xa, gt=gt: e.dma_start(out=y_p[gt * 128:(gt + 1) * 128, :], in_=xa), reads=[("xres", t)])
                    else:
                        P.dma("sp", lambda e, xa=xa, gt=gt, par=par: e.dma_start(out=Xs[1 - par, gt * 128:(gt + 1) * 128, :], in_=xa),
                              reads=[("xres", t)], writes=[("Xs", 1 - par, gt)])
                        to_xT_scratch(xa, ("xres", t), 1 - par, gt)
        xsT = sb("xsT", [128, 16, 4], BF16)
        qsT = sb("qsT", [128, 8, 4], BF16); ksT = sb("ksT", [128, 8, 4], BF16); zasT = sb("zasT", [128, 8, 4], BF16)
        PgsT = sb("PgsT", [128, 8, 4], BF16); AgsT = sb("AgsT", [128, 8, 4], BF16); msT = sb("msT", [128, 16, 4], BF16)
        dTs = sb("dTs", [128, 8, 4], BF16)
        iqsT = sb("iqsT", [64, 4, 16], BF16)
        iksT = sb("iksT", [64, 4], BF16)
        iws = sb("iws", [4, 16], F32)
        iwcol = sb("iwcol", [64, 1], F32)
        sbst = sb("sbst", [15, 4, 4], BF16); sbnew = sb("sbnew", [4, 4, 4], BF16)
        pt_sb = sb("pt_sb", [128, 1], I32); ptf = sb("ptf", [128, 1], F32); rowbase = sb("rowbase", [128, 1], F32)
        pgidx = sb("pgidx", [128, 1], I32)
        iota_i = sb("iota_i", [128, 128], I32); iota_f = sb("iota_f", [128, 128], F32); idx_all = sb("idx_all", [128, 128], I32)
        negm_s = sb("negm_s", [128, 4], F32); bones = sb("bones_sb", [128, 128], F32); dmask = sb("dmask", [32, 8], F32)
        ssm = sb("ssm", [128, 64], F32)
        s_lo = ssm[:, 0:1]; s_mid = ssm[:, 1:2]; s_cnt = ssm[:, 2:3]; s_B = ssm[:, 3:4]; s_am = ssm[:, 4:5]; s_am2 = ssm[:, 5:6]
        s_w0 = ssm[:, 6:7]; s_rden = ssm[0:32, 7:8]; s_wtab = ssm[:, 16:16 + 40]
        s_gei = sb("s_gei", [128, 1], I32)
        xs_scr = dscr("xs_scr", [2, 4, D])
        iw_scr = dscr("iw_scr", [4, 16])

        SR = R1[0:4, :]
        ups = SR[:, 0:1024]; zps = SR[:, 1024:2048]; zas = SR[:, 2048:3072]; ksb = SR[:, 3072:4096]; vsb = SR[:, 4096:5120]
        iqs = SR[:, 5120:6144]; gps = SR[:, 6144:8192]; gas = SR[:, 8192:10240]; ms = SR[:, 10240:12288]
        xst = SR[:, 12288:16384].bitcast(F32)
        gbs = R0[0:4, 0:8192].bitcast(F32).rearrange("p (a n) -> p a n", a=2)
        ik_raw = R0[:, :].bitcast(F32)
        ik_bf = R2[:, 0:8192]
        Kc = [R2[:, 8192 + i * 2048:8192 + (i + 1) * 2048].bitcast(F32) for i in range(2)]
        Vc = [R2[:, 12288 + i * 2048:12288 + (i + 1) * 2048].bitcast(F32) for i in range(2)]
        Kbf = [R2[:, 16384 + i * 1024:16384 + (i + 1) * 1024] for i in range(2)]
        Vbf = [R2[:, 18432 + i * 1024:18432 + (i + 1) * 1024] for i in range(2)]
        KTs = [R2[:, 20480 + i * 1024:20480 + (i + 1) * 1024].rearrange("p (h n) -> p h n", h=8) for i in range(2)]
        ikT_s = RW[0:64, 0:16388]
        Wsc = RW[0:64, 16400:20496]
        sc_s = RW[:, 20496:21528].bitcast(F32)
        junk_s = RW[:, 21528:22044]
        mask_s = RW[:, 22044:22560]
        Tm = RW[:, 22560:23072].rearrange("p (g n) -> p g n", g=4)
        Tn = RW[0:4, 23072:23200]
        Rs = [RW[0:64, 23200 + i * 512:23200 + (i + 1) * 512] for i in range(2)]
        Es = RW[:, 24224:24288].bitcast(F32)
        PTs = [RW[:, 24288 + i * 32:24288 + (i + 1) * 32] for i in range(2)]
        wmix_s = RW[:, 16400:18448].rearrange("p (g c e) -> p g c e", g=4, c=2)
        psc_s = RW[0:4, 18448:20496].bitcast(F32)
        pgs_f = RW[0:4, 20496:22544].bitcast(F32)
        od_tmp = RW[0:32, 0:2048].bitcast(F32)
        od = RW[0:32, 2048:2304].bitcast(F32)
        odn = RW[0:32, 2304:2432]

        def bar():
            P.barrier(lambda e: e.memset(fdummy[:], 0.0))

        def load_w2(dram_ap_2d, krows, ncols, k0=0):
            i = rr["w"] % 2
            rr["w"] += 1
            buf = wbuf[i]
            kt = krows // 128
            P.dma("pool", lambda e: e.dma_start(out=buf[:, k0:k0 + kt, 0:ncols], in_=dram_ap_2d.rearrange("(k p) n -> p k n", p=128)),
                  writes=[("wbuf", i)])
            return buf, ("wbuf", i)

        def tr_small(src_ap, np_, ncol, dst_ap, key_r, key_w, bk=3, dst_rearr=None):
            P.op("pe", lambda e: e.transpose(pbf(bk)[0:ncol, 0:np_], src_ap, ident[0:np_, 0:np_]), reads=[key_r, "ident"], writes=[("pb", bk)])
            P.op("act", lambda e: e.copy(dst_ap, pbf(bk)[0:ncol, 0:np_]), reads=[("pb", bk)], writes=[key_w])

        def tr_tok(src_bf, key_r, dst, key_w, nblk):
            for j in range(nblk):
                P.op("pe", lambda e, j=j: e.transpose(pbf(3)[:, j * 4:(j + 1) * 4], src_bf[:, j * 128:(j + 1) * 128], ident[0:4, 0:4]),
                     reads=[key_r, "ident"], writes=[("pb", 3)])
            P.op("act", lambda e: e.copy(dst, pbf(3)[:, 0:nblk * 4].rearrange("p (j n) -> p j n", j=nblk)), reads=[("pb", 3)], writes=[key_w])

        def ln4(x_ap, xkey):
            for j in range(4):
                P.op("dve", lambda e, j=j: e.bn_stats(stat[0:4, j, :], x_ap[:, j * 512:(j + 1) * 512]), reads=[xkey], writes=[("stat", j)])
            P.op("dve", lambda e: e.bn_aggr(mv[0:4, :], stat[0:4].rearrange("p a b -> p (a b)")), reads=[("stat", j) for j in range(4)], writes=["mv"])
            P.op("act", lambda e: e.activation(rstd[0:4, :], mv[0:4, 1:2], AF.Sqrt, bias=eps_t[0:4, :]), reads=["mv", "eps"], writes=["rstd"])
            P.op("dve", lambda e: e.reciprocal(rstd[0:4, :], rstd[0:4, :]), reads=["rstd"], writes=["rstd"])
            P.op("dve", lambda e: e.tensor_scalar(x_ap, x_ap, mv[0:4, 0:1], rstd[0:4, :], ALU.subtract, ALU.mult), reads=[xkey, "mv", "rstd"], writes=[xkey])
            P.op("dve", lambda e: e.tensor_tensor(x_ap, x_ap, gbs[:, 0, :], ALU.mult), reads=[xkey, "gbs"], writes=[xkey])
            P.op("dve", lambda e: e.tensor_tensor(x_ap, x_ap, gbs[:, 1, :], ALU.add), reads=[xkey, "gbs"], writes=[xkey])

        def xs_to_T():
            P.op("act", lambda e: e.copy(xbf_st[0:4, :], xst), reads=["xst"], writes=["xbf"])
            tr_tok(xbf_st[0:4, :], "xbf", xsT[:], "xsT", 16)

        def sample_init():
            bar()
            P.dma("pool", lambda e: e.dma_start(out=sbst[:], in_=sb_st_d.rearrange("g s t -> s g t")), writes=["sbst"])
            P.dma("pool", lambda e: e.dma_start(out=sbnew[:], in_=sb_new_d.rearrange("g s t -> s g t")), writes=["sbnew"])
            P.dma("sp", lambda e: e.dma_start(out=negm_s[:], in_=negm_d), writes=["negm"])
            P.dma("sp", lambda e: e.dma_start(out=bones[:], in_=bones_d), writes=["bones"])
            P.dma("sp", lambda e: e.dma_start(out=dmask[:], in_=dmask_d), writes=["dmask"])
            P.dma("sp", lambda e: e.dma_start(out=pt_sb[:], in_=pt), writes=["pt"])
            P.op("dve", lambda e: e.tensor_copy(ptf[:], pt_sb[:]), reads=["pt"], writes=["ptf"])
            P.op("pool", lambda e: e.iota(iota_i[:], [[1, 128]], base=0, channel_multiplier=0), writes=["iota_i"])
            P.op("dve", lambda e: e.tensor_copy(iota_f[:], iota_i[:]), reads=["iota_i"], writes=["iota_f"])
            P.dma("sp", lambda e: e.dma_start(out=gbs[:, 0, :], in_=lneg.partition_broadcast(4)), writes=["gbs"])
            P.dma("sp", lambda e: e.dma_start(out=gbs[:, 1, :], in_=lneb.partition_broadcast(4)), writes=["gbs"])
            P.dma("sp", lambda e: e.dma_start(out=xst, in_=xs_in), writes=["xst"])
            ln4(xst, "xst")
            P.dma("sp", lambda e: e.dma_start(out=xs_scr[0], in_=xst), reads=["xst"], writes=["xs_scr"])
            xs_to_T()
            bar()

        S_BLOCKS = []
        for kind, c0 in (("up", C_UP), ("zp", C_ZP), ("q", C_Q), ("k", C_K), ("v", C_V), ("za", C_ZA), ("iq", C_IQ)):
            S_BLOCKS.append((kind, c0, 512, 0)); S_BLOCKS.append((kind, c0 + 512, 512, 1))
        S_BLOCKS.append(("ikiw", C_IK, 80, 0))
        for kind, c0 in (("gp", C_GP), ("ga", C_GA)):
            for sub in range(4):
                S_BLOCKS.append((kind, c0 + sub * 512, 512, sub))

        def rope4(bk, nh, half, cos_ap, sin_ap, out_ap, okey):
            n = nh * 2 * half
            src = pb[bk][0:4, 0:n].rearrange("p (h two f) -> p h two f", h=nh, two=2)
            cb = cos_ap.rearrange("p (a b f) -> p a b f", a=1, b=1).to_broadcast([4, nh, 2, half])
            sbb = sin_ap.rearrange("p (a b f) -> p a b f", a=1, b=1).to_broadcast([4, nh, 2, half])
            A = ropeA[0:4, 0:n].rearrange("p (h two f) -> p h two f", h=nh, two=2)
            B = ropeB[0:4, 0:n].rearrange("p (h two f) -> p h two f", h=nh, two=2)
            o = out_ap.rearrange("p (h two f) -> p h two f", h=nh, two=2)
            P.op("dve", lambda e: e.tensor_tensor(A, src, cb, ALU.mult), reads=[("pb", bk), "sropetab"], writes=["ropeA"])
            P.op("dve", lambda e: e.tensor_tensor(B, src, sbb, ALU.mult), reads=[("pb", bk), "sropetab"], writes=["ropeB"])
            P.op("dve", lambda e: e.tensor_tensor(o[:, :, 0, :], A[:, :, 0, :], B[:, :, 1, :], ALU.subtract), reads=["ropeA", "ropeB"], writes=[okey])
            P.op("dve", lambda e: e.tensor_tensor(o[:, :, 1, :], A[:, :, 1, :], B[:, :, 0, :], ALU.add), reads=["ropeA", "ropeB"], writes=[okey, "ropeA", "ropeB"])

        def sample_pass(l):
            last = (l == DEPTH - 1)
            par = l % 2
            bar()
            for tab, src in ((cos128, src128), (sin128, srs128), (cos64, src64), (sin64, srs64)):
                P.dma("sp", lambda e, tab=tab, src=src: e.dma_start(out=tab[0:4, 0, :], in_=src), writes=["sropetab"])
            for (kind, c0, ncols, sub) in S_BLOCKS:
                wb, wkey = load_w2(w_in[l, :, c0:c0 + ncols], D, ncols)
                bk = rr["pb"] % 2
                rr["pb"] += 1
                for kt in range(16):
                    P.op("pe", lambda e, kt=kt, bk=bk, wb=wb, ncols=ncols: e.matmul(pb[bk][0:4, 0:ncols], lhsT=xsT[:, kt, :], rhs=wb[:, kt, 0:ncols],
                                                                                 start=(kt == 0), stop=(kt == 15)), reads=["xsT", wkey], writes=[("pb", bk)])
                pk = ("pb", bk)
                cs = slice(sub * 512, sub * 512 + 512)
                if kind == "up":
                    sf = st_f32[sub]
                    P.op("act", lambda e, sf=sf, bk=bk: e.copy(sf[0:4, :], pb[bk][0:4, :]), reads=[pk], writes=[("stf", sub)])
                    P.dma("sp", lambda e, sf=sf, cs=cs: e.dma_start(out=pl_s[l, 11:15, cs], in_=sf[0:4, :]), reads=[("stf", sub)])
                    P.op("dve", lambda e, bk=bk, cs=cs: e.tensor_copy(ups[:, cs], pb[bk][0:4, :]), reads=[pk], writes=["ups"])
                elif kind == "zp":
                    P.op("act", lambda e, bk=bk, cs=cs: e.activation(zps[:, cs], pb[bk][0:4, :], AF.Silu), reads=[pk], writes=["zps"])
                elif kind == "za":
                    P.op("act", lambda e, bk=bk, cs=cs: e.activation(zas[:, cs], pb[bk][0:4, :], AF.Silu), reads=[pk], writes=["zas"])
                    if sub == 1:
                        tr_tok(zas, "zas", zasT[:], "zasT", 8)
                elif kind == "q":
                    rope4(bk, 4, 64, cos128[0:4, 0, :], sin128[0:4, 0, :], iqs[:, cs], "iqs")
                    if sub == 1:
                        tr_tok(iqs, "iqs", qsT[:], "qsT", 8)
                elif kind == "k":
                    sf = st_f32[sub]
                    rope4(bk, 4, 64, cos128[0:4, 0, :], sin128[0:4, 0, :], sf[0:4, :], ("stf", sub))
                    P.dma("sp", lambda e, sf=sf, cs=cs: e.dma_start(out=k_s[l, :, cs], in_=sf[0:4, :]), reads=[("stf", sub)])
                    P.op("act", lambda e, sf=sf, cs=cs: e.copy(ksb[:, cs], sf[0:4, :]), reads=[("stf", sub)], writes=["ksb"])
                    if sub == 1:
                        tr_tok(ksb, "ksb", ksT[:], "ksT", 8)
                elif kind == "v":
                    sf = st_f32[sub]
                    P.op("act", lambda e, sf=sf, bk=bk: e.copy(sf[0:4, :], pb[bk][0:4, :]), reads=[pk], writes=[("stf", sub)])
                    P.dma("sp", lambda e, sf=sf, cs=cs: e.dma_start(out=v_s[l, :, cs], in_=sf[0:4, :]), reads=[("stf", sub)])
                    P.op("dve", lambda e, bk=bk, cs=cs: e.tensor_copy(vsb[:, cs], pb[bk][0:4, :]), reads=[pk], writes=["vsb"])
                elif kind == "iq":
                    rope4(bk, 8, 32, cos64[0:4, 0, :], sin64[0:4, 0, :], iqs[:, cs], "iqs")
                    if sub == 1:
                        for hh in range(16):
                            P.op("pe", lambda e, hh=hh: e.transpose(pbf(3)[0:64, hh * 4:(hh + 1) * 4], iqs[:, hh * 64:(hh + 1) * 64], ident[0:4, 0:4]),
                                 reads=["iqs", "ident"], writes=[("pb", 3)])
                        P.op("act", lambda e: e.copy(iqsT[:], pbf(3)[0:64, 0:64].rearrange("p (h q) -> p q h", h=16)), reads=[("pb", 3)], writes=["iqsT"])
                elif kind == "ikiw":
                    rope4(bk, 1, 32, cos64[0:4, 0, :], sin64[0:4, 0, :], ik_st[0:4, :], "ikst")
                    P.dma("sp", lambda e: e.dma_start(out=ik_s[l], in_=ik_st[0:4, :]), reads=["ikst"])
                    P.op("act", lambda e: e.copy(ik2_bf[0:4, 0:64], ik_st[0:4, :]), reads=["ikst"], writes=["ik2"])
                    tr_small(ik2_bf[0:4, 0:64], 4, 64, iksT[:], "ik2", "iksT")
                    P.op("dve", lambda e, bk=bk: e.tensor_scalar(iws[:], pb[bk][0:4, 64:80], IDX_SCALE, None, ALU.mult), reads=[pk], writes=["iws"])
                    P.dma("sp", lambda e: e.dma_start(out=iw_scr, in_=iws[:]), reads=["iws"], writes=["iw_scr"])
                    P.dma("sp", lambda e: e.dma_start(out=iwcol[:], in_=iw_scr.rearrange("q (h o) -> (q h) o", o=1)), reads=["iw_scr"], writes=["iwcol"])
                elif kind == "gp":
                    P.op("act", lambda e, bk=bk, cs=cs: e.activation(gps[:, cs], pb[bk][0:4, :], AF.Sigmoid), reads=[pk], writes=["gps"])
                elif kind == "ga":
                    P.op("act", lambda e, bk=bk, cs=cs: e.activation(gas[:, cs], pb[bk][0:4, :], AF.Sigmoid), reads=[pk], writes=["gas"])
            bar()
            state_bf = ropeA[0:15, :].bitcast(BF16)
            P.dma("pool", lambda e: e.dma_start(out=state_bf, in_=stp[l]), writes=["state"])
            P.dma("sp", lambda e: e.dma_start(out=pl_s[l, 0:11, :], in_=stp[l, 4:15, :]))
            P.dma("pool", lambda e: e.dma_start(out=wmix_s, in_=w_mix[l].rearrange("g (c p) e -> p g c e", p=128)), writes=["wmix_s"])
            P.dma("sp", lambda e: e.dma_start(out=psc_s, in_=pscale[l:l + 1, :].partition_broadcast(4)), writes=["psc_s"])
            for ct in range(8):
                g = ct // 2
                P.op("pe", lambda e, ct=ct, g=g: e.matmul(pb[0][:, ct * 4:(ct + 1) * 4], lhsT=state_bf[:, ct * 128:(ct + 1) * 128], rhs=sbst[:, g, :],
                                                         start=True, stop=False), reads=["state", "sbst"], writes=[("pb", 0)])
                P.op("pe", lambda e, ct=ct, g=g: e.matmul(pb[0][:, ct * 4:(ct + 1) * 4], lhsT=ups[:, ct * 128:(ct + 1) * 128], rhs=sbnew[:, g, :],
                                                         start=False, stop=True), reads=["ups", "sbnew"], writes=[("pb", 0)])
            P.op("act", lambda e: e.copy(dTs[:], pb[0][:, 0:32].rearrange("p (c n) -> p c n", c=8)), reads=[("pb", 0)], writes=["dTs"])
            for g in range(4):
                bk = 1 + g // 2
                for c in range(2):
                    P.op("pe", lambda e, g=g, c=c, bk=bk: e.matmul(pb[bk][0:4, (g % 2) * 256:(g % 2 + 1) * 256], lhsT=dTs[:, g * 2 + c, :], rhs=wmix_s[:, g, c, :],
                                                                start=(c == 0), stop=(c == 1)), reads=["dTs", "wmix_s"], writes=[("pb", bk)])
            for hf in range(2):
                P.op("dve", lambda e, hf=hf: e.tensor_tensor(pgs_f[:, hf * 512:(hf + 1) * 512], pb[1 + hf][0:4, :], psc_s[:, hf * 512:(hf + 1) * 512], ALU.mult),
                     reads=[("pb", 1 + hf), "psc_s"], writes=["pgs_f"])
            P.op("dve", lambda e: e.tensor_tensor(iqs, pgs_f, zps, ALU.mult), reads=["pgs_f", "zps", "iqs"], writes=["iqs"])
            tr_tok(iqs, "iqs", PgsT[:], "PgsT", 8)
            bar()
            P.op("dve", lambda e: e.tensor_scalar(rowbase[:], ptf[:], 128.0, float(l * NPOOL * 128), ALU.mult, ALU.add), reads=["ptf"], writes=["rowbase"])
            P.op("dve", lambda e: e.tensor_scalar(idx_all[:], iota_f[:], rowbase[:], None, ALU.add), reads=["iota_f", "rowbase"], writes=["idx_all"])
            P.op("dve", lambda e: e.tensor_scalar(pgidx[:], ptf[:], float(l * NPOOL), None, ALU.add), reads=["ptf"], writes=["pgidx"])
            P.dma("pool", lambda e: e.indirect_dma_start(out=ik_raw, out_offset=None, in_=cik, in_offset=bass.IndirectOffsetOnAxis(ap=pgidx[:, 0:1], axis=0)),
                  reads=["pgidx"], writes=["ik_raw"])
            P.op("dve", lambda e: e.tensor_copy(ik_bf[:, 0:4096], ik_raw[:, 0:4096]), reads=["ik_raw"], writes=["ik_bf0"])
            P.op("pool", lambda e: e.tensor_copy(ik_bf[:, 4096:8192], ik_raw[:, 4096:8192]), reads=["ik_raw"], writes=["ik_bf1"])
            for s0 in range(0, 128, 8):
                bk = (s0 // 8) % 2
                for j in range(8):
                    s = s0 + j
                    P.op("pe", lambda e, s=s, j=j, bk=bk: e.transpose(pbf(bk)[0:64, j * 128:(j + 1) * 128], ik_bf[:, s * 64:(s + 1) * 64], ident[:]),
                         reads=["ik_bf0" if s < 64 else "ik_bf1", "ident"], writes=[("pb", bk)])
                if bk == 0:
                    P.op("act", lambda e, s0=s0, bk=bk: e.copy(ikT_s[:, s0 * 128:(s0 + 8) * 128], pbf(bk)[0:64, :]), reads=[("pb", bk)], writes=["ikT_s"])
                else:
                    P.op("dve", lambda e, s0=s0, bk=bk: e.tensor_copy(ikT_s[:, s0 * 128:(s0 + 8) * 128], pbf(bk)[0:64, :]), reads=[("pb", bk)], writes=["ikT_s"])
            P.op("act", lambda e: e.copy(ikT_s[:, 16384:16388], iksT[:]), reads=["iksT"], writes=["ikT_s"])
            P.dma("pool", lambda e: e.dma_start(out=Wsc, in_=esel_d), writes=["Wsc"])
            P.op("dve", lambda e: e.tensor_scalar(Wsc, Wsc, iwcol[:], None, ALU.mult), reads=["Wsc", "iwcol"], writes=["Wsc"])
            iq_flat = iqsT[:].rearrange("p q h -> p (q h)")
            for c in range(33):
                kn = 512 if c < 32 else 4
                bk = c % 2
                rs_ = Rs[c % 2]
                rkey = ("Rs", c % 2)
                P.op("pe", lambda e, c=c, kn=kn, bk=bk: e.matmul(pb[bk][0:64, 0:kn], lhsT=iq_flat, rhs=ikT_s[:, c * 512:c * 512 + kn], start=True, stop=True),
                     reads=["iqsT", "ikT_s"], writes=[("pb", bk)])
                if c % 2 == 0:
                    P.op("act", lambda e, rs_=rs_, bk=bk, kn=kn: e.activation(rs_[:, 0:kn], pb[bk][0:64, 0:kn], AF.Relu), reads=[("pb", bk)], writes=[rkey])
                else:
                    P.op("dve", lambda e, rs_=rs_, bk=bk, kn=kn: e.tensor_scalar(rs_[:, 0:kn], pb[bk][0:64, 0:kn], 0.0, None, ALU.max), reads=[("pb", bk)], writes=[rkey])
                if c < 32:
                    P.op("pe", lambda e, c=c, rs_=rs_: e.matmul(pb[2][:, 0:512], lhsT=Wsc[:, c * 128:(c + 1) * 128], rhs=rs_[:, 0:512], start=(c == 0), stop=(c == 31)),
                         reads=["Wsc", rkey], writes=[("pb", 2)])
                else:
                    P.op("pe", lambda e, rs_=rs_: e.matmul(pb[3][:, 0:4], lhsT=Wsc[:, 0:128], rhs=rs_[:, 0:4], start=True, stop=True),
                         reads=["Wsc", rkey], writes=[("pb", 3)])
            P.op("act", lambda e: e.copy(sc_s[:, 0:512], pb[2][:, :]), reads=[("pb", 2)], writes=["sc_s"])
            P.op("dve", lambda e: e.tensor_reduce(s_am2, pb[3][:, 0:4], AX.X, ALU.max, apply_absolute_value=True), reads=[("pb", 3)], writes=["s_am2"])
            P.op("dve", lambda e: e.tensor_tensor(sc_s[:, 512:516], pb[3][:, 0:4], negm_s[:], ALU.add), reads=[("pb", 3), "negm"], writes=["sc_s"])
            P.op("dve", lambda e: e.tensor_reduce(s_am, sc_s[:, 0:512], AX.X, ALU.max, apply_absolute_value=True), reads=["sc_s"], writes=["s_am"])
            P.op("dve", lambda e: e.tensor_tensor(s_am, s_am, s_am2, ALU.max), reads=["s_am", "s_am2"], writes=["s_am"])
            P.op("pe", lambda e: e.matmul(pb[5][:, 0:1], lhsT=bones[:], rhs=s_am, start=True, stop=True), reads=["bones", "s_am"], writes=[("pb", 5)])
            P.op("dve", lambda e: e.tensor_scalar(s_B, pb[5][:, 0:1], 1.01, 1.0, ALU.mult, ALU.add), reads=[("pb", 5)], writes=["s_B"])
            P.op("dve", lambda e: e.tensor_scalar(s_lo, s_B, -1.0, None, ALU.mult), reads=["s_B"], writes=["s_lo"])
            P.op("dve", lambda e: e.tensor_scalar(s_w0, s_B, 2.0, None, ALU.mult), reads=["s_B"], writes=["s_w0"])
            P.op("dve", lambda e: e.tensor_scalar(s_wtab[:, 0:32], pow2[:, 0:32], s_w0, None, ALU.mult), reads=["s_w0", "pow2"], writes=["s_wtab"])
            for n in range(32):
                P.op("dve", lambda e, n=n: e.tensor_tensor(s_mid, s_lo, s_wtab[:, n:n + 1], ALU.add), reads=["s_lo", "s_wtab"], writes=["s_mid"])
                P.op("dve", lambda e: e.tensor_scalar(junk_s, sc_s, s_mid, None, ALU.is_ge, ALU.add, accum_out=s_cnt), reads=["sc_s", "s_mid"], writes=["junk_s", "s_cnt"])
                P.op("pe", lambda e: e.matmul(pb[5][:, 0:1], lhsT=bones[:], rhs=s_cnt, start=True, stop=True), reads=["bones", "s_cnt"], writes=[("pb", 5)])
                P.op("dve", lambda e: e.tensor_scalar(s_gei[:], pb[5][:, 0:1], 255.5, None, ALU.is_ge), reads=[("pb", 5)], writes=["s_gei"])
                P.op("dve", lambda e: e.copy_predicated(s_lo, s_gei[:], s_mid), reads=["s_gei", "s_mid"], writes=["s_lo"])
            P.op("dve", lambda e: e.tensor_scalar(mask_s, sc_s, s_lo, None, ALU.is_ge), reads=["sc_s", "s_lo"], writes=["mask_s"])
            for g in range(4):
                P.op("pe", lambda e, g=g: e.transpose(pbf(3)[:, g * 128:(g + 1) * 128], mask_s[:, g * 128:(g + 1) * 128], ident[:]), reads=["mask_s", "ident"], writes=[("pb", 3)])
            P.op("act", lambda e: e.copy(Tm, pbf(3)[:, 0:512].rearrange("p (g n) -> p g n", g=4)), reads=[("pb", 3)], writes=["Tm"])
            P.op("pe", lambda e: e.transpose(pbf(3)[0:4, 0:128], mask_s[:, 512:516], ident[:]), reads=["mask_s", "ident"], writes=[("pb", 3)])
            P.op("act", lambda e: e.copy(Tn, pbf(3)[0:4, 0:128]), reads=[("pb", 3)], writes=["Tn"])
            for s in range(129):
                b2 = s % 2
                if s < 128:
                    c, g = s // 4, s % 4
                    P.dma("pool", lambda e, s=s, b2=b2: e.indirect_dma_start(out=Kc[b2], out_offset=None, in_=ck, in_offset=bass.IndirectOffsetOnAxis(ap=idx_all[:, s:s + 1], axis=0)),
                          reads=["idx_all"], writes=[("Kc", b2)])
                    P.dma("pool", lambda e, s=s, b2=b2: e.indirect_dma_start(out=Vc[b2], out_offset=None, in_=cv, in_offset=bass.IndirectOffsetOnAxis(ap=idx_all[:, s:s + 1], axis=0)),
                          reads=["idx_all"], writes=[("Vc", b2)])
                    P.op("dve", lambda e, b2=b2: e.tensor_copy(Kbf[b2], Kc[b2]), reads=[("Kc", b2)], writes=[("Kbf", b2)])
                    P.op("act", lambda e, b2=b2: e.copy(Vbf[b2], Vc[b2]), reads=[("Vc", b2)], writes=[("Vbf", b2)])
                    for hd in range(8):
                        P.op("pe", lambda e, hd=hd, b2=b2: e.transpose(pbf(b2)[:, hd * 128:(hd + 1) * 128], Kbf[b2][:, hd * 128:(hd + 1) * 128], ident[:]),
                             reads=[("Kbf", b2), "ident"], writes=[("pb", b2)])
                    P.op("act", lambda e, b2=b2: e.copy(KTs[b2], pbf(b2).rearrange("p (h n) -> p h n", h=8)), reads=[("pb", b2)], writes=[("KTs", b2)])
                    for hd in range(8):
                        P.op("pe", lambda e, hd=hd, b2=b2: e.matmul(pb[4][:, hd * 4:(hd + 1) * 4], lhsT=KTs[b2][:, hd, :], rhs=qsT[:, hd, :], start=True, stop=True),
                             reads=[("KTs", b2), "qsT"], writes=[("pb", 4)])
                    npp = 128
                    mk = Tm[:, g, c:c + 97:32]
                    vsrc = Vbf[b2]
                    vkey = ("Vbf", b2)
                else:
                    for hd in range(8):
                        P.op("pe", lambda e, hd=hd: e.matmul(pb[4][0:4, hd * 4:(hd + 1) * 4], lhsT=ksT[:, hd, :], rhs=qsT[:, hd, :], start=True, stop=True),
                             reads=["ksT", "qsT"], writes=[("pb", 4)])
                    npp = 4
                    mk = Tn[:, 0:97:32]
                    vsrc = vsb
                    vkey = "vsb"
                P.op("act", lambda e, npp=npp: e.activation(Es[0:npp, :], pb[4][0:npp, 0:32], AF.Exp, scale=SM_SCALE), reads=[("pb", 4)], writes=["Es"])
                pts = PTs[b2]
                P.op("dve", lambda e, npp=npp, mk=mk, pts=pts: e.tensor_tensor(
                    pts[0:npp, :].rearrange("p (h q) -> p h q", h=8), Es[0:npp, :].rearrange("p (h q) -> p h q", h=8),
                    mk.rearrange("p (o q) -> p o q", o=1).to_broadcast([npp, 8, 4]), ALU.mult),
                    reads=["Es", "Tm", "Tn"], writes=[("PTs", b2)])
                for hf in range(2):
                    P.op("pe", lambda e, hf=hf, npp=npp, pts=pts, vsrc=vsrc, s=s: e.matmul(
                        pb[6 + hf][0:32, :], lhsT=pts[0:npp, :], rhs=vsrc[0:npp, hf * 512:(hf + 1) * 512], start=(s == 0), stop=(s == 128), skip_group_check=True),
                        reads=[("PTs", b2), vkey], writes=[("pb", 6 + hf)])
                P.op("pe", lambda e, npp=npp, pts=pts, s=s: e.matmul(pb[5][0:32, 8:9], lhsT=pts[0:npp, :], rhs=ones_bf[0:npp, 0:1], start=(s == 0), stop=(s == 128), skip_group_check=True),
                     reads=[("PTs", b2), "ones"], writes=[("pb", 5)])
            P.op("dve", lambda e: e.reciprocal(s_rden, pb[5][0:32, 8:9]), reads=[("pb", 5)], writes=["s_rden"])
            for hf in range(2):
                P.op("dve", lambda e, hf=hf: e.tensor_tensor(
                    od_tmp[:, hf * 512:(hf + 1) * 512].rearrange("p (h d) -> p h d", h=4), pb[6 + hf][0:32, :].rearrange("p (h d) -> p h d", h=4),
                    dmask[:, hf * 4:(hf + 1) * 4].rearrange("p (h o) -> p h o", o=1).to_broadcast([32, 4, 128]), ALU.mult),
                    reads=[("pb", 6 + hf), "dmask", "ikT_s"], writes=["od_tmp", "ikT_s"])
            P.op("dve", lambda e: e.tensor_reduce(od, od_tmp.rearrange("p (h d) -> p d h", h=8), AX.X, ALU.add), reads=["od_tmp"], writes=["od"])
            P.op("dve", lambda e: e.tensor_scalar(odn, od, s_rden, None, ALU.mult), reads=["od", "s_rden"], writes=["odn"])
            P.op("pe", lambda e: e.transpose(pbf(3)[:, 0:32], odn, ident[0:32, 0:32]), reads=["odn", "ident"], writes=[("pb", 3)])
            P.op("dve", lambda e: e.tensor_tensor(AgsT[:], pbf(3)[:, 0:32].rearrange("p (h q) -> p h q", h=8), zasT[:], ALU.mult), reads=[("pb", 3), "zasT"], writes=["AgsT"])
            bar()
            for jb in range(4):
                j0 = jb * 512
                i1 = rr["w"] % 2; rr["w"] += 1
                bA = wbuf[i1]; kA = ("wbuf", i1)
                P.dma("pool", lambda e, bA=bA, j0=j0: e.dma_start(out=bA[:, 0:8, :], in_=w_po[l, :, j0:j0 + 512].rearrange("(k p) n -> p k n", p=128)), writes=[kA])
                P.dma("pool", lambda e, bA=bA, j0=j0: e.dma_start(out=bA[:, 8:16, :], in_=w_ao[l, :, j0:j0 + 512].rearrange("(k p) n -> p k n", p=128)), writes=[kA])
                for kt in range(8):
                    P.op("pe", lambda e, bA=bA, kt=kt: e.matmul(pb[0][0:4, :], lhsT=PgsT[:, kt, :], rhs=bA[:, kt, :], start=(kt == 0), stop=(kt == 7)), reads=["PgsT", kA], writes=[("pb", 0)])
                for kt in range(8):
                    P.op("pe", lambda e, bA=bA, kt=kt: e.matmul(pb[1][0:4, :], lhsT=AgsT[:, kt, :], rhs=bA[:, 8 + kt, :], start=(kt == 0), stop=(kt == 7)), reads=["AgsT", kA], writes=[("pb", 1)])
                P.op("dve", lambda e, j0=j0: e.tensor_tensor(ropeA[0:4, :], pb[0][0:4, :], gps[:, j0:j0 + 512], ALU.mult), reads=[("pb", 0), "gps"], writes=["ropeA"])
                P.op("dve", lambda e, j0=j0: e.tensor_tensor(ropeB[0:4, :], pb[1][0:4, :], gas[:, j0:j0 + 512], ALU.mult), reads=[("pb", 1), "gas"], writes=["ropeB"])
                P.op("dve", lambda e, j0=j0: e.tensor_tensor(ms[:, j0:j0 + 512], ropeA[0:4, :], ropeB[0:4, :], ALU.add), reads=["ropeA", "ropeB"], writes=["ms"])
            tr_tok(ms, "ms", msT[:], "msT", 16)
            P.dma("sp", lambda e: e.dma_start(out=xst, in_=xs_scr[par]), reads=["xs_scr"], writes=["xst"])
            P.dma("sp", lambda e: e.dma_start(out=gbs[:, 0, :], in_=lng[l:l + 1, :].partition_broadcast(4)), writes=["gbs"])
            P.dma("sp", lambda e: e.dma_start(out=gbs[:, 1, :], in_=lnb[l:l + 1, :].partition_broadcast(4)), writes=["gbs"])
            for jb in range(4):
                j0 = jb * 512
                wo, kwo = load_w2(w_o[l, :, j0:j0 + 512], D, 512)
                bk = jb % 2
                for kt in range(16):
                    P.op("pe", lambda e, wo=wo, kt=kt, bk=bk: e.matmul(pb[bk][0:4, :], lhsT=msT[:, kt, :], rhs=wo[:, kt, :], start=(kt == 0), stop=(kt == 15)), reads=["msT", kwo], writes=[("pb", bk)])
                P.op("dve", lambda e, bk=bk, j0=j0: e.scalar_tensor_tensor(xst[:, j0:j0 + 512], xst[:, j0:j0 + 512], ALPHA, pb[bk][0:4, :], ALU.mult, ALU.add),
                     reads=[("pb", bk), "xst"], writes=["xst"])
            ln4(xst, "xst")
            if last:
                P.dma("sp", lambda e: e.dma_start(out=y_s, in_=xst), reads=["xst"])
            else:
                P.dma("sp", lambda e: e.dma_start(out=xs_scr[1 - par], in_=xst), reads=["xst"], writes=["xs_scr"])
                xs_to_T()
            bar()

        if stop is None and with_sample:
            sample_init()
        for l_ in range(DEPTH if stop is None else 1):
            for h_ in range(2 if stop is None else 1):
                pass_body(l_, h_)
            if stop is None and with_sample:
                sample_pass(l_)
        P.emit()
    return nc


_NC_CACHE = {}


def _get_nc(npool):
    if npool not in _NC_CACHE:
        _NC_CACHE[npool] = build(npool)
    return _NC_CACHE[npool]


def make_in_maps(inputs, n_cores=8):
    f = lambda a: np.ascontiguousarray(np.asarray(a))
    x_prompt = f(inputs["x_prompt"]); x_sample = f(inputs["x_sample"])
    cache_k = f(inputs["cache_k"]); cache_v = f(inputs["cache_v"]); cache_ik = f(inputs["cache_idx_k"])
    npool = cache_k.shape[1]
    ck = cache_k.reshape(DEPTH * npool * 128, 1024)
    cv = cache_v.reshape(DEPTH * npool * 128, 1024)
    cik = cache_ik.reshape(DEPTH * npool, 8192)
    state_pool = f(inputs["state_pool"]); page_table = f(inputs["page_table"]).astype(np.int32)
    cst = _consts()
    shared = {
        "ck": ck, "cv": cv, "cik": cik,
        "lneg": f(inputs["ln_emb_g"]).reshape(1, D), "lneb": f(inputs["ln_emb_b"]).reshape(1, D),
        "w_in": f(inputs["w_in"]), "w_mix": f(inputs["w_pool_mix"]), "pscale": f(inputs["pool_scale"]),
        "w_po": f(inputs["w_pool_out"]), "w_ao": f(inputs["w_attn_out"]), "w_o": f(inputs["w_o"]),
        "lng": f(inputs["ln_g"]), "lnb": f(inputs["ln_b"]),
    }
    shared.update(cst)
    maps = []
    for c in range(n_cores):
        m = dict(shared)
        m["xp"] = x_prompt[c % 4]
        m["xs"] = x_sample[c]
        m["stp"] = np.ascontiguousarray(state_pool[:, c])
        m["pt"] = np.ascontiguousarray(page_table[c].reshape(128, 1))
        maps.append(m)
    return maps, npool


def kernel(**inputs):
    maps, npool = make_in_maps(inputs)
    nc = _get_nc(npool)
    res = run_bass_kernel_spmd(nc, maps, core_ids=list(range(8)))
    r = res.results
    y_prompt = np.stack([r[b]["y_p"] for b in range(4)]).reshape(4, SEQ, D)
    y_sample = np.stack([r[c]["y_s"] for c in range(8)]).reshape(8, 4, D)
    k_prompt = np.stack([r[b]["k_p"] for b in range(4)], axis=1).reshape(DEPTH, 4, SEQ, 8, 128)
    v_prompt = np.stack([r[b]["v_p"] for b in range(4)], axis=1).reshape(DEPTH, 4, SEQ, 8, 128)
    ik_prompt = np.stack([r[b]["ik_p"] for b in range(4)], axis=1).reshape(DEPTH, 4, SEQ, 64)
    pl_prompt = np.stack([r[b]["pl_p"] for b in range(4)], axis=1).reshape(DEPTH, 4, 15, PW)
    k_sample = np.stack([r[c]["k_s"] for c in range(8)], axis=1).reshape(DEPTH, 8, 4, 8, 128)
    v_sample = np.stack([r[c]["v_s"] for c in range(8)], axis=1).reshape(DEPTH, 8, 4, 8, 128)
    ik_sample = np.stack([r[c]["ik_s"] for c in range(8)], axis=1).reshape(DEPTH, 8, 4, 64)
    pl_sample = np.stack([r[c]["pl_s"] for c in range(8)], axis=1).reshape(DEPTH, 8, 15, PW)
    return (y_prompt, y_sample, k_prompt, v_prompt, ik_prompt, pl_prompt, k_sample, v_sample, ik_sample, pl_sample)
```

```python
import os as _os
import itertools
import numpy as np
from contextlib import ExitStack
import concourse.bass as bass
import concourse.mybir as mybir
from concourse.bass_utils import run_bass_kernel_spmd

F32 = mybir.dt.float32
BF16 = mybir.dt.bfloat16
I32 = mybir.dt.int32
U32 = mybir.dt.uint32
ALU = mybir.AluOpType
AF = mybir.ActivationFunctionType
AX = mybir.AxisListType

ENGS = ("pe", "act", "dve", "pool", "sp")

D = 2048
SEQ = 2048
NT = 16
HT = 8
DEPTH = 2
PW = 1024
INW = 11344
C_UP, C_ZP, C_Q, C_K, C_V, C_ZA, C_IQ, C_IK, C_IW, C_GP, C_GA = (
    0, 1024, 2048, 3072, 4096, 5120, 6144, 7168, 7232, 7248, 9296)
IDX_SCALE = float((16 * 64) ** -0.5)
ALPHA = float((2 * DEPTH) ** 0.25)
LN_EPS = 1e-5
NEG = -1.0e30
NBIS = 30
SM_SCALE = float(128 ** -0.5)
PAST = 16384
NPG = 128


class Op:
    __slots__ = ("eng", "fn", "deps", "is_dma", "sem_idx", "count", "signal", "idx")

    def __init__(self, eng, fn, is_dma):
        self.eng = eng
        self.fn = fn
        self.deps = set()
        self.is_dma = is_dma
        self.sem_idx = None
        self.count = None
        self.signal = False
        self.idx = None


class Prog:
    N_DMA_SEMS = 14

    def __init__(self, nc):
        self.nc = nc
        self.ops = []
        self.last_writer = {}
        self.readers = {}
        self.dma_slot_last = {}
        self.dma_rr = {e: 0 for e in ENGS}
        self.final_ops = []
        self.after = None
        self.last_on = {}

    def barrier(self, fn):
        deps = set(self.last_on.values()) | set(self.dma_slot_last.values())
        o = Op("pool", fn, False)
        o.idx = len(self.ops)
        self.ops.append(o)
        o.deps = set(d for d in deps if d is not None)
        if self.after is not None:
            o.deps.add(self.after)
        self.after = o
        self.last_on["pool"] = o
        return o

    def _track(self, op, reads, writes):
        pr = [r for r in reads if isinstance(r, tuple) and r[0] == "pb"]
        if pr:
            reads = [r for r in reads if not (isinstance(r, tuple) and r[0] == "pb")]
            writes = list(writes) + pr
        for r in reads:
            w = self.last_writer.get(r)
            if w is not None:
                op.deps.add(w)
        for w_ in writes:
            w = self.last_writer.get(w_)
            if w is not None:
                op.deps.add(w)
            for rd in self.readers.get(w_, ()):
                op.deps.add(rd)
        for r in reads:
            self.readers.setdefault(r, []).append(op)
        for w_ in writes:
            self.last_writer[w_] = op
            self.readers[w_] = []
        op.deps.discard(op)

    def op(self, eng, fn, reads=(), writes=()):
        o = Op(eng, fn, False)
        o.idx = len(self.ops)
        self.ops.append(o)
        self._track(o, reads, writes)
        if self.after is not None:
            o.deps.add(self.after)
        self.last_on[eng] = o
        return o

    def dma(self, eng, fn, reads=(), writes=(), final=False):
        o = Op(eng, fn, True)
        o.idx = len(self.ops)
        self.ops.append(o)
        slot = self.dma_rr[eng] % self.N_DMA_SEMS
        self.dma_rr[eng] += 1
        o.sem_idx = slot
        prev = self.dma_slot_last.get((eng, slot))
        if prev is not None:
            o.deps.add(prev)
        self.dma_slot_last[(eng, slot)] = o
        self._track(o, reads, writes)
        if self.after is not None:
            o.deps.add(self.after)
        self.last_on[eng] = o
        if final:
            self.final_ops.append(o)
        return o

    def emit(self):
        nc = self.nc
        for o in self.ops:
            for d in o.deps:
                if d.is_dma or not (d.eng == "pe" and o.eng == "pe"):
                    d.signal = True
            if o.is_dma:
                o.signal = True
        per = {e: [] for e in ENGS}
        for o in self.ops:
            per[o.eng].append(o)
        for e in ENGS:
            c = 0
            for o in per[e]:
                if (not o.is_dma) and o.signal:
                    c += 1
                    o.count = c
        dma_cnt = {}
        for o in self.ops:
            if o.is_dma:
                k = (o.eng, o.sem_idx)
                dma_cnt[k] = dma_cnt.get(k, 0) + 16
                o.count = dma_cnt[k]
        with ExitStack() as es:
            esem = {e: es.enter_context(nc.semaphore("s_" + e)) for e in ENGS}
            dsem = {}
            for e in ENGS:
                for s in range(min(self.N_DMA_SEMS, self.dma_rr[e])):
                    dsem[(e, s)] = es.enter_context(nc.semaphore("d_%s_%d" % (e, s)))
            block = es.enter_context(nc.Block())

            def semof(d):
                return dsem[(d.eng, d.sem_idx)] if d.is_dma else esem[d.eng]

            def body(ename):
                def f(eng):
                    waited = {}
                    for o in per[ename]:
                        for d in sorted(o.deps, key=lambda x: x.idx):
                            if (not d.is_dma) and d.eng == "pe" and ename == "pe":
                                continue
                            key = (d.eng, d.sem_idx) if d.is_dma else d.eng
                            if waited.get(key, 0) >= d.count:
                                continue
                            eng.wait_ge(semof(d), d.count)
                            waited[key] = d.count
                        ins = o.fn(eng)
                        if o.signal:
                            ins.then_inc(semof(o), 16 if o.is_dma else 1)
                    if ename == "sp":
                        for k_, v_ in dma_cnt.items():
                            eng.wait_ge(dsem[k_], v_)
                return f

            block.tensor(body("pe"))
            block.scalar(body("act"))
            block.vector(body("dve"))
            block.gpsimd(body("pool"))
            block.sync(body("sp"))


def _rope_tab(pos, half):
    inv = (np.float32(10000.0) ** (-np.arange(half, dtype=np.float32) / np.float32(half))).astype(np.float32)
    ang = pos.astype(np.float32)[:, None] * inv[None, :]
    return np.cos(ang).astype(np.float32), np.sin(ang).astype(np.float32)


def _consts():
    c = {}
    pos_p = np.arange(SEQ)
    pos_s = PAST + np.arange(4)
    c["rc128"], c["rs128"] = _rope_tab(pos_p, 64)
    c["rc64"], c["rs64"] = _rope_tab(pos_p, 32)
    c["src128"], c["srs128"] = _rope_tab(pos_s, 64)
    c["src64"], c["srs64"] = _rope_tab(pos_s, 32)
    bands = np.zeros((12, 128, 128), np.float32)
    sb_st = np.zeros((4, 15, 4), np.float32)
    sb_new = np.zeros((4, 4, 4), np.float32)
    for g, w in enumerate((2, 4, 8, 16)):
        for t in range(128):
            for s in range(max(0, t - w + 1), t + 1):
                bands[g, s, t] += 1.0 / w
                bands[4 + g, s, t] += 1.0 / min(t + 1, w)
            bands[g, t, t] -= 1.0
            bands[4 + g, t, t] -= 1.0
            for j in range(t + 1, w):
                bands[8 + g, 128 + t - j, t] += 1.0 / w
        for t in range(4):
            for j in range(w):
                r = 15 + t - j
                if r >= 15:
                    sb_new[g, r - 15, t] += 1.0 / w
                else:
                    sb_st[g, r, t] += 1.0 / w
            sb_new[g, t, t] -= 1.0
    c["bands"] = bands
    c["sb_st"] = sb_st
    c["sb_new"] = sb_new
    cm = np.zeros((128, 128), np.float32)
    cm[np.triu_indices(128, 1)] = NEG
    c["causal"] = cm
    c["pow2"] = (0.5 ** np.arange(1, 33, dtype=np.float32)).astype(np.float32)[None, :]
    esel = np.zeros((64, 32, 128), np.float32)
    for q in range(4):
        for hh in range(16):
            for ch in range(32):
                esel[q * 16 + hh, ch, q * 32 + ch] = 1.0
    c["esel"] = esel.reshape(64, 4096)
    negm = np.full((128, 4), NEG, np.float32)
    for q in range(4):
        negm[q * 32, 0:q + 1] = 0.0
    c["negm"] = negm
    bo = np.zeros((128, 128), np.float32)
    for q in range(4):
        bo[q * 32:(q + 1) * 32, q * 32:(q + 1) * 32] = 1.0
    c["bones"] = bo
    dm = np.zeros((32, 8), np.float32)
    for hh in range(8):
        dm[hh * 4:(hh + 1) * 4, hh] = 1.0
    c["dmaskc"] = dm
    return c


def build(NPOOL=1280, with_sample=True, stop=None):
    nc = bass.Bass("TRN2", target_bir_lowering=False)
    P = Prog(nc)

    def din(name, shape, dt=F32):
        return nc.dram_tensor(name, list(shape), dt, kind="ExternalInput").ap()

    def dout(name, shape, dt=F32):
        return nc.dram_tensor(name, list(shape), dt, kind="ExternalOutput").ap()

    def dscr(name, shape, dt=F32):
        return nc.dram_tensor(name, list(shape), dt, kind="Internal").ap()

    xp = din("xp", [SEQ, D])
    xs_in = din("xs", [4, D])
    ck = din("ck", [DEPTH * NPOOL * 128, 1024])
    cv = din("cv", [DEPTH * NPOOL * 128, 1024])
    cik = din("cik", [DEPTH * NPOOL, 8192])
    stp = din("stp", [DEPTH, 15, PW])
    pt = din("pt", [128, 1], I32)
    lneg = din("lneg", [1, D])
    lneb = din("lneb", [1, D])
    w_in = din("w_in", [DEPTH, D, INW])
    w_mix = din("w_mix", [DEPTH, 4, 256, 256])
    pscale = din("pscale", [DEPTH, PW])
    w_po = din("w_po", [DEPTH, PW, D])
    w_ao = din("w_ao", [DEPTH, PW, D])
    w_o = din("w_o", [DEPTH, D, D])
    lng = din("lng", [DEPTH, D])
    lnb = din("lnb", [DEPTH, D])
    rc128 = din("rc128", [SEQ, 64]); rs128 = din("rs128", [SEQ, 64])
    rc64 = din("rc64", [SEQ, 32]); rs64 = din("rs64", [SEQ, 32])
    src128 = din("src128", [4, 64]); srs128 = din("srs128", [4, 64])
    src64 = din("src64", [4, 32]); srs64 = din("srs64", [4, 32])
    bands_d = din("bands", [12, 128, 128])
    sb_st_d = din("sb_st", [4, 15, 4])
    sb_new_d = din("sb_new", [4, 4, 4])
    causal_d = din("causal", [128, 128])
    pow2_d = din("pow2", [1, 32])
    esel_d = din("esel", [64, 4096])
    negm_d = din("negm", [128, 4])
    bones_d = din("bones", [128, 128])
    dmask_d = din("dmaskc", [32, 8])

    y_p = dout("y_p", [SEQ, D]); y_s = dout("y_s", [4, D])
    k_p = dout("k_p", [DEPTH, SEQ, 1024]); v_p = dout("v_p", [DEPTH, SEQ, 1024])
    ik_p = dout("ik_p", [DEPTH, SEQ, 64]); pl_p = dout("pl_p", [DEPTH, 15, PW])
    k_s = dout("k_s", [DEPTH, 4, 1024]); v_s = dout("v_s", [DEPTH, 4, 1024])
    ik_s = dout("ik_s", [DEPTH, 4, 64]); pl_s = dout("pl_s", [DEPTH, 15, PW])

    dbg = stop is not None or _os.environ.get("DBG")
    if dbg:
        dbg_ag = dout("dbg_ag", [SEQ, 1024], BF16)
        dbg_pg = dout("dbg_pg", [SEQ, 1024], BF16)
    Xs = dscr("Xs", [2, SEQ, D])
    xTs = dscr("xTs", [2, 128, NT, 16, 128], BF16)
    kTs = dscr("kTs", [128, NT, 8, 128], BF16)
    Vs = dscr("Vs", [SEQ, 1024], BF16)
    ikTs = dscr("ikTs", [128, SEQ], BF16)

    es = ExitStack()
    with es:
        def sb(name, shape, dt):
            return es.enter_context(nc.sbuf_tensor(name, list(shape), dt))

        def ps(name, shape, dt):
            return es.enter_context(nc.psum_tensor(name, list(shape), dt))

        R0 = sb("R0", [128, 16384], BF16)
        R1 = sb("R1", [128, 16384], BF16)
        R2 = sb("R2", [128, 24576], BF16)
        RW = sb("RW", [128, 24576], BF16)
        xT = R0[:, :].rearrange("p (t k n) -> p t k n", t=HT, k=16)
        mT = xT
        up_sb = R1[:, 0:8192].rearrange("p (t n) -> p t n", t=HT)
        zp_sb = R1[:, 8192:16384].rearrange("p (t n) -> p t n", t=HT)
        m_sb = R1[:, :].rearrange("p (t n) -> p t n", t=HT)
        qT = R2[:, 0:8192].rearrange("p (t h n) -> p t h n", t=HT, h=8)
        AgT = qT
        iqT = R2[:, 8192:16384].rearrange("p (t h n) -> p t h n", t=HT, h=8)
        PgT = iqT
        za_sb = R2[:, 16384:24576].rearrange("p (t n) -> p t n", t=HT)

        ikT2 = sb("ikT2", [128, SEQ], BF16)
        ident = sb("ident", [128, 128], BF16)
        identf = sb("identf", [128, 128], F32)
        bands = sb("bands_sb", [128, 12, 128], BF16)
        causal = sb("causal_sb", [128, 128], F32)
        pow2 = sb("pow2_sb", [128, 32], F32)
        cos128 = sb("cos128", [128, HT, 64], F32); sin128 = sb("sin128", [128, HT, 64], F32)
        cos64 = sb("cos64", [128, HT, 32], F32); sin64 = sb("sin64", [128, HT, 32], F32)
        iw_sb = sb("iw_sb", [128, HT, 16], F32)
        up_last = sb("up_last", [128, 1024], BF16)
        ones_bf = sb("ones_bf", [128, 1], BF16)
        stat = sb("stat", [128, 4, 6], F32)
        mv = sb("mv", [128, 2], F32)
        rstd = sb("rstd", [128, 1], F32)
        eps_t = sb("eps_t", [128, 1], F32)
        xt_st = [RW[:, i * 4096:(i + 1) * 4096].bitcast(F32) for i in range(2)]
        xbf_st = sb("xbf_st", [128, D], BF16)
        ropeA = sb("ropeA", [128, 512], F32)
        ropeB = sb("ropeB", [128, 512], F32)
        st_f32 = [sb("st_f32_%d" % i, [128, 512], F32) for i in range(2)]
        st_bf = [sb("st_bf%d" % i, [128, 1024], BF16) for i in range(2)]
        ik_st = sb("ik_st", [128, 64], F32)
        ik2_bf = sb("ik2_bf", [128, 128], BF16)
        tr_st = [sb("tr_st%d" % i, [128, 8, 128], BF16) for i in range(2)]
        gb_bc = R1[:, 8192:16384].bitcast(F32).rearrange("p (a n) -> p a n", a=2)
        GBKEYS = ["gb"] + [("zp", t_) for t_ in range(HT)] + [("m", t_) for t_ in range(4, 8)]

        pb = [ps("pb%d" % i, [128, 512], F32) for i in range(8)]

        def pbf(i):
            return pb[i][:, :].bitcast(BF16)

        P.op("pool", lambda e: e.memset(identf[:], 0.0), writes=["identf"])
        P.op("pool", lambda e: e.affine_select(identf[:], identf[:], [[1, 128]], ALU.not_equal, 1.0,
                                               base=0, channel_multiplier=-1), reads=["identf"], writes=["identf"])
        P.op("dve", lambda e: e.tensor_copy(ident[:], identf[:]), reads=["identf"], writes=["ident"])
        P.op("dve", lambda e: e.memset(ones_bf[:], 1.0), writes=["ones"])
        P.op("dve", lambda e: e.memset(eps_t[:], LN_EPS), writes=["eps"])
        P.dma("pool", lambda e: e.dma_start(out=bands[:], in_=bands_d.rearrange("g s t -> s g t")), writes=["bands"])
        P.dma("sp", lambda e: e.dma_start(out=causal[:], in_=causal_d), writes=["causal"])
        P.dma("sp", lambda e: e.dma_start(out=pow2[:], in_=pow2_d.partition_broadcast(128)), writes=["pow2"])

        rr = {"pb": 0, "st": 0, "tr": 0, "xt": 0, "w": 0}
        fdummy = sb("fdummy", [128, 1], F32)
        bis_sb = sb("bis_sb", [128, 4, 40], F32)
        gei_sb = sb("gei_sb", [128, 4], I32)
        P2KEYS = ["wmix", "psc", ("dT", 0), ("dT", 1), ("pgf", 0), ("pgf", 1)]

        def fence(old, new):
            P.op("pool", lambda e: e.memset(fdummy[:], 0.0), writes=list(old) + list(new))

        def layer_norm_tile(src_key, x_ap, g_key, out_key):
            for j in range(4):
                P.op("dve", lambda e, j=j: e.bn_stats(stat[:, j, :], x_ap[:, j * 512:(j + 1) * 512]),
                     reads=[src_key], writes=[("stat", j)])
            P.op("dve", lambda e: e.bn_aggr(mv[:], stat[:].rearrange("p a b -> p (a b)")),
                 reads=[("stat", j) for j in range(4)], writes=["mv"])
            P.op("act", lambda e: e.activation(rstd[:], mv[:, 1:2], AF.Sqrt, bias=eps_t[:]), reads=["mv", "eps"], writes=["rstd"])
            P.op("dve", lambda e: e.reciprocal(rstd[:], rstd[:]), reads=["rstd"], writes=["rstd"])
            P.op("dve", lambda e: e.tensor_scalar(x_ap, x_ap, mv[:, 0:1], rstd[:], ALU.subtract, ALU.mult),
                 reads=[src_key, "mv", "rstd"], writes=[src_key])
            P.op("pool", lambda e: e.tensor_tensor(x_ap, x_ap, gb_bc[:, 0, :], ALU.mult), reads=[src_key] + GBKEYS, writes=[src_key])
            P.op("dve", lambda e: e.tensor_tensor(x_ap, x_ap, gb_bc[:, 1, :], ALU.add), reads=[src_key] + GBKEYS, writes=[out_key, src_key])

        def to_xT_scratch(x_ap, x_key, par, gt, np_=128):
            P.op("act", lambda e: e.copy(xbf_st[:np_, :], x_ap), reads=[x_key], writes=["xbf"])
            for g in range(2):
                bk = 2 + g
                for j in range(8):
                    kt = g * 8 + j
                    P.op("pe", lambda e, kt=kt, j=j, bk=bk: e.transpose(pbf(bk)[:, j * 128:j * 128 + np_], xbf_st[:np_, kt * 128:(kt + 1) * 128], ident[:np_, :np_]),
                         reads=["xbf", "ident"], writes=[("pb", bk)])
                tr = tr_st[g]
                P.op("act" if g == 0 else "dve",
                     (lambda e, tr=tr, bk=bk: e.copy(tr[:], pbf(bk).rearrange("p (j n) -> p j n", j=8))) if g == 0 else
                     (lambda e, tr=tr, bk=bk: e.tensor_copy(tr[:], pbf(bk).rearrange("p (j n) -> p j n", j=8))),
                     reads=[("pb", bk)], writes=[("tr", g)])
                P.dma("sp", lambda e, tr=tr, g=g: e.dma_start(out=xTs[par, :, gt, g * 8:(g + 1) * 8, :], in_=tr[:]),
                      reads=[("tr", g)], writes=[("xTs", par, gt)])

        P.dma("sp", lambda e: e.dma_start(out=gb_bc[:, 0, :], in_=lneg.partition_broadcast(128)), writes=GBKEYS)
        P.dma("sp", lambda e: e.dma_start(out=gb_bc[:, 1, :], in_=lneb.partition_broadcast(128)), writes=GBKEYS)
        for gt in range(NT):
            xt = xt_st[gt % 2]
            xk = ("xt", gt % 2)
            P.dma("sp", lambda e, xt=xt, gt=gt: e.dma_start(out=xt[:], in_=xp[gt * 128:(gt + 1) * 128, :]), writes=[xk])
            layer_norm_tile(xk, xt[:], "gb", xk)
            P.dma("sp", lambda e, xt=xt, gt=gt: e.dma_start(out=Xs[0, gt * 128:(gt + 1) * 128, :], in_=xt[:]),
                  reads=[xk], writes=[("Xs", 0, gt)])
            to_xT_scratch(xt[:], xk, 0, gt)

        def rope_ps(bk, nh, half, cos_ap, sin_ap, out_ap, np_=128):
            n = nh * 2 * half
            src = pb[bk][:np_, 0:n].rearrange("p (h two f) -> p h two f", h=nh, two=2)
            cb = cos_ap.rearrange("p (a b f) -> p a b f", a=1, b=1).to_broadcast([np_, nh, 2, half])
            sbb = sin_ap.rearrange("p (a b f) -> p a b f", a=1, b=1).to_broadcast([np_, nh, 2, half])
            A = ropeA[:np_, 0:n].rearrange("p (h two f) -> p h two f", h=nh, two=2)
            B = ropeB[:np_, 0:n].rearrange("p (h two f) -> p h two f", h=nh, two=2)
            o = out_ap.rearrange("p (h two f) -> p h two f", h=nh, two=2)
            P.op("dve", lambda e: e.tensor_tensor(A, src, cb, ALU.mult), reads=[("pb", bk), "ropetab"], writes=["ropeA"])
            P.op("dve", lambda e: e.tensor_tensor(B, src, sbb, ALU.mult), reads=[("pb", bk), "ropetab"], writes=["ropeB"])
            return A, B, o

        def rope_finish(A, B, o, okey):
            P.op("pool", lambda e: e.tensor_tensor(o[:, :, 0, :], A[:, :, 0, :], B[:, :, 1, :], ALU.subtract),
                 reads=["ropeA", "ropeB"], writes=[okey])
            P.op("pool", lambda e: e.tensor_tensor(o[:, :, 1, :], A[:, :, 1, :], B[:, :, 0, :], ALU.add),
                 reads=["ropeA", "ropeB"], writes=[okey, "ropeA", "ropeB"])

        def transposes8(src_bf, src_key, dst_ap, dst_key, bk, evac_eng="act"):
            for j in range(8):
                P.op("pe", lambda e, j=j: e.transpose(pbf(bk)[:, j * 128:(j + 1) * 128], src_bf[:, j * 128:(j + 1) * 128], ident[:]),
                     reads=[src_key, "ident"], writes=[("pb", bk)])
            if evac_eng == "act":
                P.op("act", lambda e: e.copy(dst_ap, pbf(bk).rearrange("p (j n) -> p j n", j=8)), reads=[("pb", bk)], writes=[dst_key])
            else:
                P.op("dve", lambda e: e.tensor_copy(dst_ap, pbf(bk).rearrange("p (j n) -> p j n", j=8)), reads=[("pb", bk)], writes=[dst_key])

        def transposes4(src_bf, src_key, dst_ap, dst_key, bk, evac_eng="act"):
            for j in range(4):
                P.op("pe", lambda e, j=j: e.transpose(pbf(bk)[:, j * 128:(j + 1) * 128], src_bf[:, j * 128:(j + 1) * 128], ident[:]),
                     reads=[src_key, "ident"], writes=[("pb", bk)])
            if evac_eng == "act":
                P.op("act", lambda e: e.copy(dst_ap, pbf(bk)[:, 0:512].rearrange("p (j n) -> p j n", j=4)), reads=[("pb", bk)], writes=[dst_key])
            else:
                P.op("dve", lambda e: e.tensor_copy(dst_ap, pbf(bk)[:, 0:512].rearrange("p (j n) -> p j n", j=4)), reads=[("pb", bk)], writes=[dst_key])

        P1_BLOCKS = []
        for kind, c0 in (("up", C_UP), ("zp", C_ZP), ("q", C_Q), ("k", C_K), ("v", C_V), ("za", C_ZA), ("iq", C_IQ)):
            P1_BLOCKS.append((kind, c0, 512, 0))
            P1_BLOCKS.append((kind, c0 + 512, 512, 1))
        P1_BLOCKS.append(("ikiw", C_IK, 80, 0))
        if _os.environ.get("P1KINDS"):
            P1_BLOCKS = [b for b in P1_BLOCKS if b[0] in _os.environ["P1KINDS"].split(",")]

        wbuf = [RW[:, i * 8192:(i + 1) * 8192].rearrange("p (k n) -> p k n", k=16) for i in range(3)]

        def load_w(dram_ap_2d, krows, ncols, key):
            i = rr["w"] % 3
            rr["w"] += 1
            buf = wbuf[i]
            kt = krows // 128
            P.dma("pool", lambda e: e.dma_start(out=buf[:, 0:kt, 0:ncols], in_=dram_ap_2d.rearrange("(k p) n -> p k n", p=128)),
                  writes=[("wbuf", i)])
            return buf, ("wbuf", i)

        fence([("xt", 0), ("xt", 1)], [("wbuf", i_) for i_ in range(3)])
        def pass_body(l, h):
            if stop == "P0":
                return
            par = l % 2
            last = (l == DEPTH - 1)
            if True:
                P.dma("sp", lambda e, h=h, par=par: e.dma_start(out=xT[:], in_=xTs[par, :, h * HT:(h + 1) * HT, :, :]),
                      reads=[("xTs", par, h * HT + t) for t in range(HT)], writes=[("xT", t) for t in range(HT)])
                for nm, tab, src, hf in (("cos128", cos128, rc128, 64), ("sin128", sin128, rs128, 64),
                                         ("cos64", cos64, rc64, 32), ("sin64", sin64, rs64, 32)):
                    P.dma("sp", lambda e, tab=tab, src=src, h=h: e.dma_start(
                        out=tab[:], in_=src[h * 1024:(h + 1) * 1024, :].rearrange("(t p) f -> p t f", p=128)),
                        writes=["ropetab"])
                if h == 1:
                    P.dma("sp", lambda e: e.dma_start(out=ikT2[:, 0:1024], in_=ikTs[:, 0:1024]),
                          reads=[("ikTs", c) for c in range(HT)], writes=[("ikT2", c) for c in range(HT)])

                for (kind, c0, ncols, sub) in P1_BLOCKS:
                    wb, wkey = load_w(w_in[l, :, c0:c0 + ncols], D, ncols, None)
                    for t in range(HT):
                        gt = h * HT + t
                        bk = rr["pb"] % 2
                        rr["pb"] += 1
                        for kt in range(16):
                            P.op("pe", lambda e, t=t, kt=kt, bk=bk, wb=wb, ncols=ncols: e.matmul(
                                pb[bk][:, 0:ncols], lhsT=xT[:, t, kt, :], rhs=wb[:, kt, 0:ncols], start=(kt == 0), stop=(kt == 15)),
                                reads=[("xT", t), wkey], writes=[("pb", bk)])
                        pk = ("pb", bk)
                        cs = slice(sub * 512, sub * 512 + 512)
                        if kind == "up":
                            P.op("act", lambda e, t=t, bk=bk, cs=cs: e.copy(up_sb[:, t, cs], pb[bk][:, :]), reads=[pk], writes=[("up", t)])
                            if h == 1 and t == HT - 1:
                                sf = st_f32[sub]
                                P.op("dve", lambda e, bk=bk, sf=sf: e.tensor_copy(sf[:], pb[bk][:, :]), reads=[pk], writes=[("stf", sub)])
                                P.dma("sp", lambda e, sf=sf, cs=cs: e.dma_start(out=pl_p[l, :, cs], in_=sf[113:128, :]),
                                      reads=[("stf", sub)], final=True)
                        elif kind == "zp":
                            P.op("act", lambda e, t=t, bk=bk, cs=cs: e.activation(zp_sb[:, t, cs], pb[bk][:, :], AF.Silu), reads=[pk], writes=[("zp", t)])
                        elif kind == "za":
                            P.op("act", lambda e, t=t, bk=bk, cs=cs: e.activation(za_sb[:, t, cs], pb[bk][:, :], AF.Silu), reads=[pk], writes=[("za", t)])
                        elif kind == "q":
                            A, B, o = rope_ps(bk, 4, 64, cos128[:, t, :], sin128[:, t, :], st_bf[0][:, cs])
                            rope_finish(A, B, o, ("stbf", 0))
                            transposes4(st_bf[0][:, cs], ("stbf", 0), qT[:, t, sub * 4:(sub + 1) * 4, :], ("qT", t), 2)
                        elif kind == "iq":
                            A, B, o = rope_ps(bk, 8, 32, cos64[:, t, :], sin64[:, t, :], st_bf[1][:, cs])
                            rope_finish(A, B, o, ("stbf", 1))
                            transposes4(st_bf[1][:, cs], ("stbf", 1), iqT[:, t, sub * 4:(sub + 1) * 4, :], ("iqT", t), 3, evac_eng="dve")
                        elif kind == "k":
                            sf = st_f32[sub]
                            A, B, o = rope_ps(bk, 4, 64, cos128[:, t, :], sin128[:, t, :], sf[:, :])
                            rope_finish(A, B, o, ("stf", sub))
                            P.dma("sp", lambda e, sf=sf, gt=gt, cs=cs: e.dma_start(out=k_p[l, gt * 128:(gt + 1) * 128, cs], in_=sf[:]),
                                  reads=[("stf", sub)], final=True)
                            P.op("act", lambda e, sf=sf, cs=cs: e.copy(st_bf[0][:, cs], sf[:]), reads=[("stf", sub)], writes=[("stbf", 0)])
                            tr = tr_st[0]
                            transposes4(st_bf[0][:, cs], ("stbf", 0), tr[:, 0:4, :], ("tr", 0), 2)
                            P.dma("sp", lambda e, tr=tr, gt=gt, sub=sub: e.dma_start(out=kTs[:, gt, sub * 4:(sub + 1) * 4, :], in_=tr[:, 0:4, :]),
                                  reads=[("tr", 0)], writes=[("kTs", gt)])
                        elif kind == "v":
                            sf = st_f32[sub]
                            P.op("act", lambda e, bk=bk, sf=sf: e.copy(sf[:], pb[bk][:, :]), reads=[pk], writes=[("stf", sub)])
                            if not _os.environ.get("SKIPVP"):
                                P.dma("sp", lambda e, sf=sf, gt=gt, cs=cs: e.dma_start(out=v_p[l, gt * 128:(gt + 1) * 128, cs], in_=sf[:]),
                                      reads=[("stf", sub)], final=True)
                            P.op("dve", lambda e, bk=bk, cs=cs: e.tensor_copy(st_bf[1][:, cs], pb[bk][:, :]), reads=[pk], writes=[("stbf", 1)])
                            P.dma("sp", lambda e, gt=gt, cs=cs: e.dma_start(out=Vs[gt * 128:(gt + 1) * 128, cs], in_=st_bf[1][:, cs]),
                                  reads=[("stbf", 1)], writes=[("Vs", gt)])
                        elif kind == "ikiw":
                            A, B, o = rope_ps(bk, 1, 32, cos64[:, t, :], sin64[:, t, :], ik_st[:, :])
                            rope_finish(A, B, o, "ikst")
                            P.dma("sp", lambda e, gt=gt: e.dma_start(out=ik_p[l, gt * 128:(gt + 1) * 128, :], in_=ik_st[:]),
                                  reads=["ikst"], final=True)
                            P.op("act", lambda e: e.copy(ik2_bf[:, 0:64], ik_st[:]), reads=["ikst"], writes=["ik2"])
                            P.op("act", lambda e: e.copy(ik2_bf[:, 64:128], ik_st[:]), reads=["ikst"], writes=["ik2"])
                            P.op("pe", lambda e: e.transpose(pbf(3)[:, 0:128], ik2_bf[:], ident[:]), reads=["ik2", "ident"], writes=[("pb", 3)])
                            P.op("act", lambda e, gt=gt: e.copy(ikT2[:, gt * 128:(gt + 1) * 128], pbf(3)[:, 0:128]),
                                 reads=[("pb", 3)], writes=[("ikT2", gt)])
                            if h == 0:
                                P.dma("sp", lambda e, gt=gt: e.dma_start(out=ikTs[:, gt * 128:(gt + 1) * 128], in_=ikT2[:, gt * 128:(gt + 1) * 128]),
                                      reads=[("ikT2", gt)], writes=[("ikTs", gt)])
                            P.op("dve", lambda e, t=t, bk=bk: e.tensor_scalar(iw_sb[:, t, :], pb[bk][:, 64:80], IDX_SCALE, None, ALU.mult),
                                 reads=[pk], writes=[("iw", t)])

                if stop == "P1":
                    return
                sc = RW[:, 0:4096].bitcast(F32)
                junk = RW[:, 4096:6144]
                maskb = RW[:, 6144:8192]
                maskT = RW[:, 8192:10240].rearrange("p (c n) -> p c n", c=NT)
                Rb = [RW[:, i * 512:(i + 1) * 512] for i in range(8)]
                LBANKS = [0, 1, 4, 5, 6, 7]
                diagW = RW[:, 12288:14336].rearrange("p (h n) -> p h n", h=16)
                Eb = [RW[:, 14336 + i * 1024:14336 + (i + 1) * 1024].rearrange("p (h n) -> p h n", h=8) for i in range(2)]
                PTb = [RW[:, 16384 + i * 1024:16384 + (i + 1) * 1024].rearrange("p (h n) -> p h n", h=8) for i in range(2)]
                kTc = [RW[:, 18432 + i * 1024:18432 + (i + 1) * 1024].rearrange("p (h n) -> p h n", h=8) for i in range(2)]
                Vc = [RW[:, 20480 + i * 1024:20480 + (i + 1) * 1024].rearrange("p (h n) -> p h n", h=8) for i in range(2)]
                smallf = RW[:, 22528:22528 + 256].bitcast(F32)
                lo = smallf[:, 0:1]; mid = smallf[:, 1:2]; cnt = smallf[:, 2:3]; wtab = smallf[:, 8:8 + 32]
                rmax = smallf[:, 3:4]; rden = smallf[:, 48:56]
                gei = RW[:, 22528 + 256:22528 + 258].bitcast(I32)
                On = RW[:, 23040:24064].bitcast(F32)
                ag_bf = st_bf[0]
                wk = ["wk_sc", "wk_junk", "wk_mask", "wk_maskT", "wk_diag", "wk_small"]
                P3KEYS = (wk + ["wk_On"] + [("Rb", i_) for i_ in range(8)] + [("Eb", i_, j_) for i_ in range(2) for j_ in range(2)]
                          + [("PT", i_, j_) for i_ in range(2) for j_ in range(2)] + [("kTc", i_) for i_ in range(2)] + [("Vc", i_) for i_ in range(2)])
                WKEYS = [("wbuf", i_) for i_ in range(3)]
                fence(WKEYS + P2KEYS, P3KEYS)
                rrR_box = [0]
                bis = bis_sb
                scb = [R0[:, ch_ * 4096:(ch_ + 1) * 4096].bitcast(F32) for ch_ in range(4)]
                fence([("xT", t_) for t_ in range(HT)], [("sc", ch_) for ch_ in range(4)])

                def p3_indexer(t):
                    rrR = rrR_box[0]
                    ch = t % 4
                    sc = scb[ch]
                    sckey = ("sc", ch)
                    lo = bis[:, ch, 0:1]; rmax = bis[:, ch, 3:4]; wtab = bis[:, ch, 8:40]
                    bkey = ("bs", ch)
                    i = h * HT + t
                    L = (i + 1) * 128
                    for hh in range(16):
                        P.op("dve", lambda e, hh=hh, t=t: e.tensor_scalar(diagW[:, hh, :], ident[:], iw_sb[:, t, hh:hh + 1], None, ALU.mult),
                             reads=["ident", ("iw", t)], writes=["wk_diag"])
                    nkb = (L + 511) // 512
                    for kb in range(nkb):
                        k0 = kb * 512
                        kn = min(512, L - k0)
                        pend = []

                        def emit_H(hh_, rb_, rkey_, kn_):
                            P.op("pe", lambda e: e.matmul(pb[2][:, 0:kn_], lhsT=diagW[:, hh_, :], rhs=rb_[:, 0:kn_], start=(hh_ == 0), stop=(hh_ == 15)),
                                 reads=["wk_diag", rkey_], writes=[("pb", 2)])

                        for hh in range(16):
                            g, half = hh // 2, hh % 2
                            bk = LBANKS[rr["pb"] % 6]
                            rr["pb"] += 1
                            P.op("pe", lambda e, t=t, g=g, half=half, bk=bk, k0=k0, kn=kn: e.matmul(
                                pb[bk][:, 0:kn], lhsT=iqT[half * 64:(half + 1) * 64, t, g, :], rhs=ikT2[half * 64:(half + 1) * 64, k0:k0 + kn],
                                start=True, stop=True),
                                reads=[("iqT", t)] + [("ikT2", c) for c in range(k0 // 128, (k0 + kn) // 128)], writes=[("pb", bk)])
                            rb = Rb[rrR % 8]
                            rkey = ("Rb", rrR % 8)
                            rrR += 1
                            if hh % 2 == 0:
                                P.op("act", lambda e, rb=rb, bk=bk, kn=kn: e.activation(rb[:, 0:kn], pb[bk][:, 0:kn], AF.Relu),
                                     reads=[("pb", bk)], writes=[rkey])
                            else:
                                P.op("dve", lambda e, rb=rb, bk=bk, kn=kn: e.tensor_scalar(rb[:, 0:kn], pb[bk][:, 0:kn], 0.0, None, ALU.max),
                                     reads=[("pb", bk)], writes=[rkey])
                            pend.append((hh, rb, rkey, kn))
                            if len(pend) > 3:
                                emit_H(*pend.pop(0))
                        for p_ in pend:
                            emit_H(*p_)
                        P.op("act", lambda e, k0=k0, kn=kn: e.copy(sc[:, k0:k0 + kn], pb[2][:, 0:kn]), reads=[("pb", 2)], writes=[sckey])
                    P.op("dve", lambda e, L=L: e.tensor_tensor(sc[:, L - 128:L], sc[:, L - 128:L], causal[:], ALU.add),
                         reads=[sckey, "causal"], writes=[sckey])

                    if i >= 2:
                        Lf = L - 128
                        P.op("dve", lambda e, L=L: e.tensor_reduce(rmax, sc[:, 0:L], AX.X, ALU.max), reads=[sckey], writes=[bkey])
                        P.op("dve", lambda e, Lf=Lf: e.tensor_reduce(lo, sc[:, 0:Lf], AX.X, ALU.min), reads=[sckey], writes=[bkey])
                        P.op("dve", lambda e: e.tensor_tensor(rmax, rmax, lo, ALU.subtract), reads=[bkey], writes=[bkey])
                        P.op("dve", lambda e: e.tensor_scalar(wtab, pow2[:, 0:32], rmax, None, ALU.mult), reads=[bkey, "pow2"], writes=[bkey])
                    rrR_box[0] = rrR

                def p3_bis_iter(t, n):
                    i = h * HT + t
                    if i < 2:
                        return
                    L = (i + 1) * 128
                    ch = t % 4
                    sc = scb[ch]
                    sckey = ("sc", ch); bkey = ("bs", ch)
                    lo = bis[:, ch, 0:1]; mid = bis[:, ch, 1:2]; cnt = bis[:, ch, 2:3]; wtab = bis[:, ch, 8:40]
                    gei = gei_sb[:, ch:ch + 1]
                    P.op("dve", lambda e: e.tensor_tensor(mid, lo, wtab[:, n:n + 1], ALU.add), reads=[bkey], writes=[bkey])
                    P.op("dve", lambda e: e.tensor_scalar(junk[:, 0:L], sc[:, 0:L], mid, None, ALU.is_ge, ALU.add, accum_out=cnt),
                         reads=[sckey, bkey], writes=["wk_junk", bkey])
                    P.op("dve", lambda e: e.tensor_scalar(gei, cnt, 255.5, None, ALU.is_ge), reads=[bkey], writes=[bkey])
                    P.op("dve", lambda e: e.copy_predicated(lo, gei, mid), reads=[bkey], writes=[bkey])

                def p3_finish(t):
                    i = h * HT + t
                    L = (i + 1) * 128
                    ch = t % 4
                    sc = scb[ch]
                    sckey = ("sc", ch); bkey = ("bs", ch)
                    lo = bis[:, ch, 0:1]
                    if i >= 2:
                        P.op("dve", lambda e: e.tensor_scalar(maskb[:, 0:L], sc[:, 0:L], lo, None, ALU.is_ge),
                             reads=[sckey, bkey], writes=["wk_mask"])
                    else:
                        P.op("dve", lambda e: e.tensor_scalar(maskb[:, 0:L], sc[:, 0:L], -1.0e29, None, ALU.is_ge),
                             reads=[sckey], writes=["wk_mask"])
                    for c0 in range(0, i + 1, 8):
                        cn = min(8, i + 1 - c0)
                        for j in range(cn):
                            c = c0 + j
                            P.op("pe", lambda e, c=c, j=j: e.transpose(pbf(3)[:, j * 128:(j + 1) * 128], maskb[:, c * 128:(c + 1) * 128], ident[:]),
                                 reads=["wk_mask", "ident"], writes=[("pb", 3)])
                        P.op("act", lambda e, c0=c0, cn=cn: e.copy(maskT[:, c0:c0 + cn, :], pbf(3)[:, 0:cn * 128].rearrange("p (c n) -> p c n", c=cn)),
                             reads=[("pb", 3)], writes=["wk_maskT"])
                    SBK = [(4, 5), (0, 1)]

                    def ld_k(c):
                        kb_ = kTc[c % 2]
                        P.dma("sp", lambda e: e.dma_start(out=kb_[:], in_=kTs[:, c, :, :]), reads=[("kTs", c)], writes=[("kTc", c % 2)])

                    def ld_v(c):
                        vb_ = Vc[c % 2]
                        P.dma("sp", lambda e: e.dma_start(out=vb_[:], in_=Vs[c * 128:(c + 1) * 128, :].rearrange("p (h n) -> p h n", h=8)),
                              reads=[("Vs", c)], writes=[("Vc", c % 2)])

                    def emit_S(c):
                        kb_ = kTc[c % 2]
                        b0, b1 = SBK[c % 2]
                        for hd in range(8):
                            bk = b0 if hd < 4 else b1
                            P.op("pe", lambda e, hd=hd, bk=bk: e.matmul(pb[bk][:, (hd % 4) * 128:(hd % 4 + 1) * 128],
                                                                       lhsT=kb_[:, hd, :], rhs=qT[:, t, hd, :], start=True, stop=True),
                                 reads=[("kTc", c % 2), ("qT", t)], writes=[("pb", bk)])

                    def emit_E(c):
                        eb = Eb[c % 2]; ptb = PTb[c % 2]
                        b0, b1 = SBK[c % 2]
                        for hf, bk in ((0, b0), (1, b1)):
                            P.op("act", lambda e, hf=hf, bk=bk: e.activation(eb[:, hf * 4:(hf + 1) * 4, :],
                                                                           pb[bk][:, :].rearrange("p (h n) -> p h n", h=4), AF.Exp, scale=SM_SCALE),
                                 reads=[("pb", bk)], writes=[("Eb", c % 2, hf)])
                        P.op("pool", lambda e: e.tensor_tensor(ptb[:, 0:4, :], eb[:, 0:4, :], maskT[:, c:c + 1, :].to_broadcast([128, 4, 128]), ALU.mult),
                             reads=[("Eb", c % 2, 0), "wk_maskT"], writes=[("PT", c % 2, 0)])
                        P.op("dve", lambda e: e.tensor_tensor(ptb[:, 4:8, :], eb[:, 4:8, :], maskT[:, c:c + 1, :].to_broadcast([128, 4, 128]), ALU.mult),
                             reads=[("Eb", c % 2, 1), "wk_maskT"], writes=[("PT", c % 2, 1)])

                    def emit_PV(c):
                        ptb = PTb[c % 2]; vb_ = Vc[c % 2]
                        for hd in range(8):
                            P.op("pe", lambda e, hd=hd: e.matmul(
                                pb[6 + hd // 4][:, (hd % 4) * 128:(hd % 4 + 1) * 128], lhsT=ptb[:, hd, :], rhs=vb_[:, hd, :],
                                start=(c == 0 and hd % 4 == 0), stop=(c == i), skip_group_check=True),
                                reads=[("PT", c % 2, hd // 4), ("Vc", c % 2)], writes=[("pb", 6 + hd // 4)])
                            P.op("pe", lambda e, hd=hd: e.matmul(
                                pb[3][:, 256 + hd:256 + hd + 1], lhsT=ptb[:, hd, :], rhs=ones_bf[:, 0:1],
                                start=(c == 0 and hd == 0), stop=(c == i), skip_group_check=True),
                                reads=[("PT", c % 2, hd // 4), "ones"], writes=[("pb", 3)])

                    for c in range(min(2, i + 1)):
                        ld_k(c); ld_v(c); emit_S(c); emit_E(c)
                    for c in range(i + 1):
                        if c + 2 <= i:
                            ld_k(c + 2); emit_S(c + 2)
                        emit_PV(c)
                        if c + 2 <= i:
                            ld_v(c + 2); emit_E(c + 2)
                    P.op("dve", lambda e: e.reciprocal(rden, pb[3][:, 256:264]), reads=[("pb", 3)], writes=["wk_small"])
                    for hf in range(2):
                        P.op("dve", lambda e, hf=hf: e.tensor_tensor(
                            On.rearrange("p (h n) -> p h n", h=4), pb[6 + hf][:, :].rearrange("p (h n) -> p h n", h=4),
                            rden[:, hf * 4:(hf + 1) * 4].rearrange("p (h o) -> p h o", o=1).to_broadcast([128, 4, 128]), ALU.mult),
                            reads=[("pb", 6 + hf), "wk_small"], writes=["wk_On"])
                        P.op("pool", lambda e, hf=hf, t=t: e.tensor_tensor(ag_bf[:, hf * 512:(hf + 1) * 512], On, za_sb[:, t, hf * 512:(hf + 1) * 512], ALU.mult),
                             reads=["wk_On", ("za", t)], writes=[("stbf", 0), "wk_On"])
                    if dbg and l == 0:
                        P.dma("sp", lambda e, i=i: e.dma_start(out=dbg_ag[i * 128:(i + 1) * 128, :], in_=ag_bf[:]), reads=[("stbf", 0)])
                    transposes8(ag_bf, ("stbf", 0), AgT[:, t, :, :], ("qT", t), 0)


                for g0 in (0, 4):
                    for t in range(g0, g0 + 4):
                        p3_indexer(t)
                    for n in range(NBIS):
                        for t in range(g0, g0 + 4):
                            p3_bis_iter(t, n)
                    for t in range(g0, g0 + 4):
                        p3_finish(t)

                if stop == "P3":
                    return
                wmix_sb = RW[:, 0:2048].rearrange("p (g c e) -> p g c e", g=4, c=2)
                psc_bc = RW[:, 2048:4096].bitcast(F32)
                dT_sb = RW[:, 4096:5120].rearrange("p (c n) -> p c n", c=8)
                pg_f = RW[:, 5120:7168].bitcast(F32)
                pg_bf = st_bf[1]
                guard = []
                fence(P3KEYS + WKEYS, P2KEYS)
                P.dma("pool", lambda e, l=l: e.dma_start(out=wmix_sb, in_=w_mix[l].rearrange("g (c p) e -> p g c e", p=128)),
                      reads=guard, writes=["wmix"] + guard)
                P.dma("sp", lambda e, l=l: e.dma_start(out=psc_bc, in_=pscale[l:l + 1, :].partition_broadcast(128)),
                      reads=guard, writes=["psc"] + guard)
                for t in range(HT):
                    gt = h * HT + t
                    for ct in range(8):
                        g = ct // 2
                        bk = ct // 4
                        first = (gt == 0)
                        bidx = (4 + g) if first else g
                        P.op("pe", lambda e, t=t, ct=ct, bk=bk, bidx=bidx, first=first: e.matmul(
                            pb[bk][:, (ct % 4) * 128:(ct % 4 + 1) * 128], lhsT=up_sb[:, t, ct * 128:(ct + 1) * 128], rhs=bands[:, bidx, :],
                            start=True, stop=first), reads=[("up", t), "bands"], writes=[("pb", bk)])
                        if not first:
                            if t > 0:
                                prev = up_sb[:, t - 1, ct * 128:(ct + 1) * 128]; pkey = ("up", t - 1)
                            else:
                                prev = up_last[:, ct * 128:(ct + 1) * 128]; pkey = "up_last"
                            P.op("pe", lambda e, ct=ct, bk=bk, g=g, prev=prev: e.matmul(
                                pb[bk][:, (ct % 4) * 128:(ct % 4 + 1) * 128], lhsT=prev, rhs=bands[:, 8 + g, :], start=False, stop=True),
                                reads=[pkey, "bands"], writes=[("pb", bk)])
                    for bk in range(2):
                        P.op("act", lambda e, bk=bk: e.copy(dT_sb[:, bk * 4:(bk + 1) * 4, :], pb[bk][:, :].rearrange("p (c n) -> p c n", c=4)),
                             reads=[("pb", bk)] + guard, writes=[("dT", bk)] + guard)
                    for g in range(4):
                        bk = 2 + g // 2
                        for c in range(2):
                            P.op("pe", lambda e, g=g, c=c, bk=bk: e.matmul(pb[bk][:, (g % 2) * 256:(g % 2 + 1) * 256], lhsT=dT_sb[:, g * 2 + c, :],
                                                                        rhs=wmix_sb[:, g, c, :], start=(c == 0), stop=(c == 1)),
                                 reads=[("dT", (g * 2 + c) // 4), "wmix"], writes=[("pb", bk)])
                    for hf in range(2):
                        P.op("dve", lambda e, hf=hf: e.tensor_tensor(pg_f[:, hf * 512:(hf + 1) * 512], pb[2 + hf][:, :], psc_bc[:, hf * 512:(hf + 1) * 512], ALU.mult),
                             reads=[("pb", 2 + hf), "psc"] + guard, writes=[("pgf", hf)] + guard)
                        P.op("pool", lambda e, hf=hf, t=t: e.tensor_tensor(pg_bf[:, hf * 512:(hf + 1) * 512], pg_f[:, hf * 512:(hf + 1) * 512],
                                                                          zp_sb[:, t, hf * 512:(hf + 1) * 512], ALU.mult),
                             reads=[("pgf", hf), ("zp", t)], writes=[("stbf", 1)])
                    if dbg and l == 0:
                        P.dma("sp", lambda e, gt=gt: e.dma_start(out=dbg_pg[gt * 128:(gt + 1) * 128, :], in_=pg_bf[:]), reads=[("stbf", 1)])
                    transposes8(pg_bf, ("stbf", 1), PgT[:, t, :, :], ("iqT", t), 4, evac_eng="dve")
                if h == 0:
                    P.op("pool", lambda e: e.tensor_copy(up_last[:], up_sb[:, HT - 1, :]), reads=[("up", HT - 1)], writes=["up_last"])

                if stop == "P2":
                    return
                fence(P3KEYS + P2KEYS, WKEYS)
                fence([("sc", ch_) for ch_ in range(4)], [("xT", t_) for t_ in range(HT)])
                P.dma("sp", lambda e: e.dma_start(out=xT[:], in_=xTs[par, :, h * HT:(h + 1) * HT, :, :]),
                      reads=[("xTs", par, h * HT + t_) for t_ in range(HT)], writes=[("xT", t_) for t_ in range(HT)])
                sgb = st_f32
                prodb = [ropeA, ropeB]

                def m_keys(t):
                    if t < 4:
                        return [("m", t), ("up", 2 * t), ("up", 2 * t + 1)]
                    return [("m", t), ("zp", 2 * (t - 4)), ("zp", 2 * (t - 4) + 1)]

                for jb in range(8):
                    j0 = jb * 256
                    i1 = rr["w"] % 3; rr["w"] += 1
                    i2 = rr["w"] % 3; rr["w"] += 1
                    bA = wbuf[i1]; bB = wbuf[i2]
                    kA = ("wbuf", i1); kB = ("wbuf", i2)
                    P.dma("pool", lambda e, bA=bA, j0=j0, l=l: e.dma_start(out=bA[:, 0:8, 0:256], in_=w_po[l, :, j0:j0 + 256].rearrange("(k p) n -> p k n", p=128)), writes=[kA])
                    P.dma("pool", lambda e, bA=bA, j0=j0, l=l: e.dma_start(out=bA[:, 8:16, 0:256], in_=w_ao[l, :, j0:j0 + 256].rearrange("(k p) n -> p k n", p=128)), writes=[kA])
                    P.dma("pool", lambda e, bB=bB, j0=j0, l=l: e.dma_start(out=bB[:, 0:16, 0:256], in_=w_in[l, :, C_GP + j0:C_GP + j0 + 256].rearrange("(k p) n -> p k n", p=128)), writes=[kB])
                    P.dma("pool", lambda e, bB=bB, j0=j0, l=l: e.dma_start(out=bB[:, 0:16, 256:512], in_=w_in[l, :, C_GA + j0:C_GA + j0 + 256].rearrange("(k p) n -> p k n", p=128)), writes=[kB])
                    for t in range(HT):
                        pr = (t % 2) * 2
                        for kt in range(8):
                            P.op("pe", lambda e, bA=bA, t=t, kt=kt, pr=pr: e.matmul(pb[pr][:, 0:256], lhsT=PgT[:, t, kt, :], rhs=bA[:, kt, 0:256], start=(kt == 0), stop=(kt == 7)),
                                 reads=[("iqT", t), kA], writes=[("pb", pr)])
                        for kt in range(8):
                            P.op("pe", lambda e, bA=bA, t=t, kt=kt, pr=pr: e.matmul(pb[pr][:, 256:512], lhsT=AgT[:, t, kt, :], rhs=bA[:, 8 + kt, 0:256], start=(kt == 0), stop=(kt == 7)),
                                 reads=[("qT", t), kA], writes=[("pb", pr)])
                        for kt in range(16):
                            P.op("pe", lambda e, bB=bB, t=t, kt=kt, pr=pr: e.matmul(pb[pr + 1][:, 0:256], lhsT=xT[:, t, kt, :], rhs=bB[:, kt, 0:256], start=(kt == 0), stop=(kt == 15)),
                                 reads=[("xT", t), kB], writes=[("pb", pr + 1)])
                        for kt in range(16):
                            P.op("pe", lambda e, bB=bB, t=t, kt=kt, pr=pr: e.matmul(pb[pr + 1][:, 256:512], lhsT=xT[:, t, kt, :], rhs=bB[:, kt, 256:512], start=(kt == 0), stop=(kt == 15)),
                                 reads=[("xT", t), kB], writes=[("pb", pr + 1)])
                        sg = sgb[t % 2]; prod = prodb[t % 2]
                        pkey = "ropeA" if t % 2 == 0 else "ropeB"
                        P.op("act", lambda e, sg=sg, pr=pr: e.activation(sg[:], pb[pr + 1][:, :], AF.Sigmoid), reads=[("pb", pr + 1)], writes=[("stf", t % 2)])
                        P.op("dve", lambda e, sg=sg, prod=prod, pr=pr: e.tensor_tensor(prod[:], pb[pr][:, :], sg[:], ALU.mult),
                             reads=[("pb", pr), ("stf", t % 2)], writes=[pkey])
                        P.op("pool", lambda e, prod=prod, t=t, j0=j0: e.tensor_tensor(m_sb[:, t, j0:j0 + 256], prod[:, 0:256], prod[:, 256:512], ALU.add),
                             reads=[pkey], writes=m_keys(t))

                if stop == "P4":
                    return
                for t in range(HT):
                    for g in range(2):
                        bk = 4 + g
                        for j in range(8):
                            kt = g * 8 + j
                            P.op("pe", lambda e, t=t, kt=kt, j=j, bk=bk: e.transpose(pbf(bk)[:, j * 128:(j + 1) * 128], m_sb[:, t, kt * 128:(kt + 1) * 128], ident[:]),
                                 reads=[("m", t), "ident"], writes=[("pb", bk)])
                        if g == 0:
                            P.op("act", lambda e, t=t, bk=bk: e.copy(mT[:, t, 0:8, :], pbf(bk).rearrange("p (j n) -> p j n", j=8)), reads=[("pb", bk)], writes=[("xT", t)])
                        else:
                            P.op("dve", lambda e, t=t, bk=bk: e.tensor_copy(mT[:, t, 8:16, :], pbf(bk).rearrange("p (j n) -> p j n", j=8)), reads=[("pb", bk)], writes=[("xT", t)])

                def xres_ap(j):
                    if j < 6:
                        return R2[:, j * 4096:(j + 1) * 4096].bitcast(F32)
                    return R1[:, (j - 6) * 4096:(j - 5) * 4096].bitcast(F32)

                def xres_keys(j):
                    ks = [("xres", j)]
                    if j < 2:
                        ks += [("qT", tt) for tt in range(j * 4, j * 4 + 4)]
                    elif j < 4:
                        ks += [("iqT", tt) for tt in range((j - 2) * 4, (j - 2) * 4 + 4)]
                    elif j < 6:
                        ks += [("za", tt) for tt in range((j - 4) * 4, (j - 4) * 4 + 4)]
                    else:
                        ks += [("up", tt) for tt in range((j - 6) * 4, (j - 6) * 4 + 4)] + [("m", 2 * (j - 6)), ("m", 2 * (j - 6) + 1)]
                    return ks

                P.dma("sp", lambda e, l=l: e.dma_start(out=gb_bc[:, 0, :], in_=lng[l:l + 1, :].partition_broadcast(128)), writes=GBKEYS)
                P.dma("sp", lambda e, l=l: e.dma_start(out=gb_bc[:, 1, :], in_=lnb[l:l + 1, :].partition_broadcast(128)), writes=GBKEYS)
                for t in range(HT):
                    gt = h * HT + t
                    xa = xres_ap(t)
                    P.dma("sp", lambda e, xa=xa, gt=gt, par=par: e.dma_start(out=xa, in_=Xs[par, gt * 128:(gt + 1) * 128, :]),
                          reads=[("Xs", par, gt)], writes=xres_keys(t))
                for jb in range(4):
                    j0 = jb * 512
                    wo, kwo = load_w(w_o[l, :, j0:j0 + 512], D, 512, None)
                    for t in range(HT):
                        bk = rr["pb"] % 2
                        rr["pb"] += 1
                        for kt in range(16):
                            P.op("pe", lambda e, wo=wo, t=t, kt=kt, bk=bk: e.matmul(pb[bk][:, :], lhsT=mT[:, t, kt, :], rhs=wo[:, kt, :], start=(kt == 0), stop=(kt == 15)),
                                 reads=[("xT", t), kwo], writes=[("pb", bk)])
                        xa = xres_ap(t)
                        P.op("dve", lambda e, xa=xa, bk=bk, j0=j0: e.scalar_tensor_tensor(xa[:, j0:j0 + 512], xa[:, j0:j0 + 512], ALPHA, pb[bk][:, :], ALU.mult, ALU.add),
                             reads=[("pb", bk), ("xres", t)], writes=[("xres", t)])
                for t in range(HT):
                    gt = h * HT + t
                    xa = xres_ap(t)
                    layer_norm_tile(("xres", t), xa, "gb", ("xres", t))
                    if last:
                        P.dma("sp", lambda e, xa=xa, gt=gt: e.dma_start(out=y_p[gt * 128:(gt + 1) * 128, :], in_=xa), reads=[("xres", t)])
                    else:
                        P.dma("sp", lambda e, xa=xa, gt=gt, par=par: e.dma_start(out=Xs[1 - par, gt * 128:(gt + 1) * 128, :], in_=xa),
                              reads=[("xres", t)], writes=[("Xs", 1 - par, gt)])
                        to_xT_scratch(xa, ("xres", t), 1 - par, gt)
        xsT = sb("xsT", [128, 16, 4], BF16)
        qsT = sb("qsT", [128, 8, 4], BF16); ksT = sb("ksT", [128, 8, 4], BF16); zasT = sb("zasT", [128, 8, 4], BF16)
        PgsT = sb("PgsT", [128, 8, 4], BF16); AgsT = sb("AgsT", [128, 8, 4], BF16); msT = sb("msT", [128, 16, 4], BF16)
        dTs = sb("dTs", [128, 8, 4], BF16)
        iqsT = sb("iqsT", [64, 4, 16], BF16)
        iksT = sb("iksT", [64, 4], BF16)
        iws = sb("iws", [4, 16], F32)
        iwcol = sb("iwcol", [64, 1], F32)
        sbst = sb("sbst", [15, 4, 4], BF16); sbnew = sb("sbnew", [4, 4, 4], BF16)
        pt_sb = sb("pt_sb", [128, 1], I32); ptf = sb("ptf", [128, 1], F32); rowbase = sb("rowbase", [128, 1], F32)
        pgidx = sb("pgidx", [128, 1], I32)
        iota_i = sb("iota_i", [128, 128], I32); iota_f = sb("iota_f", [128, 128], F32); idx_all = sb("idx_all", [128, 128], I32)
        negm_s = sb("negm_s", [128, 4], F32); bones = sb("bones_sb", [128, 128], F32); dmask = sb("dmask", [32, 8], F32)
        ssm = sb("ssm", [128, 64], F32)
        s_lo = ssm[:, 0:1]; s_mid = ssm[:, 1:2]; s_cnt = ssm[:, 2:3]; s_B = ssm[:, 3:4]; s_am = ssm[:, 4:5]; s_am2 = ssm[:, 5:6]
        s_w0 = ssm[:, 6:7]; s_rden = ssm[0:32, 7:8]; s_wtab = ssm[:, 16:16 + 40]
        s_gei = sb("s_gei", [128, 1], I32)
        xs_scr = dscr("xs_scr", [2, 4, D])
        iw_scr = dscr("iw_scr", [4, 16])

        SR = R1[0:4, :]
        ups = SR[:, 0:1024]; zps = SR[:, 1024:2048]; zas = SR[:, 2048:3072]; ksb = SR[:, 3072:4096]; vsb = SR[:, 4096:5120]
        iqs = SR[:, 5120:6144]; gps = SR[:, 6144:8192]; gas = SR[:, 8192:10240]; ms = SR[:, 10240:12288]
        xst = SR[:, 12288:16384].bitcast(F32)
        gbs = R0[0:4, 0:8192].bitcast(F32).rearrange("p (a n) -> p a n", a=2)
        ik_raw = R0[:, :].bitcast(F32)
        ik_bf = R2[:, 0:8192]
        Kc = [R2[:, 8192 + i * 2048:8192 + (i + 1) * 2048].bitcast(F32) for i in range(2)]
        Vc = [R2[:, 12288 + i * 2048:12288 + (i + 1) * 2048].bitcast(F32) for i in range(2)]
        Kbf = [R2[:, 16384 + i * 1024:16384 + (i + 1) * 1024] for i in range(2)]
        Vbf = [R2[:, 18432 + i * 1024:18432 + (i + 1) * 1024] for i in range(2)]
        KTs = [R2[:, 20480 + i * 1024:20480 + (i + 1) * 1024].rearrange("p (h n) -> p h n", h=8) for i in range(2)]
        ikT_s = RW[0:64, 0:16388]
        Wsc = RW[0:64, 16400:20496]
        sc_s = RW[:, 20496:21528].bitcast(F32)
        junk_s = RW[:, 21528:22044]
        mask_s = RW[:, 22044:22560]
        Tm = RW[:, 22560:23072].rearrange("p (g n) -> p g n", g=4)
        Tn = RW[0:4, 23072:23200]
        Rs = [RW[0:64, 23200 + i * 512:23200 + (i + 1) * 512] for i in range(2)]
        Es = RW[:, 24224:24288].bitcast(F32)
        PTs = [RW[:, 24288 + i * 32:24288 + (i + 1) * 32] for i in range(2)]
        wmix_s = RW[:, 16400:18448].rearrange("p (g c e) -> p g c e", g=4, c=2)
        psc_s = RW[0:4, 18448:20496].bitcast(F32)
        pgs_f = RW[0:4, 20496:22544].bitcast(F32)
        od_tmp = RW[0:32, 0:2048].bitcast(F32)
        od = RW[0:32, 2048:2304].bitcast(F32)
        odn = RW[0:32, 2304:2432]

        def bar():
            P.barrier(lambda e: e.memset(fdummy[:], 0.0))

        def load_w2(dram_ap_2d, krows, ncols, k0=0):
            i = rr["w"] % 2
            rr["w"] += 1
            buf = wbuf[i]
            kt = krows // 128
            P.dma("pool", lambda e: e.dma_start(out=buf[:, k0:k0 + kt, 0:ncols], in_=dram_ap_2d.rearrange("(k p) n -> p k n", p=128)),
                  writes=[("wbuf", i)])
            return buf, ("wbuf", i)

        def tr_small(src_ap, np_, ncol, dst_ap, key_r, key_w, bk=3, dst_rearr=None):
            P.op("pe", lambda e: e.transpose(pbf(bk)[0:ncol, 0:np_], src_ap, ident[0:np_, 0:np_]), reads=[key_r, "ident"], writes=[("pb", bk)])
            P.op("act", lambda e: e.copy(dst_ap, pbf(bk)[0:ncol, 0:np_]), reads=[("pb", bk)], writes=[key_w])

        def tr_tok(src_bf, key_r, dst, key_w, nblk):
            for j in range(nblk):
                P.op("pe", lambda e, j=j: e.transpose(pbf(3)[:, j * 4:(j + 1) * 4], src_bf[:, j * 128:(j + 1) * 128], ident[0:4, 0:4]),
                     reads=[key_r, "ident"], writes=[("pb", 3)])
            P.op("act", lambda e: e.copy(dst, pbf(3)[:, 0:nblk * 4].rearrange("p (j n) -> p j n", j=nblk)), reads=[("pb", 3)], writes=[key_w])

        def ln4(x_ap, xkey):
            for j in range(4):
                P.op("dve", lambda e, j=j: e.bn_stats(stat[0:4, j, :], x_ap[:, j * 512:(j + 1) * 512]), reads=[xkey], writes=[("stat", j)])
            P.op("dve", lambda e: e.bn_aggr(mv[0:4, :], stat[0:4].rearrange("p a b -> p (a b)")), reads=[("stat", j) for j in range(4)], writes=["mv"])
            P.op("act", lambda e: e.activation(rstd[0:4, :], mv[0:4, 1:2], AF.Sqrt, bias=eps_t[0:4, :]), reads=["mv", "eps"], writes=["rstd"])
            P.op("dve", lambda e: e.reciprocal(rstd[0:4, :], rstd[0:4, :]), reads=["rstd"], writes=["rstd"])
            P.op("dve", lambda e: e.tensor_scalar(x_ap, x_ap, mv[0:4, 0:1], rstd[0:4, :], ALU.subtract, ALU.mult), reads=[xkey, "mv", "rstd"], writes=[xkey])
            P.op("dve", lambda e: e.tensor_tensor(x_ap, x_ap, gbs[:, 0, :], ALU.mult), reads=[xkey, "gbs"], writes=[xkey])
            P.op("dve", lambda e: e.tensor_tensor(x_ap, x_ap, gbs[:, 1, :], ALU.add), reads=[xkey, "gbs"], writes=[xkey])

        def xs_to_T():
            P.op("act", lambda e: e.copy(xbf_st[0:4, :], xst), reads=["xst"], writes=["xbf"])
            tr_tok(xbf_st[0:4, :], "xbf", xsT[:], "xsT", 16)

        def sample_init():
            bar()
            P.dma("pool", lambda e: e.dma_start(out=sbst[:], in_=sb_st_d.rearrange("g s t -> s g t")), writes=["sbst"])
            P.dma("pool", lambda e: e.dma_start(out=sbnew[:], in_=sb_new_d.rearrange("g s t -> s g t")), writes=["sbnew"])
            P.dma("sp", lambda e: e.dma_start(out=negm_s[:], in_=negm_d), writes=["negm"])
            P.dma("sp", lambda e: e.dma_start(out=bones[:], in_=bones_d), writes=["bones"])
            P.dma("sp", lambda e: e.dma_start(out=dmask[:], in_=dmask_d), writes=["dmask"])
            P.dma("sp", lambda e: e.dma_start(out=pt_sb[:], in_=pt), writes=["pt"])
            P.op("dve", lambda e: e.tensor_copy(ptf[:], pt_sb[:]), reads=["pt"], writes=["ptf"])
            P.op("pool", lambda e: e.iota(iota_i[:], [[1, 128]], base=0, channel_multiplier=0), writes=["iota_i"])
            P.op("dve", lambda e: e.tensor_copy(iota_f[:], iota_i[:]), reads=["iota_i"], writes=["iota_f"])
            P.dma("sp", lambda e: e.dma_start(out=gbs[:, 0, :], in_=lneg.partition_broadcast(4)), writes=["gbs"])
            P.dma("sp", lambda e: e.dma_start(out=gbs[:, 1, :], in_=lneb.partition_broadcast(4)), writes=["gbs"])
            P.dma("sp", lambda e: e.dma_start(out=xst, in_=xs_in), writes=["xst"])
            ln4(xst, "xst")
            P.dma("sp", lambda e: e.dma_start(out=xs_scr[0], in_=xst), reads=["xst"], writes=["xs_scr"])
            xs_to_T()
            bar()

        S_BLOCKS = []
        for kind, c0 in (("up", C_UP), ("zp", C_ZP), ("q", C_Q), ("k", C_K), ("v", C_V), ("za", C_ZA), ("iq", C_IQ)):
            S_BLOCKS.append((kind, c0, 512, 0)); S_BLOCKS.append((kind, c0 + 512, 512, 1))
        S_BLOCKS.append(("ikiw", C_IK, 80, 0))
        for kind, c0 in (("gp", C_GP), ("ga", C_GA)):
            for sub in range(4):
                S_BLOCKS.append((kind, c0 + sub * 512, 512, sub))

        def rope4(bk, nh, half, cos_ap, sin_ap, out_ap, okey):
            n = nh * 2 * half
            src = pb[bk][0:4, 0:n].rearrange("p (h two f) -> p h two f", h=nh, two=2)
            cb = cos_ap.rearrange("p (a b f) -> p a b f", a=1, b=1).to_broadcast([4, nh, 2, half])
            sbb = sin_ap.rearrange("p (a b f) -> p a b f", a=1, b=1).to_broadcast([4, nh, 2, half])
            A = ropeA[0:4, 0:n].rearrange("p (h two f) -> p h two f", h=nh, two=2)
            B = ropeB[0:4, 0:n].rearrange("p (h two f) -> p h two f", h=nh, two=2)
            o = out_ap.rearrange("p (h two f) -> p h two f", h=nh, two=2)
            P.op("dve", lambda e: e.tensor_tensor(A, src, cb, ALU.mult), reads=[("pb", bk), "sropetab"], writes=["ropeA"])
            P.op("dve", lambda e: e.tensor_tensor(B, src, sbb, ALU.mult), reads=[("pb", bk), "sropetab"], writes=["ropeB"])
            P.op("dve", lambda e: e.tensor_tensor(o[:, :, 0, :], A[:, :, 0, :], B[:, :, 1, :], ALU.subtract), reads=["ropeA", "ropeB"], writes=[okey])
            P.op("dve", lambda e: e.tensor_tensor(o[:, :, 1, :], A[:, :, 1, :], B[:, :, 0, :], ALU.add), reads=["ropeA", "ropeB"], writes=[okey, "ropeA", "ropeB"])

        def sample_pass(l):
            last = (l == DEPTH - 1)
            par = l % 2
            bar()
            for tab, src in ((cos128, src128), (sin128, srs128), (cos64, src64), (sin64, srs64)):
                P.dma("sp", lambda e, tab=tab, src=src: e.dma_start(out=tab[0:4, 0, :], in_=src), writes=["sropetab"])
            for (kind, c0, ncols, sub) in S_BLOCKS:
                wb, wkey = load_w2(w_in[l, :, c0:c0 + ncols], D, ncols)
                bk = rr["pb"] % 2
                rr["pb"] += 1
                for kt in range(16):
                    P.op("pe", lambda e, kt=kt, bk=bk, wb=wb, ncols=ncols: e.matmul(pb[bk][0:4, 0:ncols], lhsT=xsT[:, kt, :], rhs=wb[:, kt, 0:ncols],
                                                                                 start=(kt == 0), stop=(kt == 15)), reads=["xsT", wkey], writes=[("pb", bk)])
                pk = ("pb", bk)
                cs = slice(sub * 512, sub * 512 + 512)
                if kind == "up":
                    sf = st_f32[sub]
                    P.op("act", lambda e, sf=sf, bk=bk: e.copy(sf[0:4, :], pb[bk][0:4, :]), reads=[pk], writes=[("stf", sub)])
                    P.dma("sp", lambda e, sf=sf, cs=cs: e.dma_start(out=pl_s[l, 11:15, cs], in_=sf[0:4, :]), reads=[("stf", sub)])
                    P.op("dve", lambda e, bk=bk, cs=cs: e.tensor_copy(ups[:, cs], pb[bk][0:4, :]), reads=[pk], writes=["ups"])
                elif kind == "zp":
                    P.op("act", lambda e, bk=bk, cs=cs: e.activation(zps[:, cs], pb[bk][0:4, :], AF.Silu), reads=[pk], writes=["zps"])
                elif kind == "za":
                    P.op("act", lambda e, bk=bk, cs=cs: e.activation(zas[:, cs], pb[bk][0:4, :], AF.Silu), reads=[pk], writes=["zas"])
                    if sub == 1:
                        tr_tok(zas, "zas", zasT[:], "zasT", 8)
                elif kind == "q":
                    rope4(bk, 4, 64, cos128[0:4, 0, :], sin128[0:4, 0, :], iqs[:, cs], "iqs")
                    if sub == 1:
                        tr_tok(iqs, "iqs", qsT[:], "qsT", 8)
                elif kind == "k":
                    sf = st_f32[sub]
                    rope4(bk, 4, 64, cos128[0:4, 0, :], sin128[0:4, 0, :], sf[0:4, :], ("stf", sub))
                    P.dma("sp", lambda e, sf=sf, cs=cs: e.dma_start(out=k_s[l, :, cs], in_=sf[0:4, :]), reads=[("stf", sub)])
                    P.op("act", lambda e, sf=sf, cs=cs: e.copy(ksb[:, cs], sf[0:4, :]), reads=[("stf", sub)], writes=["ksb"])
                    if sub == 1:
                        tr_tok(ksb, "ksb", ksT[:], "ksT", 8)
                elif kind == "v":
                    sf = st_f32[sub]
                    P.op("act", lambda e, sf=sf, bk=bk: e.copy(sf[0:4, :], pb[bk][0:4, :]), reads=[pk], writes=[("stf", sub)])
                    P.dma("sp", lambda e, sf=sf, cs=cs: e.dma_start(out=v_s[l, :, cs], in_=sf[0:4, :]), reads=[("stf", sub)])
                    P.op("dve", lambda e, bk=bk, cs=cs: e.tensor_copy(vsb[:, cs], pb[bk][0:4, :]), reads=[pk], writes=["vsb"])
                elif kind == "iq":
                    rope4(bk, 8, 32, cos64[0:4, 0, :], sin64[0:4, 0, :], iqs[:, cs], "iqs")
                    if sub == 1:
                        for hh in range(16):
                            P.op("pe", lambda e, hh=hh: e.transpose(pbf(3)[0:64, hh * 4:(hh + 1) * 4], iqs[:, hh * 64:(hh + 1) * 64], ident[0:4, 0:4]),
                                 reads=["iqs", "ident"], writes=[("pb", 3)])
                        P.op("act", lambda e: e.copy(iqsT[:], pbf(3)[0:64, 0:64].rearrange("p (h q) -> p q h", h=16)), reads=[("pb", 3)], writes=["iqsT"])
                elif kind == "ikiw":
                    rope4(bk, 1, 32, cos64[0:4, 0, :], sin64[0:4, 0, :], ik_st[0:4, :], "ikst")
                    P.dma("sp", lambda e: e.dma_start(out=ik_s[l], in_=ik_st[0:4, :]), reads=["ikst"])
                    P.op("act", lambda e: e.copy(ik2_bf[0:4, 0:64], ik_st[0:4, :]), reads=["ikst"], writes=["ik2"])
                    tr_small(ik2_bf[0:4, 0:64], 4, 64, iksT[:], "ik2", "iksT")
                    P.op("dve", lambda e, bk=bk: e.tensor_scalar(iws[:], pb[bk][0:4, 64:80], IDX_SCALE, None, ALU.mult), reads=[pk], writes=["iws"])
                    P.dma("sp", lambda e: e.dma_start(out=iw_scr, in_=iws[:]), reads=["iws"], writes=["iw_scr"])
                    P.dma("sp", lambda e: e.dma_start(out=iwcol[:], in_=iw_scr.rearrange("q (h o) -> (q h) o", o=1)), reads=["iw_scr"], writes=["iwcol"])
                elif kind == "gp":
                    P.op("act", lambda e, bk=bk, cs=cs: e.activation(gps[:, cs], pb[bk][0:4, :], AF.Sigmoid), reads=[pk], writes=["gps"])
                elif kind == "ga":
                    P.op("act", lambda e, bk=bk, cs=cs: e.activation(gas[:, cs], pb[bk][0:4, :], AF.Sigmoid), reads=[pk], writes=["gas"])
            bar()
            state_bf = ropeA[0:15, :].bitcast(BF16)
            P.dma("pool", lambda e: e.dma_start(out=state_bf, in_=stp[l]), writes=["state"])
            P.dma("sp", lambda e: e.dma_start(out=pl_s[l, 0:11, :], in_=stp[l, 4:15, :]))
            P.dma("pool", lambda e: e.dma_start(out=wmix_s, in_=w_mix[l].rearrange("g (c p) e -> p g c e", p=128)), writes=["wmix_s"])
            P.dma("sp", lambda e: e.dma_start(out=psc_s, in_=pscale[l:l + 1, :].partition_broadcast(4)), writes=["psc_s"])
            for ct in range(8):
                g = ct // 2
                P.op("pe", lambda e, ct=ct, g=g: e.matmul(pb[0][:, ct * 4:(ct + 1) * 4], lhsT=state_bf[:, ct * 128:(ct + 1) * 128], rhs=sbst[:, g, :],
                                                         start=True, stop=False), reads=["state", "sbst"], writes=[("pb", 0)])
                P.op("pe", lambda e, ct=ct, g=g: e.matmul(pb[0][:, ct * 4:(ct + 1) * 4], lhsT=ups[:, ct * 128:(ct + 1) * 128], rhs=sbnew[:, g, :],
                                                         start=False, stop=True), reads=["ups", "sbnew"], writes=[("pb", 0)])
            P.op("act", lambda e: e.copy(dTs[:], pb[0][:, 0:32].rearrange("p (c n) -> p c n", c=8)), reads=[("pb", 0)], writes=["dTs"])
            for g in range(4):
                bk = 1 + g // 2
                for c in range(2):
                    P.op("pe", lambda e, g=g, c=c, bk=bk: e.matmul(pb[bk][0:4, (g % 2) * 256:(g % 2 + 1) * 256], lhsT=dTs[:, g * 2 + c, :], rhs=wmix_s[:, g, c, :],
                                                                start=(c == 0), stop=(c == 1)), reads=["dTs", "wmix_s"], writes=[("pb", bk)])
            for hf in range(2):
                P.op("dve", lambda e, hf=hf: e.tensor_tensor(pgs_f[:, hf * 512:(hf + 1) * 512], pb[1 + hf][0:4, :], psc_s[:, hf * 512:(hf + 1) * 512], ALU.mult),
                     reads=[("pb", 1 + hf), "psc_s"], writes=["pgs_f"])
            P.op("dve", lambda e: e.tensor_tensor(iqs, pgs_f, zps, ALU.mult), reads=["pgs_f", "zps", "iqs"], writes=["iqs"])
            tr_tok(iqs, "iqs", PgsT[:], "PgsT", 8)
            bar()
            P.op("dve", lambda e: e.tensor_scalar(rowbase[:], ptf[:], 128.0, float(l * NPOOL * 128), ALU.mult, ALU.add), reads=["ptf"], writes=["rowbase"])
            P.op("dve", lambda e: e.tensor_scalar(idx_all[:], iota_f[:], rowbase[:], None, ALU.add), reads=["iota_f", "rowbase"], writes=["idx_all"])
            P.op("dve", lambda e: e.tensor_scalar(pgidx[:], ptf[:], float(l * NPOOL), None, ALU.add), reads=["ptf"], writes=["pgidx"])
            P.dma("pool", lambda e: e.indirect_dma_start(out=ik_raw, out_offset=None, in_=cik, in_offset=bass.IndirectOffsetOnAxis(ap=pgidx[:, 0:1], axis=0)),
                  reads=["pgidx"], writes=["ik_raw"])
            P.op("dve", lambda e: e.tensor_copy(ik_bf[:, 0:4096], ik_raw[:, 0:4096]), reads=["ik_raw"], writes=["ik_bf0"])
            P.op("pool", lambda e: e.tensor_copy(ik_bf[:, 4096:8192], ik_raw[:, 4096:8192]), reads=["ik_raw"], writes=["ik_bf1"])
            for s0 in range(0, 128, 8):
                bk = (s0 // 8) % 2
                for j in range(8):
                    s = s0 + j
                    P.op("pe", lambda e, s=s, j=j, bk=bk: e.transpose(pbf(bk)[0:64, j * 128:(j + 1) * 128], ik_bf[:, s * 64:(s + 1) * 64], ident[:]),
                         reads=["ik_bf0" if s < 64 else "ik_bf1", "ident"], writes=[("pb", bk)])
                if bk == 0:
                    P.op("act", lambda e, s0=s0, bk=bk: e.copy(ikT_s[:, s0 * 128:(s0 + 8) * 128], pbf(bk)[0:64, :]), reads=[("pb", bk)], writes=["ikT_s"])
                else:
                    P.op("dve", lambda e, s0=s0, bk=bk: e.tensor_copy(ikT_s[:, s0 * 128:(s0 + 8) * 128], pbf(bk)[0:64, :]), reads=[("pb", bk)], writes=["ikT_s"])
            P.op("act", lambda e: e.copy(ikT_s[:, 16384:16388], iksT[:]), reads=["iksT"], writes=["ikT_s"])
            P.dma("pool", lambda e: e.dma_start(out=Wsc, in_=esel_d), writes=["Wsc"])
            P.op("dve", lambda e: e.tensor_scalar(Wsc, Wsc, iwcol[:], None, ALU.mult), reads=["Wsc", "iwcol"], writes=["Wsc"])
            iq_flat = iqsT[:].rearrange("p q h -> p (q h)")
            for c in range(33):
                kn = 512 if c < 32 else 4
                bk = c % 2
                rs_ = Rs[c % 2]
                rkey = ("Rs", c % 2)
                P.op("pe", lambda e, c=c, kn=kn, bk=bk: e.matmul(pb[bk][0:64, 0:kn], lhsT=iq_flat, rhs=ikT_s[:, c * 512:c * 512 + kn], start=True, stop=True),
                     reads=["iqsT", "ikT_s"], writes=[("pb", bk)])
                if c % 2 == 0:
                    P.op("act", lambda e, rs_=rs_, bk=bk, kn=kn: e.activation(rs_[:, 0:kn], pb[bk][0:64, 0:kn], AF.Relu), reads=[("pb", bk)], writes=[rkey])
                else:
                    P.op("dve", lambda e, rs_=rs_, bk=bk, kn=kn: e.tensor_scalar(rs_[:, 0:kn], pb[bk][0:64, 0:kn], 0.0, None, ALU.max), reads=[("pb", bk)], writes=[rkey])
                if c < 32:
                    P.op("pe", lambda e, c=c, rs_=rs_: e.matmul(pb[2][:, 0:512], lhsT=Wsc[:, c * 128:(c + 1) * 128], rhs=rs_[:, 0:512], start=(c == 0), stop=(c == 31)),
                         reads=["Wsc", rkey], writes=[("pb", 2)])
                else:
                    P.op("pe", lambda e, rs_=rs_: e.matmul(pb[3][:, 0:4], lhsT=Wsc[:, 0:128], rhs=rs_[:, 0:4], start=True, stop=True),
                         reads=["Wsc", rkey], writes=[("pb", 3)])
            P.op("act", lambda e: e.copy(sc_s[:, 0:512], pb[2][:, :]), reads=[("pb", 2)], writes=["sc_s"])
            P.op("dve", lambda e: e.tensor_reduce(s_am2, pb[3][:, 0:4], AX.X, ALU.max, apply_absolute_value=True), reads=[("pb", 3)], writes=["s_am2"])
            P.op("dve", lambda e: e.tensor_tensor(sc_s[:, 512:516], pb[3][:, 0:4], negm_s[:], ALU.add), reads=[("pb", 3), "negm"], writes=["sc_s"])
            P.op("dve", lambda e: e.tensor_reduce(s_am, sc_s[:, 0:512], AX.X, ALU.max, apply_absolute_value=True), reads=["sc_s"], writes=["s_am"])
            P.op("dve", lambda e: e.tensor_tensor(s_am, s_am, s_am2, ALU.max), reads=["s_am", "s_am2"], writes=["s_am"])
            P.op("pe", lambda e: e.matmul(pb[5][:, 0:1], lhsT=bones[:], rhs=s_am, start=True, stop=True), reads=["bones", "s_am"], writes=[("pb", 5)])
            P.op("dve", lambda e: e.tensor_scalar(s_B, pb[5][:, 0:1], 1.01, 1.0, ALU.mult, ALU.add), reads=[("pb", 5)], writes=["s_B"])
            P.op("dve", lambda e: e.tensor_scalar(s_lo, s_B, -1.0, None, ALU.mult), reads=["s_B"], writes=["s_lo"])
            P.op("dve", lambda e: e.tensor_scalar(s_w0, s_B, 2.0, None, ALU.mult), reads=["s_B"], writes=["s_w0"])
            P.op("dve", lambda e: e.tensor_scalar(s_wtab[:, 0:32], pow2[:, 0:32], s_w0, None, ALU.mult), reads=["s_w0", "pow2"], writes=["s_wtab"])
            for n in range(32):
                P.op("dve", lambda e, n=n: e.tensor_tensor(s_mid, s_lo, s_wtab[:, n:n + 1], ALU.add), reads=["s_lo", "s_wtab"], writes=["s_mid"])
                P.op("dve", lambda e: e.tensor_scalar(junk_s, sc_s, s_mid, None, ALU.is_ge, ALU.add, accum_out=s_cnt), reads=["sc_s", "s_mid"], writes=["junk_s", "s_cnt"])
                P.op("pe", lambda e: e.matmul(pb[5][:, 0:1], lhsT=bones[:], rhs=s_cnt, start=True, stop=True), reads=["bones", "s_cnt"], writes=[("pb", 5)])
                P.op("dve", lambda e: e.tensor_scalar(s_gei[:], pb[5][:, 0:1], 255.5, None, ALU.is_ge), reads=[("pb", 5)], writes=["s_gei"])
                P.op("dve", lambda e: e.copy_predicated(s_lo, s_gei[:], s_mid), reads=["s_gei", "s_mid"], writes=["s_lo"])
            P.op("dve", lambda e: e.tensor_scalar(mask_s, sc_s, s_lo, None, ALU.is_ge), reads=["sc_s", "s_lo"], writes=["mask_s"])
            for g in range(4):
                P.op("pe", lambda e, g=g: e.transpose(pbf(3)[:, g * 128:(g + 1) * 128], mask_s[:, g * 128:(g + 1) * 128], ident[:]), reads=["mask_s", "ident"], writes=[("pb", 3)])
            P.op("act", lambda e: e.copy(Tm, pbf(3)[:, 0:512].rearrange("p (g n) -> p g n", g=4)), reads=[("pb", 3)], writes=["Tm"])
            P.op("pe", lambda e: e.transpose(pbf(3)[0:4, 0:128], mask_s[:, 512:516], ident[:]), reads=["mask_s", "ident"], writes=[("pb", 3)])
            P.op("act", lambda e: e.copy(Tn, pbf(3)[0:4, 0:128]), reads=[("pb", 3)], writes=["Tn"])
            for s in range(129):
                b2 = s % 2
                if s < 128:
                    c, g = s // 4, s % 4
                    P.dma("pool", lambda e, s=s, b2=b2: e.indirect_dma_start(out=Kc[b2], out_offset=None, in_=ck, in_offset=bass.IndirectOffsetOnAxis(ap=idx_all[:, s:s + 1], axis=0)),
                          reads=["idx_all"], writes=[("Kc", b2)])
                    P.dma("pool", lambda e, s=s, b2=b2: e.indirect_dma_start(out=Vc[b2], out_offset=None, in_=cv, in_offset=bass.IndirectOffsetOnAxis(ap=idx_all[:, s:s + 1], axis=0)),
                          reads=["idx_all"], writes=[("Vc", b2)])
                    P.op("dve", lambda e, b2=b2: e.tensor_copy(Kbf[b2], Kc[b2]), reads=[("Kc", b2)], writes=[("Kbf", b2)])
                    P.op("act", lambda e, b2=b2: e.copy(Vbf[b2], Vc[b2]), reads=[("Vc", b2)], writes=[("Vbf", b2)])
                    for hd in range(8):
                        P.op("pe", lambda e, hd=hd, b2=b2: e.transpose(pbf(b2)[:, hd * 128:(hd + 1) * 128], Kbf[b2][:, hd * 128:(hd + 1) * 128], ident[:]),
                             reads=[("Kbf", b2), "ident"], writes=[("pb", b2)])
                    P.op("act", lambda e, b2=b2: e.copy(KTs[b2], pbf(b2).rearrange("p (h n) -> p h n", h=8)), reads=[("pb", b2)], writes=[("KTs", b2)])
                    for hd in range(8):
                        P.op("pe", lambda e, hd=hd, b2=b2: e.matmul(pb[4][:, hd * 4:(hd + 1) * 4], lhsT=KTs[b2][:, hd, :], rhs=qsT[:, hd, :], start=True, stop=True),
                             reads=[("KTs", b2), "qsT"], writes=[("pb", 4)])
                    npp = 128
                    mk = Tm[:, g, c:c + 97:32]
                    vsrc = Vbf[b2]
                    vkey = ("Vbf", b2)
                else:
                    for hd in range(8):
                        P.op("pe", lambda e, hd=hd: e.matmul(pb[4][0:4, hd * 4:(hd + 1) * 4], lhsT=ksT[:, hd, :], rhs=qsT[:, hd, :], start=True, stop=True),
                             reads=["ksT", "qsT"], writes=[("pb", 4)])
                    npp = 4
                    mk = Tn[:, 0:97:32]
                    vsrc = vsb
                    vkey = "vsb"
                P.op("act", lambda e, npp=npp: e.activation(Es[0:npp, :], pb[4][0:npp, 0:32], AF.Exp, scale=SM_SCALE), reads=[("pb", 4)], writes=["Es"])
                pts = PTs[b2]
                P.op("dve", lambda e, npp=npp, mk=mk, pts=pts: e.tensor_tensor(
                    pts[0:npp, :].rearrange("p (h q) -> p h q", h=8), Es[0:npp, :].rearrange("p (h q) -> p h q", h=8),
                    mk.rearrange("p (o q) -> p o q", o=1).to_broadcast([npp, 8, 4]), ALU.mult),
                    reads=["Es", "Tm", "Tn"], writes=[("PTs", b2)])
                for hf in range(2):
                    P.op("pe", lambda e, hf=hf, npp=npp, pts=pts, vsrc=vsrc, s=s: e.matmul(
                        pb[6 + hf][0:32, :], lhsT=pts[0:npp, :], rhs=vsrc[0:npp, hf * 512:(hf + 1) * 512], start=(s == 0), stop=(s == 128), skip_group_check=True),
                        reads=[("PTs", b2), vkey], writes=[("pb", 6 + hf)])
                P.op("pe", lambda e, npp=npp, pts=pts, s=s: e.matmul(pb[5][0:32, 8:9], lhsT=pts[0:npp, :], rhs=ones_bf[0:npp, 0:1], start=(s == 0), stop=(s == 128), skip_group_check=True),
                     reads=[("PTs", b2), "ones"], writes=[("pb", 5)])
            P.op("dve", lambda e: e.reciprocal(s_rden, pb[5][0:32, 8:9]), reads=[("pb", 5)], writes=["s_rden"])
            for hf in range(2):
                P.op("dve", lambda e, hf=hf: e.tensor_tensor(
                    od_tmp[:, hf * 512:(hf + 1) * 512].rearrange("p (h d) -> p h d", h=4), pb[6 + hf][0:32, :].rearrange("p (h d) -> p h d", h=4),
                    dmask[:, hf * 4:(hf + 1) * 4].rearrange("p (h o) -> p h o", o=1).to_broadcast([32, 4, 128]), ALU.mult),
                    reads=[("pb", 6 + hf), "dmask", "ikT_s"], writes=["od_tmp", "ikT_s"])
            P.op("dve", lambda e: e.tensor_reduce(od, od_tmp.rearrange("p (h d) -> p d h", h=8), AX.X, ALU.add), reads=["od_tmp"], writes=["od"])
            P.op("dve", lambda e: e.tensor_scalar(odn, od, s_rden, None, ALU.mult), reads=["od", "s_rden"], writes=["odn"])
            P.op("pe", lambda e: e.transpose(pbf(3)[:, 0:32], odn, ident[0:32, 0:32]), reads=["odn", "ident"], writes=[("pb", 3)])
            P.op("dve", lambda e: e.tensor_tensor(AgsT[:], pbf(3)[:, 0:32].rearrange("p (h q) -> p h q", h=8), zasT[:], ALU.mult), reads=[("pb", 3), "zasT"], writes=["AgsT"])
            bar()
            for jb in range(4):
                j0 = jb * 512
                i1 = rr["w"] % 2; rr["w"] += 1
                bA = wbuf[i1]; kA = ("wbuf", i1)
                P.dma("pool", lambda e, bA=bA, j0=j0: e.dma_start(out=bA[:, 0:8, :], in_=w_po[l, :, j0:j0 + 512].rearrange("(k p) n -> p k n", p=128)), writes=[kA])
                P.dma("pool", lambda e, bA=bA, j0=j0: e.dma_start(out=bA[:, 8:16, :], in_=w_ao[l, :, j0:j0 + 512].rearrange("(k p) n -> p k n", p=128)), writes=[kA])
                for kt in range(8):
                    P.op("pe", lambda e, bA=bA, kt=kt: e.matmul(pb[0][0:4, :], lhsT=PgsT[:, kt, :], rhs=bA[:, kt, :], start=(kt == 0), stop=(kt == 7)), reads=["PgsT", kA], writes=[("pb", 0)])
                for kt in range(8):
                    P.op("pe", lambda e, bA=bA, kt=kt: e.matmul(pb[1][0:4, :], lhsT=AgsT[:, kt, :], rhs=bA[:, 8 + kt, :], start=(kt == 0), stop=(kt == 7)), reads=["AgsT", kA], writes=[("pb", 1)])
                P.op("dve", lambda e, j0=j0: e.tensor_tensor(ropeA[0:4, :], pb[0][0:4, :], gps[:, j0:j0 + 512], ALU.mult), reads=[("pb", 0), "gps"], writes=["ropeA"])
                P.op("dve", lambda e, j0=j0: e.tensor_tensor(ropeB[0:4, :], pb[1][0:4, :], gas[:, j0:j0 + 512], ALU.mult), reads=[("pb", 1), "gas"], writes=["ropeB"])
                P.op("dve", lambda e, j0=j0: e.tensor_tensor(ms[:, j0:j0 + 512], ropeA[0:4, :], ropeB[0:4, :], ALU.add), reads=["ropeA", "ropeB"], writes=["ms"])
            tr_tok(ms, "ms", msT[:], "msT", 16)
            P.dma("sp", lambda e: e.dma_start(out=xst, in_=xs_scr[par]), reads=["xs_scr"], writes=["xst"])
            P.dma("sp", lambda e: e.dma_start(out=gbs[:, 0, :], in_=lng[l:l + 1, :].partition_broadcast(4)), writes=["gbs"])
            P.dma("sp", lambda e: e.dma_start(out=gbs[:, 1, :], in_=lnb[l:l + 1, :].partition_broadcast(4)), writes=["gbs"])
            for jb in range(4):
                j0 = jb * 512
                wo, kwo = load_w2(w_o[l, :, j0:j0 + 512], D, 512)
                bk = jb % 2
                for kt in range(16):
                    P.op("pe", lambda e, wo=wo, kt=kt, bk=bk: e.matmul(pb[bk][0:4, :], lhsT=msT[:, kt, :], rhs=wo[:, kt, :], start=(kt == 0), stop=(kt == 15)), reads=["msT", kwo], writes=[("pb", bk)])
                P.op("dve", lambda e, bk=bk, j0=j0: e.scalar_tensor_tensor(xst[:, j0:j0 + 512], xst[:, j0:j0 + 512], ALPHA, pb[bk][0:4, :], ALU.mult, ALU.add),
                     reads=[("pb", bk), "xst"], writes=["xst"])
            ln4(xst, "xst")
            if last:
                P.dma("sp", lambda e: e.dma_start(out=y_s, in_=xst), reads=["xst"])
            else:
                P.dma("sp", lambda e: e.dma_start(out=xs_scr[1 - par], in_=xst), reads=["xst"], writes=["xs_scr"])
                xs_to_T()
            bar()

        if stop is None and with_sample:
            sample_init()
        for l_ in range(DEPTH if stop is None else 1):
            for h_ in range(2 if stop is None else 1):
                pass_body(l_, h_)
            if stop is None and with_sample:
                sample_pass(l_)
        P.emit()
    return nc


_NC_CACHE = {}


def _get_nc(npool):
    if npool not in _NC_CACHE:
        _NC_CACHE[npool] = build(npool)
    return _NC_CACHE[npool]


def make_in_maps(inputs, n_cores=8):
    f = lambda a: np.ascontiguousarray(np.asarray(a))
    x_prompt = f(inputs["x_prompt"]); x_sample = f(inputs["x_sample"])
    cache_k = f(inputs["cache_k"]); cache_v = f(inputs["cache_v"]); cache_ik = f(inputs["cache_idx_k"])
    npool = cache_k.shape[1]
    ck = cache_k.reshape(DEPTH * npool * 128, 1024)
    cv = cache_v.reshape(DEPTH * npool * 128, 1024)
    cik = cache_ik.reshape(DEPTH * npool, 8192)
    state_pool = f(inputs["state_pool"]); page_table = f(inputs["page_table"]).astype(np.int32)
    cst = _consts()
    shared = {
        "ck": ck, "cv": cv, "cik": cik,
        "lneg": f(inputs["ln_emb_g"]).reshape(1, D), "lneb": f(inputs["ln_emb_b"]).reshape(1, D),
        "w_in": f(inputs["w_in"]), "w_mix": f(inputs["w_pool_mix"]), "pscale": f(inputs["pool_scale"]),
        "w_po": f(inputs["w_pool_out"]), "w_ao": f(inputs["w_attn_out"]), "w_o": f(inputs["w_o"]),
        "lng": f(inputs["ln_g"]), "lnb": f(inputs["ln_b"]),
    }
    shared.update(cst)
    maps = []
    for c in range(n_cores):
        m = dict(shared)
        m["xp"] = x_prompt[c % 4]
        m["xs"] = x_sample[c]
        m["stp"] = np.ascontiguousarray(state_pool[:, c])
        m["pt"] = np.ascontiguousarray(page_table[c].reshape(128, 1))
        maps.append(m)
    return maps, npool


def kernel(**inputs):
    maps, npool = make_in_maps(inputs)
    nc = _get_nc(npool)
    res = run_bass_kernel_spmd(nc, maps, core_ids=list(range(8)))
    r = res.results
    y_prompt = np.stack([r[b]["y_p"] for b in range(4)]).reshape(4, SEQ, D)
    y_sample = np.stack([r[c]["y_s"] for c in range(8)]).reshape(8, 4, D)
    k_prompt = np.stack([r[b]["k_p"] for b in range(4)], axis=1).reshape(DEPTH, 4, SEQ, 8, 128)
    v_prompt = np.stack([r[b]["v_p"] for b in range(4)], axis=1).reshape(DEPTH, 4, SEQ, 8, 128)
    ik_prompt = np.stack([r[b]["ik_p"] for b in range(4)], axis=1).reshape(DEPTH, 4, SEQ, 64)
    pl_prompt = np.stack([r[b]["pl_p"] for b in range(4)], axis=1).reshape(DEPTH, 4, 15, PW)
    k_sample = np.stack([r[c]["k_s"] for c in range(8)], axis=1).reshape(DEPTH, 8, 4, 8, 128)
    v_sample = np.stack([r[c]["v_s"] for c in range(8)], axis=1).reshape(DEPTH, 8, 4, 8, 128)
    ik_sample = np.stack([r[c]["ik_s"] for c in range(8)], axis=1).reshape(DEPTH, 8, 4, 64)
    pl_sample = np.stack([r[c]["pl_s"] for c in range(8)], axis=1).reshape(DEPTH, 8, 15, PW)
    return (y_prompt, y_sample, k_prompt, v_prompt, ik_prompt, pl_prompt, k_sample, v_sample, ik_sample, pl_sample)
```

```python
import os as _os
import itertools
import numpy as np
from contextlib import ExitStack
import concourse.bass as bass
import concourse.mybir as mybir
from concourse.bass_utils import run_bass_kernel_spmd

F32 = mybir.dt.float32
BF16 = mybir.dt.bfloat16
I32 = mybir.dt.int32
U32 = mybir.dt.uint32
ALU = mybir.AluOpType
AF = mybir.ActivationFunctionType
AX = mybir.AxisListType

ENGS = ("pe", "act", "dve", "pool", "sp")

D = 2048
SEQ = 2048
NT = 16
HT = 8
DEPTH = 2
PW = 1024
INW = 11344
C_UP, C_ZP, C_Q, C_K, C_V, C_ZA, C_IQ, C_IK, C_IW, C_GP, C_GA = (
    0, 1024, 2048, 3072, 4096, 5120, 6144, 7168, 7232, 7248, 9296)
IDX_SCALE = float((16 * 64) ** -0.5)
ALPHA = float((2 * DEPTH) ** 0.25)
LN_EPS = 1e-5
NEG = -1.0e30
NBIS = 30
SM_SCALE = float(128 ** -0.5)
PAST = 16384
NPG = 128


class Op:
    __slots__ = ("eng", "fn", "deps", "is_dma", "sem_idx", "count", "signal", "idx")

    def __init__(self, eng, fn, is_dma):
        self.eng = eng
        self.fn = fn
        self.deps = set()
        self.is_dma = is_dma
        self.sem_idx = None
        self.count = None
        self.signal = False
        self.idx = None


class Prog:
    N_DMA_SEMS = 14

    def __init__(self, nc):
        self.nc = nc
        self.ops = []
        self.last_writer = {}
        self.readers = {}
        self.dma_slot_last = {}
        self.dma_rr = {e: 0 for e in ENGS}
        self.final_ops = []
        self.after = None
        self.last_on = {}

    def barrier(self, fn):
        deps = set(self.last_on.values()) | set(self.dma_slot_last.values())
        o = Op("pool", fn, False)
        o.idx = len(self.ops)
        self.ops.append(o)
        o.deps = set(d for d in deps if d is not None)
        if self.after is not None:
            o.deps.add(self.after)
        self.after = o
        self.last_on["pool"] = o
        return o

    def _track(self, op, reads, writes):
        pr = [r for r in reads if isinstance(r, tuple) and r[0] == "pb"]
        if pr:
            reads = [r for r in reads if not (isinstance(r, tuple) and r[0] == "pb")]
            writes = list(writes) + pr
        for r in reads:
            w = self.last_writer.get(r)
            if w is not None:
                op.deps.add(w)
        for w_ in writes:
            w = self.last_writer.get(w_)
            if w is not None:
                op.deps.add(w)
            for rd in self.readers.get(w_, ()):
                op.deps.add(rd)
        for r in reads:
            self.readers.setdefault(r, []).append(op)
        for w_ in writes:
            self.last_writer[w_] = op
            self.readers[w_] = []
        op.deps.discard(op)

    def op(self, eng, fn, reads=(), writes=()):
        o = Op(eng, fn, False)
        o.idx = len(self.ops)
        self.ops.append(o)
        self._track(o, reads, writes)
        if self.after is not None:
            o.deps.add(self.after)
        self.last_on[eng] = o
        return o

    def dma(self, eng, fn, reads=(), writes=(), final=False):
        o = Op(eng, fn, True)
        o.idx = len(self.ops)
        self.ops.append(o)
        slot = self.dma_rr[eng] % self.N_DMA_SEMS
        self.dma_rr[eng] += 1
        o.sem_idx = slot
        prev = self.dma_slot_last.get((eng, slot))
        if prev is not None:
            o.deps.add(prev)
        self.dma_slot_last[(eng, slot)] = o
        self._track(o, reads, writes)
        if self.after is not None:
            o.deps.add(self.after)
        self.last_on[eng] = o
        if final:
            self.final_ops.append(o)
        return o

    def emit(self):
        nc = self.nc
        for o in self.ops:
            for d in o.deps:
                if d.is_dma or not (d.eng == "pe" and o.eng == "pe"):
                    d.signal = True
            if o.is_dma:
                o.signal = True
        per = {e: [] for e in ENGS}
        for o in self.ops:
            per[o.eng].append(o)
        for e in ENGS:
            c = 0
            for o in per[e]:
                if (not o.is_dma) and o.signal:
                    c += 1
                    o.count = c
        dma_cnt = {}
        for o in self.ops:
            if o.is_dma:
                k = (o.eng, o.sem_idx)
                dma_cnt[k] = dma_cnt.get(k, 0) + 16
                o.count = dma_cnt[k]
        with ExitStack() as es:
            esem = {e: es.enter_context(nc.semaphore("s_" + e)) for e in ENGS}
            dsem = {}
            for e in ENGS:
                for s in range(min(self.N_DMA_SEMS, self.dma_rr[e])):
                    dsem[(e, s)] = es.enter_context(nc.semaphore("d_%s_%d" % (e, s)))
            block = es.enter_context(nc.Block())

            def semof(d):
                return dsem[(d.eng, d.sem_idx)] if d.is_dma else esem[d.eng]

            def body(ename):
                def f(eng):
                    waited = {}
                    for o in per[ename]:
                        for d in sorted(o.deps, key=lambda x: x.idx):
                            if (not d.is_dma) and d.eng == "pe" and ename == "pe":
                                continue
                            key = (d.eng, d.sem_idx) if d.is_dma else d.eng
                            if waited.get(key, 0) >= d.count:
                                continue
                            eng.wait_ge(semof(d), d.count)
                            waited[key] = d.count
                        ins = o.fn(eng)
                        if o.signal:
                            ins.then_inc(semof(o), 16 if o.is_dma else 1)
                    if ename == "sp":
                        for k_, v_ in dma_cnt.items():
                            eng.wait_ge(dsem[k_], v_)
                return f

            block.tensor(body("pe"))
            block.scalar(body("act"))
            block.vector(body("dve"))
            block.gpsimd(body("pool"))
            block.sync(body("sp"))


def _rope_tab(pos, half):
    inv = (np.float32(10000.0) ** (-np.arange(half, dtype=np.float32) / np.float32(half))).astype(np.float32)
    ang = pos.astype(np.float32)[:, None] * inv[None, :]
    return np.cos(ang).astype(np.float32), np.sin(ang).astype(np.float32)


def _consts():
    c = {}
    pos_p = np.arange(SEQ)
    pos_s = PAST + np.arange(4)
    c["rc128"], c["rs128"] = _rope_tab(pos_p, 64)
    c["rc64"], c["rs64"] = _rope_tab(pos_p, 32)
    c["src128"], c["srs128"] = _rope_tab(pos_s, 64)
    c["src64"], c["srs64"] = _rope_tab(pos_s, 32)
    bands = np.zeros((12, 128, 128), np.float32)
    sb_st = np.zeros((4, 15, 4), np.float32)
    sb_new = np.zeros((4, 4, 4), np.float32)
    for g, w in enumerate((2, 4, 8, 16)):
        for t in range(128):
            for s in range(max(0, t - w + 1), t + 1):
                bands[g, s, t] += 1.0 / w
                bands[4 + g, s, t] += 1.0 / min(t + 1, w)
            bands[g, t, t] -= 1.0
            bands[4 + g, t, t] -= 1.0
            for j in range(t + 1, w):
                bands[8 + g, 128 + t - j, t] += 1.0 / w
        for t in range(4):
            for j in range(w):
                r = 15 + t - j
                if r >= 15:
                    sb_new[g, r - 15, t] += 1.0 / w
                else:
                    sb_st[g, r, t] += 1.0 / w
            sb_new[g, t, t] -= 1.0
    c["bands"] = bands
    c["sb_st"] = sb_st
    c["sb_new"] = sb_new
    cm = np.zeros((128, 128), np.float32)
    cm[np.triu_indices(128, 1)] = NEG
    c["causal"] = cm
    c["pow2"] = (0.5 ** np.arange(1, 33, dtype=np.float32)).astype(np.float32)[None, :]
    esel = np.zeros((64, 32, 128), np.float32)
    for q in range(4):
        for hh in range(16):
            for ch in range(32):
                esel[q * 16 + hh, ch, q * 32 + ch] = 1.0
    c["esel"] = esel.reshape(64, 4096)
    negm = np.full((128, 4), NEG, np.float32)
    for q in range(4):
        negm[q * 32, 0:q + 1] = 0.0
    c["negm"] = negm
    bo = np.zeros((128, 128), np.float32)
    for q in range(4):
        bo[q * 32:(q + 1) * 32, q * 32:(q + 1) * 32] = 1.0
    c["bones"] = bo
    dm = np.zeros((32, 8), np.float32)
    for hh in range(8):
        dm[hh * 4:(hh + 1) * 4, hh] = 1.0
    c["dmaskc"] = dm
    return c


def build(NPOOL=1280, with_sample=True, stop=None):
    nc = bass.Bass("TRN2", target_bir_lowering=False)
    P = Prog(nc)

    def din(name, shape, dt=F32):
        return nc.dram_tensor(name, list(shape), dt, kind="ExternalInput").ap()

    def dout(name, shape, dt=F32):
        return nc.dram_tensor(name, list(shape), dt, kind="ExternalOutput").ap()

    def dscr(name, shape, dt=F32):
        return nc.dram_tensor(name, list(shape), dt, kind="Internal").ap()

    xp = din("xp", [SEQ, D])
    xs_in = din("xs", [4, D])
    ck = din("ck", [DEPTH * NPOOL * 128, 1024])
    cv = din("cv", [DEPTH * NPOOL * 128, 1024])
    cik = din("cik", [DEPTH * NPOOL, 8192])
    stp = din("stp", [DEPTH, 15, PW])
    pt = din("pt", [128, 1], I32)
    lneg = din("lneg", [1, D])
    lneb = din("lneb", [1, D])
    w_in = din("w_in", [DEPTH, D, INW])
    w_mix = din("w_mix", [DEPTH, 4, 256, 256])
    pscale = din("pscale", [DEPTH, PW])
    w_po = din("w_po", [DEPTH, PW, D])
    w_ao = din("w_ao", [DEPTH, PW, D])
    w_o = din("w_o", [DEPTH, D, D])
    lng = din("lng", [DEPTH, D])
    lnb = din("lnb", [DEPTH, D])
    rc128 = din("rc128", [SEQ, 64]); rs128 = din("rs128", [SEQ, 64])
    rc64 = din("rc64", [SEQ, 32]); rs64 = din("rs64", [SEQ, 32])
    src128 = din("src128", [4, 64]); srs128 = din("srs128", [4, 64])
    src64 = din("src64", [4, 32]); srs64 = din("srs64", [4, 32])
    bands_d = din("bands", [12, 128, 128])
    sb_st_d = din("sb_st", [4, 15, 4])
    sb_new_d = din("sb_new", [4, 4, 4])
    causal_d = din("causal", [128, 128])
    pow2_d = din("pow2", [1, 32])
    esel_d = din("esel", [64, 4096])
    negm_d = din("negm", [128, 4])
    bones_d = din("bones", [128, 128])
    dmask_d = din("dmaskc", [32, 8])

    y_p = dout("y_p", [SEQ, D]); y_s = dout("y_s", [4, D])
    k_p = dout("k_p", [DEPTH, SEQ, 1024]); v_p = dout("v_p", [DEPTH, SEQ, 1024])
    ik_p = dout("ik_p", [DEPTH, SEQ, 64]); pl_p = dout("pl_p", [DEPTH, 15, PW])
    k_s = dout("k_s", [DEPTH, 4, 1024]); v_s = dout("v_s", [DEPTH, 4, 1024])
    ik_s = dout("ik_s", [DEPTH, 4, 64]); pl_s = dout("pl_s", [DEPTH, 15, PW])

    dbg = stop is not None or _os.environ.get("DBG")
    if dbg:
        dbg_ag = dout("dbg_ag", [SEQ, 1024], BF16)
        dbg_pg = dout("dbg_pg", [SEQ, 1024], BF16)
    Xs = dscr("Xs", [2, SEQ, D])
    xTs = dscr("xTs", [2, 128, NT, 16, 128], BF16)
    kTs = dscr("kTs", [128, NT, 8, 128], BF16)
    Vs = dscr("Vs", [SEQ, 1024], BF16)
    ikTs = dscr("ikTs", [128, SEQ], BF16)

    es = ExitStack()
    with es:
        def sb(name, shape, dt):
            return es.enter_context(nc.sbuf_tensor(name, list(shape), dt))

        def ps(name, shape, dt):
            return es.enter_context(nc.psum_tensor(name, list(shape), dt))

        R0 = sb("R0", [128, 16384], BF16)
        R1 = sb("R1", [128, 16384], BF16)
        R2 = sb("R2", [128, 24576], BF16)
        RW = sb("RW", [128, 24576], BF16)
        xT = R0[:, :].rearrange("p (t k n) -> p t k n", t=HT, k=16)
        mT = xT
        up_sb = R1[:, 0:8192].rearrange("p (t n) -> p t n", t=HT)
        zp_sb = R1[:, 8192:16384].rearrange("p (t n) -> p t n", t=HT)
        m_sb = R1[:, :].rearrange("p (t n) -> p t n", t=HT)
        qT = R2[:, 0:8192].rearrange("p (t h n) -> p t h n", t=HT, h=8)
        AgT = qT
        iqT = R2[:, 8192:16384].rearrange("p (t h n) -> p t h n", t=HT, h=8)
        PgT = iqT
        za_sb = R2[:, 16384:24576].rearrange("p (t n) -> p t n", t=HT)

        ikT2 = sb("ikT2", [128, SEQ], BF16)
        ident = sb("ident", [128, 128], BF16)
        identf = sb("identf", [128, 128], F32)
        bands = sb("bands_sb", [128, 12, 128], BF16)
        causal = sb("causal_sb", [128, 128], F32)
        pow2 = sb("pow2_sb", [128, 32], F32)
        cos128 = sb("cos128", [128, HT, 64], F32); sin128 = sb("sin128", [128, HT, 64], F32)
        cos64 = sb("cos64", [128, HT, 32], F32); sin64 = sb("sin64", [128, HT, 32], F32)
        iw_sb = sb("iw_sb", [128, HT, 16], F32)
        up_last = sb("up_last", [128, 1024], BF16)
        ones_bf = sb("ones_bf", [128, 1], BF16)
        stat = sb("stat", [128, 4, 6], F32)
        mv = sb("mv", [128, 2], F32)
        rstd = sb("rstd", [128, 1], F32)
        eps_t = sb("eps_t", [128, 1], F32)
        xt_st = [RW[:, i * 4096:(i + 1) * 4096].bitcast(F32) for i in range(2)]
        xbf_st = sb("xbf_st", [128, D], BF16)
        ropeA = sb("ropeA", [128, 512], F32)
        ropeB = sb("ropeB", [128, 512], F32)
        st_f32 = [sb("st_f32_%d" % i, [128, 512], F32) for i in range(2)]
        st_bf = [sb("st_bf%d" % i, [128, 1024], BF16) for i in range(2)]
        ik_st = sb("ik_st", [128, 64], F32)
        ik2_bf = sb("ik2_bf", [128, 128], BF16)
        tr_st = [sb("tr_st%d" % i, [128, 8, 128], BF16) for i in range(2)]
        gb_bc = R1[:, 8192:16384].bitcast(F32).rearrange("p (a n) -> p a n", a=2)
        GBKEYS = ["gb"] + [("zp", t_) for t_ in range(HT)] + [("m", t_) for t_ in range(4, 8)]

        pb = [ps("pb%d" % i, [128, 512], F32) for i in range(8)]

        def pbf(i):
            return pb[i][:, :].bitcast(BF16)

        P.op("pool", lambda e: e.memset(identf[:], 0.0), writes=["identf"])
        P.op("pool", lambda e: e.affine_select(identf[:], identf[:], [[1, 128]], ALU.not_equal, 1.0,
                                               base=0, channel_multiplier=-1), reads=["identf"], writes=["identf"])
        P.op("dve", lambda e: e.tensor_copy(ident[:], identf[:]), reads=["identf"], writes=["ident"])
        P.op("dve", lambda e: e.memset(ones_bf[:], 1.0), writes=["ones"])
        P.op("dve", lambda e: e.memset(eps_t[:], LN_EPS), writes=["eps"])
        P.dma("pool", lambda e: e.dma_start(out=bands[:], in_=bands_d.rearrange("g s t -> s g t")), writes=["bands"])
        P.dma("sp", lambda e: e.dma_start(out=causal[:], in_=causal_d), writes=["causal"])
        P.dma("sp", lambda e: e.dma_start(out=pow2[:], in_=pow2_d.partition_broadcast(128)), writes=["pow2"])

        rr = {"pb": 0, "st": 0, "tr": 0, "xt": 0, "w": 0}
        fdummy = sb("fdummy", [128, 1], F32)
        bis_sb = sb("bis_sb", [128, 4, 40], F32)
        gei_sb = sb("gei_sb", [128, 4], I32)
        P2KEYS = ["wmix", "psc", ("dT", 0), ("dT", 1), ("pgf", 0), ("pgf", 1)]

        def fence(old, new):
            P.op("pool", lambda e: e.memset(fdummy[:], 0.0), writes=list(old) + list(new))

        def layer_norm_tile(src_key, x_ap, g_key, out_key):
            for j in range(4):
                P.op("dve", lambda e, j=j: e.bn_stats(stat[:, j, :], x_ap[:, j * 512:(j + 1) * 512]),
                     reads=[src_key], writes=[("stat", j)])
            P.op("dve", lambda e: e.bn_aggr(mv[:], stat[:].rearrange("p a b -> p (a b)")),
                 reads=[("stat", j) for j in range(4)], writes=["mv"])
            P.op("act", lambda e: e.activation(rstd[:], mv[:, 1:2], AF.Sqrt, bias=eps_t[:]), reads=["mv", "eps"], writes=["rstd"])
            P.op("dve", lambda e: e.reciprocal(rstd[:], rstd[:]), reads=["rstd"], writes=["rstd"])
            P.op("dve", lambda e: e.tensor_scalar(x_ap, x_ap, mv[:, 0:1], rstd[:], ALU.subtract, ALU.mult),
                 reads=[src_key, "mv", "rstd"], writes=[src_key])
            P.op("pool", lambda e: e.tensor_tensor(x_ap, x_ap, gb_bc[:, 0, :], ALU.mult), reads=[src_key] + GBKEYS, writes=[src_key])
            P.op("dve", lambda e: e.tensor_tensor(x_ap, x_ap, gb_bc[:, 1, :], ALU.add), reads=[src_key] + GBKEYS, writes=[out_key, src_key])

        def to_xT_scratch(x_ap, x_key, par, gt, np_=128):
            P.op("act", lambda e: e.copy(xbf_st[:np_, :], x_ap), reads=[x_key], writes=["xbf"])
            for g in range(2):
                bk = 2 + g
                for j in range(8):
                    kt = g * 8 + j
                    P.op("pe", lambda e, kt=kt, j=j, bk=bk: e.transpose(pbf(bk)[:, j * 128:j * 128 + np_], xbf_st[:np_, kt * 128:(kt + 1) * 128], ident[:np_, :np_]),
                         reads=["xbf", "ident"], writes=[("pb", bk)])
                tr = tr_st[g]
                P.op("act" if g == 0 else "dve",
                     (lambda e, tr=tr, bk=bk: e.copy(tr[:], pbf(bk).rearrange("p (j n) -> p j n", j=8))) if g == 0 else
                     (lambda e, tr=tr, bk=bk: e.tensor_copy(tr[:], pbf(bk).rearrange("p (j n) -> p j n", j=8))),
                     reads=[("pb", bk)], writes=[("tr", g)])
                P.dma("sp", lambda e, tr=tr, g=g: e.dma_start(out=xTs[par, :, gt, g * 8:(g + 1) * 8, :], in_=tr[:]),
                      reads=[("tr", g)], writes=[("xTs", par, gt)])

        P.dma("sp", lambda e: e.dma_start(out=gb_bc[:, 0, :], in_=lneg.partition_broadcast(128)), writes=GBKEYS)
        P.dma("sp", lambda e: e.dma_start(out=gb_bc[:, 1, :], in_=lneb.partition_broadcast(128)), writes=GBKEYS)
        for gt in range(NT):
            xt = xt_st[gt % 2]
            xk = ("xt", gt % 2)
            P.dma("sp", lambda e, xt=xt, gt=gt: e.dma_start(out=xt[:], in_=xp[gt * 128:(gt + 1) * 128, :]), writes=[xk])
            layer_norm_tile(xk, xt[:], "gb", xk)
            P.dma("sp", lambda e, xt=xt, gt=gt: e.dma_start(out=Xs[0, gt * 128:(gt + 1) * 128, :], in_=xt[:]),
                  reads=[xk], writes=[("Xs", 0, gt)])
            to_xT_scratch(xt[:], xk, 0, gt)

        def rope_ps(bk, nh, half, cos_ap, sin_ap, out_ap, np_=128):
            n = nh * 2 * half
            src = pb[bk][:np_, 0:n].rearrange("p (h two f) -> p h two f", h=nh, two=2)
            cb = cos_ap.rearrange("p (a b f) -> p a b f", a=1, b=1).to_broadcast([np_, nh, 2, half])
            sbb = sin_ap.rearrange("p (a b f) -> p a b f", a=1, b=1).to_broadcast([np_, nh, 2, half])
            A = ropeA[:np_, 0:n].rearrange("p (h two f) -> p h two f", h=nh, two=2)
            B = ropeB[:np_, 0:n].rearrange("p (h two f) -> p h two f", h=nh, two=2)
            o = out_ap.rearrange("p (h two f) -> p h two f", h=nh, two=2)
            P.op("dve", lambda e: e.tensor_tensor(A, src, cb, ALU.mult), reads=[("pb", bk), "ropetab"], writes=["ropeA"])
            P.op("dve", lambda e: e.tensor_tensor(B, src, sbb, ALU.mult), reads=[("pb", bk), "ropetab"], writes=["ropeB"])
            return A, B, o

        def rope_finish(A, B, o, okey):
            P.op("pool", lambda e: e.tensor_tensor(o[:, :, 0, :], A[:, :, 0, :], B[:, :, 1, :], ALU.subtract),
                 reads=["ropeA", "ropeB"], writes=[okey])
            P.op("pool", lambda e: e.tensor_tensor(o[:, :, 1, :], A[:, :, 1, :], B[:, :, 0, :], ALU.add),
                 reads=["ropeA", "ropeB"], writes=[okey, "ropeA", "ropeB"])

        def transposes8(src_bf, src_key, dst_ap, dst_key, bk, evac_eng="act"):
            for j in range(8):
                P.op("pe", lambda e, j=j: e.transpose(pbf(bk)[:, j * 128:(j + 1) * 128], src_bf[:, j * 128:(j + 1) * 128], ident[:]),
                     reads=[src_key, "ident"], writes=[("pb", bk)])
            if evac_eng == "act":
                P.op("act", lambda e: e.copy(dst_ap, pbf(bk).rearrange("p (j n) -> p j n", j=8)), reads=[("pb", bk)], writes=[dst_key])
            else:
                P.op("dve", lambda e: e.tensor_copy(dst_ap, pbf(bk).rearrange("p (j n) -> p j n", j=8)), reads=[("pb", bk)], writes=[dst_key])

        def transposes4(src_bf, src_key, dst_ap, dst_key, bk, evac_eng="act"):
            for j in range(4):
                P.op("pe", lambda e, j=j: e.transpose(pbf(bk)[:, j * 128:(j + 1) * 128], src_bf[:, j * 128:(j + 1) * 128], ident[:]),
                     reads=[src_key, "ident"], writes=[("pb", bk)])
            if evac_eng == "act":
                P.op("act", lambda e: e.copy(dst_ap, pbf(bk)[:, 0:512].rearrange("p (j n) -> p j n", j=4)), reads=[("pb", bk)], writes=[dst_key])
            else:
                P.op("dve", lambda e: e.tensor_copy(dst_ap, pbf(bk)[:, 0:512].rearrange("p (j n) -> p j n", j=4)), reads=[("pb", bk)], writes=[dst_key])

        P1_BLOCKS = []
        for kind, c0 in (("up", C_UP), ("zp", C_ZP), ("q", C_Q), ("k", C_K), ("v", C_V), ("za", C_ZA), ("iq", C_IQ)):
            P1_BLOCKS.append((kind, c0, 512, 0))
            P1_BLOCKS.append((kind, c0 + 512, 512, 1))
        P1_BLOCKS.append(("ikiw", C_IK, 80, 0))
        if _os.environ.get("P1KINDS"):
            P1_BLOCKS = [b for b in P1_BLOCKS if b[0] in _os.environ["P1KINDS"].split(",")]

        wbuf = [RW[:, i * 8192:(i + 1) * 8192].rearrange("p (k n) -> p k n", k=16) for i in range(3)]

        def load_w(dram_ap_2d, krows, ncols, key):
            i = rr["w"] % 3
            rr["w"] += 1
            buf = wbuf[i]
            kt = krows // 128
            P.dma("pool", lambda e: e.dma_start(out=buf[:, 0:kt, 0:ncols], in_=dram_ap_2d.rearrange("(k p) n -> p k n", p=128)),
                  writes=[("wbuf", i)])
            return buf, ("wbuf", i)

        fence([("xt", 0), ("xt", 1)], [("wbuf", i_) for i_ in range(3)])
        def pass_body(l, h):
            if stop == "P0":
                return
            par = l % 2
            last = (l == DEPTH - 1)
            if True:
                P.dma("sp", lambda e, h=h, par=par: e.dma_start(out=xT[:], in_=xTs[par, :, h * HT:(h + 1) * HT, :, :]),
                      reads=[("xTs", par, h * HT + t) for t in range(HT)], writes=[("xT", t) for t in range(HT)])
                for nm, tab, src, hf in (("cos128", cos128, rc128, 64), ("sin128", sin128, rs128, 64),
                                         ("cos64", cos64, rc64, 32), ("sin64", sin64, rs64, 32)):
                    P.dma("sp", lambda e, tab=tab, src=src, h=h: e.dma_start(
                        out=tab[:], in_=src[h * 1024:(h + 1) * 1024, :].rearrange("(t p) f -> p t f", p=128)),
                        writes=["ropetab"])
                if h == 1:
                    P.dma("sp", lambda e: e.dma_start(out=ikT2[:, 0:1024], in_=ikTs[:, 0:1024]),
                          reads=[("ikTs", c) for c in range(HT)], writes=[("ikT2", c) for c in range(HT)])

                for (kind, c0, ncols, sub) in P1_BLOCKS:
                    wb, wkey = load_w(w_in[l, :, c0:c0 + ncols], D, ncols, None)
                    deferred = []
                    for t in range(HT):
                        gt = h * HT + t
                        bk = (0, 1, 4, 5, 6, 7)[rr["pb"] % 6]
                        rr["pb"] += 1
                        for kt in range(16):
                            P.op("pe", lambda e, t=t, kt=kt, bk=bk, wb=wb, ncols=ncols: e.matmul(
                                pb[bk][:, 0:ncols], lhsT=xT[:, t, kt, :], rhs=wb[:, kt, 0:ncols], start=(kt == 0), stop=(kt == 15)),
                                reads=[("xT", t), wkey], writes=[("pb", bk)])
                        for f_ in deferred:
                            f_()
                        deferred = []
                        sb2 = t % 2
                        pk = ("pb", bk)
                        cs = slice(sub * 512, sub * 512 + 512)
                        if kind == "up":
                            P.op("act", lambda e, t=t, bk=bk, cs=cs: e.copy(up_sb[:, t, cs], pb[bk][:, :]), reads=[pk], writes=[("up", t)])
                            if h == 1 and t == HT - 1:
                                sf = st_f32[sub]
                                P.op("dve", lambda e, bk=bk, sf=sf: e.tensor_copy(sf[:], pb[bk][:, :]), reads=[pk], writes=[("stf", sub)])
                                P.dma("sp", lambda e, sf=sf, cs=cs: e.dma_start(out=pl_p[l, :, cs], in_=sf[113:128, :]),
                                      reads=[("stf", sub)], final=True)
                        elif kind == "zp":
                            P.op("act", lambda e, t=t, bk=bk, cs=cs: e.activation(zp_sb[:, t, cs], pb[bk][:, :], AF.Silu), reads=[pk], writes=[("zp", t)])
                        elif kind == "za":
                            P.op("act", lambda e, t=t, bk=bk, cs=cs: e.activation(za_sb[:, t, cs], pb[bk][:, :], AF.Silu), reads=[pk], writes=[("za", t)])
                        elif kind == "q":
                            A, B, o = rope_ps(bk, 4, 64, cos128[:, t, :], sin128[:, t, :], st_bf[sb2][:, cs])
                            rope_finish(A, B, o, ("stbf", sb2))
                            deferred.append(lambda t=t, sb2=sb2, cs=cs, sub=sub: transposes4(st_bf[sb2][:, cs], ("stbf", sb2), qT[:, t, sub * 4:(sub + 1) * 4, :], ("qT", t), 2))
                        elif kind == "iq":
                            A, B, o = rope_ps(bk, 8, 32, cos64[:, t, :], sin64[:, t, :], st_bf[sb2][:, cs])
                            rope_finish(A, B, o, ("stbf", sb2))
                            deferred.append(lambda t=t, sb2=sb2, cs=cs, sub=sub: transposes4(st_bf[sb2][:, cs], ("stbf", sb2), iqT[:, t, sub * 4:(sub + 1) * 4, :], ("iqT", t), 3, evac_eng="dve"))
                        elif kind == "k":
                            sf = st_f32[sb2]
                            A, B, o = rope_ps(bk, 4, 64, cos128[:, t, :], sin128[:, t, :], sf[:, :])
                            rope_finish(A, B, o, ("stf", sb2))
                            P.dma("sp", lambda e, sf=sf, gt=gt, cs=cs: e.dma_start(out=k_p[l, gt * 128:(gt + 1) * 128, cs], in_=sf[:]),
                                  reads=[("stf", sb2)], final=True)
                            P.op("act", lambda e, sf=sf, cs=cs, sb2=sb2: e.copy(st_bf[sb2][:, cs], sf[:]), reads=[("stf", sb2)], writes=[("stbf", sb2)])

                            def k_tail(t=t, sb2=sb2, cs=cs, sub=sub, gt=gt):
                                tr = tr_st[sb2]
                                transposes4(st_bf[sb2][:, cs], ("stbf", sb2), tr[:, 0:4, :], ("tr", sb2), 2)
                                P.dma("sp", lambda e: e.dma_start(out=kTs[:, gt, sub * 4:(sub + 1) * 4, :], in_=tr[:, 0:4, :]),
                                      reads=[("tr", sb2)], writes=[("kTs", gt)])
                            deferred.append(k_tail)
                        elif kind == "v":
                            sf = st_f32[sb2]
                            P.op("act", lambda e, bk=bk, sf=sf: e.copy(sf[:], pb[bk][:, :]), reads=[pk], writes=[("stf", sb2)])
                            P.dma("sp", lambda e, sf=sf, gt=gt, cs=cs: e.dma_start(out=v_p[l, gt * 128:(gt + 1) * 128, cs], in_=sf[:]),
                                  reads=[("stf", sb2)], final=True)
                            P.op("dve", lambda e, bk=bk, cs=cs, sb2=sb2: e.tensor_copy(st_bf[sb2][:, cs], pb[bk][:, :]), reads=[pk], writes=[("stbf", sb2)])
                            P.dma("sp", lambda e, gt=gt, cs=cs, sb2=sb2: e.dma_start(out=Vs[gt * 128:(gt + 1) * 128, cs], in_=st_bf[sb2][:, cs]),
                                  reads=[("stbf", sb2)], writes=[("Vs", gt)])
                        elif kind == "ikiw":
                            A, B, o = rope_ps(bk, 1, 32, cos64[:, t, :], sin64[:, t, :], ik_st[:, :])
                            rope_finish(A, B, o, "ikst")
                            P.dma("sp", lambda e, gt=gt: e.dma_start(out=ik_p[l, gt * 128:(gt + 1) * 128, :], in_=ik_st[:]),
                                  reads=["ikst"], final=True)
                            P.op("act", lambda e: e.copy(ik2_bf[:, 0:64], ik_st[:]), reads=["ikst"], writes=["ik2"])
                            P.op("act", lambda e: e.copy(ik2_bf[:, 64:128], ik_st[:]), reads=["ikst"], writes=["ik2"])

                            def ik_tail(gt=gt):
                                P.op("pe", lambda e: e.transpose(pbf(3)[:, 0:128], ik2_bf[:], ident[:]), reads=["ik2", "ident"], writes=[("pb", 3)])
                                P.op("act", lambda e: e.copy(ikT2[:, gt * 128:(gt + 1) * 128], pbf(3)[:, 0:128]),
                                     reads=[("pb", 3)], writes=[("ikT2", gt)])
                                if h == 0:
                                    P.dma("sp", lambda e: e.dma_start(out=ikTs[:, gt * 128:(gt + 1) * 128], in_=ikT2[:, gt * 128:(gt + 1) * 128]),
                                          reads=[("ikT2", gt)], writes=[("ikTs", gt)])
                            deferred.append(ik_tail)
                            P.op("dve", lambda e, t=t, bk=bk: e.tensor_scalar(iw_sb[:, t, :], pb[bk][:, 64:80], IDX_SCALE, None, ALU.mult),
                                 reads=[pk], writes=[("iw", t)])
                    for f_ in deferred:
                        f_()
                    deferred = []

                if stop == "P1":
                    return
                sc = RW[:, 0:4096].bitcast(F32)
                junk = RW[:, 4096:6144]
                maskb = RW[:, 6144:8192]
                maskT = RW[:, 8192:10240].rearrange("p (c n) -> p c n", c=NT)
                Rb = [RW[:, i * 512:(i + 1) * 512] for i in range(8)]
                LBANKS = [0, 1, 4, 5, 6, 7]
                diagW = RW[:, 12288:14336].rearrange("p (h n) -> p h n", h=16)
                Eb = [RW[:, 14336 + i * 1024:14336 + (i + 1) * 1024].rearrange("p (h n) -> p h n", h=8) for i in range(2)]
                PTb = [RW[:, 16384 + i * 1024:16384 + (i + 1) * 1024].rearrange("p (h n) -> p h n", h=8) for i in range(2)]
                kTc = [RW[:, 18432 + i * 1024:18432 + (i + 1) * 1024].rearrange("p (h n) -> p h n", h=8) for i in range(2)]
                Vc = [RW[:, 20480 + i * 1024:20480 + (i + 1) * 1024].rearrange("p (h n) -> p h n", h=8) for i in range(2)]
                smallf = RW[:, 22528:22528 + 256].bitcast(F32)
                lo = smallf[:, 0:1]; mid = smallf[:, 1:2]; cnt = smallf[:, 2:3]; wtab = smallf[:, 8:8 + 32]
                rmax = smallf[:, 3:4]; rden = smallf[:, 48:56]
                gei = RW[:, 22528 + 256:22528 + 258].bitcast(I32)
                On = RW[:, 23040:24064].bitcast(F32)
                ag_bf = st_bf[0]
                wk = ["wk_sc", "wk_junk", "wk_mask", "wk_maskT", "wk_diag", "wk_small"]
                P3KEYS = (wk + ["wk_On"] + [("Rb", i_) for i_ in range(8)] + [("Eb", i_, j_) for i_ in range(2) for j_ in range(2)]
                          + [("PT", i_, j_) for i_ in range(2) for j_ in range(2)] + [("kTc", i_) for i_ in range(2)] + [("Vc", i_) for i_ in range(2)])
                WKEYS = [("wbuf", i_) for i_ in range(3)]
                fence(WKEYS + P2KEYS, P3KEYS)
                rrR_box = [0]
                bis = bis_sb
                scb = [R0[:, ch_ * 4096:(ch_ + 1) * 4096].bitcast(F32) for ch_ in range(4)]
                fence([("xT", t_) for t_ in range(HT)], [("sc", ch_) for ch_ in range(4)])

                def p3_indexer(t):
                    rrR = rrR_box[0]
                    ch = t % 4
                    sc = scb[ch]
                    sckey = ("sc", ch)
                    lo = bis[:, ch, 0:1]; rmax = bis[:, ch, 3:4]; wtab = bis[:, ch, 8:40]
                    bkey = ("bs", ch)
                    i = h * HT + t
                    L = (i + 1) * 128
                    for hh in range(16):
                        P.op("dve", lambda e, hh=hh, t=t: e.tensor_scalar(diagW[:, hh, :], ident[:], iw_sb[:, t, hh:hh + 1], None, ALU.mult),
                             reads=["ident", ("iw", t)], writes=["wk_diag"])
                    nkb = (L + 511) // 512
                    for kb in range(nkb):
                        k0 = kb * 512
                        kn = min(512, L - k0)
                        pend = []

                        def emit_H(hh_, rb_, rkey_, kn_):
                            P.op("pe", lambda e: e.matmul(pb[2][:, 0:kn_], lhsT=diagW[:, hh_, :], rhs=rb_[:, 0:kn_], start=(hh_ == 0), stop=(hh_ == 15)),
                                 reads=["wk_diag", rkey_], writes=[("pb", 2)])

                        for hh in range(16):
                            g, half = hh // 2, hh % 2
                            bk = LBANKS[rr["pb"] % 6]
                            rr["pb"] += 1
                            P.op("pe", lambda e, t=t, g=g, half=half, bk=bk, k0=k0, kn=kn: e.matmul(
                                pb[bk][:, 0:kn], lhsT=iqT[half * 64:(half + 1) * 64, t, g, :], rhs=ikT2[half * 64:(half + 1) * 64, k0:k0 + kn],
                                start=True, stop=True),
                                reads=[("iqT", t)] + [("ikT2", c) for c in range(k0 // 128, (k0 + kn) // 128)], writes=[("pb", bk)])
                            rb = Rb[rrR % 8]
                            rkey = ("Rb", rrR % 8)
                            rrR += 1
                            if hh % 2 == 0:
                                P.op("act", lambda e, rb=rb, bk=bk, kn=kn: e.activation(rb[:, 0:kn], pb[bk][:, 0:kn], AF.Relu),
                                     reads=[("pb", bk)], writes=[rkey])
                            else:
                                P.op("dve", lambda e, rb=rb, bk=bk, kn=kn: e.tensor_scalar(rb[:, 0:kn], pb[bk][:, 0:kn], 0.0, None, ALU.max),
                                     reads=[("pb", bk)], writes=[rkey])
                            pend.append((hh, rb, rkey, kn))
                            if len(pend) > 3:
                                emit_H(*pend.pop(0))
                        for p_ in pend:
                            emit_H(*p_)
                        P.op("act", lambda e, k0=k0, kn=kn: e.copy(sc[:, k0:k0 + kn], pb[2][:, 0:kn]), reads=[("pb", 2)], writes=[sckey])
                    P.op("dve", lambda e, L=L: e.tensor_tensor(sc[:, L - 128:L], sc[:, L - 128:L], causal[:], ALU.add),
                         reads=[sckey, "causal"], writes=[sckey])

                    if i >= 2:
                        Lf = L - 128
                        P.op("dve", lambda e, L=L: e.tensor_reduce(rmax, sc[:, 0:L], AX.X, ALU.max), reads=[sckey], writes=[bkey])
                        P.op("dve", lambda e, Lf=Lf: e.tensor_reduce(lo, sc[:, 0:Lf], AX.X, ALU.min), reads=[sckey], writes=[bkey])
                        P.op("dve", lambda e: e.tensor_tensor(rmax, rmax, lo, ALU.subtract), reads=[bkey], writes=[bkey])
                        P.op("dve", lambda e: e.tensor_scalar(wtab, pow2[:, 0:32], rmax, None, ALU.mult), reads=[bkey, "pow2"], writes=[bkey])
                    rrR_box[0] = rrR

                def p3_bis_iter(t, n):
                    i = h * HT + t
                    if i < 2:
                        return
                    L = (i + 1) * 128
                    ch = t % 4
                    sc = scb[ch]
                    sckey = ("sc", ch); bkey = ("bs", ch)
                    lo = bis[:, ch, 0:1]; mid = bis[:, ch, 1:2]; cnt = bis[:, ch, 2:3]; wtab = bis[:, ch, 8:40]
                    gei = gei_sb[:, ch:ch + 1]
                    P.op("dve", lambda e: e.tensor_tensor(mid, lo, wtab[:, n:n + 1], ALU.add), reads=[bkey], writes=[bkey])
                    P.op("dve", lambda e: e.tensor_scalar(junk[:, 0:L], sc[:, 0:L], mid, None, ALU.is_ge, ALU.add, accum_out=cnt),
                         reads=[sckey, bkey], writes=["wk_junk", bkey])
                    P.op("dve", lambda e: e.tensor_scalar(gei, cnt, 255.5, None, ALU.is_ge), reads=[bkey], writes=[bkey])
                    P.op("dve", lambda e: e.copy_predicated(lo, gei, mid), reads=[bkey], writes=[bkey])

                def p3_finish(t):
                    i = h * HT + t
                    L = (i + 1) * 128
                    ch = t % 4
                    sc = scb[ch]
                    sckey = ("sc", ch); bkey = ("bs", ch)
                    lo = bis[:, ch, 0:1]
                    if i >= 2:
                        P.op("dve", lambda e: e.tensor_scalar(maskb[:, 0:L], sc[:, 0:L], lo, None, ALU.is_ge),
                             reads=[sckey, bkey], writes=["wk_mask"])
                    else:
                        P.op("dve", lambda e: e.tensor_scalar(maskb[:, 0:L], sc[:, 0:L], -1.0e29, None, ALU.is_ge),
                             reads=[sckey], writes=["wk_mask"])
                    for c0 in range(0, i + 1, 8):
                        cn = min(8, i + 1 - c0)
                        for j in range(cn):
                            c = c0 + j
                            P.op("pe", lambda e, c=c, j=j: e.transpose(pbf(3)[:, j * 128:(j + 1) * 128], maskb[:, c * 128:(c + 1) * 128], ident[:]),
                                 reads=["wk_mask", "ident"], writes=[("pb", 3)])
                        P.op("act", lambda e, c0=c0, cn=cn: e.copy(maskT[:, c0:c0 + cn, :], pbf(3)[:, 0:cn * 128].rearrange("p (c n) -> p c n", c=cn)),
                             reads=[("pb", 3)], writes=["wk_maskT"])
                    SBK = [(4, 5), (0, 1)]

                    def ld_k(c):
                        kb_ = kTc[c % 2]
                        P.dma("sp", lambda e: e.dma_start(out=kb_[:], in_=kTs[:, c, :, :]), reads=[("kTs", c)], writes=[("kTc", c % 2)])

                    def ld_v(c):
                        vb_ = Vc[c % 2]
                        P.dma("sp", lambda e: e.dma_start(out=vb_[:], in_=Vs[c * 128:(c + 1) * 128, :].rearrange("p (h n) -> p h n", h=8)),
                              reads=[("Vs", c)], writes=[("Vc", c % 2)])

                    def emit_S(c):
                        kb_ = kTc[c % 2]
                        b0, b1 = SBK[c % 2]
                        for hd in range(8):
                            bk = b0 if hd < 4 else b1
                            P.op("pe", lambda e, hd=hd, bk=bk: e.matmul(pb[bk][:, (hd % 4) * 128:(hd % 4 + 1) * 128],
                                                                       lhsT=kb_[:, hd, :], rhs=qT[:, t, hd, :], start=True, stop=True),
                                 reads=[("kTc", c % 2), ("qT", t)], writes=[("pb", bk)])

                    def emit_E(c):
                        eb = Eb[c % 2]; ptb = PTb[c % 2]
                        b0, b1 = SBK[c % 2]
                        for hf, bk in ((0, b0), (1, b1)):
                            P.op("act", lambda e, hf=hf, bk=bk: e.activation(eb[:, hf * 4:(hf + 1) * 4, :],
                                                                           pb[bk][:, :].rearrange("p (h n) -> p h n", h=4), AF.Exp, scale=SM_SCALE),
                                 reads=[("pb", bk)], writes=[("Eb", c % 2, hf)])
                        P.op("pool", lambda e: e.tensor_tensor(ptb[:, 0:4, :], eb[:, 0:4, :], maskT[:, c:c + 1, :].to_broadcast([128, 4, 128]), ALU.mult),
                             reads=[("Eb", c % 2, 0), "wk_maskT"], writes=[("PT", c % 2, 0)])
                        P.op("dve", lambda e: e.tensor_tensor(ptb[:, 4:8, :], eb[:, 4:8, :], maskT[:, c:c + 1, :].to_broadcast([128, 4, 128]), ALU.mult),
                             reads=[("Eb", c % 2, 1), "wk_maskT"], writes=[("PT", c % 2, 1)])

                    def emit_PV(c):
                        ptb = PTb[c % 2]; vb_ = Vc[c % 2]
                        for hd in range(8):
                            P.op("pe", lambda e, hd=hd: e.matmul(
                                pb[6 + hd // 4][:, (hd % 4) * 128:(hd % 4 + 1) * 128], lhsT=ptb[:, hd, :], rhs=vb_[:, hd, :],
                                start=(c == 0 and hd % 4 == 0), stop=(c == i), skip_group_check=True),
                                reads=[("PT", c % 2, hd // 4), ("Vc", c % 2)], writes=[("pb", 6 + hd // 4)])
                            P.op("pe", lambda e, hd=hd: e.matmul(
                                pb[3][:, 256 + hd:256 + hd + 1], lhsT=ptb[:, hd, :], rhs=ones_bf[:, 0:1],
                                start=(c == 0 and hd == 0), stop=(c == i), skip_group_check=True),
                                reads=[("PT", c % 2, hd // 4), "ones"], writes=[("pb", 3)])

                    for c in range(min(2, i + 1)):
                        ld_k(c); ld_v(c); emit_S(c); emit_E(c)
                    for c in range(i + 1):
                        if c + 2 <= i:
                            ld_k(c + 2); emit_S(c + 2)
                        emit_PV(c)
                        if c + 2 <= i:
                            ld_v(c + 2); emit_E(c + 2)
                    P.op("dve", lambda e: e.reciprocal(rden, pb[3][:, 256:264]), reads=[("pb", 3)], writes=["wk_small"])
                    for hf in range(2):
                        P.op("dve", lambda e, hf=hf: e.tensor_tensor(
                            On.rearrange("p (h n) -> p h n", h=4), pb[6 + hf][:, :].rearrange("p (h n) -> p h n", h=4),
                            rden[:, hf * 4:(hf + 1) * 4].rearrange("p (h o) -> p h o", o=1).to_broadcast([128, 4, 128]), ALU.mult),
                            reads=[("pb", 6 + hf), "wk_small"], writes=["wk_On"])
                        P.op("pool", lambda e, hf=hf, t=t: e.tensor_tensor(ag_bf[:, hf * 512:(hf + 1) * 512], On, za_sb[:, t, hf * 512:(hf + 1) * 512], ALU.mult),
                             reads=["wk_On", ("za", t)], writes=[("stbf", 0), "wk_On"])
                    if dbg and l == 0:
                        P.dma("sp", lambda e, i=i: e.dma_start(out=dbg_ag[i * 128:(i + 1) * 128, :], in_=ag_bf[:]), reads=[("stbf", 0)])
                    transposes8(ag_bf, ("stbf", 0), AgT[:, t, :, :], ("qT", t), 0)


                for g0 in (0, 4):
                    for t in range(g0, g0 + 4):
                        p3_indexer(t)
                    for n in range(NBIS):
                        for t in range(g0, g0 + 4):
                            p3_bis_iter(t, n)
                    for t in range(g0, g0 + 4):
                        p3_finish(t)

                if stop == "P3":
                    return
                wmix_sb = RW[:, 0:2048].rearrange("p (g c e) -> p g c e", g=4, c=2)
                psc_bc = RW[:, 2048:4096].bitcast(F32)
                dT_sb = RW[:, 4096:5120].rearrange("p (c n) -> p c n", c=8)
                pg_f = RW[:, 5120:7168].bitcast(F32)
                pg_bf = st_bf[1]
                guard = []
                fence(P3KEYS + WKEYS, P2KEYS)
                P.dma("pool", lambda e, l=l: e.dma_start(out=wmix_sb, in_=w_mix[l].rearrange("g (c p) e -> p g c e", p=128)),
                      reads=guard, writes=["wmix"] + guard)
                P.dma("sp", lambda e, l=l: e.dma_start(out=psc_bc, in_=pscale[l:l + 1, :].partition_broadcast(128)),
                      reads=guard, writes=["psc"] + guard)
                for t in range(HT):
                    gt = h * HT + t
                    for ct in range(8):
                        g = ct // 2
                        bk = ct // 4
                        first = (gt == 0)
                        bidx = (4 + g) if first else g
                        P.op("pe", lambda e, t=t, ct=ct, bk=bk, bidx=bidx, first=first: e.matmul(
                            pb[bk][:, (ct % 4) * 128:(ct % 4 + 1) * 128], lhsT=up_sb[:, t, ct * 128:(ct + 1) * 128], rhs=bands[:, bidx, :],
                            start=True, stop=first), reads=[("up", t), "bands"], writes=[("pb", bk)])
                        if not first:
                            if t > 0:
                                prev = up_sb[:, t - 1, ct * 128:(ct + 1) * 128]; pkey = ("up", t - 1)
                            else:
                                prev = up_last[:, ct * 128:(ct + 1) * 128]; pkey = "up_last"
                            P.op("pe", lambda e, ct=ct, bk=bk, g=g, prev=prev: e.matmul(
                                pb[bk][:, (ct % 4) * 128:(ct % 4 + 1) * 128], lhsT=prev, rhs=bands[:, 8 + g, :], start=False, stop=True),
                                reads=[pkey, "bands"], writes=[("pb", bk)])
                    for bk in range(2):
                        P.op("act", lambda e, bk=bk: e.copy(dT_sb[:, bk * 4:(bk + 1) * 4, :], pb[bk][:, :].rearrange("p (c n) -> p c n", c=4)),
                             reads=[("pb", bk)] + guard, writes=[("dT", bk)] + guard)
                    for g in range(4):
                        bk = 2 + g // 2
                        for c in range(2):
                            P.op("pe", lambda e, g=g, c=c, bk=bk: e.matmul(pb[bk][:, (g % 2) * 256:(g % 2 + 1) * 256], lhsT=dT_sb[:, g * 2 + c, :],
                                                                        rhs=wmix_sb[:, g, c, :], start=(c == 0), stop=(c == 1)),
                                 reads=[("dT", (g * 2 + c) // 4), "wmix"], writes=[("pb", bk)])
                    for hf in range(2):
                        P.op("dve", lambda e, hf=hf: e.tensor_tensor(pg_f[:, hf * 512:(hf + 1) * 512], pb[2 + hf][:, :], psc_bc[:, hf * 512:(hf + 1) * 512], ALU.mult),
                             reads=[("pb", 2 + hf), "psc"] + guard, writes=[("pgf", hf)] + guard)
                        P.op("pool", lambda e, hf=hf, t=t: e.tensor_tensor(pg_bf[:, hf * 512:(hf + 1) * 512], pg_f[:, hf * 512:(hf + 1) * 512],
                                                                          zp_sb[:, t, hf * 512:(hf + 1) * 512], ALU.mult),
                             reads=[("pgf", hf), ("zp", t)], writes=[("stbf", 1)])
                    if dbg and l == 0:
                        P.dma("sp", lambda e, gt=gt: e.dma_start(out=dbg_pg[gt * 128:(gt + 1) * 128, :], in_=pg_bf[:]), reads=[("stbf", 1)])
                    transposes8(pg_bf, ("stbf", 1), PgT[:, t, :, :], ("iqT", t), 4, evac_eng="dve")
                if h == 0:
                    P.op("pool", lambda e: e.tensor_copy(up_last[:], up_sb[:, HT - 1, :]), reads=[("up", HT - 1)], writes=["up_last"])

                if stop == "P2":
                    return
                fence(P3KEYS + P2KEYS, WKEYS)
                fence([("sc", ch_) for ch_ in range(4)], [("xT", t_) for t_ in range(HT)])
                P.dma("sp", lambda e: e.dma_start(out=xT[:], in_=xTs[par, :, h * HT:(h + 1) * HT, :, :]),
                      reads=[("xTs", par, h * HT + t_) for t_ in range(HT)], writes=[("xT", t_) for t_ in range(HT)])
                sgb = st_f32
                prodb = [ropeA, ropeB]

                def m_keys(t):
                    if t < 4:
                        return [("m", t), ("up", 2 * t), ("up", 2 * t + 1)]
                    return [("m", t), ("zp", 2 * (t - 4)), ("zp", 2 * (t - 4) + 1)]

                P4W = [("p4w", s_, x_) for s_ in range(2) for x_ in "AB"]
                fence(WKEYS, P4W)
                for jb in range(8):
                    j0 = jb * 256
                    s_ = jb % 2
                    bA = RW[:, s_ * 12288:s_ * 12288 + 4096].rearrange("p (k n) -> p k n", k=16)
                    bB = RW[:, s_ * 12288 + 4096:(s_ + 1) * 12288].rearrange("p (k n) -> p k n", k=16)
                    kA = ("p4w", s_, "A"); kB = ("p4w", s_, "B")
                    P.dma("pool", lambda e, bA=bA, j0=j0, l=l: e.dma_start(out=bA[:, 0:8, 0:256], in_=w_po[l, :, j0:j0 + 256].rearrange("(k p) n -> p k n", p=128)), writes=[kA])
                    P.dma("pool", lambda e, bA=bA, j0=j0, l=l: e.dma_start(out=bA[:, 8:16, 0:256], in_=w_ao[l, :, j0:j0 + 256].rearrange("(k p) n -> p k n", p=128)), writes=[kA])
                    P.dma("pool", lambda e, bB=bB, j0=j0, l=l: e.dma_start(out=bB[:, 0:16, 0:256], in_=w_in[l, :, C_GP + j0:C_GP + j0 + 256].rearrange("(k p) n -> p k n", p=128)), writes=[kB])
                    P.dma("pool", lambda e, bB=bB, j0=j0, l=l: e.dma_start(out=bB[:, 0:16, 256:512], in_=w_in[l, :, C_GA + j0:C_GA + j0 + 256].rearrange("(k p) n -> p k n", p=128)), writes=[kB])
                    for t in range(HT):
                        pr = (t % 2) * 2
                        for kt in range(8):
                            P.op("pe", lambda e, bA=bA, t=t, kt=kt, pr=pr: e.matmul(pb[pr][:, 0:256], lhsT=PgT[:, t, kt, :], rhs=bA[:, kt, 0:256], start=(kt == 0), stop=(kt == 7)),
                                 reads=[("iqT", t), kA], writes=[("pb", pr)])
                        for kt in range(8):
                            P.op("pe", lambda e, bA=bA, t=t, kt=kt, pr=pr: e.matmul(pb[pr][:, 256:512], lhsT=AgT[:, t, kt, :], rhs=bA[:, 8 + kt, 0:256], start=(kt == 0), stop=(kt == 7)),
                                 reads=[("qT", t), kA], writes=[("pb", pr)])
                        for kt in range(16):
                            P.op("pe", lambda e, bB=bB, t=t, kt=kt, pr=pr: e.matmul(pb[pr + 1][:, 0:256], lhsT=xT[:, t, kt, :], rhs=bB[:, kt, 0:256], start=(kt == 0), stop=(kt == 15)),
                                 reads=[("xT", t), kB], writes=[("pb", pr + 1)])
                        for kt in range(16):
                            P.op("pe", lambda e, bB=bB, t=t, kt=kt, pr=pr: e.matmul(pb[pr + 1][:, 256:512], lhsT=xT[:, t, kt, :], rhs=bB[:, kt, 256:512], start=(kt == 0), stop=(kt == 15)),
                                 reads=[("xT", t), kB], writes=[("pb", pr + 1)])
                        sg = sgb[t % 2]; prod = prodb[t % 2]
                        pkey = "ropeA" if t % 2 == 0 else "ropeB"
                        P.op("act", lambda e, sg=sg, pr=pr: e.activation(sg[:], pb[pr + 1][:, :], AF.Sigmoid), reads=[("pb", pr + 1)], writes=[("stf", t % 2)])
                        P.op("dve", lambda e, sg=sg, prod=prod, pr=pr: e.tensor_tensor(prod[:], pb[pr][:, :], sg[:], ALU.mult),
                             reads=[("pb", pr), ("stf", t % 2)], writes=[pkey])
                        P.op("pool", lambda e, prod=prod, t=t, j0=j0: e.tensor_tensor(m_sb[:, t, j0:j0 + 256], prod[:, 0:256], prod[:, 256:512], ALU.add),
                             reads=[pkey], writes=m_keys(t))

                if stop == "P4":
                    return
                fence(P4W, WKEYS)
                for t in range(HT):
                    for g in range(2):
                        bk = 4 + g
                        for j in range(8):
                            kt = g * 8 + j
                            P.op("pe", lambda e, t=t, kt=kt, j=j, bk=bk: e.transpose(pbf(bk)[:, j * 128:(j + 1) * 128], m_sb[:, t, kt * 128:(kt + 1) * 128], ident[:]),
                                 reads=[("m", t), "ident"], writes=[("pb", bk)])
                        if g == 0:
                            P.op("act", lambda e, t=t, bk=bk: e.copy(mT[:, t, 0:8, :], pbf(bk).rearrange("p (j n) -> p j n", j=8)), reads=[("pb", bk)], writes=[("xT", t)])
                        else:
                            P.op("dve", lambda e, t=t, bk=bk: e.tensor_copy(mT[:, t, 8:16, :], pbf(bk).rearrange("p (j n) -> p j n", j=8)), reads=[("pb", bk)], writes=[("xT", t)])

                def xres_ap(j):
                    if j < 6:
                        return R2[:, j * 4096:(j + 1) * 4096].bitcast(F32)
                    return R1[:, (j - 6) * 4096:(j - 5) * 4096].bitcast(F32)

                def xres_keys(j):
                    ks = [("xres", j)]
                    if j < 2:
                        ks += [("qT", tt) for tt in range(j * 4, j * 4 + 4)]
                    elif j < 4:
                        ks += [("iqT", tt) for tt in range((j - 2) * 4, (j - 2) * 4 + 4)]
                    elif j < 6:
                        ks += [("za", tt) for tt in range((j - 4) * 4, (j - 4) * 4 + 4)]
                    else:
                        ks += [("up", tt) for tt in range((j - 6) * 4, (j - 6) * 4 + 4)] + [("m", 2 * (j - 6)), ("m", 2 * (j - 6) + 1)]
                    return ks

                P.dma("sp", lambda e, l=l: e.dma_start(out=gb_bc[:, 0, :], in_=lng[l:l + 1, :].partition_broadcast(128)), writes=GBKEYS)
                P.dma("sp", lambda e, l=l: e.dma_start(out=gb_bc[:, 1, :], in_=lnb[l:l + 1, :].partition_broadcast(128)), writes=GBKEYS)
                for t in range(HT):
                    gt = h * HT + t
                    xa = xres_ap(t)
                    P.dma("sp", lambda e, xa=xa, gt=gt, par=par: e.dma_start(out=xa, in_=Xs[par, gt * 128:(gt + 1) * 128, :]),
                          reads=[("Xs", par, gt)], writes=xres_keys(t))
                for jb in range(4):
                    j0 = jb * 512
                    wo, kwo = load_w(w_o[l, :, j0:j0 + 512], D, 512, None)
                    for t in range(HT):
                        bk = rr["pb"] % 2
                        rr["pb"] += 1
                        for kt in range(16):
                            P.op("pe", lambda e, wo=wo, t=t, kt=kt, bk=bk: e.matmul(pb[bk][:, :], lhsT=mT[:, t, kt, :], rhs=wo[:, kt, :], start=(kt == 0), stop=(kt == 15)),
                                 reads=[("xT", t), kwo], writes=[("pb", bk)])
                        xa = xres_ap(t)
                        P.op("dve", lambda e, xa=xa, bk=bk, j0=j0: e.scalar_tensor_tensor(xa[:, j0:j0 + 512], xa[:, j0:j0 + 512], ALPHA, pb[bk][:, :], ALU.mult, ALU.add),
                             reads=[("pb", bk), ("xres", t)], writes=[("xres", t)])
                for t in range(HT):
                    gt = h * HT + t
                    xa = xres_ap(t)
                    layer_norm_tile(("xres", t), xa, "gb", ("xres", t))
                    if last:
                        P.dma("sp", lambda e, xa=xa, gt=gt: e.dma_start(out=y_p[gt * 128:(gt + 1) * 128, :], in_=xa), reads=[("xres", t)])
                    else:
                        P.dma("sp", lambda e, xa=xa, gt=gt, par=par: e.dma_start(out=Xs[1 - par, gt * 128:(gt + 1) * 128, :], in_=xa),
                              reads=[("xres", t)], writes=[("Xs", 1 - par, gt)])
                        to_xT_scratch(xa, ("xres", t), 1 - par, gt)
        xsT = sb("xsT", [128, 16, 4], BF16)
        qsT = sb("qsT", [128, 8, 4], BF16); ksT = sb("ksT", [128, 8, 4], BF16); zasT = sb("zasT", [128, 8, 4], BF16)
        PgsT = sb("PgsT", [128, 8, 4], BF16); AgsT = sb("AgsT", [128, 8, 4], BF16); msT = sb("msT", [128, 16, 4], BF16)
        dTs = sb("dTs", [128, 8, 4], BF16)
        iqsT = sb("iqsT", [64, 4, 16], BF16)
        iksT = sb("iksT", [64, 4], BF16)
        iws = sb("iws", [4, 16], F32)
        iwcol = sb("iwcol", [64, 1], F32)
        sbst = sb("sbst", [15, 4, 4], BF16); sbnew = sb("sbnew", [4, 4, 4], BF16)
        pt_sb = sb("pt_sb", [128, 1], I32); ptf = sb("ptf", [128, 1], F32); rowbase = sb("rowbase", [128, 1], F32)
        pgidx = sb("pgidx", [128, 1], I32)
        iota_i = sb("iota_i", [128, 128], I32); iota_f = sb("iota_f", [128, 128], F32); idx_all = sb("idx_all", [128, 128], I32)
        negm_s = sb("negm_s", [128, 4], F32); bones = sb("bones_sb", [128, 128], F32); dmask = sb("dmask", [32, 8], F32)
        ssm = sb("ssm", [128, 64], F32)
        s_lo = ssm[:, 0:1]; s_mid = ssm[:, 1:2]; s_cnt = ssm[:, 2:3]; s_B = ssm[:, 3:4]; s_am = ssm[:, 4:5]; s_am2 = ssm[:, 5:6]
        s_w0 = ssm[:, 6:7]; s_rden = ssm[0:32, 7:8]; s_wtab = ssm[:, 16:16 + 40]
        s_gei = sb("s_gei", [128, 1], I32)
        xs_scr = dscr("xs_scr", [2, 4, D])
        iw_scr = dscr("iw_scr", [4, 16])

        SR = R1[0:4, :]
        ups = SR[:, 0:1024]; zps = SR[:, 1024:2048]; zas = SR[:, 2048:3072]; ksb = SR[:, 3072:4096]; vsb = SR[:, 4096:5120]
        iqs = SR[:, 5120:6144]; gps = SR[:, 6144:8192]; gas = SR[:, 8192:10240]; ms = SR[:, 10240:12288]
        xst = SR[:, 12288:16384].bitcast(F32)
        gbs = R0[0:4, 0:8192].bitcast(F32).rearrange("p (a n) -> p a n", a=2)
        ik_raw = R0[:, :].bitcast(F32)
        ik_bf = R2[:, 0:8192]
        Kc = [R2[:, 8192 + i * 2048:8192 + (i + 1) * 2048].bitcast(F32) for i in range(2)]
        Vc = [R2[:, 12288 + i * 2048:12288 + (i + 1) * 2048].bitcast(F32) for i in range(2)]
        Kbf = [R2[:, 16384 + i * 1024:16384 + (i + 1) * 1024] for i in range(2)]
        Vbf = [R2[:, 18432 + i * 1024:18432 + (i + 1) * 1024] for i in range(2)]
        KTs = [R2[:, 20480 + i * 1024:20480 + (i + 1) * 1024].rearrange("p (h n) -> p h n", h=8) for i in range(2)]
        ikT_s = RW[0:64, 0:16388]
        Wsc = RW[0:64, 16400:20496]
        sc_s = RW[:, 20496:21528].bitcast(F32)
        junk_s = RW[:, 21528:22044]
        mask_s = RW[:, 22044:22560]
        Tm = RW[:, 22560:23072].rearrange("p (g n) -> p g n", g=4)
        Tn = RW[0:4, 23072:23200]
        Rs = [RW[0:64, 23200 + i * 512:23200 + (i + 1) * 512] for i in range(2)]
        Es = RW[:, 24224:24288].bitcast(F32)
        PTs = [RW[:, 24288 + i * 32:24288 + (i + 1) * 32] for i in range(2)]
        wmix_s = RW[:, 16400:18448].rearrange("p (g c e) -> p g c e", g=4, c=2)
        psc_s = RW[0:4, 18448:20496].bitcast(F32)
        pgs_f = RW[0:4, 20496:22544].bitcast(F32)
        od_tmp = RW[0:32, 0:2048].bitcast(F32)
        od = RW[0:32, 2048:2304].bitcast(F32)
        odn = RW[0:32, 2304:2432]

        def bar():
            P.barrier(lambda e: e.memset(fdummy[:], 0.0))

        def load_w2(dram_ap_2d, krows, ncols, k0=0):
            i = rr["w"] % 2
            rr["w"] += 1
            buf = wbuf[i]
            kt = krows // 128
            P.dma("pool", lambda e: e.dma_start(out=buf[:, k0:k0 + kt, 0:ncols], in_=dram_ap_2d.rearrange("(k p) n -> p k n", p=128)),
                  writes=[("wbuf", i)])
            return buf, ("wbuf", i)

        def tr_small(src_ap, np_, ncol, dst_ap, key_r, key_w, bk=3, dst_rearr=None):
            P.op("pe", lambda e: e.transpose(pbf(bk)[0:ncol, 0:np_], src_ap, ident[0:np_, 0:np_]), reads=[key_r, "ident"], writes=[("pb", bk)])
            P.op("act", lambda e: e.copy(dst_ap, pbf(bk)[0:ncol, 0:np_]), reads=[("pb", bk)], writes=[key_w])

        def tr_tok(src_bf, key_r, dst, key_w, nblk):
            for j in range(nblk):
                P.op("pe", lambda e, j=j: e.transpose(pbf(3)[:, j * 4:(j + 1) * 4], src_bf[:, j * 128:(j + 1) * 128], ident[0:4, 0:4]),
                     reads=[key_r, "ident"], writes=[("pb", 3)])
            P.op("act", lambda e: e.copy(dst, pbf(3)[:, 0:nblk * 4].rearrange("p (j n) -> p j n", j=nblk)), reads=[("pb", 3)], writes=[key_w])

        def ln4(x_ap, xkey):
            for j in range(4):
                P.op("dve", lambda e, j=j: e.bn_stats(stat[0:4, j, :], x_ap[:, j * 512:(j + 1) * 512]), reads=[xkey], writes=[("stat", j)])
            P.op("dve", lambda e: e.bn_aggr(mv[0:4, :], stat[0:4].rearrange("p a b -> p (a b)")), reads=[("stat", j) for j in range(4)], writes=["mv"])
            P.op("act", lambda e: e.activation(rstd[0:4, :], mv[0:4, 1:2], AF.Sqrt, bias=eps_t[0:4, :]), reads=["mv", "eps"], writes=["rstd"])
            P.op("dve", lambda e: e.reciprocal(rstd[0:4, :], rstd[0:4, :]), reads=["rstd"], writes=["rstd"])
            P.op("dve", lambda e: e.tensor_scalar(x_ap, x_ap, mv[0:4, 0:1], rstd[0:4, :], ALU.subtract, ALU.mult), reads=[xkey, "mv", "rstd"], writes=[xkey])
            P.op("dve", lambda e: e.tensor_tensor(x_ap, x_ap, gbs[:, 0, :], ALU.mult), reads=[xkey, "gbs"], writes=[xkey])
            P.op("dve", lambda e: e.tensor_tensor(x_ap, x_ap, gbs[:, 1, :], ALU.add), reads=[xkey, "gbs"], writes=[xkey])

        def xs_to_T():
            P.op("act", lambda e: e.copy(xbf_st[0:4, :], xst), reads=["xst"], writes=["xbf"])
            tr_tok(xbf_st[0:4, :], "xbf", xsT[:], "xsT", 16)

        def sample_init():
            bar()
            P.dma("pool", lambda e: e.dma_start(out=sbst[:], in_=sb_st_d.rearrange("g s t -> s g t")), writes=["sbst"])
            P.dma("pool", lambda e: e.dma_start(out=sbnew[:], in_=sb_new_d.rearrange("g s t -> s g t")), writes=["sbnew"])
            P.dma("sp", lambda e: e.dma_start(out=negm_s[:], in_=negm_d), writes=["negm"])
            P.dma("sp", lambda e: e.dma_start(out=bones[:], in_=bones_d), writes=["bones"])
            P.dma("sp", lambda e: e.dma_start(out=dmask[:], in_=dmask_d), writes=["dmask"])
            P.dma("sp", lambda e: e.dma_start(out=pt_sb[:], in_=pt), writes=["pt"])
            P.op("dve", lambda e: e.tensor_copy(ptf[:], pt_sb[:]), reads=["pt"], writes=["ptf"])
            P.op("pool", lambda e: e.iota(iota_i[:], [[1, 128]], base=0, channel_multiplier=0), writes=["iota_i"])
            P.op("dve", lambda e: e.tensor_copy(iota_f[:], iota_i[:]), reads=["iota_i"], writes=["iota_f"])
            P.dma("sp", lambda e: e.dma_start(out=gbs[:, 0, :], in_=lneg.partition_broadcast(4)), writes=["gbs"])
            P.dma("sp", lambda e: e.dma_start(out=gbs[:, 1, :], in_=lneb.partition_broadcast(4)), writes=["gbs"])
            P.dma("sp", lambda e: e.dma_start(out=xst, in_=xs_in), writes=["xst"])
            ln4(xst, "xst")
            P.dma("sp", lambda e: e.dma_start(out=xs_scr[0], in_=xst), reads=["xst"], writes=["xs_scr"])
            xs_to_T()
            bar()

        S_BLOCKS = []
        for kind, c0 in (("up", C_UP), ("zp", C_ZP), ("q", C_Q), ("k", C_K), ("v", C_V), ("za", C_ZA), ("iq", C_IQ)):
            S_BLOCKS.append((kind, c0, 512, 0)); S_BLOCKS.append((kind, c0 + 512, 512, 1))
        S_BLOCKS.append(("ikiw", C_IK, 80, 0))
        for kind, c0 in (("gp", C_GP), ("ga", C_GA)):
            for sub in range(4):
                S_BLOCKS.append((kind, c0 + sub * 512, 512, sub))

        def rope4(bk, nh, half, cos_ap, sin_ap, out_ap, okey):
            n = nh * 2 * half
            src = pb[bk][0:4, 0:n].rearrange("p (h two f) -> p h two f", h=nh, two=2)
            cb = cos_ap.rearrange("p (a b f) -> p a b f", a=1, b=1).to_broadcast([4, nh, 2, half])
            sbb = sin_ap.rearrange("p (a b f) -> p a b f", a=1, b=1).to_broadcast([4, nh, 2, half])
            A = ropeA[0:4, 0:n].rearrange("p (h two f) -> p h two f", h=nh, two=2)
            B = ropeB[0:4, 0:n].rearrange("p (h two f) -> p h two f", h=nh, two=2)
            o = out_ap.rearrange("p (h two f) -> p h two f", h=nh, two=2)
            P.op("dve", lambda e: e.tensor_tensor(A, src, cb, ALU.mult), reads=[("pb", bk), "sropetab"], writes=["ropeA"])
            P.op("dve", lambda e: e.tensor_tensor(B, src, sbb, ALU.mult), reads=[("pb", bk), "sropetab"], writes=["ropeB"])
            P.op("dve", lambda e: e.tensor_tensor(o[:, :, 0, :], A[:, :, 0, :], B[:, :, 1, :], ALU.subtract), reads=["ropeA", "ropeB"], writes=[okey])
            P.op("dve", lambda e: e.tensor_tensor(o[:, :, 1, :], A[:, :, 1, :], B[:, :, 0, :], ALU.add), reads=["ropeA", "ropeB"], writes=[okey, "ropeA", "ropeB"])

        def sample_pass(l):
            last = (l == DEPTH - 1)
            par = l % 2
            bar()
            for tab, src in ((cos128, src128), (sin128, srs128), (cos64, src64), (sin64, srs64)):
                P.dma("sp", lambda e, tab=tab, src=src: e.dma_start(out=tab[0:4, 0, :], in_=src), writes=["sropetab"])
            for (kind, c0, ncols, sub) in S_BLOCKS:
                wb, wkey = load_w2(w_in[l, :, c0:c0 + ncols], D, ncols)
                bk = rr["pb"] % 2
                rr["pb"] += 1
                for kt in range(16):
                    P.op("pe", lambda e, kt=kt, bk=bk, wb=wb, ncols=ncols: e.matmul(pb[bk][0:4, 0:ncols], lhsT=xsT[:, kt, :], rhs=wb[:, kt, 0:ncols],
                                                                                 start=(kt == 0), stop=(kt == 15)), reads=["xsT", wkey], writes=[("pb", bk)])
                pk = ("pb", bk)
                cs = slice(sub * 512, sub * 512 + 512)
                if kind == "up":
                    sf = st_f32[sub]
                    P.op("act", lambda e, sf=sf, bk=bk: e.copy(sf[0:4, :], pb[bk][0:4, :]), reads=[pk], writes=[("stf", sub)])
                    P.dma("sp", lambda e, sf=sf, cs=cs: e.dma_start(out=pl_s[l, 11:15, cs], in_=sf[0:4, :]), reads=[("stf", sub)])
                    P.op("dve", lambda e, bk=bk, cs=cs: e.tensor_copy(ups[:, cs], pb[bk][0:4, :]), reads=[pk], writes=["ups"])
                elif kind == "zp":
                    P.op("act", lambda e, bk=bk, cs=cs: e.activation(zps[:, cs], pb[bk][0:4, :], AF.Silu), reads=[pk], writes=["zps"])
                elif kind == "za":
                    P.op("act", lambda e, bk=bk, cs=cs: e.activation(zas[:, cs], pb[bk][0:4, :], AF.Silu), reads=[pk], writes=["zas"])
                    if sub == 1:
                        tr_tok(zas, "zas", zasT[:], "zasT", 8)
                elif kind == "q":
                    rope4(bk, 4, 64, cos128[0:4, 0, :], sin128[0:4, 0, :], iqs[:, cs], "iqs")
                    if sub == 1:
                        tr_tok(iqs, "iqs", qsT[:], "qsT", 8)
                elif kind == "k":
                    sf = st_f32[sub]
                    rope4(bk, 4, 64, cos128[0:4, 0, :], sin128[0:4, 0, :], sf[0:4, :], ("stf", sub))
                    P.dma("sp", lambda e, sf=sf, cs=cs: e.dma_start(out=k_s[l, :, cs], in_=sf[0:4, :]), reads=[("stf", sub)])
                    P.op("act", lambda e, sf=sf, cs=cs: e.copy(ksb[:, cs], sf[0:4, :]), reads=[("stf", sub)], writes=["ksb"])
                    if sub == 1:
                        tr_tok(ksb, "ksb", ksT[:], "ksT", 8)
                elif kind == "v":
                    sf = st_f32[sub]
                    P.op("act", lambda e, sf=sf, bk=bk: e.copy(sf[0:4, :], pb[bk][0:4, :]), reads=[pk], writes=[("stf", sub)])
                    P.dma("sp", lambda e, sf=sf, cs=cs: e.dma_start(out=v_s[l, :, cs], in_=sf[0:4, :]), reads=[("stf", sub)])
                    P.op("dve", lambda e, bk=bk, cs=cs: e.tensor_copy(vsb[:, cs], pb[bk][0:4, :]), reads=[pk], writes=["vsb"])
                elif kind == "iq":
                    rope4(bk, 8, 32, cos64[0:4, 0, :], sin64[0:4, 0, :], iqs[:, cs], "iqs")
                    if sub == 1:
                        for hh in range(16):
                            P.op("pe", lambda e, hh=hh: e.transpose(pbf(3)[0:64, hh * 4:(hh + 1) * 4], iqs[:, hh * 64:(hh + 1) * 64], ident[0:4, 0:4]),
                                 reads=["iqs", "ident"], writes=[("pb", 3)])
                        P.op("act", lambda e: e.copy(iqsT[:], pbf(3)[0:64, 0:64].rearrange("p (h q) -> p q h", h=16)), reads=[("pb", 3)], writes=["iqsT"])
                elif kind == "ikiw":
                    rope4(bk, 1, 32, cos64[0:4, 0, :], sin64[0:4, 0, :], ik_st[0:4, :], "ikst")
                    P.dma("sp", lambda e: e.dma_start(out=ik_s[l], in_=ik_st[0:4, :]), reads=["ikst"])
                    P.op("act", lambda e: e.copy(ik2_bf[0:4, 0:64], ik_st[0:4, :]), reads=["ikst"], writes=["ik2"])
                    tr_small(ik2_bf[0:4, 0:64], 4, 64, iksT[:], "ik2", "iksT")
                    P.op("dve", lambda e, bk=bk: e.tensor_scalar(iws[:], pb[bk][0:4, 64:80], IDX_SCALE, None, ALU.mult), reads=[pk], writes=["iws"])
                    P.dma("sp", lambda e: e.dma_start(out=iw_scr, in_=iws[:]), reads=["iws"], writes=["iw_scr"])
                    P.dma("sp", lambda e: e.dma_start(out=iwcol[:], in_=iw_scr.rearrange("q (h o) -> (q h) o", o=1)), reads=["iw_scr"], writes=["iwcol"])
                elif kind == "gp":
                    P.op("act", lambda e, bk=bk, cs=cs: e.activation(gps[:, cs], pb[bk][0:4, :], AF.Sigmoid), reads=[pk], writes=["gps"])
                elif kind == "ga":
                    P.op("act", lambda e, bk=bk, cs=cs: e.activation(gas[:, cs], pb[bk][0:4, :], AF.Sigmoid), reads=[pk], writes=["gas"])
            bar()
            state_bf = ropeA[0:15, :].bitcast(BF16)
            P.dma("pool", lambda e: e.dma_start(out=state_bf, in_=stp[l]), writes=["state"])
            P.dma("sp", lambda e: e.dma_start(out=pl_s[l, 0:11, :], in_=stp[l, 4:15, :]))
            P.dma("pool", lambda e: e.dma_start(out=wmix_s, in_=w_mix[l].rearrange("g (c p) e -> p g c e", p=128)), writes=["wmix_s"])
            P.dma("sp", lambda e: e.dma_start(out=psc_s, in_=pscale[l:l + 1, :].partition_broadcast(4)), writes=["psc_s"])
            for ct in range(8):
                g = ct // 2
                P.op("pe", lambda e, ct=ct, g=g: e.matmul(pb[0][:, ct * 4:(ct + 1) * 4], lhsT=state_bf[:, ct * 128:(ct + 1) * 128], rhs=sbst[:, g, :],
                                                         start=True, stop=False), reads=["state", "sbst"], writes=[("pb", 0)])
                P.op("pe", lambda e, ct=ct, g=g: e.matmul(pb[0][:, ct * 4:(ct + 1) * 4], lhsT=ups[:, ct * 128:(ct + 1) * 128], rhs=sbnew[:, g, :],
                                                         start=False, stop=True), reads=["ups", "sbnew"], writes=[("pb", 0)])
            P.op("act", lambda e: e.copy(dTs[:], pb[0][:, 0:32].rearrange("p (c n) -> p c n", c=8)), reads=[("pb", 0)], writes=["dTs"])
            for g in range(4):
                bk = 1 + g // 2
                for c in range(2):
                    P.op("pe", lambda e, g=g, c=c, bk=bk: e.matmul(pb[bk][0:4, (g % 2) * 256:(g % 2 + 1) * 256], lhsT=dTs[:, g * 2 + c, :], rhs=wmix_s[:, g, c, :],
                                                                start=(c == 0), stop=(c == 1)), reads=["dTs", "wmix_s"], writes=[("pb", bk)])
            for hf in range(2):
                P.op("dve", lambda e, hf=hf: e.tensor_tensor(pgs_f[:, hf * 512:(hf + 1) * 512], pb[1 + hf][0:4, :], psc_s[:, hf * 512:(hf + 1) * 512], ALU.mult),
                     reads=[("pb", 1 + hf), "psc_s"], writes=["pgs_f"])
            P.op("dve", lambda e: e.tensor_tensor(iqs, pgs_f, zps, ALU.mult), reads=["pgs_f", "zps", "iqs"], writes=["iqs"])
            tr_tok(iqs, "iqs", PgsT[:], "PgsT", 8)
            bar()
            P.op("dve", lambda e: e.tensor_scalar(rowbase[:], ptf[:], 128.0, float(l * NPOOL * 128), ALU.mult, ALU.add), reads=["ptf"], writes=["rowbase"])
            P.op("dve", lambda e: e.tensor_scalar(idx_all[:], iota_f[:], rowbase[:], None, ALU.add), reads=["iota_f", "rowbase"], writes=["idx_all"])
            P.op("dve", lambda e: e.tensor_scalar(pgidx[:], ptf[:], float(l * NPOOL), None, ALU.add), reads=["ptf"], writes=["pgidx"])
            P.dma("pool", lambda e: e.indirect_dma_start(out=ik_raw, out_offset=None, in_=cik, in_offset=bass.IndirectOffsetOnAxis(ap=pgidx[:, 0:1], axis=0)),
                  reads=["pgidx"], writes=["ik_raw"])
            P.op("dve", lambda e: e.tensor_copy(ik_bf[:, 0:4096], ik_raw[:, 0:4096]), reads=["ik_raw"], writes=["ik_bf0"])
            P.op("pool", lambda e: e.tensor_copy(ik_bf[:, 4096:8192], ik_raw[:, 4096:8192]), reads=["ik_raw"], writes=["ik_bf1"])
            for s0 in range(0, 128, 8):
                bk = (s0 // 8) % 2
                for j in range(8):
                    s = s0 + j
                    P.op("pe", lambda e, s=s, j=j, bk=bk: e.transpose(pbf(bk)[0:64, j * 128:(j + 1) * 128], ik_bf[:, s * 64:(s + 1) * 64], ident[:]),
                         reads=["ik_bf0" if s < 64 else "ik_bf1", "ident"], writes=[("pb", bk)])
                if bk == 0:
                    P.op("act", lambda e, s0=s0, bk=bk: e.copy(ikT_s[:, s0 * 128:(s0 + 8) * 128], pbf(bk)[0:64, :]), reads=[("pb", bk)], writes=["ikT_s"])
                else:
                    P.op("dve", lambda e, s0=s0, bk=bk: e.tensor_copy(ikT_s[:, s0 * 128:(s0 + 8) * 128], pbf(bk)[0:64, :]), reads=[("pb", bk)], writes=["ikT_s"])
            P.op("act", lambda e: e.copy(ikT_s[:, 16384:16388], iksT[:]), reads=["iksT"], writes=["ikT_s"])
            P.dma("pool", lambda e: e.dma_start(out=Wsc, in_=esel_d), writes=["Wsc"])
            P.op("dve", lambda e: e.tensor_scalar(Wsc, Wsc, iwcol[:], None, ALU.mult), reads=["Wsc", "iwcol"], writes=["Wsc"])
            iq_flat = iqsT[:].rearrange("p q h -> p (q h)")
            for c in range(33):
                kn = 512 if c < 32 else 4
                bk = c % 2
                rs_ = Rs[c % 2]
                rkey = ("Rs", c % 2)
                P.op("pe", lambda e, c=c, kn=kn, bk=bk: e.matmul(pb[bk][0:64, 0:kn], lhsT=iq_flat, rhs=ikT_s[:, c * 512:c * 512 + kn], start=True, stop=True),
                     reads=["iqsT", "ikT_s"], writes=[("pb", bk)])
                if c % 2 == 0:
                    P.op("act", lambda e, rs_=rs_, bk=bk, kn=kn: e.activation(rs_[:, 0:kn], pb[bk][0:64, 0:kn], AF.Relu), reads=[("pb", bk)], writes=[rkey])
                else:
                    P.op("dve", lambda e, rs_=rs_, bk=bk, kn=kn: e.tensor_scalar(rs_[:, 0:kn], pb[bk][0:64, 0:kn], 0.0, None, ALU.max), reads=[("pb", bk)], writes=[rkey])
                if c < 32:
                    P.op("pe", lambda e, c=c, rs_=rs_: e.matmul(pb[2][:, 0:512], lhsT=Wsc[:, c * 128:(c + 1) * 128], rhs=rs_[:, 0:512], start=(c == 0), stop=(c == 31)),
                         reads=["Wsc", rkey], writes=[("pb", 2)])
                else:
                    P.op("pe", lambda e, rs_=rs_: e.matmul(pb[3][:, 0:4], lhsT=Wsc[:, 0:128], rhs=rs_[:, 0:4], start=True, stop=True),
                         reads=["Wsc", rkey], writes=[("pb", 3)])
            P.op("act", lambda e: e.copy(sc_s[:, 0:512], pb[2][:, :]), reads=[("pb", 2)], writes=["sc_s"])
            P.op("dve", lambda e: e.tensor_reduce(s_am2, pb[3][:, 0:4], AX.X, ALU.max, apply_absolute_value=True), reads=[("pb", 3)], writes=["s_am2"])
            P.op("dve", lambda e: e.tensor_tensor(sc_s[:, 512:516], pb[3][:, 0:4], negm_s[:], ALU.add), reads=[("pb", 3), "negm"], writes=["sc_s"])
            P.op("dve", lambda e: e.tensor_reduce(s_am, sc_s[:, 0:512], AX.X, ALU.max, apply_absolute_value=True), reads=["sc_s"], writes=["s_am"])
            P.op("dve", lambda e: e.tensor_tensor(s_am, s_am, s_am2, ALU.max), reads=["s_am", "s_am2"], writes=["s_am"])
            P.op("pe", lambda e: e.matmul(pb[5][:, 0:1], lhsT=bones[:], rhs=s_am, start=True, stop=True), reads=["bones", "s_am"], writes=[("pb", 5)])
            P.op("dve", lambda e: e.tensor_scalar(s_B, pb[5][:, 0:1], 1.01, 1.0, ALU.mult, ALU.add), reads=[("pb", 5)], writes=["s_B"])
            P.op("dve", lambda e: e.tensor_scalar(s_lo, s_B, -1.0, None, ALU.mult), reads=["s_B"], writes=["s_lo"])
            P.op("dve", lambda e: e.tensor_scalar(s_w0, s_B, 2.0, None, ALU.mult), reads=["s_B"], writes=["s_w0"])
            P.op("dve", lambda e: e.tensor_scalar(s_wtab[:, 0:32], pow2[:, 0:32], s_w0, None, ALU.mult), reads=["s_w0", "pow2"], writes=["s_wtab"])
            for n in range(32):
                P.op("dve", lambda e, n=n: e.tensor_tensor(s_mid, s_lo, s_wtab[:, n:n + 1], ALU.add), reads=["s_lo", "s_wtab"], writes=["s_mid"])
                P.op("dve", lambda e: e.tensor_scalar(junk_s, sc_s, s_mid, None, ALU.is_ge, ALU.add, accum_out=s_cnt), reads=["sc_s", "s_mid"], writes=["junk_s", "s_cnt"])
                P.op("pe", lambda e: e.matmul(pb[5][:, 0:1], lhsT=bones[:], rhs=s_cnt, start=True, stop=True), reads=["bones", "s_cnt"], writes=[("pb", 5)])
                P.op("dve", lambda e: e.tensor_scalar(s_gei[:], pb[5][:, 0:1], 255.5, None, ALU.is_ge), reads=[("pb", 5)], writes=["s_gei"])
                P.op("dve", lambda e: e.copy_predicated(s_lo, s_gei[:], s_mid), reads=["s_gei", "s_mid"], writes=["s_lo"])
            P.op("dve", lambda e: e.tensor_scalar(mask_s, sc_s, s_lo, None, ALU.is_ge), reads=["sc_s", "s_lo"], writes=["mask_s"])
            for g in range(4):
                P.op("pe", lambda e, g=g: e.transpose(pbf(3)[:, g * 128:(g + 1) * 128], mask_s[:, g * 128:(g + 1) * 128], ident[:]), reads=["mask_s", "ident"], writes=[("pb", 3)])
            P.op("act", lambda e: e.copy(Tm, pbf(3)[:, 0:512].rearrange("p (g n) -> p g n", g=4)), reads=[("pb", 3)], writes=["Tm"])
            P.op("pe", lambda e: e.transpose(pbf(3)[0:4, 0:128], mask_s[:, 512:516], ident[:]), reads=["mask_s", "ident"], writes=[("pb", 3)])
            P.op("act", lambda e: e.copy(Tn, pbf(3)[0:4, 0:128]), reads=[("pb", 3)], writes=["Tn"])
            for s in range(129):
                b2 = s % 2
                if s < 128:
                    c, g = s // 4, s % 4
                    P.dma("pool", lambda e, s=s, b2=b2: e.indirect_dma_start(out=Kc[b2], out_offset=None, in_=ck, in_offset=bass.IndirectOffsetOnAxis(ap=idx_all[:, s:s + 1], axis=0)),
                          reads=["idx_all"], writes=[("Kc", b2)])
                    P.dma("pool", lambda e, s=s, b2=b2: e.indirect_dma_start(out=Vc[b2], out_offset=None, in_=cv, in_offset=bass.IndirectOffsetOnAxis(ap=idx_all[:, s:s + 1], axis=0)),
                          reads=["idx_all"], writes=[("Vc", b2)])
                    P.op("dve", lambda e, b2=b2: e.tensor_copy(Kbf[b2], Kc[b2]), reads=[("Kc", b2)], writes=[("Kbf", b2)])
                    P.op("act", lambda e, b2=b2: e.copy(Vbf[b2], Vc[b2]), reads=[("Vc", b2)], writes=[("Vbf", b2)])
                    for hd in range(8):
                        P.op("pe", lambda e, hd=hd, b2=b2: e.transpose(pbf(b2)[:, hd * 128:(hd + 1) * 128], Kbf[b2][:, hd * 128:(hd + 1) * 128], ident[:]),
                             reads=[("Kbf", b2), "ident"], writes=[("pb", b2)])
                    P.op("act", lambda e, b2=b2: e.copy(KTs[b2], pbf(b2).rearrange("p (h n) -> p h n", h=8)), reads=[("pb", b2)], writes=[("KTs", b2)])
                    for hd in range(8):
                        P.op("pe", lambda e, hd=hd, b2=b2: e.matmul(pb[4][:, hd * 4:(hd + 1) * 4], lhsT=KTs[b2][:, hd, :], rhs=qsT[:, hd, :], start=True, stop=True),
                             reads=[("KTs", b2), "qsT"], writes=[("pb", 4)])
                    npp = 128
                    mk = Tm[:, g, c:c + 97:32]
                    vsrc = Vbf[b2]
                    vkey = ("Vbf", b2)
                else:
                    for hd in range(8):
                        P.op("pe", lambda e, hd=hd: e.matmul(pb[4][0:4, hd * 4:(hd + 1) * 4], lhsT=ksT[:, hd, :], rhs=qsT[:, hd, :], start=True, stop=True),
                             reads=["ksT", "qsT"], writes=[("pb", 4)])
                    npp = 4
                    mk = Tn[:, 0:97:32]
                    vsrc = vsb
                    vkey = "vsb"
                P.op("act", lambda e, npp=npp: e.activation(Es[0:npp, :], pb[4][0:npp, 0:32], AF.Exp, scale=SM_SCALE), reads=[("pb", 4)], writes=["Es"])
                pts = PTs[b2]
                P.op("dve", lambda e, npp=npp, mk=mk, pts=pts: e.tensor_tensor(
                    pts[0:npp, :].rearrange("p (h q) -> p h q", h=8), Es[0:npp, :].rearrange("p (h q) -> p h q", h=8),
                    mk.rearrange("p (o q) -> p o q", o=1).to_broadcast([npp, 8, 4]), ALU.mult),
                    reads=["Es", "Tm", "Tn"], writes=[("PTs", b2)])
                for hf in range(2):
                    P.op("pe", lambda e, hf=hf, npp=npp, pts=pts, vsrc=vsrc, s=s: e.matmul(
                        pb[6 + hf][0:32, :], lhsT=pts[0:npp, :], rhs=vsrc[0:npp, hf * 512:(hf + 1) * 512], start=(s == 0), stop=(s == 128), skip_group_check=True),
                        reads=[("PTs", b2), vkey], writes=[("pb", 6 + hf)])
                P.op("pe", lambda e, npp=npp, pts=pts, s=s: e.matmul(pb[5][0:32, 8:9], lhsT=pts[0:npp, :], rhs=ones_bf[0:npp, 0:1], start=(s == 0), stop=(s == 128), skip_group_check=True),
                     reads=[("PTs", b2), "ones"], writes=[("pb", 5)])
            P.op("dve", lambda e: e.reciprocal(s_rden, pb[5][0:32, 8:9]), reads=[("pb", 5)], writes=["s_rden"])
            for hf in range(2):
                P.op("dve", lambda e, hf=hf: e.tensor_tensor(
                    od_tmp[:, hf * 512:(hf + 1) * 512].rearrange("p (h d) -> p h d", h=4), pb[6 + hf][0:32, :].rearrange("p (h d) -> p h d", h=4),
                    dmask[:, hf * 4:(hf + 1) * 4].rearrange("p (h o) -> p h o", o=1).to_broadcast([32, 4, 128]), ALU.mult),
                    reads=[("pb", 6 + hf), "dmask", "ikT_s"], writes=["od_tmp", "ikT_s"])
            P.op("dve", lambda e: e.tensor_reduce(od, od_tmp.rearrange("p (h d) -> p d h", h=8), AX.X, ALU.add), reads=["od_tmp"], writes=["od"])
            P.op("dve", lambda e: e.tensor_scalar(odn, od, s_rden, None, ALU.mult), reads=["od", "s_rden"], writes=["odn"])
            P.op("pe", lambda e: e.transpose(pbf(3)[:, 0:32], odn, ident[0:32, 0:32]), reads=["odn", "ident"], writes=[("pb", 3)])
            P.op("dve", lambda e: e.tensor_tensor(AgsT[:], pbf(3)[:, 0:32].rearrange("p (h q) -> p h q", h=8), zasT[:], ALU.mult), reads=[("pb", 3), "zasT"], writes=["AgsT"])
            bar()
            for jb in range(4):
                j0 = jb * 512
                i1 = rr["w"] % 2; rr["w"] += 1
                bA = wbuf[i1]; kA = ("wbuf", i1)
                P.dma("pool", lambda e, bA=bA, j0=j0: e.dma_start(out=bA[:, 0:8, :], in_=w_po[l, :, j0:j0 + 512].rearrange("(k p) n -> p k n", p=128)), writes=[kA])
                P.dma("pool", lambda e, bA=bA, j0=j0: e.dma_start(out=bA[:, 8:16, :], in_=w_ao[l, :, j0:j0 + 512].rearrange("(k p) n -> p k n", p=128)), writes=[kA])
                for kt in range(8):
                    P.op("pe", lambda e, bA=bA, kt=kt: e.matmul(pb[0][0:4, :], lhsT=PgsT[:, kt, :], rhs=bA[:, kt, :], start=(kt == 0), stop=(kt == 7)), reads=["PgsT", kA], writes=[("pb", 0)])
                for kt in range(8):
                    P.op("pe", lambda e, bA=bA, kt=kt: e.matmul(pb[1][0:4, :], lhsT=AgsT[:, kt, :], rhs=bA[:, 8 + kt, :], start=(kt == 0), stop=(kt == 7)), reads=["AgsT", kA], writes=[("pb", 1)])
                P.op("dve", lambda e, j0=j0: e.tensor_tensor(ropeA[0:4, :], pb[0][0:4, :], gps[:, j0:j0 + 512], ALU.mult), reads=[("pb", 0), "gps"], writes=["ropeA"])
                P.op("dve", lambda e, j0=j0: e.tensor_tensor(ropeB[0:4, :], pb[1][0:4, :], gas[:, j0:j0 + 512], ALU.mult), reads=[("pb", 1), "gas"], writes=["ropeB"])
                P.op("dve", lambda e, j0=j0: e.tensor_tensor(ms[:, j0:j0 + 512], ropeA[0:4, :], ropeB[0:4, :], ALU.add), reads=["ropeA", "ropeB"], writes=["ms"])
            tr_tok(ms, "ms", msT[:], "msT", 16)
            P.dma("sp", lambda e: e.dma_start(out=xst, in_=xs_scr[par]), reads=["xs_scr"], writes=["xst"])
            P.dma("sp", lambda e: e.dma_start(out=gbs[:, 0, :], in_=lng[l:l + 1, :].partition_broadcast(4)), writes=["gbs"])
            P.dma("sp", lambda e: e.dma_start(out=gbs[:, 1, :], in_=lnb[l:l + 1, :].partition_broadcast(4)), writes=["gbs"])
            for jb in range(4):
                j0 = jb * 512
                wo, kwo = load_w2(w_o[l, :, j0:j0 + 512], D, 512)
                bk = jb % 2
                for kt in range(16):
                    P.op("pe", lambda e, wo=wo, kt=kt, bk=bk: e.matmul(pb[bk][0:4, :], lhsT=msT[:, kt, :], rhs=wo[:, kt, :], start=(kt == 0), stop=(kt == 15)), reads=["msT", kwo], writes=[("pb", bk)])
                P.op("dve", lambda e, bk=bk, j0=j0: e.scalar_tensor_tensor(xst[:, j0:j0 + 512], xst[:, j0:j0 + 512], ALPHA, pb[bk][0:4, :], ALU.mult, ALU.add),
                     reads=[("pb", bk), "xst"], writes=["xst"])
            ln4(xst, "xst")
            if last:
                P.dma("sp", lambda e: e.dma_start(out=y_s, in_=xst), reads=["xst"])
            else:
                P.dma("sp", lambda e: e.dma_start(out=xs_scr[1 - par], in_=xst), reads=["xst"], writes=["xs_scr"])
                xs_to_T()
            bar()

        if stop is None and with_sample:
            sample_init()
        for l_ in range(DEPTH if stop is None else 1):
            for h_ in range(2 if stop is None else 1):
                pass_body(l_, h_)
            if stop is None and with_sample:
                sample_pass(l_)
        P.emit()
    return nc


_NC_CACHE = {}


def _get_nc(npool):
    if npool not in _NC_CACHE:
        _NC_CACHE[npool] = build(npool)
    return _NC_CACHE[npool]


def make_in_maps(inputs, n_cores=8):
    f = lambda a: np.ascontiguousarray(np.asarray(a))
    x_prompt = f(inputs["x_prompt"]); x_sample = f(inputs["x_sample"])
    cache_k = f(inputs["cache_k"]); cache_v = f(inputs["cache_v"]); cache_ik = f(inputs["cache_idx_k"])
    npool = cache_k.shape[1]
    ck = cache_k.reshape(DEPTH * npool * 128, 1024)
    cv = cache_v.reshape(DEPTH * npool * 128, 1024)
    cik = cache_ik.reshape(DEPTH * npool, 8192)
    state_pool = f(inputs["state_pool"]); page_table = f(inputs["page_table"]).astype(np.int32)
    cst = _consts()
    shared = {
        "ck": ck, "cv": cv, "cik": cik,
        "lneg": f(inputs["ln_emb_g"]).reshape(1, D), "lneb": f(inputs["ln_emb_b"]).reshape(1, D),
        "w_in": f(inputs["w_in"]), "w_mix": f(inputs["w_pool_mix"]), "pscale": f(inputs["pool_scale"]),
        "w_po": f(inputs["w_pool_out"]), "w_ao": f(inputs["w_attn_out"]), "w_o": f(inputs["w_o"]),
        "lng": f(inputs["ln_g"]), "lnb": f(inputs["ln_b"]),
    }
    shared.update(cst)
    maps = []
    for c in range(n_cores):
        m = dict(shared)
        m["xp"] = x_prompt[c % 4]
        m["xs"] = x_sample[c]
        m["stp"] = np.ascontiguousarray(state_pool[:, c])
        m["pt"] = np.ascontiguousarray(page_table[c].reshape(128, 1))
        maps.append(m)
    return maps, npool


def kernel(**inputs):
    maps, npool = make_in_maps(inputs)
    nc = _get_nc(npool)
    res = run_bass_kernel_spmd(nc, maps, core_ids=list(range(8)))
    r = res.results
    y_prompt = np.stack([r[b]["y_p"] for b in range(4)]).reshape(4, SEQ, D)
    y_sample = np.stack([r[c]["y_s"] for c in range(8)]).reshape(8, 4, D)
    k_prompt = np.stack([r[b]["k_p"] for b in range(4)], axis=1).reshape(DEPTH, 4, SEQ, 8, 128)
    v_prompt = np.stack([r[b]["v_p"] for b in range(4)], axis=1).reshape(DEPTH, 4, SEQ, 8, 128)
    ik_prompt = np.stack([r[b]["ik_p"] for b in range(4)], axis=1).reshape(DEPTH, 4, SEQ, 64)
    pl_prompt = np.stack([r[b]["pl_p"] for b in range(4)], axis=1).reshape(DEPTH, 4, 15, PW)
    k_sample = np.stack([r[c]["k_s"] for c in range(8)], axis=1).reshape(DEPTH, 8, 4, 8, 128)
    v_sample = np.stack([r[c]["v_s"] for c in range(8)], axis=1).reshape(DEPTH, 8, 4, 8, 128)
    ik_sample = np.stack([r[c]["ik_s"] for c in range(8)], axis=1).reshape(DEPTH, 8, 4, 64)
    pl_sample = np.stack([r[c]["pl_s"] for c in range(8)], axis=1).reshape(DEPTH, 8, 15, PW)
    return (y_prompt, y_sample, k_prompt, v_prompt, ik_prompt, pl_prompt, k_sample, v_sample, ik_sample, pl_sample)
```

```python
import os as _os
import itertools
import numpy as np
from contextlib import ExitStack
import concourse.bass as bass
import concourse.mybir as mybir
from concourse.bass_utils import run_bass_kernel_spmd

F32 = mybir.dt.float32
BF16 = mybir.dt.bfloat16
I32 = mybir.dt.int32
U32 = mybir.dt.uint32
ALU = mybir.AluOpType
AF = mybir.ActivationFunctionType
AX = mybir.AxisListType

ENGS = ("pe", "act", "dve", "pool", "sp")

D = 2048
SEQ = 2048
NT = 16
HT = 8
DEPTH = 2
PW = 1024
INW = 11344
C_UP, C_ZP, C_Q, C_K, C_V, C_ZA, C_IQ, C_IK, C_IW, C_GP, C_GA = (
    0, 1024, 2048, 3072, 4096, 5120, 6144, 7168, 7232, 7248, 9296)
IDX_SCALE = float((16 * 64) ** -0.5)
ALPHA = float((2 * DEPTH) ** 0.25)
LN_EPS = 1e-5
NEG = -1.0e30
NBIS = 30
SM_SCALE = float(128 ** -0.5)
PAST = 16384
NPG = 128


class Op:
    __slots__ = ("eng", "fn", "deps", "is_dma", "sem_idx", "count", "signal", "idx")

    def __init__(self, eng, fn, is_dma):
        self.eng = eng
        self.fn = fn
        self.deps = set()
        self.is_dma = is_dma
        self.sem_idx = None
        self.count = None
        self.signal = False
        self.idx = None


class Prog:
    N_DMA_SEMS = 14

    def __init__(self, nc):
        self.nc = nc
        self.ops = []
        self.last_writer = {}
        self.readers = {}
        self.dma_slot_last = {}
        self.dma_rr = {e: 0 for e in ENGS}
        self.final_ops = []
        self.after = None
        self.last_on = {}

    def barrier(self, fn):
        deps = set(self.last_on.values()) | set(self.dma_slot_last.values())
        o = Op("pool", fn, False)
        o.idx = len(self.ops)
        self.ops.append(o)
        o.deps = set(d for d in deps if d is not None)
        if self.after is not None:
            o.deps.add(self.after)
        self.after = o
        self.last_on["pool"] = o
        return o

    def _track(self, op, reads, writes):
        pr = [r for r in reads if isinstance(r, tuple) and r[0] == "pb"]
        if pr:
            reads = [r for r in reads if not (isinstance(r, tuple) and r[0] == "pb")]
            writes = list(writes) + pr
        for r in reads:
            w = self.last_writer.get(r)
            if w is not None:
                op.deps.add(w)
        for w_ in writes:
            w = self.last_writer.get(w_)
            if w is not None:
                op.deps.add(w)
            for rd in self.readers.get(w_, ()):
                op.deps.add(rd)
        for r in reads:
            self.readers.setdefault(r, []).append(op)
        for w_ in writes:
            self.last_writer[w_] = op
            self.readers[w_] = []
        op.deps.discard(op)

    def op(self, eng, fn, reads=(), writes=()):
        o = Op(eng, fn, False)
        o.idx = len(self.ops)
        self.ops.append(o)
        self._track(o, reads, writes)
        if self.after is not None:
            o.deps.add(self.after)
        self.last_on[eng] = o
        return o

    def dma(self, eng, fn, reads=(), writes=(), final=False):
        o = Op(eng, fn, True)
        o.idx = len(self.ops)
        self.ops.append(o)
        slot = self.dma_rr[eng] % self.N_DMA_SEMS
        self.dma_rr[eng] += 1
        o.sem_idx = slot
        prev = self.dma_slot_last.get((eng, slot))
        if prev is not None:
            o.deps.add(prev)
        self.dma_slot_last[(eng, slot)] = o
        self._track(o, reads, writes)
        if self.after is not None:
            o.deps.add(self.after)
        self.last_on[eng] = o
        if final:
            self.final_ops.append(o)
        return o

    def emit(self):
        nc = self.nc
        for o in self.ops:
            for d in o.deps:
                if d.is_dma or not (d.eng == "pe" and o.eng == "pe"):
                    d.signal = True
            if o.is_dma:
                o.signal = True
        per = {e: [] for e in ENGS}
        for o in self.ops:
            per[o.eng].append(o)
        for e in ENGS:
            c = 0
            for o in per[e]:
                if (not o.is_dma) and o.signal:
                    c += 1
                    o.count = c
        dma_cnt = {}
        for o in self.ops:
            if o.is_dma:
                k = (o.eng, o.sem_idx)
                dma_cnt[k] = dma_cnt.get(k, 0) + 16
                o.count = dma_cnt[k]
        with ExitStack() as es:
            esem = {e: es.enter_context(nc.semaphore("s_" + e)) for e in ENGS}
            dsem = {}
            for e in ENGS:
                for s in range(min(self.N_DMA_SEMS, self.dma_rr[e])):
                    dsem[(e, s)] = es.enter_context(nc.semaphore("d_%s_%d" % (e, s)))
            block = es.enter_context(nc.Block())

            def semof(d):
                return dsem[(d.eng, d.sem_idx)] if d.is_dma else esem[d.eng]

            def body(ename):
                def f(eng):
                    waited = {}
                    for o in per[ename]:
                        for d in sorted(o.deps, key=lambda x: x.idx):
                            if (not d.is_dma) and d.eng == "pe" and ename == "pe":
                                continue
                            key = (d.eng, d.sem_idx) if d.is_dma else d.eng
                            if waited.get(key, 0) >= d.count:
                                continue
                            eng.wait_ge(semof(d), d.count)
                            waited[key] = d.count
                        ins = o.fn(eng)
                        if o.signal:
                            ins.then_inc(semof(o), 16 if o.is_dma else 1)
                    if ename == "sp":
                        for k_, v_ in dma_cnt.items():
                            eng.wait_ge(dsem[k_], v_)
                return f

            block.tensor(body("pe"))
            block.scalar(body("act"))
            block.vector(body("dve"))
            block.gpsimd(body("pool"))
            block.sync(body("sp"))


def _rope_tab(pos, half):
    inv = (np.float32(10000.0) ** (-np.arange(half, dtype=np.float32) / np.float32(half))).astype(np.float32)
    ang = pos.astype(np.float32)[:, None] * inv[None, :]
    return np.cos(ang).astype(np.float32), np.sin(ang).astype(np.float32)


def _consts():
    c = {}
    pos_p = np.arange(SEQ)
    pos_s = PAST + np.arange(4)
    c["rc128"], c["rs128"] = _rope_tab(pos_p, 64)
    c["rc64"], c["rs64"] = _rope_tab(pos_p, 32)
    c["src128"], c["srs128"] = _rope_tab(pos_s, 64)
    c["src64"], c["srs64"] = _rope_tab(pos_s, 32)
    bands = np.zeros((12, 128, 128), np.float32)
    sb_st = np.zeros((4, 15, 4), np.float32)
    sb_new = np.zeros((4, 4, 4), np.float32)
    for g, w in enumerate((2, 4, 8, 16)):
        for t in range(128):
            for s in range(max(0, t - w + 1), t + 1):
                bands[g, s, t] += 1.0 / w
                bands[4 + g, s, t] += 1.0 / min(t + 1, w)
            bands[g, t, t] -= 1.0
            bands[4 + g, t, t] -= 1.0
            for j in range(t + 1, w):
                bands[8 + g, 128 + t - j, t] += 1.0 / w
        for t in range(4):
            for j in range(w):
                r = 15 + t - j
                if r >= 15:
                    sb_new[g, r - 15, t] += 1.0 / w
                else:
                    sb_st[g, r, t] += 1.0 / w
            sb_new[g, t, t] -= 1.0
    c["bands"] = bands
    c["sb_st"] = sb_st
    c["sb_new"] = sb_new
    cm = np.zeros((128, 128), np.float32)
    cm[np.triu_indices(128, 1)] = NEG
    c["causal"] = cm
    c["pow2"] = (0.5 ** np.arange(1, 33, dtype=np.float32)).astype(np.float32)[None, :]
    esel = np.zeros((64, 32, 128), np.float32)
    for q in range(4):
        for hh in range(16):
            for ch in range(32):
                esel[q * 16 + hh, ch, q * 32 + ch] = 1.0
    c["esel"] = esel.reshape(64, 4096)
    negm = np.full((128, 4), NEG, np.float32)
    for q in range(4):
        negm[q * 32, 0:q + 1] = 0.0
    c["negm"] = negm
    bo = np.zeros((128, 128), np.float32)
    for q in range(4):
        bo[q * 32:(q + 1) * 32, q * 32:(q + 1) * 32] = 1.0
    c["bones"] = bo
    dm = np.zeros((32, 8), np.float32)
    for hh in range(8):
        dm[hh * 4:(hh + 1) * 4, hh] = 1.0
    c["dmaskc"] = dm
    return c


def build(NPOOL=1280, with_sample=True, stop=None):
    nc = bass.Bass("TRN2", target_bir_lowering=False)
    P = Prog(nc)

    def din(name, shape, dt=F32):
        return nc.dram_tensor(name, list(shape), dt, kind="ExternalInput").ap()

    def dout(name, shape, dt=F32):
        return nc.dram_tensor(name, list(shape), dt, kind="ExternalOutput").ap()

    def dscr(name, shape, dt=F32):
        return nc.dram_tensor(name, list(shape), dt, kind="Internal").ap()

    xp = din("xp", [SEQ, D])
    xs_in = din("xs", [4, D])
    ck = din("ck", [DEPTH * NPOOL * 128, 1024])
    cv = din("cv", [DEPTH * NPOOL * 128, 1024])
    cik = din("cik", [DEPTH * NPOOL, 8192])
    stp = din("stp", [DEPTH, 15, PW])
    pt = din("pt", [128, 1], I32)
    lneg = din("lneg", [1, D])
    lneb = din("lneb", [1, D])
    w_in = din("w_in", [DEPTH, D, INW])
    w_mix = din("w_mix", [DEPTH, 4, 256, 256])
    pscale = din("pscale", [DEPTH, PW])
    w_po = din("w_po", [DEPTH, PW, D])
    w_ao = din("w_ao", [DEPTH, PW, D])
    w_o = din("w_o", [DEPTH, D, D])
    lng = din("lng", [DEPTH, D])
    lnb = din("lnb", [DEPTH, D])
    rc128 = din("rc128", [SEQ, 64]); rs128 = din("rs128", [SEQ, 64])
    rc64 = din("rc64", [SEQ, 32]); rs64 = din("rs64", [SEQ, 32])
    src128 = din("src128", [4, 64]); srs128 = din("srs128", [4, 64])
    src64 = din("src64", [4, 32]); srs64 = din("srs64", [4, 32])
    bands_d = din("bands", [12, 128, 128])
    sb_st_d = din("sb_st", [4, 15, 4])
    sb_new_d = din("sb_new", [4, 4, 4])
    causal_d = din("causal", [128, 128])
    pow2_d = din("pow2", [1, 32])
    esel_d = din("esel", [64, 4096])
    negm_d = din("negm", [128, 4])
    bones_d = din("bones", [128, 128])
    dmask_d = din("dmaskc", [32, 8])

    y_p = dout("y_p", [SEQ, D]); y_s = dout("y_s", [4, D])
    k_p = dout("k_p", [DEPTH, SEQ, 1024]); v_p = dout("v_p", [DEPTH, SEQ, 1024])
    ik_p = dout("ik_p", [DEPTH, SEQ, 64]); pl_p = dout("pl_p", [DEPTH, 15, PW])
    k_s = dout("k_s", [DEPTH, 4, 1024]); v_s = dout("v_s", [DEPTH, 4, 1024])
    ik_s = dout("ik_s", [DEPTH, 4, 64]); pl_s = dout("pl_s", [DEPTH, 15, PW])

    dbg = stop is not None or _os.environ.get("DBG")
    if dbg:
        dbg_ag = dout("dbg_ag", [SEQ, 1024], BF16)
        dbg_pg = dout("dbg_pg", [SEQ, 1024], BF16)
    Xs = dscr("Xs", [2, SEQ, D])
    xTs = dscr("xTs", [2, 128, NT, 16, 128], BF16)
    kTs = dscr("kTs", [128, NT, 8, 128], BF16)
    Vs = dscr("Vs", [SEQ, 1024], BF16)
    ikTs = dscr("ikTs", [128, SEQ], BF16)
    sproj = dscr("sproj", [4, INW])

    es = ExitStack()
    with es:
        def sb(name, shape, dt):
            return es.enter_context(nc.sbuf_tensor(name, list(shape), dt))

        def ps(name, shape, dt):
            return es.enter_context(nc.psum_tensor(name, list(shape), dt))

        R0 = sb("R0", [128, 16384], BF16)
        R1 = sb("R1", [128, 16384], BF16)
        R2 = sb("R2", [128, 24576], BF16)
        RW = sb("RW", [128, 24576], BF16)
        xT = R0[:, :].rearrange("p (t k n) -> p t k n", t=HT, k=16)
        mT = xT
        up_sb = R1[:, 0:8192].rearrange("p (t n) -> p t n", t=HT)
        zp_sb = R1[:, 8192:16384].rearrange("p (t n) -> p t n", t=HT)
        m_sb = R1[:, :].rearrange("p (t n) -> p t n", t=HT)
        qT = R2[:, 0:8192].rearrange("p (t h n) -> p t h n", t=HT, h=8)
        AgT = qT
        iqT = R2[:, 8192:16384].rearrange("p (t h n) -> p t h n", t=HT, h=8)
        PgT = iqT
        za_sb = R2[:, 16384:24576].rearrange("p (t n) -> p t n", t=HT)

        ikT2 = sb("ikT2", [128, SEQ], BF16)
        ident = sb("ident", [128, 128], BF16)
        identf = sb("identf", [128, 128], F32)
        bands = sb("bands_sb", [128, 12, 128], BF16)
        causal = sb("causal_sb", [128, 128], F32)
        pow2 = sb("pow2_sb", [128, 32], F32)
        cos128 = sb("cos128", [128, HT, 64], F32); sin128 = sb("sin128", [128, HT, 64], F32)
        cos64 = sb("cos64", [128, HT, 32], F32); sin64 = sb("sin64", [128, HT, 32], F32)
        iw_sb = sb("iw_sb", [128, HT, 16], F32)
        up_last = sb("up_last", [128, 1024], BF16)
        ones_bf = sb("ones_bf", [128, 1], BF16)
        stat = sb("stat", [128, 4, 6], F32)
        mv = sb("mv", [128, 2], F32)
        rstd = sb("rstd", [128, 1], F32)
        eps_t = sb("eps_t", [128, 1], F32)
        xt_st = [RW[:, i * 4096:(i + 1) * 4096].bitcast(F32) for i in range(2)]
        xbf_st = sb("xbf_st", [128, D], BF16)
        ropeA = sb("ropeA", [128, 512], F32)
        ropeB = sb("ropeB", [128, 512], F32)
        st_f32 = [sb("st_f32_%d" % i, [128, 512], F32) for i in range(2)]
        st_bf = [sb("st_bf%d" % i, [128, 1024], BF16) for i in range(2)]
        ik_st = sb("ik_st", [128, 64], F32)
        ik2_bf = sb("ik2_bf", [128, 128], BF16)
        tr_st = [sb("tr_st%d" % i, [128, 8, 128], BF16) for i in range(2)]
        gb_bc = R1[:, 8192:16384].bitcast(F32).rearrange("p (a n) -> p a n", a=2)
        GBKEYS = ["gb"] + [("zp", t_) for t_ in range(HT)] + [("m", t_) for t_ in range(4, 8)]

        pb = [ps("pb%d" % i, [128, 512], F32) for i in range(8)]

        def pbf(i):
            return pb[i][:, :].bitcast(BF16)

        P.op("pool", lambda e: e.memset(identf[:], 0.0), writes=["identf"])
        P.op("pool", lambda e: e.affine_select(identf[:], identf[:], [[1, 128]], ALU.not_equal, 1.0,
                                               base=0, channel_multiplier=-1), reads=["identf"], writes=["identf"])
        P.op("dve", lambda e: e.tensor_copy(ident[:], identf[:]), reads=["identf"], writes=["ident"])
        P.op("dve", lambda e: e.memset(ones_bf[:], 1.0), writes=["ones"])
        P.op("dve", lambda e: e.memset(eps_t[:], LN_EPS), writes=["eps"])
        P.dma("pool", lambda e: e.dma_start(out=bands[:], in_=bands_d.rearrange("g s t -> s g t")), writes=["bands"])
        P.dma("sp", lambda e: e.dma_start(out=causal[:], in_=causal_d), writes=["causal"])
        P.dma("sp", lambda e: e.dma_start(out=pow2[:], in_=pow2_d.partition_broadcast(128)), writes=["pow2"])

        rr = {"pb": 0, "st": 0, "tr": 0, "xt": 0, "w": 0}
        fdummy = sb("fdummy", [128, 1], F32)
        sp_in = sb("sp_in", [4, 1, 512], F32)
        xsT_ = sb("xsT", [128, 16, 4], BF16)
        sp_out = sb("sp_out", [4, 1, 512], F32)
        print("SBUF remaining", nc.sbuf_bytes_remaining)
        bis_sb = sb("bis_sb", [128, 4, 40], F32)
        gei_sb = sb("gei_sb", [128, 4], I32)
        P2KEYS = ["wmix", "psc", ("dT", 0), ("dT", 1), ("pgf", 0), ("pgf", 1)]

        def fence(old, new):
            P.op("pool", lambda e: e.memset(fdummy[:], 0.0), writes=list(old) + list(new))

        def layer_norm_tile(src_key, x_ap, g_key, out_key):
            for j in range(4):
                P.op("dve", lambda e, j=j: e.bn_stats(stat[:, j, :], x_ap[:, j * 512:(j + 1) * 512]),
                     reads=[src_key], writes=[("stat", j)])
            P.op("dve", lambda e: e.bn_aggr(mv[:], stat[:].rearrange("p a b -> p (a b)")),
                 reads=[("stat", j) for j in range(4)], writes=["mv"])
            P.op("act", lambda e: e.activation(rstd[:], mv[:, 1:2], AF.Sqrt, bias=eps_t[:]), reads=["mv", "eps"], writes=["rstd"])
            P.op("dve", lambda e: e.reciprocal(rstd[:], rstd[:]), reads=["rstd"], writes=["rstd"])
            P.op("dve", lambda e: e.tensor_scalar(x_ap, x_ap, mv[:, 0:1], rstd[:], ALU.subtract, ALU.mult),
                 reads=[src_key, "mv", "rstd"], writes=[src_key])
            P.op("pool", lambda e: e.tensor_tensor(x_ap, x_ap, gb_bc[:, 0, :], ALU.mult), reads=[src_key] + GBKEYS, writes=[src_key])
            P.op("dve", lambda e: e.tensor_tensor(x_ap, x_ap, gb_bc[:, 1, :], ALU.add), reads=[src_key] + GBKEYS, writes=[out_key, src_key])

        def to_xT_scratch(x_ap, x_key, par, gt, np_=128):
            P.op("act", lambda e: e.copy(xbf_st[:np_, :], x_ap), reads=[x_key], writes=["xbf"])
            for g in range(2):
                bk = 2 + g
                for j in range(8):
                    kt = g * 8 + j
                    P.op("pe", lambda e, kt=kt, j=j, bk=bk: e.transpose(pbf(bk)[:, j * 128:j * 128 + np_], xbf_st[:np_, kt * 128:(kt + 1) * 128], ident[:np_, :np_]),
                         reads=["xbf", "ident"], writes=[("pb", bk)])
                tr = tr_st[g]
                P.op("act" if g == 0 else "dve",
                     (lambda e, tr=tr, bk=bk: e.copy(tr[:], pbf(bk).rearrange("p (j n) -> p j n", j=8))) if g == 0 else
                     (lambda e, tr=tr, bk=bk: e.tensor_copy(tr[:], pbf(bk).rearrange("p (j n) -> p j n", j=8))),
                     reads=[("pb", bk)], writes=[("tr", g)])
                P.dma("sp", lambda e, tr=tr, g=g: e.dma_start(out=xTs[par, :, gt, g * 8:(g + 1) * 8, :], in_=tr[:]),
                      reads=[("tr", g)], writes=[("xTs", par, gt)])

        P.dma("sp", lambda e: e.dma_start(out=gb_bc[:, 0, :], in_=lneg.partition_broadcast(128)), writes=GBKEYS)
        P.dma("sp", lambda e: e.dma_start(out=gb_bc[:, 1, :], in_=lneb.partition_broadcast(128)), writes=GBKEYS)
        for gt in range(NT):
            xt = xt_st[gt % 2]
            xk = ("xt", gt % 2)
            P.dma("sp", lambda e, xt=xt, gt=gt: e.dma_start(out=xt[:], in_=xp[gt * 128:(gt + 1) * 128, :]), writes=[xk])
            layer_norm_tile(xk, xt[:], "gb", xk)
            P.dma("sp", lambda e, xt=xt, gt=gt: e.dma_start(out=Xs[0, gt * 128:(gt + 1) * 128, :], in_=xt[:]),
                  reads=[xk], writes=[("Xs", 0, gt)])
            to_xT_scratch(xt[:], xk, 0, gt)

        def rope_ps(bk, nh, half, cos_ap, sin_ap, out_ap, np_=128):
            n = nh * 2 * half
            src = pb[bk][:np_, 0:n].rearrange("p (h two f) -> p h two f", h=nh, two=2)
            cb = cos_ap.rearrange("p (a b f) -> p a b f", a=1, b=1).to_broadcast([np_, nh, 2, half])
            sbb = sin_ap.rearrange("p (a b f) -> p a b f", a=1, b=1).to_broadcast([np_, nh, 2, half])
            A = ropeA[:np_, 0:n].rearrange("p (h two f) -> p h two f", h=nh, two=2)
            B = ropeB[:np_, 0:n].rearrange("p (h two f) -> p h two f", h=nh, two=2)
            o = out_ap.rearrange("p (h two f) -> p h two f", h=nh, two=2)
            P.op("dve", lambda e: e.tensor_tensor(A, src, cb, ALU.mult), reads=[("pb", bk), "ropetab"], writes=["ropeA"])
            P.op("dve", lambda e: e.tensor_tensor(B, src, sbb, ALU.mult), reads=[("pb", bk), "ropetab"], writes=["ropeB"])
            return A, B, o

        def rope_finish(A, B, o, okey):
            P.op("pool", lambda e: e.tensor_tensor(o[:, :, 0, :], A[:, :, 0, :], B[:, :, 1, :], ALU.subtract),
                 reads=["ropeA", "ropeB"], writes=[okey])
            P.op("pool", lambda e: e.tensor_tensor(o[:, :, 1, :], A[:, :, 1, :], B[:, :, 0, :], ALU.add),
                 reads=["ropeA", "ropeB"], writes=[okey, "ropeA", "ropeB"])

        def transposes8(src_bf, src_key, dst_ap, dst_key, bk, evac_eng="act"):
            for j in range(8):
                P.op("pe", lambda e, j=j: e.transpose(pbf(bk)[:, j * 128:(j + 1) * 128], src_bf[:, j * 128:(j + 1) * 128], ident[:]),
                     reads=[src_key, "ident"], writes=[("pb", bk)])
            if evac_eng == "act":
                P.op("act", lambda e: e.copy(dst_ap, pbf(bk).rearrange("p (j n) -> p j n", j=8)), reads=[("pb", bk)], writes=[dst_key])
            else:
                P.op("dve", lambda e: e.tensor_copy(dst_ap, pbf(bk).rearrange("p (j n) -> p j n", j=8)), reads=[("pb", bk)], writes=[dst_key])

        def transposes4(src_bf, src_key, dst_ap, dst_key, bk, evac_eng="act"):
            for j in range(4):
                P.op("pe", lambda e, j=j: e.transpose(pbf(bk)[:, j * 128:(j + 1) * 128], src_bf[:, j * 128:(j + 1) * 128], ident[:]),
                     reads=[src_key, "ident"], writes=[("pb", bk)])
            if evac_eng == "act":
                P.op("act", lambda e: e.copy(dst_ap, pbf(bk)[:, 0:512].rearrange("p (j n) -> p j n", j=4)), reads=[("pb", bk)], writes=[dst_key])
            else:
                P.op("dve", lambda e: e.tensor_copy(dst_ap, pbf(bk)[:, 0:512].rearrange("p (j n) -> p j n", j=4)), reads=[("pb", bk)], writes=[dst_key])

        P1_BLOCKS = []
        for kind, c0 in (("up", C_UP), ("zp", C_ZP), ("q", C_Q), ("k", C_K), ("v", C_V), ("za", C_ZA), ("iq", C_IQ)):
            P1_BLOCKS.append((kind, c0, 512, 0))
            P1_BLOCKS.append((kind, c0 + 512, 512, 1))
        P1_BLOCKS.append(("ikiw", C_IK, 80, 0))
        if _os.environ.get("P1KINDS"):
            P1_BLOCKS = [b for b in P1_BLOCKS if b[0] in _os.environ["P1KINDS"].split(",")]

        wbuf = [RW[:, i * 8192:(i + 1) * 8192].rearrange("p (k n) -> p k n", k=16) for i in range(3)]

        def load_w(dram_ap_2d, krows, ncols, key):
            i = rr["w"] % 3
            rr["w"] += 1
            buf = wbuf[i]
            kt = krows // 128
            P.dma("pool", lambda e: e.dma_start(out=buf[:, 0:kt, 0:ncols], in_=dram_ap_2d.rearrange("(k p) n -> p k n", p=128)),
                  writes=[("wbuf", i)])
            return buf, ("wbuf", i)

        fence([("xt", 0), ("xt", 1)], [("wbuf", i_) for i_ in range(3)])
        def pass_body(l, h):
            if stop == "P0":
                return
            par = l % 2
            last = (l == DEPTH - 1)
            if True:
                P.dma("sp", lambda e, h=h, par=par: e.dma_start(out=xT[:], in_=xTs[par, :, h * HT:(h + 1) * HT, :, :]),
                      reads=[("xTs", par, h * HT + t) for t in range(HT)], writes=[("xT", t) for t in range(HT)])
                for nm, tab, src, hf in (("cos128", cos128, rc128, 64), ("sin128", sin128, rs128, 64),
                                         ("cos64", cos64, rc64, 32), ("sin64", sin64, rs64, 32)):
                    P.dma("sp", lambda e, tab=tab, src=src, h=h: e.dma_start(
                        out=tab[:], in_=src[h * 1024:(h + 1) * 1024, :].rearrange("(t p) f -> p t f", p=128)),
                        writes=["ropetab"])
                if h == 1:
                    P.dma("sp", lambda e: e.dma_start(out=ikT2[:, 0:1024], in_=ikTs[:, 0:1024]),
                          reads=[("ikTs", c) for c in range(HT)], writes=[("ikT2", c) for c in range(HT)])

                for (kind, c0, ncols, sub) in P1_BLOCKS:
                    wb, wkey = load_w(w_in[l, :, c0:c0 + ncols], D, ncols, None)
                    deferred = []
                    for t in range(HT):
                        gt = h * HT + t
                        bk = (0, 1, 4, 5, 6, 7)[rr["pb"] % 6]
                        rr["pb"] += 1
                        for kt in range(16):
                            P.op("pe", lambda e, t=t, kt=kt, bk=bk, wb=wb, ncols=ncols: e.matmul(
                                pb[bk][:, 0:ncols], lhsT=xT[:, t, kt, :], rhs=wb[:, kt, 0:ncols], start=(kt == 0), stop=(kt == 15)),
                                reads=[("xT", t), wkey], writes=[("pb", bk)])
                        for f_ in deferred:
                            f_()
                        deferred = []
                        sb2 = t % 2
                        pk = ("pb", bk)
                        cs = slice(sub * 512, sub * 512 + 512)
                        if kind == "up":
                            P.op("act", lambda e, t=t, bk=bk, cs=cs: e.copy(up_sb[:, t, cs], pb[bk][:, :]), reads=[pk], writes=[("up", t)])
                            if h == 1 and t == HT - 1:
                                sf = st_f32[sub]
                                P.op("dve", lambda e, bk=bk, sf=sf: e.tensor_copy(sf[:], pb[bk][:, :]), reads=[pk], writes=[("stf", sub)])
                                P.dma("sp", lambda e, sf=sf, cs=cs: e.dma_start(out=pl_p[l, :, cs], in_=sf[113:128, :]),
                                      reads=[("stf", sub)], final=True)
                        elif kind == "zp":
                            P.op("act", lambda e, t=t, bk=bk, cs=cs: e.activation(zp_sb[:, t, cs], pb[bk][:, :], AF.Silu), reads=[pk], writes=[("zp", t)])
                        elif kind == "za":
                            P.op("act", lambda e, t=t, bk=bk, cs=cs: e.activation(za_sb[:, t, cs], pb[bk][:, :], AF.Silu), reads=[pk], writes=[("za", t)])
                        elif kind == "q":
                            A, B, o = rope_ps(bk, 4, 64, cos128[:, t, :], sin128[:, t, :], st_bf[sb2][:, cs])
                            rope_finish(A, B, o, ("stbf", sb2))
                            deferred.append(lambda t=t, sb2=sb2, cs=cs, sub=sub: transposes4(st_bf[sb2][:, cs], ("stbf", sb2), qT[:, t, sub * 4:(sub + 1) * 4, :], ("qT", t), 2))
                        elif kind == "iq":
                            A, B, o = rope_ps(bk, 8, 32, cos64[:, t, :], sin64[:, t, :], st_bf[sb2][:, cs])
                            rope_finish(A, B, o, ("stbf", sb2))
                            deferred.append(lambda t=t, sb2=sb2, cs=cs, sub=sub: transposes4(st_bf[sb2][:, cs], ("stbf", sb2), iqT[:, t, sub * 4:(sub + 1) * 4, :], ("iqT", t), 3, evac_eng="dve"))
                        elif kind == "k":
                            sf = st_f32[sb2]
                            A, B, o = rope_ps(bk, 4, 64, cos128[:, t, :], sin128[:, t, :], sf[:, :])
                            rope_finish(A, B, o, ("stf", sb2))
                            P.dma("sp", lambda e, sf=sf, gt=gt, cs=cs: e.dma_start(out=k_p[l, gt * 128:(gt + 1) * 128, cs], in_=sf[:]),
                                  reads=[("stf", sb2)], final=True)
                            P.op("act", lambda e, sf=sf, cs=cs, sb2=sb2: e.copy(st_bf[sb2][:, cs], sf[:]), reads=[("stf", sb2)], writes=[("stbf", sb2)])

                            def k_tail(t=t, sb2=sb2, cs=cs, sub=sub, gt=gt):
                                tr = tr_st[sb2]
                                transposes4(st_bf[sb2][:, cs], ("stbf", sb2), tr[:, 0:4, :], ("tr", sb2), 2)
                                P.dma("sp", lambda e: e.dma_start(out=kTs[:, gt, sub * 4:(sub + 1) * 4, :], in_=tr[:, 0:4, :]),
                                      reads=[("tr", sb2)], writes=[("kTs", gt)])
                            deferred.append(k_tail)
                        elif kind == "v":
                            sf = st_f32[sb2]
                            P.op("act", lambda e, bk=bk, sf=sf: e.copy(sf[:], pb[bk][:, :]), reads=[pk], writes=[("stf", sb2)])
                            P.dma("sp", lambda e, sf=sf, gt=gt, cs=cs: e.dma_start(out=v_p[l, gt * 128:(gt + 1) * 128, cs], in_=sf[:]),
                                  reads=[("stf", sb2)], final=True)
                            P.op("dve", lambda e, bk=bk, cs=cs, sb2=sb2: e.tensor_copy(st_bf[sb2][:, cs], pb[bk][:, :]), reads=[pk], writes=[("stbf", sb2)])
                            P.dma("sp", lambda e, gt=gt, cs=cs, sb2=sb2: e.dma_start(out=Vs[gt * 128:(gt + 1) * 128, cs], in_=st_bf[sb2][:, cs]),
                                  reads=[("stbf", sb2)], writes=[("Vs", gt)])
                        elif kind == "ikiw":
                            A, B, o = rope_ps(bk, 1, 32, cos64[:, t, :], sin64[:, t, :], ik_st[:, :])
                            rope_finish(A, B, o, "ikst")
                            P.dma("sp", lambda e, gt=gt: e.dma_start(out=ik_p[l, gt * 128:(gt + 1) * 128, :], in_=ik_st[:]),
                                  reads=["ikst"], final=True)
                            P.op("act", lambda e: e.copy(ik2_bf[:, 0:64], ik_st[:]), reads=["ikst"], writes=["ik2"])
                            P.op("act", lambda e: e.copy(ik2_bf[:, 64:128], ik_st[:]), reads=["ikst"], writes=["ik2"])

                            def ik_tail(gt=gt):
                                P.op("pe", lambda e: e.transpose(pbf(3)[:, 0:128], ik2_bf[:], ident[:]), reads=["ik2", "ident"], writes=[("pb", 3)])
                                P.op("act", lambda e: e.copy(ikT2[:, gt * 128:(gt + 1) * 128], pbf(3)[:, 0:128]),
                                     reads=[("pb", 3)], writes=[("ikT2", gt)])
                                if h == 0:
                                    P.dma("sp", lambda e: e.dma_start(out=ikTs[:, gt * 128:(gt + 1) * 128], in_=ikT2[:, gt * 128:(gt + 1) * 128]),
                                          reads=[("ikT2", gt)], writes=[("ikTs", gt)])
                            deferred.append(ik_tail)
                            P.op("dve", lambda e, t=t, bk=bk: e.tensor_scalar(iw_sb[:, t, :], pb[bk][:, 64:80], IDX_SCALE, None, ALU.mult),
                                 reads=[pk], writes=[("iw", t)])
                    for f_ in deferred:
                        f_()
                    deferred = []
                    if h == 0 and with_sample and stop is None:
                        bk = (0, 1, 4, 5, 6, 7)[rr["pb"] % 6]
                        rr["pb"] += 1
                        so = 0
                        for kt in range(16):
                            P.op("pe", lambda e, kt=kt, bk=bk, wb=wb, ncols=ncols: e.matmul(
                                pb[bk][0:4, 0:ncols], lhsT=xsT_[:, kt, :], rhs=wb[:, kt, 0:ncols], start=(kt == 0), stop=(kt == 15)),
                                reads=["xsT", wkey], writes=[("pb", bk)])
                        P.op("act", lambda e, bk=bk, so=so, ncols=ncols: e.copy(sp_out[:, so, 0:ncols], pb[bk][0:4, 0:ncols]), reads=[("pb", bk)], writes=[("spo", so)])
                        P.dma("sp", lambda e, so=so, c0=c0, ncols=ncols: e.dma_start(out=sproj[:, c0:c0 + ncols], in_=sp_out[:, so, 0:ncols]),
                              reads=[("spo", so)], writes=[("sproj", c0)])

                if stop == "P1":
                    return
                sc = RW[:, 0:4096].bitcast(F32)
                junk = RW[:, 4096:6144]
                maskb = RW[:, 6144:8192]
                maskT = RW[:, 8192:10240].rearrange("p (c n) -> p c n", c=NT)
                Rb = [RW[:, i * 512:(i + 1) * 512] for i in range(8)]
                LBANKS = [0, 1, 4, 5, 6, 7]
                diagW = RW[:, 12288:14336].rearrange("p (h n) -> p h n", h=16)
                Eb = [RW[:, 14336 + i * 1024:14336 + (i + 1) * 1024].rearrange("p (h n) -> p h n", h=8) for i in range(2)]
                PTb = [RW[:, 16384 + i * 1024:16384 + (i + 1) * 1024].rearrange("p (h n) -> p h n", h=8) for i in range(2)]
                kTc = [RW[:, 18432 + i * 1024:18432 + (i + 1) * 1024].rearrange("p (h n) -> p h n", h=8) for i in range(2)]
                Vc = [RW[:, 20480 + i * 1024:20480 + (i + 1) * 1024].rearrange("p (h n) -> p h n", h=8) for i in range(2)]
                smallf = RW[:, 22528:22528 + 256].bitcast(F32)
                lo = smallf[:, 0:1]; mid = smallf[:, 1:2]; cnt = smallf[:, 2:3]; wtab = smallf[:, 8:8 + 32]
                rmax = smallf[:, 3:4]; rden = smallf[:, 48:56]
                gei = RW[:, 22528 + 256:22528 + 258].bitcast(I32)
                On = RW[:, 23040:24064].bitcast(F32)
                ag_bf = st_bf[0]
                wk = ["wk_sc", "wk_junk", "wk_mask", "wk_maskT", "wk_diag", "wk_small"]
                P3KEYS = (wk + ["wk_On"] + [("Rb", i_) for i_ in range(8)] + [("Eb", i_, j_) for i_ in range(2) for j_ in range(2)]
                          + [("PT", i_, j_) for i_ in range(2) for j_ in range(2)] + [("kTc", i_) for i_ in range(2)] + [("Vc", i_) for i_ in range(2)])
                WKEYS = [("wbuf", i_) for i_ in range(3)]
                fence(WKEYS + P2KEYS, P3KEYS)
                rrR_box = [0]
                bis = bis_sb
                scb = [R0[:, ch_ * 4096:(ch_ + 1) * 4096].bitcast(F32) for ch_ in range(4)]
                fence([("xT", t_) for t_ in range(HT)], [("sc", ch_) for ch_ in range(4)])

                def p3_indexer(t):
                    rrR = rrR_box[0]
                    ch = t % 4
                    sc = scb[ch]
                    sckey = ("sc", ch)
                    lo = bis[:, ch, 0:1]; rmax = bis[:, ch, 3:4]; wtab = bis[:, ch, 8:40]
                    bkey = ("bs", ch)
                    i = h * HT + t
                    L = (i + 1) * 128
                    for hh in range(16):
                        P.op("dve", lambda e, hh=hh, t=t: e.tensor_scalar(diagW[:, hh, :], ident[:], iw_sb[:, t, hh:hh + 1], None, ALU.mult),
                             reads=["ident", ("iw", t)], writes=["wk_diag"])
                    nkb = (L + 511) // 512
                    for kb in range(nkb):
                        k0 = kb * 512
                        kn = min(512, L - k0)
                        pend = []

                        def emit_H(hh_, rb_, rkey_, kn_):
                            P.op("pe", lambda e: e.matmul(pb[2][:, 0:kn_], lhsT=diagW[:, hh_, :], rhs=rb_[:, 0:kn_], start=(hh_ == 0), stop=(hh_ == 15)),
                                 reads=["wk_diag", rkey_], writes=[("pb", 2)])

                        for hh in range(16):
                            g, half = hh // 2, hh % 2
                            bk = LBANKS[rr["pb"] % 6]
                            rr["pb"] += 1
                            P.op("pe", lambda e, t=t, g=g, half=half, bk=bk, k0=k0, kn=kn: e.matmul(
                                pb[bk][:, 0:kn], lhsT=iqT[half * 64:(half + 1) * 64, t, g, :], rhs=ikT2[half * 64:(half + 1) * 64, k0:k0 + kn],
                                start=True, stop=True),
                                reads=[("iqT", t)] + [("ikT2", c) for c in range(k0 // 128, (k0 + kn) // 128)], writes=[("pb", bk)])
                            rb = Rb[rrR % 8]
                            rkey = ("Rb", rrR % 8)
                            rrR += 1
                            if hh % 2 == 0:
                                P.op("act", lambda e, rb=rb, bk=bk, kn=kn: e.activation(rb[:, 0:kn], pb[bk][:, 0:kn], AF.Relu),
                                     reads=[("pb", bk)], writes=[rkey])
                            else:
                                P.op("dve", lambda e, rb=rb, bk=bk, kn=kn: e.tensor_scalar(rb[:, 0:kn], pb[bk][:, 0:kn], 0.0, None, ALU.max),
                                     reads=[("pb", bk)], writes=[rkey])
                            pend.append((hh, rb, rkey, kn))
                            if len(pend) > 3:
                                emit_H(*pend.pop(0))
                        for p_ in pend:
                            emit_H(*p_)
                        P.op("act", lambda e, k0=k0, kn=kn: e.copy(sc[:, k0:k0 + kn], pb[2][:, 0:kn]), reads=[("pb", 2)], writes=[sckey])
                    P.op("dve", lambda e, L=L: e.tensor_tensor(sc[:, L - 128:L], sc[:, L - 128:L], causal[:], ALU.add),
                         reads=[sckey, "causal"], writes=[sckey])

                    if i >= 2:
                        Lf = L - 128
                        P.op("dve", lambda e, L=L: e.tensor_reduce(rmax, sc[:, 0:L], AX.X, ALU.max), reads=[sckey], writes=[bkey])
                        P.op("dve", lambda e, Lf=Lf: e.tensor_reduce(lo, sc[:, 0:Lf], AX.X, ALU.min), reads=[sckey], writes=[bkey])
                        P.op("dve", lambda e: e.tensor_tensor(rmax, rmax, lo, ALU.subtract), reads=[bkey], writes=[bkey])
                        P.op("dve", lambda e: e.tensor_scalar(wtab, pow2[:, 0:32], rmax, None, ALU.mult), reads=[bkey, "pow2"], writes=[bkey])
                    rrR_box[0] = rrR

                def p3_bis_iter(t, n):
                    i = h * HT + t
                    if i < 2:
                        return
                    L = (i + 1) * 128
                    ch = t % 4
                    sc = scb[ch]
                    sckey = ("sc", ch); bkey = ("bs", ch)
                    lo = bis[:, ch, 0:1]; mid = bis[:, ch, 1:2]; cnt = bis[:, ch, 2:3]; wtab = bis[:, ch, 8:40]
                    gei = gei_sb[:, ch:ch + 1]
                    P.op("dve", lambda e: e.tensor_tensor(mid, lo, wtab[:, n:n + 1], ALU.add), reads=[bkey], writes=[bkey])
                    P.op("dve", lambda e: e.tensor_scalar(junk[:, 0:L], sc[:, 0:L], mid, None, ALU.is_ge, ALU.add, accum_out=cnt),
                         reads=[sckey, bkey], writes=["wk_junk", bkey])
                    P.op("dve", lambda e: e.tensor_scalar(gei, cnt, 255.5, None, ALU.is_ge), reads=[bkey], writes=[bkey])
                    P.op("dve", lambda e: e.copy_predicated(lo, gei, mid), reads=[bkey], writes=[bkey])

                def p3_finish(t):
                    i = h * HT + t
                    L = (i + 1) * 128
                    ch = t % 4
                    sc = scb[ch]
                    sckey = ("sc", ch); bkey = ("bs", ch)
                    lo = bis[:, ch, 0:1]
                    if i >= 2:
                        P.op("dve", lambda e: e.tensor_scalar(maskb[:, 0:L], sc[:, 0:L], lo, None, ALU.is_ge),
                             reads=[sckey, bkey], writes=["wk_mask"])
                    else:
                        P.op("dve", lambda e: e.tensor_scalar(maskb[:, 0:L], sc[:, 0:L], -1.0e29, None, ALU.is_ge),
                             reads=[sckey], writes=["wk_mask"])
                    for c0 in range(0, i + 1, 8):
                        cn = min(8, i + 1 - c0)
                        for j in range(cn):
                            c = c0 + j
                            P.op("pe", lambda e, c=c, j=j: e.transpose(pbf(3)[:, j * 128:(j + 1) * 128], maskb[:, c * 128:(c + 1) * 128], ident[:]),
                                 reads=["wk_mask", "ident"], writes=[("pb", 3)])
                        P.op("act", lambda e, c0=c0, cn=cn: e.copy(maskT[:, c0:c0 + cn, :], pbf(3)[:, 0:cn * 128].rearrange("p (c n) -> p c n", c=cn)),
                             reads=[("pb", 3)], writes=["wk_maskT"])
                    SBK = [(4, 5), (0, 1)]

                    def ld_k(c):
                        kb_ = kTc[c % 2]
                        P.dma("sp", lambda e: e.dma_start(out=kb_[:], in_=kTs[:, c, :, :]), reads=[("kTs", c)], writes=[("kTc", c % 2)])

                    def ld_v(c):
                        vb_ = Vc[c % 2]
                        P.dma("sp", lambda e: e.dma_start(out=vb_[:], in_=Vs[c * 128:(c + 1) * 128, :].rearrange("p (h n) -> p h n", h=8)),
                              reads=[("Vs", c)], writes=[("Vc", c % 2)])

                    def emit_S(c):
                        kb_ = kTc[c % 2]
                        b0, b1 = SBK[c % 2]
                        for hd in range(8):
                            bk = b0 if hd < 4 else b1
                            P.op("pe", lambda e, hd=hd, bk=bk: e.matmul(pb[bk][:, (hd % 4) * 128:(hd % 4 + 1) * 128],
                                                                       lhsT=kb_[:, hd, :], rhs=qT[:, t, hd, :], start=True, stop=True),
                                 reads=[("kTc", c % 2), ("qT", t)], writes=[("pb", bk)])

                    def emit_E(c):
                        eb = Eb[c % 2]; ptb = PTb[c % 2]
                        b0, b1 = SBK[c % 2]
                        for hf, bk in ((0, b0), (1, b1)):
                            P.op("act", lambda e, hf=hf, bk=bk: e.activation(eb[:, hf * 4:(hf + 1) * 4, :],
                                                                           pb[bk][:, :].rearrange("p (h n) -> p h n", h=4), AF.Exp, scale=SM_SCALE),
                                 reads=[("pb", bk)], writes=[("Eb", c % 2, hf)])
                        P.op("pool", lambda e: e.tensor_tensor(ptb[:, 0:4, :], eb[:, 0:4, :], maskT[:, c:c + 1, :].to_broadcast([128, 4, 128]), ALU.mult),
                             reads=[("Eb", c % 2, 0), "wk_maskT"], writes=[("PT", c % 2, 0)])
                        P.op("dve", lambda e: e.tensor_tensor(ptb[:, 4:8, :], eb[:, 4:8, :], maskT[:, c:c + 1, :].to_broadcast([128, 4, 128]), ALU.mult),
                             reads=[("Eb", c % 2, 1), "wk_maskT"], writes=[("PT", c % 2, 1)])

                    def emit_PV(c):
                        ptb = PTb[c % 2]; vb_ = Vc[c % 2]
                        for hd in range(8):
                            P.op("pe", lambda e, hd=hd: e.matmul(
                                pb[6 + hd // 4][:, (hd % 4) * 128:(hd % 4 + 1) * 128], lhsT=ptb[:, hd, :], rhs=vb_[:, hd, :],
                                start=(c == 0 and hd % 4 == 0), stop=(c == i), skip_group_check=True),
                                reads=[("PT", c % 2, hd // 4), ("Vc", c % 2)], writes=[("pb", 6 + hd // 4)])
                            P.op("pe", lambda e, hd=hd: e.matmul(
                                pb[3][:, 256 + hd:256 + hd + 1], lhsT=ptb[:, hd, :], rhs=ones_bf[:, 0:1],
                                start=(c == 0 and hd == 0), stop=(c == i), skip_group_check=True),
                                reads=[("PT", c % 2, hd // 4), "ones"], writes=[("pb", 3)])

                    for c in range(min(2, i + 1)):
                        ld_k(c); ld_v(c); emit_S(c); emit_E(c)
                    for c in range(i + 1):
                        if c + 2 <= i:
                            ld_k(c + 2); emit_S(c + 2)
                        emit_PV(c)
                        if c + 2 <= i:
                            ld_v(c + 2); emit_E(c + 2)
                    P.op("dve", lambda e: e.reciprocal(rden, pb[3][:, 256:264]), reads=[("pb", 3)], writes=["wk_small"])
                    for hf in range(2):
                        P.op("dve", lambda e, hf=hf: e.tensor_tensor(
                            On.rearrange("p (h n) -> p h n", h=4), pb[6 + hf][:, :].rearrange("p (h n) -> p h n", h=4),
                            rden[:, hf * 4:(hf + 1) * 4].rearrange("p (h o) -> p h o", o=1).to_broadcast([128, 4, 128]), ALU.mult),
                            reads=[("pb", 6 + hf), "wk_small"], writes=["wk_On"])
                        P.op("pool", lambda e, hf=hf, t=t: e.tensor_tensor(ag_bf[:, hf * 512:(hf + 1) * 512], On, za_sb[:, t, hf * 512:(hf + 1) * 512], ALU.mult),
                             reads=["wk_On", ("za", t)], writes=[("stbf", 0), "wk_On"])
                    if dbg and l == 0:
                        P.dma("sp", lambda e, i=i: e.dma_start(out=dbg_ag[i * 128:(i + 1) * 128, :], in_=ag_bf[:]), reads=[("stbf", 0)])
                    transposes8(ag_bf, ("stbf", 0), AgT[:, t, :, :], ("qT", t), 0)


                for g0 in (0, 4):
                    for t in range(g0, g0 + 4):
                        p3_indexer(t)
                    for n in range(NBIS):
                        for t in range(g0, g0 + 4):
                            p3_bis_iter(t, n)
                    for t in range(g0, g0 + 4):
                        p3_finish(t)

                if stop == "P3":
                    return
                wmix_sb = RW[:, 0:2048].rearrange("p (g c e) -> p g c e", g=4, c=2)
                psc_bc = RW[:, 2048:4096].bitcast(F32)
                dT_sb = RW[:, 4096:5120].rearrange("p (c n) -> p c n", c=8)
                pg_f = RW[:, 5120:7168].bitcast(F32)
                pg_bf = st_bf[1]
                guard = []
                fence(P3KEYS + WKEYS, P2KEYS)
                P.dma("pool", lambda e, l=l: e.dma_start(out=wmix_sb, in_=w_mix[l].rearrange("g (c p) e -> p g c e", p=128)),
                      reads=guard, writes=["wmix"] + guard)
                P.dma("sp", lambda e, l=l: e.dma_start(out=psc_bc, in_=pscale[l:l + 1, :].partition_broadcast(128)),
                      reads=guard, writes=["psc"] + guard)
                for t in range(HT):
                    gt = h * HT + t
                    for ct in range(8):
                        g = ct // 2
                        bk = ct // 4
                        first = (gt == 0)
                        bidx = (4 + g) if first else g
                        P.op("pe", lambda e, t=t, ct=ct, bk=bk, bidx=bidx, first=first: e.matmul(
                            pb[bk][:, (ct % 4) * 128:(ct % 4 + 1) * 128], lhsT=up_sb[:, t, ct * 128:(ct + 1) * 128], rhs=bands[:, bidx, :],
                            start=True, stop=first), reads=[("up", t), "bands"], writes=[("pb", bk)])
                        if not first:
                            if t > 0:
                                prev = up_sb[:, t - 1, ct * 128:(ct + 1) * 128]; pkey = ("up", t - 1)
                            else:
                                prev = up_last[:, ct * 128:(ct + 1) * 128]; pkey = "up_last"
                            P.op("pe", lambda e, ct=ct, bk=bk, g=g, prev=prev: e.matmul(
                                pb[bk][:, (ct % 4) * 128:(ct % 4 + 1) * 128], lhsT=prev, rhs=bands[:, 8 + g, :], start=False, stop=True),
                                reads=[pkey, "bands"], writes=[("pb", bk)])
                    for bk in range(2):
                        P.op("act", lambda e, bk=bk: e.copy(dT_sb[:, bk * 4:(bk + 1) * 4, :], pb[bk][:, :].rearrange("p (c n) -> p c n", c=4)),
                             reads=[("pb", bk)] + guard, writes=[("dT", bk)] + guard)
                    for g in range(4):
                        bk = 2 + g // 2
                        for c in range(2):
                            P.op("pe", lambda e, g=g, c=c, bk=bk: e.matmul(pb[bk][:, (g % 2) * 256:(g % 2 + 1) * 256], lhsT=dT_sb[:, g * 2 + c, :],
                                                                        rhs=wmix_sb[:, g, c, :], start=(c == 0), stop=(c == 1)),
                                 reads=[("dT", (g * 2 + c) // 4), "wmix"], writes=[("pb", bk)])
                    for hf in range(2):
                        P.op("dve", lambda e, hf=hf: e.tensor_tensor(pg_f[:, hf * 512:(hf + 1) * 512], pb[2 + hf][:, :], psc_bc[:, hf * 512:(hf + 1) * 512], ALU.mult),
                             reads=[("pb", 2 + hf), "psc"] + guard, writes=[("pgf", hf)] + guard)
                        P.op("pool", lambda e, hf=hf, t=t: e.tensor_tensor(pg_bf[:, hf * 512:(hf + 1) * 512], pg_f[:, hf * 512:(hf + 1) * 512],
                                                                          zp_sb[:, t, hf * 512:(hf + 1) * 512], ALU.mult),
                             reads=[("pgf", hf), ("zp", t)], writes=[("stbf", 1)])
                    if dbg and l == 0:
                        P.dma("sp", lambda e, gt=gt: e.dma_start(out=dbg_pg[gt * 128:(gt + 1) * 128, :], in_=pg_bf[:]), reads=[("stbf", 1)])
                    transposes8(pg_bf, ("stbf", 1), PgT[:, t, :, :], ("iqT", t), 4, evac_eng="dve")
                if h == 0:
                    P.op("pool", lambda e: e.tensor_copy(up_last[:], up_sb[:, HT - 1, :]), reads=[("up", HT - 1)], writes=["up_last"])

                if stop == "P2":
                    return
                fence(P3KEYS + P2KEYS, WKEYS)
                fence([("sc", ch_) for ch_ in range(4)], [("xT", t_) for t_ in range(HT)])
                P.dma("sp", lambda e: e.dma_start(out=xT[:], in_=xTs[par, :, h * HT:(h + 1) * HT, :, :]),
                      reads=[("xTs", par, h * HT + t_) for t_ in range(HT)], writes=[("xT", t_) for t_ in range(HT)])
                sgb = st_f32
                prodb = [ropeA, ropeB]

                def m_keys(t):
                    if t < 4:
                        return [("m", t), ("up", 2 * t), ("up", 2 * t + 1)]
                    return [("m", t), ("zp", 2 * (t - 4)), ("zp", 2 * (t - 4) + 1)]

                P4W = [("p4w", s_, x_) for s_ in range(2) for x_ in "AB"]
                fence(WKEYS, P4W)
                for jb in range(8):
                    j0 = jb * 256
                    s_ = jb % 2
                    bA = RW[:, s_ * 12288:s_ * 12288 + 4096].rearrange("p (k n) -> p k n", k=16)
                    bB = RW[:, s_ * 12288 + 4096:(s_ + 1) * 12288].rearrange("p (k n) -> p k n", k=16)
                    kA = ("p4w", s_, "A"); kB = ("p4w", s_, "B")
                    P.dma("pool", lambda e, bA=bA, j0=j0, l=l: e.dma_start(out=bA[:, 0:8, 0:256], in_=w_po[l, :, j0:j0 + 256].rearrange("(k p) n -> p k n", p=128)), writes=[kA])
                    P.dma("pool", lambda e, bA=bA, j0=j0, l=l: e.dma_start(out=bA[:, 8:16, 0:256], in_=w_ao[l, :, j0:j0 + 256].rearrange("(k p) n -> p k n", p=128)), writes=[kA])
                    P.dma("pool", lambda e, bB=bB, j0=j0, l=l: e.dma_start(out=bB[:, 0:16, 0:256], in_=w_in[l, :, C_GP + j0:C_GP + j0 + 256].rearrange("(k p) n -> p k n", p=128)), writes=[kB])
                    P.dma("pool", lambda e, bB=bB, j0=j0, l=l: e.dma_start(out=bB[:, 0:16, 256:512], in_=w_in[l, :, C_GA + j0:C_GA + j0 + 256].rearrange("(k p) n -> p k n", p=128)), writes=[kB])
                    for t in range(HT):
                        pr = (t % 2) * 2
                        for kt in range(8):
                            P.op("pe", lambda e, bA=bA, t=t, kt=kt, pr=pr: e.matmul(pb[pr][:, 0:256], lhsT=PgT[:, t, kt, :], rhs=bA[:, kt, 0:256], start=(kt == 0), stop=(kt == 7)),
                                 reads=[("iqT", t), kA], writes=[("pb", pr)])
                        for kt in range(8):
                            P.op("pe", lambda e, bA=bA, t=t, kt=kt, pr=pr: e.matmul(pb[pr][:, 256:512], lhsT=AgT[:, t, kt, :], rhs=bA[:, 8 + kt, 0:256], start=(kt == 0), stop=(kt == 7)),
                                 reads=[("qT", t), kA], writes=[("pb", pr)])
                        for kt in range(16):
                            P.op("pe", lambda e, bB=bB, t=t, kt=kt, pr=pr: e.matmul(pb[pr + 1][:, 0:256], lhsT=xT[:, t, kt, :], rhs=bB[:, kt, 0:256], start=(kt == 0), stop=(kt == 15)),
                                 reads=[("xT", t), kB], writes=[("pb", pr + 1)])
                        for kt in range(16):
                            P.op("pe", lambda e, bB=bB, t=t, kt=kt, pr=pr: e.matmul(pb[pr + 1][:, 256:512], lhsT=xT[:, t, kt, :], rhs=bB[:, kt, 256:512], start=(kt == 0), stop=(kt == 15)),
                                 reads=[("xT", t), kB], writes=[("pb", pr + 1)])
                        sg = sgb[t % 2]; prod = prodb[t % 2]
                        pkey = "ropeA" if t % 2 == 0 else "ropeB"
                        P.op("act", lambda e, sg=sg, pr=pr: e.activation(sg[:], pb[pr + 1][:, :], AF.Sigmoid), reads=[("pb", pr + 1)], writes=[("stf", t % 2)])
                        P.op("dve", lambda e, sg=sg, prod=prod, pr=pr: e.tensor_tensor(prod[:], pb[pr][:, :], sg[:], ALU.mult),
                             reads=[("pb", pr), ("stf", t % 2)], writes=[pkey])
                        P.op("pool", lambda e, prod=prod, t=t, j0=j0: e.tensor_tensor(m_sb[:, t, j0:j0 + 256], prod[:, 0:256], prod[:, 256:512], ALU.add),
                             reads=[pkey], writes=m_keys(t))
                    if h == 0 and with_sample and stop is None:
                        so = 0
                        for kt in range(16):
                            P.op("pe", lambda e, bB=bB, kt=kt: e.matmul(pb[4][0:4, :], lhsT=xsT_[:, kt, :], rhs=bB[:, kt, :], start=(kt == 0), stop=(kt == 15)),
                                 reads=["xsT", kB], writes=[("pb", 4)])
                        P.op("act", lambda e, so=so: e.copy(sp_out[:, so, :], pb[4][0:4, :]), reads=[("pb", 4)], writes=[("spo", so)])
                        P.dma("sp", lambda e, so=so, j0=j0: e.dma_start(out=sproj[:, C_GP + j0:C_GP + j0 + 256], in_=sp_out[:, so, 0:256]),
                              reads=[("spo", so)], writes=[("sproj", C_GP + j0)])
                        P.dma("sp", lambda e, so=so, j0=j0: e.dma_start(out=sproj[:, C_GA + j0:C_GA + j0 + 256], in_=sp_out[:, so, 256:512]),
                              reads=[("spo", so)], writes=[("sproj", C_GA + j0)])

                if stop == "P4":
                    return
                fence(P4W, WKEYS)
                for t in range(HT):
                    for g in range(2):
                        bk = 4 + g
                        for j in range(8):
                            kt = g * 8 + j
                            P.op("pe", lambda e, t=t, kt=kt, j=j, bk=bk: e.transpose(pbf(bk)[:, j * 128:(j + 1) * 128], m_sb[:, t, kt * 128:(kt + 1) * 128], ident[:]),
                                 reads=[("m", t), "ident"], writes=[("pb", bk)])
                        if g == 0:
                            P.op("act", lambda e, t=t, bk=bk: e.copy(mT[:, t, 0:8, :], pbf(bk).rearrange("p (j n) -> p j n", j=8)), reads=[("pb", bk)], writes=[("xT", t)])
                        else:
                            P.op("dve", lambda e, t=t, bk=bk: e.tensor_copy(mT[:, t, 8:16, :], pbf(bk).rearrange("p (j n) -> p j n", j=8)), reads=[("pb", bk)], writes=[("xT", t)])

                def xres_ap(j):
                    if j < 6:
                        return R2[:, j * 4096:(j + 1) * 4096].bitcast(F32)
                    return R1[:, (j - 6) * 4096:(j - 5) * 4096].bitcast(F32)

                def xres_keys(j):
                    ks = [("xres", j)]
                    if j < 2:
                        ks += [("qT", tt) for tt in range(j * 4, j * 4 + 4)]
                    elif j < 4:
                        ks += [("iqT", tt) for tt in range((j - 2) * 4, (j - 2) * 4 + 4)]
                    elif j < 6:
                        ks += [("za", tt) for tt in range((j - 4) * 4, (j - 4) * 4 + 4)]
                    else:
                        ks += [("up", tt) for tt in range((j - 6) * 4, (j - 6) * 4 + 4)] + [("m", 2 * (j - 6)), ("m", 2 * (j - 6) + 1)]
                    return ks

                P.dma("sp", lambda e, l=l: e.dma_start(out=gb_bc[:, 0, :], in_=lng[l:l + 1, :].partition_broadcast(128)), writes=GBKEYS)
                P.dma("sp", lambda e, l=l: e.dma_start(out=gb_bc[:, 1, :], in_=lnb[l:l + 1, :].partition_broadcast(128)), writes=GBKEYS)
                for t in range(HT):
                    gt = h * HT + t
                    xa = xres_ap(t)
                    P.dma("sp", lambda e, xa=xa, gt=gt, par=par: e.dma_start(out=xa, in_=Xs[par, gt * 128:(gt + 1) * 128, :]),
                          reads=[("Xs", par, gt)], writes=xres_keys(t))
                for jb in range(4):
                    j0 = jb * 512
                    wo, kwo = load_w(w_o[l, :, j0:j0 + 512], D, 512, None)
                    for t in range(HT):
                        bk = rr["pb"] % 2
                        rr["pb"] += 1
                        for kt in range(16):
                            P.op("pe", lambda e, wo=wo, t=t, kt=kt, bk=bk: e.matmul(pb[bk][:, :], lhsT=mT[:, t, kt, :], rhs=wo[:, kt, :], start=(kt == 0), stop=(kt == 15)),
                                 reads=[("xT", t), kwo], writes=[("pb", bk)])
                        xa = xres_ap(t)
                        P.op("dve", lambda e, xa=xa, bk=bk, j0=j0: e.scalar_tensor_tensor(xa[:, j0:j0 + 512], xa[:, j0:j0 + 512], ALPHA, pb[bk][:, :], ALU.mult, ALU.add),
                             reads=[("pb", bk), ("xres", t)], writes=[("xres", t)])
                for t in range(HT):
                    gt = h * HT + t
                    xa = xres_ap(t)
                    layer_norm_tile(("xres", t), xa, "gb", ("xres", t))
                    if last:
                        P.dma("sp", lambda e, xa=xa, gt=gt: e.dma_start(out=y_p[gt * 128:(gt + 1) * 128, :], in_=xa), reads=[("xres", t)])
                    else:
                        P.dma("sp", lambda e, xa=xa, gt=gt, par=par: e.dma_start(out=Xs[1 - par, gt * 128:(gt + 1) * 128, :], in_=xa),
                              reads=[("xres", t)], writes=[("Xs", 1 - par, gt)])
                        to_xT_scratch(xa, ("xres", t), 1 - par, gt)
        xsT = xsT_
        qsT = sb("qsT", [128, 8, 4], BF16); ksT = sb("ksT", [128, 8, 4], BF16); zasT = sb("zasT", [128, 8, 4], BF16)
        PgsT = sb("PgsT", [128, 8, 4], BF16); AgsT = sb("AgsT", [128, 8, 4], BF16); msT = sb("msT", [128, 16, 4], BF16)
        dTs = sb("dTs", [128, 8, 4], BF16)
        iqsT = sb("iqsT", [64, 4, 16], BF16)
        iksT = sb("iksT", [64, 4], BF16)
        iws = sb("iws", [4, 16], F32)
        iwcol = sb("iwcol", [64, 1], F32)
        sbst = sb("sbst", [15, 4, 4], BF16); sbnew = sb("sbnew", [4, 4, 4], BF16)
        pt_sb = sb("pt_sb", [128, 1], I32); ptf = sb("ptf", [128, 1], F32); rowbase = sb("rowbase", [128, 1], F32)
        pgidx = sb("pgidx", [128, 1], I32)
        iota_i = sb("iota_i", [128, 128], I32); iota_f = sb("iota_f", [128, 128], F32); idx_all = sb("idx_all", [128, 128], I32)
        negm_s = sb("negm_s", [128, 4], F32); bones = sb("bones_sb", [128, 128], F32); dmask = sb("dmask", [32, 8], F32)
        ssm = sb("ssm", [128, 64], F32)
        s_lo = ssm[:, 0:1]; s_mid = ssm[:, 1:2]; s_cnt = ssm[:, 2:3]; s_B = ssm[:, 3:4]; s_am = ssm[:, 4:5]; s_am2 = ssm[:, 5:6]
        s_w0 = ssm[:, 6:7]; s_rden = ssm[0:32, 7:8]; s_wtab = ssm[:, 16:16 + 40]
        s_gei = sb("s_gei", [128, 1], I32)
        xs_scr = dscr("xs_scr", [2, 4, D])
        iw_scr = dscr("iw_scr", [4, 16])

        SR = R1[0:4, :]
        ups = SR[:, 0:1024]; zps = SR[:, 1024:2048]; zas = SR[:, 2048:3072]; ksb = SR[:, 3072:4096]; vsb = SR[:, 4096:5120]
        iqs = SR[:, 5120:6144]; gps = SR[:, 6144:8192]; gas = SR[:, 8192:10240]; ms = SR[:, 10240:12288]
        xst = SR[:, 12288:16384].bitcast(F32)
        gbs = R0[0:4, 0:8192].bitcast(F32).rearrange("p (a n) -> p a n", a=2)
        ik_raw = R0[:, :].bitcast(F32)
        ik_bf = R2[:, 0:8192]
        Kc = [R2[:, 8192 + i * 2048:8192 + (i + 1) * 2048].bitcast(F32) for i in range(2)]
        Vc = [R2[:, 12288 + i * 2048:12288 + (i + 1) * 2048].bitcast(F32) for i in range(2)]
        Kbf = [R2[:, 16384 + i * 1024:16384 + (i + 1) * 1024] for i in range(2)]
        Vbf = [R2[:, 18432 + i * 1024:18432 + (i + 1) * 1024] for i in range(2)]
        KTs = [R2[:, 20480 + i * 1024:20480 + (i + 1) * 1024].rearrange("p (h n) -> p h n", h=8) for i in range(2)]
        ikT_s = RW[0:64, 0:16388]
        Wsc = RW[0:64, 16400:20496]
        sc_s = RW[:, 20496:21528].bitcast(F32)
        junk_s = RW[:, 21528:22044]
        mask_s = RW[:, 22044:22560]
        Tm = RW[:, 22560:23072].rearrange("p (g n) -> p g n", g=4)
        Tn = RW[0:4, 23072:23200]
        Rs = [RW[0:64, 23200 + i * 512:23200 + (i + 1) * 512] for i in range(2)]
        Es = RW[:, 24224:24288].bitcast(F32)
        PTs = [RW[:, 24288 + i * 32:24288 + (i + 1) * 32] for i in range(2)]
        wmix_s = RW[:, 16400:18448].rearrange("p (g c e) -> p g c e", g=4, c=2)
        psc_s = RW[0:4, 18448:20496].bitcast(F32)
        pgs_f = RW[0:4, 20496:22544].bitcast(F32)
        od_tmp = RW[0:32, 0:2048].bitcast(F32)
        od = RW[0:32, 2048:2304].bitcast(F32)
        odn = RW[0:32, 2304:2432]

        def bar():
            P.barrier(lambda e: e.memset(fdummy[:], 0.0))

        def load_w2(dram_ap_2d, krows, ncols, k0=0):
            i = rr["w"] % 2
            rr["w"] += 1
            buf = wbuf[i]
            kt = krows // 128
            P.dma("pool", lambda e: e.dma_start(out=buf[:, k0:k0 + kt, 0:ncols], in_=dram_ap_2d.rearrange("(k p) n -> p k n", p=128)),
                  writes=[("wbuf", i)])
            return buf, ("wbuf", i)

        def tr_small(src_ap, np_, ncol, dst_ap, key_r, key_w, bk=3, dst_rearr=None):
            P.op("pe", lambda e: e.transpose(pbf(bk)[0:ncol, 0:np_], src_ap, ident[0:np_, 0:np_]), reads=[key_r, "ident"], writes=[("pb", bk)])
            P.op("act", lambda e: e.copy(dst_ap, pbf(bk)[0:ncol, 0:np_]), reads=[("pb", bk)], writes=[key_w])

        def tr_tok(src_bf, key_r, dst, key_w, nblk):
            for j in range(nblk):
                P.op("pe", lambda e, j=j: e.transpose(pbf(3)[:, j * 4:(j + 1) * 4], src_bf[:, j * 128:(j + 1) * 128], ident[0:4, 0:4]),
                     reads=[key_r, "ident"], writes=[("pb", 3)])
            P.op("act", lambda e: e.copy(dst, pbf(3)[:, 0:nblk * 4].rearrange("p (j n) -> p j n", j=nblk)), reads=[("pb", 3)], writes=[key_w])

        def ln4(x_ap, xkey):
            for j in range(4):
                P.op("dve", lambda e, j=j: e.bn_stats(stat[0:4, j, :], x_ap[:, j * 512:(j + 1) * 512]), reads=[xkey], writes=[("stat", j)])
            P.op("dve", lambda e: e.bn_aggr(mv[0:4, :], stat[0:4].rearrange("p a b -> p (a b)")), reads=[("stat", j) for j in range(4)], writes=["mv"])
            P.op("act", lambda e: e.activation(rstd[0:4, :], mv[0:4, 1:2], AF.Sqrt, bias=eps_t[0:4, :]), reads=["mv", "eps"], writes=["rstd"])
            P.op("dve", lambda e: e.reciprocal(rstd[0:4, :], rstd[0:4, :]), reads=["rstd"], writes=["rstd"])
            P.op("dve", lambda e: e.tensor_scalar(x_ap, x_ap, mv[0:4, 0:1], rstd[0:4, :], ALU.subtract, ALU.mult), reads=[xkey, "mv", "rstd"], writes=[xkey])
            P.op("dve", lambda e: e.tensor_tensor(x_ap, x_ap, gbs[:, 0, :], ALU.mult), reads=[xkey, "gbs"], writes=[xkey])
            P.op("dve", lambda e: e.tensor_tensor(x_ap, x_ap, gbs[:, 1, :], ALU.add), reads=[xkey, "gbs"], writes=[xkey])

        def xs_to_T():
            P.op("act", lambda e: e.copy(xbf_st[0:4, :], xst), reads=["xst"], writes=["xbf"])
            tr_tok(xbf_st[0:4, :], "xbf", xsT[:], "xsT", 16)

        def sample_init():
            bar()
            P.dma("pool", lambda e: e.dma_start(out=sbst[:], in_=sb_st_d.rearrange("g s t -> s g t")), writes=["sbst"])
            P.dma("pool", lambda e: e.dma_start(out=sbnew[:], in_=sb_new_d.rearrange("g s t -> s g t")), writes=["sbnew"])
            P.dma("sp", lambda e: e.dma_start(out=negm_s[:], in_=negm_d), writes=["negm"])
            P.dma("sp", lambda e: e.dma_start(out=bones[:], in_=bones_d), writes=["bones"])
            P.dma("sp", lambda e: e.dma_start(out=dmask[:], in_=dmask_d), writes=["dmask"])
            P.dma("sp", lambda e: e.dma_start(out=pt_sb[:], in_=pt), writes=["pt"])
            P.op("dve", lambda e: e.tensor_copy(ptf[:], pt_sb[:]), reads=["pt"], writes=["ptf"])
            P.op("pool", lambda e: e.iota(iota_i[:], [[1, 128]], base=0, channel_multiplier=0), writes=["iota_i"])
            P.op("dve", lambda e: e.tensor_copy(iota_f[:], iota_i[:]), reads=["iota_i"], writes=["iota_f"])
            P.dma("sp", lambda e: e.dma_start(out=gbs[:, 0, :], in_=lneg.partition_broadcast(4)), writes=["gbs"])
            P.dma("sp", lambda e: e.dma_start(out=gbs[:, 1, :], in_=lneb.partition_broadcast(4)), writes=["gbs"])
            P.dma("sp", lambda e: e.dma_start(out=xst, in_=xs_in), writes=["xst"])
            ln4(xst, "xst")
            P.dma("sp", lambda e: e.dma_start(out=xs_scr[0], in_=xst), reads=["xst"], writes=["xs_scr"])
            xs_to_T()
            bar()

        S_BLOCKS = []
        for kind, c0 in (("up", C_UP), ("zp", C_ZP), ("q", C_Q), ("k", C_K), ("v", C_V), ("za", C_ZA), ("iq", C_IQ)):
            S_BLOCKS.append((kind, c0, 512, 0)); S_BLOCKS.append((kind, c0 + 512, 512, 1))
        S_BLOCKS.append(("ikiw", C_IK, 80, 0))
        for kind, c0 in (("gp", C_GP), ("ga", C_GA)):
            for sub in range(4):
                S_BLOCKS.append((kind, c0 + sub * 512, 512, sub))

        def rope4(bk, nh, half, cos_ap, sin_ap, out_ap, okey):
            n = nh * 2 * half
            src = sp_in[:, bk, 0:n].rearrange("p (h two f) -> p h two f", h=nh, two=2)
            cb = cos_ap.rearrange("p (a b f) -> p a b f", a=1, b=1).to_broadcast([4, nh, 2, half])
            sbb = sin_ap.rearrange("p (a b f) -> p a b f", a=1, b=1).to_broadcast([4, nh, 2, half])
            A = ropeA[0:4, 0:n].rearrange("p (h two f) -> p h two f", h=nh, two=2)
            B = ropeB[0:4, 0:n].rearrange("p (h two f) -> p h two f", h=nh, two=2)
            o = out_ap.rearrange("p (h two f) -> p h two f", h=nh, two=2)
            P.op("dve", lambda e: e.tensor_tensor(A, src, cb, ALU.mult), reads=[("spin", bk), "sropetab"], writes=["ropeA"])
            P.op("dve", lambda e: e.tensor_tensor(B, src, sbb, ALU.mult), reads=[("spin", bk), "sropetab"], writes=["ropeB"])
            P.op("dve", lambda e: e.tensor_tensor(o[:, :, 0, :], A[:, :, 0, :], B[:, :, 1, :], ALU.subtract), reads=["ropeA", "ropeB"], writes=[okey])
            P.op("dve", lambda e: e.tensor_tensor(o[:, :, 1, :], A[:, :, 1, :], B[:, :, 0, :], ALU.add), reads=["ropeA", "ropeB"], writes=[okey, "ropeA", "ropeB"])

        def sample_pass(l):
            last = (l == DEPTH - 1)
            par = l % 2
            bar()
            for tab, src in ((cos128, src128), (sin128, srs128), (cos64, src64), (sin64, srs64)):
                P.dma("sp", lambda e, tab=tab, src=src: e.dma_start(out=tab[0:4, 0, :], in_=src), writes=["sropetab"])
            for (kind, c0, ncols, sub) in S_BLOCKS:
                bk = 0
                P.dma("sp", lambda e, bk=bk, c0=c0, ncols=ncols: e.dma_start(out=sp_in[:, bk, 0:ncols], in_=sproj[:, c0:c0 + ncols]), writes=[("spin", bk)])
                pk = ("spin", bk)
                cs = slice(sub * 512, sub * 512 + 512)
                if kind == "up":
                    sf = st_f32[sub]
                    P.op("act", lambda e, sf=sf, bk=bk: e.copy(sf[0:4, :], sp_in[:, bk, :]), reads=[pk], writes=[("stf", sub)])
                    P.dma("sp", lambda e, sf=sf, cs=cs: e.dma_start(out=pl_s[l, 11:15, cs], in_=sf[0:4, :]), reads=[("stf", sub)])
                    P.op("dve", lambda e, bk=bk, cs=cs: e.tensor_copy(ups[:, cs], sp_in[:, bk, :]), reads=[pk], writes=["ups"])
                elif kind == "zp":
                    P.op("act", lambda e, bk=bk, cs=cs: e.activation(zps[:, cs], sp_in[:, bk, :], AF.Silu), reads=[pk], writes=["zps"])
                elif kind == "za":
                    P.op("act", lambda e, bk=bk, cs=cs: e.activation(zas[:, cs], sp_in[:, bk, :], AF.Silu), reads=[pk], writes=["zas"])
                    if sub == 1:
                        tr_tok(zas, "zas", zasT[:], "zasT", 8)
                elif kind == "q":
                    rope4(bk, 4, 64, cos128[0:4, 0, :], sin128[0:4, 0, :], iqs[:, cs], "iqs")
                    if sub == 1:
                        tr_tok(iqs, "iqs", qsT[:], "qsT", 8)
                elif kind == "k":
                    sf = st_f32[sub]
                    rope4(bk, 4, 64, cos128[0:4, 0, :], sin128[0:4, 0, :], sf[0:4, :], ("stf", sub))
                    P.dma("sp", lambda e, sf=sf, cs=cs: e.dma_start(out=k_s[l, :, cs], in_=sf[0:4, :]), reads=[("stf", sub)])
                    P.op("act", lambda e, sf=sf, cs=cs: e.copy(ksb[:, cs], sf[0:4, :]), reads=[("stf", sub)], writes=["ksb"])
                    if sub == 1:
                        tr_tok(ksb, "ksb", ksT[:], "ksT", 8)
                elif kind == "v":
                    sf = st_f32[sub]
                    P.op("act", lambda e, sf=sf, bk=bk: e.copy(sf[0:4, :], sp_in[:, bk, :]), reads=[pk], writes=[("stf", sub)])
                    P.dma("sp", lambda e, sf=sf, cs=cs: e.dma_start(out=v_s[l, :, cs], in_=sf[0:4, :]), reads=[("stf", sub)])
                    P.op("dve", lambda e, bk=bk, cs=cs: e.tensor_copy(vsb[:, cs], sp_in[:, bk, :]), reads=[pk], writes=["vsb"])
                elif kind == "iq":
                    rope4(bk, 8, 32, cos64[0:4, 0, :], sin64[0:4, 0, :], iqs[:, cs], "iqs")
                    if sub == 1:
                        for hh in range(16):
                            P.op("pe", lambda e, hh=hh: e.transpose(pbf(3)[0:64, hh * 4:(hh + 1) * 4], iqs[:, hh * 64:(hh + 1) * 64], ident[0:4, 0:4]),
                                 reads=["iqs", "ident"], writes=[("pb", 3)])
                        P.op("act", lambda e: e.copy(iqsT[:], pbf(3)[0:64, 0:64].rearrange("p (h q) -> p q h", h=16)), reads=[("pb", 3)], writes=["iqsT"])
                elif kind == "ikiw":
                    rope4(bk, 1, 32, cos64[0:4, 0, :], sin64[0:4, 0, :], ik_st[0:4, :], "ikst")
                    P.dma("sp", lambda e: e.dma_start(out=ik_s[l], in_=ik_st[0:4, :]), reads=["ikst"])
                    P.op("act", lambda e: e.copy(ik2_bf[0:4, 0:64], ik_st[0:4, :]), reads=["ikst"], writes=["ik2"])
                    tr_small(ik2_bf[0:4, 0:64], 4, 64, iksT[:], "ik2", "iksT")
                    P.op("dve", lambda e, bk=bk: e.tensor_scalar(iws[:], sp_in[:, bk, 64:80], IDX_SCALE, None, ALU.mult), reads=[pk], writes=["iws"])
                    P.dma("sp", lambda e: e.dma_start(out=iw_scr, in_=iws[:]), reads=["iws"], writes=["iw_scr"])
                    P.dma("sp", lambda e: e.dma_start(out=iwcol[:], in_=iw_scr.rearrange("q (h o) -> (q h) o", o=1)), reads=["iw_scr"], writes=["iwcol"])
                elif kind == "gp":
                    P.op("act", lambda e, bk=bk, cs=cs: e.activation(gps[:, cs], sp_in[:, bk, :], AF.Sigmoid), reads=[pk], writes=["gps"])
                elif kind == "ga":
                    P.op("act", lambda e, bk=bk, cs=cs: e.activation(gas[:, cs], sp_in[:, bk, :], AF.Sigmoid), reads=[pk], writes=["gas"])
            bar()
            state_bf = ropeA[0:15, :].bitcast(BF16)
            P.dma("pool", lambda e: e.dma_start(out=state_bf, in_=stp[l]), writes=["state"])
            P.dma("sp", lambda e: e.dma_start(out=pl_s[l, 0:11, :], in_=stp[l, 4:15, :]))
            P.dma("pool", lambda e: e.dma_start(out=wmix_s, in_=w_mix[l].rearrange("g (c p) e -> p g c e", p=128)), writes=["wmix_s"])
            P.dma("sp", lambda e: e.dma_start(out=psc_s, in_=pscale[l:l + 1, :].partition_broadcast(4)), writes=["psc_s"])
            for ct in range(8):
                g = ct // 2
                P.op("pe", lambda e, ct=ct, g=g: e.matmul(pb[0][:, ct * 4:(ct + 1) * 4], lhsT=state_bf[:, ct * 128:(ct + 1) * 128], rhs=sbst[:, g, :],
                                                         start=True, stop=False), reads=["state", "sbst"], writes=[("pb", 0)])
                P.op("pe", lambda e, ct=ct, g=g: e.matmul(pb[0][:, ct * 4:(ct + 1) * 4], lhsT=ups[:, ct * 128:(ct + 1) * 128], rhs=sbnew[:, g, :],
                                                         start=False, stop=True), reads=["ups", "sbnew"], writes=[("pb", 0)])
            P.op("act", lambda e: e.copy(dTs[:], pb[0][:, 0:32].rearrange("p (c n) -> p c n", c=8)), reads=[("pb", 0)], writes=["dTs"])
            for g in range(4):
                bk = 1 + g // 2
                for c in range(2):
                    P.op("pe", lambda e, g=g, c=c, bk=bk: e.matmul(pb[bk][0:4, (g % 2) * 256:(g % 2 + 1) * 256], lhsT=dTs[:, g * 2 + c, :], rhs=wmix_s[:, g, c, :],
                                                                start=(c == 0), stop=(c == 1)), reads=["dTs", "wmix_s"], writes=[("pb", bk)])
            for hf in range(2):
                P.op("dve", lambda e, hf=hf: e.tensor_tensor(pgs_f[:, hf * 512:(hf + 1) * 512], pb[1 + hf][0:4, :], psc_s[:, hf * 512:(hf + 1) * 512], ALU.mult),
                     reads=[("pb", 1 + hf), "psc_s"], writes=["pgs_f"])
            P.op("dve", lambda e: e.tensor_tensor(iqs, pgs_f, zps, ALU.mult), reads=["pgs_f", "zps", "iqs"], writes=["iqs"])
            tr_tok(iqs, "iqs", PgsT[:], "PgsT", 8)
            bar()
            P.op("dve", lambda e: e.tensor_scalar(rowbase[:], ptf[:], 128.0, float(l * NPOOL * 128), ALU.mult, ALU.add), reads=["ptf"], writes=["rowbase"])
            P.op("dve", lambda e: e.tensor_scalar(idx_all[:], iota_f[:], rowbase[:], None, ALU.add), reads=["iota_f", "rowbase"], writes=["idx_all"])
            P.op("dve", lambda e: e.tensor_scalar(pgidx[:], ptf[:], float(l * NPOOL), None, ALU.add), reads=["ptf"], writes=["pgidx"])
            P.dma("pool", lambda e: e.indirect_dma_start(out=ik_raw, out_offset=None, in_=cik, in_offset=bass.IndirectOffsetOnAxis(ap=pgidx[:, 0:1], axis=0)),
                  reads=["pgidx"], writes=["ik_raw"])
            P.op("dve", lambda e: e.tensor_copy(ik_bf[:, 0:4096], ik_raw[:, 0:4096]), reads=["ik_raw"], writes=["ik_bf0"])
            P.op("pool", lambda e: e.tensor_copy(ik_bf[:, 4096:8192], ik_raw[:, 4096:8192]), reads=["ik_raw"], writes=["ik_bf1"])
            for s0 in range(0, 128, 8):
                bk = (s0 // 8) % 2
                for j in range(8):
                    s = s0 + j
                    P.op("pe", lambda e, s=s, j=j, bk=bk: e.transpose(pbf(bk)[0:64, j * 128:(j + 1) * 128], ik_bf[:, s * 64:(s + 1) * 64], ident[:]),
                         reads=["ik_bf0" if s < 64 else "ik_bf1", "ident"], writes=[("pb", bk)])
                if bk == 0:
                    P.op("act", lambda e, s0=s0, bk=bk: e.copy(ikT_s[:, s0 * 128:(s0 + 8) * 128], pbf(bk)[0:64, :]), reads=[("pb", bk)], writes=["ikT_s"])
                else:
                    P.op("dve", lambda e, s0=s0, bk=bk: e.tensor_copy(ikT_s[:, s0 * 128:(s0 + 8) * 128], pbf(bk)[0:64, :]), reads=[("pb", bk)], writes=["ikT_s"])
            P.op("act", lambda e: e.copy(ikT_s[:, 16384:16388], iksT[:]), reads=["iksT"], writes=["ikT_s"])
            P.dma("pool", lambda e: e.dma_start(out=Wsc, in_=esel_d), writes=["Wsc"])
            P.op("dve", lambda e: e.tensor_scalar(Wsc, Wsc, iwcol[:], None, ALU.mult), reads=["Wsc", "iwcol"], writes=["Wsc"])
            iq_flat = iqsT[:].rearrange("p q h -> p (q h)")
            for c in range(33):
                kn = 512 if c < 32 else 4
                bk = c % 2
                rs_ = Rs[c % 2]
                rkey = ("Rs", c % 2)
                P.op("pe", lambda e, c=c, kn=kn, bk=bk: e.matmul(pb[bk][0:64, 0:kn], lhsT=iq_flat, rhs=ikT_s[:, c * 512:c * 512 + kn], start=True, stop=True),
                     reads=["iqsT", "ikT_s"], writes=[("pb", bk)])
                if c % 2 == 0:
                    P.op("act", lambda e, rs_=rs_, bk=bk, kn=kn: e.activation(rs_[:, 0:kn], pb[bk][0:64, 0:kn], AF.Relu), reads=[("pb", bk)], writes=[rkey])
                else:
                    P.op("dve", lambda e, rs_=rs_, bk=bk, kn=kn: e.tensor_scalar(rs_[:, 0:kn], pb[bk][0:64, 0:kn], 0.0, None, ALU.max), reads=[("pb", bk)], writes=[rkey])
                if c < 32:
                    P.op("pe", lambda e, c=c, rs_=rs_: e.matmul(pb[2][:, 0:512], lhsT=Wsc[:, c * 128:(c + 1) * 128], rhs=rs_[:, 0:512], start=(c == 0), stop=(c == 31)),
                         reads=["Wsc", rkey], writes=[("pb", 2)])
                else:
                    P.op("pe", lambda e, rs_=rs_: e.matmul(pb[3][:, 0:4], lhsT=Wsc[:, 0:128], rhs=rs_[:, 0:4], start=True, stop=True),
                         reads=["Wsc", rkey], writes=[("pb", 3)])
            P.op("act", lambda e: e.copy(sc_s[:, 0:512], pb[2][:, :]), reads=[("pb", 2)], writes=["sc_s"])
            P.op("dve", lambda e: e.tensor_reduce(s_am2, pb[3][:, 0:4], AX.X, ALU.max, apply_absolute_value=True), reads=[("pb", 3)], writes=["s_am2"])
            P.op("dve", lambda e: e.tensor_tensor(sc_s[:, 512:516], pb[3][:, 0:4], negm_s[:], ALU.add), reads=[("pb", 3), "negm"], writes=["sc_s"])
            P.op("dve", lambda e: e.tensor_reduce(s_am, sc_s[:, 0:512], AX.X, ALU.max, apply_absolute_value=True), reads=["sc_s"], writes=["s_am"])
            P.op("dve", lambda e: e.tensor_tensor(s_am, s_am, s_am2, ALU.max), reads=["s_am", "s_am2"], writes=["s_am"])
            P.op("pe", lambda e: e.matmul(pb[5][:, 0:1], lhsT=bones[:], rhs=s_am, start=True, stop=True), reads=["bones", "s_am"], writes=[("pb", 5)])
            P.op("dve", lambda e: e.tensor_scalar(s_B, pb[5][:, 0:1], 1.01, 1.0, ALU.mult, ALU.add), reads=[("pb", 5)], writes=["s_B"])
            P.op("dve", lambda e: e.tensor_scalar(s_lo, s_B, -1.0, None, ALU.mult), reads=["s_B"], writes=["s_lo"])
            P.op("dve", lambda e: e.tensor_scalar(s_w0, s_B, 2.0, None, ALU.mult), reads=["s_B"], writes=["s_w0"])
            P.op("dve", lambda e: e.tensor_scalar(s_wtab[:, 0:32], pow2[:, 0:32], s_w0, None, ALU.mult), reads=["s_w0", "pow2"], writes=["s_wtab"])
            for n in range(32):
                P.op("dve", lambda e, n=n: e.tensor_tensor(s_mid, s_lo, s_wtab[:, n:n + 1], ALU.add), reads=["s_lo", "s_wtab"], writes=["s_mid"])
                P.op("dve", lambda e: e.tensor_scalar(junk_s, sc_s, s_mid, None, ALU.is_ge, ALU.add, accum_out=s_cnt), reads=["sc_s", "s_mid"], writes=["junk_s", "s_cnt"])
                P.op("pe", lambda e: e.matmul(pb[5][:, 0:1], lhsT=bones[:], rhs=s_cnt, start=True, stop=True), reads=["bones", "s_cnt"], writes=[("pb", 5)])
                P.op("dve", lambda e: e.tensor_scalar(s_gei[:], pb[5][:, 0:1], 255.5, None, ALU.is_ge), reads=[("pb", 5)], writes=["s_gei"])
                P.op("dve", lambda e: e.copy_predicated(s_lo, s_gei[:], s_mid), reads=["s_gei", "s_mid"], writes=["s_lo"])
            P.op("dve", lambda e: e.tensor_scalar(mask_s, sc_s, s_lo, None, ALU.is_ge), reads=["sc_s", "s_lo"], writes=["mask_s"])
            for g in range(4):
                P.op("pe", lambda e, g=g: e.transpose(pbf(3)[:, g * 128:(g + 1) * 128], mask_s[:, g * 128:(g + 1) * 128], ident[:]), reads=["mask_s", "ident"], writes=[("pb", 3)])
            P.op("act", lambda e: e.copy(Tm, pbf(3)[:, 0:512].rearrange("p (g n) -> p g n", g=4)), reads=[("pb", 3)], writes=["Tm"])
            P.op("pe", lambda e: e.transpose(pbf(3)[0:4, 0:128], mask_s[:, 512:516], ident[:]), reads=["mask_s", "ident"], writes=[("pb", 3)])
            P.op("act", lambda e: e.copy(Tn, pbf(3)[0:4, 0:128]), reads=[("pb", 3)], writes=["Tn"])
            def A_K(j):
                b2 = j % 2
                P.dma("pool", lambda e: e.indirect_dma_start(out=Kc[b2], out_offset=None, in_=ck, in_offset=bass.IndirectOffsetOnAxis(ap=idx_all[:, j:j + 1], axis=0)),
                      reads=["idx_all"], writes=[("Kc", b2)])
                P.op("dve", lambda e: e.tensor_copy(Kbf[b2], Kc[b2]), reads=[("Kc", b2)], writes=[("Kbf", b2)])
                for hd in range(8):
                    P.op("pe", lambda e, hd=hd: e.transpose(pbf(b2)[:, hd * 128:(hd + 1) * 128], Kbf[b2][:, hd * 128:(hd + 1) * 128], ident[:]),
                         reads=[("Kbf", b2), "ident"], writes=[("pb", b2)])
                P.op("act", lambda e: e.copy(KTs[b2], pbf(b2).rearrange("p (h n) -> p h n", h=8)), reads=[("pb", b2)], writes=[("KTs", b2)])

            def A_V(j):
                b2 = j % 2
                P.dma("pool", lambda e: e.indirect_dma_start(out=Vc[b2], out_offset=None, in_=cv, in_offset=bass.IndirectOffsetOnAxis(ap=idx_all[:, j:j + 1], axis=0)),
                      reads=["idx_all"], writes=[("Vc", b2)])
                P.op("act", lambda e: e.copy(Vbf[b2], Vc[b2]), reads=[("Vc", b2)], writes=[("Vbf", b2)])

            def B_(j):
                b2 = j % 2
                if j < 128:
                    c, g = j // 4, j % 4
                    for hd in range(8):
                        P.op("pe", lambda e, hd=hd: e.matmul(pb[4][:, hd * 4:(hd + 1) * 4], lhsT=KTs[b2][:, hd, :], rhs=qsT[:, hd, :], start=True, stop=True),
                             reads=[("KTs", b2), "qsT"], writes=[("pb", 4)])
                    npp = 128
                    mk = Tm[:, g, c:c + 97:32]
                else:
                    for hd in range(8):
                        P.op("pe", lambda e, hd=hd: e.matmul(pb[4][0:4, hd * 4:(hd + 1) * 4], lhsT=ksT[:, hd, :], rhs=qsT[:, hd, :], start=True, stop=True),
                             reads=["ksT", "qsT"], writes=[("pb", 4)])
                    npp = 4
                    mk = Tn[:, 0:97:32]
                P.op("act", lambda e: e.activation(Es[0:npp, :], pb[4][0:npp, 0:32], AF.Exp, scale=SM_SCALE), reads=[("pb", 4)], writes=["Es"])
                pts = PTs[b2]
                P.op("dve", lambda e: e.tensor_tensor(
                    pts[0:npp, :].rearrange("p (h q) -> p h q", h=8), Es[0:npp, :].rearrange("p (h q) -> p h q", h=8),
                    mk.rearrange("p (o q) -> p o q", o=1).to_broadcast([npp, 8, 4]), ALU.mult),
                    reads=["Es", "Tm", "Tn"], writes=[("PTs", b2)])

            def C_(j):
                b2 = j % 2
                npp = 128 if j < 128 else 4
                vsrc = Vbf[b2] if j < 128 else vsb
                vkey = ("Vbf", b2) if j < 128 else "vsb"
                pts = PTs[b2]
                for hf in range(2):
                    P.op("pe", lambda e, hf=hf: e.matmul(
                        pb[6 + hf][0:32, :], lhsT=pts[0:npp, :], rhs=vsrc[0:npp, hf * 512:(hf + 1) * 512], start=(j == 0), stop=(j == 128), skip_group_check=True),
                        reads=[("PTs", b2), vkey], writes=[("pb", 6 + hf)])
                P.op("pe", lambda e: e.matmul(pb[5][0:32, 8:9], lhsT=pts[0:npp, :], rhs=ones_bf[0:npp, 0:1], start=(j == 0), stop=(j == 128), skip_group_check=True),
                     reads=[("PTs", b2), "ones"], writes=[("pb", 5)])

            A_K(0); A_V(0); A_K(1); A_V(1); B_(0)
            for s_ in range(129):
                if s_ + 2 <= 127:
                    A_K(s_ + 2)
                if s_ + 1 <= 128:
                    B_(s_ + 1)
                C_(s_)
                if s_ + 2 <= 127:
                    A_V(s_ + 2)
            P.op("dve", lambda e: e.reciprocal(s_rden, pb[5][0:32, 8:9]), reads=[("pb", 5)], writes=["s_rden"])
            for hf in range(2):
                P.op("dve", lambda e, hf=hf: e.tensor_tensor(
                    od_tmp[:, hf * 512:(hf + 1) * 512].rearrange("p (h d) -> p h d", h=4), pb[6 + hf][0:32, :].rearrange("p (h d) -> p h d", h=4),
                    dmask[:, hf * 4:(hf + 1) * 4].rearrange("p (h o) -> p h o", o=1).to_broadcast([32, 4, 128]), ALU.mult),
                    reads=[("pb", 6 + hf), "dmask", "ikT_s"], writes=["od_tmp", "ikT_s"])
            P.op("dve", lambda e: e.tensor_reduce(od, od_tmp.rearrange("p (h d) -> p d h", h=8), AX.X, ALU.add), reads=["od_tmp"], writes=["od"])
            P.op("dve", lambda e: e.tensor_scalar(odn, od, s_rden, None, ALU.mult), reads=["od", "s_rden"], writes=["odn"])
            P.op("pe", lambda e: e.transpose(pbf(3)[:, 0:32], odn, ident[0:32, 0:32]), reads=["odn", "ident"], writes=[("pb", 3)])
            P.op("dve", lambda e: e.tensor_tensor(AgsT[:], pbf(3)[:, 0:32].rearrange("p (h q) -> p h q", h=8), zasT[:], ALU.mult), reads=[("pb", 3), "zasT"], writes=["AgsT"])
            bar()
            for jb in range(4):
                j0 = jb * 512
                i1 = rr["w"] % 2; rr["w"] += 1
                bA = wbuf[i1]; kA = ("wbuf", i1)
                P.dma("pool", lambda e, bA=bA, j0=j0: e.dma_start(out=bA[:, 0:8, :], in_=w_po[l, :, j0:j0 + 512].rearrange("(k p) n -> p k n", p=128)), writes=[kA])
                P.dma("pool", lambda e, bA=bA, j0=j0: e.dma_start(out=bA[:, 8:16, :], in_=w_ao[l, :, j0:j0 + 512].rearrange("(k p) n -> p k n", p=128)), writes=[kA])
                for kt in range(8):
                    P.op("pe", lambda e, bA=bA, kt=kt: e.matmul(pb[0][0:4, :], lhsT=PgsT[:, kt, :], rhs=bA[:, kt, :], start=(kt == 0), stop=(kt == 7)), reads=["PgsT", kA], writes=[("pb", 0)])
                for kt in range(8):
                    P.op("pe", lambda e, bA=bA, kt=kt: e.matmul(pb[1][0:4, :], lhsT=AgsT[:, kt, :], rhs=bA[:, 8 + kt, :], start=(kt == 0), stop=(kt == 7)), reads=["AgsT", kA], writes=[("pb", 1)])
                P.op("dve", lambda e, j0=j0: e.tensor_tensor(ropeA[0:4, :], pb[0][0:4, :], gps[:, j0:j0 + 512], ALU.mult), reads=[("pb", 0), "gps"], writes=["ropeA"])
                P.op("dve", lambda e, j0=j0: e.tensor_tensor(ropeB[0:4, :], pb[1][0:4, :], gas[:, j0:j0 + 512], ALU.mult), reads=[("pb", 1), "gas"], writes=["ropeB"])
                P.op("dve", lambda e, j0=j0: e.tensor_tensor(ms[:, j0:j0 + 512], ropeA[0:4, :], ropeB[0:4, :], ALU.add), reads=["ropeA", "ropeB"], writes=["ms"])
            tr_tok(ms, "ms", msT[:], "msT", 16)
            P.dma("sp", lambda e: e.dma_start(out=xst, in_=xs_scr[par]), reads=["xs_scr"], writes=["xst"])
            P.dma("sp", lambda e: e.dma_start(out=gbs[:, 0, :], in_=lng[l:l + 1, :].partition_broadcast(4)), writes=["gbs"])
            P.dma("sp", lambda e: e.dma_start(out=gbs[:, 1, :], in_=lnb[l:l + 1, :].partition_broadcast(4)), writes=["gbs"])
            for jb in range(4):
                j0 = jb * 512
                wo, kwo = load_w2(w_o[l, :, j0:j0 + 512], D, 512)
                bk = jb % 2
                for kt in range(16):
                    P.op("pe", lambda e, wo=wo, kt=kt, bk=bk: e.matmul(pb[bk][0:4, :], lhsT=msT[:, kt, :], rhs=wo[:, kt, :], start=(kt == 0), stop=(kt == 15)), reads=["msT", kwo], writes=[("pb", bk)])
                P.op("dve", lambda e, bk=bk, j0=j0: e.scalar_tensor_tensor(xst[:, j0:j0 + 512], xst[:, j0:j0 + 512], ALPHA, pb[bk][0:4, :], ALU.mult, ALU.add),
                     reads=[("pb", bk), "xst"], writes=["xst"])
            ln4(xst, "xst")
            if last:
                P.dma("sp", lambda e: e.dma_start(out=y_s, in_=xst), reads=["xst"])
            else:
                P.dma("sp", lambda e: e.dma_start(out=xs_scr[1 - par], in_=xst), reads=["xst"], writes=["xs_scr"])
                xs_to_T()
            bar()

        if stop is None and with_sample:
            sample_init()
        for l_ in range(DEPTH if stop is None else 1):
            for h_ in range(2 if stop is None else 1):
                pass_body(l_, h_)
            if stop is None and with_sample:
                sample_pass(l_)
        P.emit()
    return nc


_NC_CACHE = {}


def _get_nc(npool):
    if npool not in _NC_CACHE:
        _NC_CACHE[npool] = build(npool)
    return _NC_CACHE[npool]


def make_in_maps(inputs, n_cores=8):
    f = lambda a: np.ascontiguousarray(np.asarray(a))
    x_prompt = f(inputs["x_prompt"]); x_sample = f(inputs["x_sample"])
    cache_k = f(inputs["cache_k"]); cache_v = f(inputs["cache_v"]); cache_ik = f(inputs["cache_idx_k"])
    npool = cache_k.shape[1]
    ck = cache_k.reshape(DEPTH * npool * 128, 1024)
    cv = cache_v.reshape(DEPTH * npool * 128, 1024)
    cik = cache_ik.reshape(DEPTH * npool, 8192)
    state_pool = f(inputs["state_pool"]); page_table = f(inputs["page_table"]).astype(np.int32)
    cst = _consts()
    shared = {
        "ck": ck, "cv": cv, "cik": cik,
        "lneg": f(inputs["ln_emb_g"]).reshape(1, D), "lneb": f(inputs["ln_emb_b"]).reshape(1, D),
        "w_in": f(inputs["w_in"]), "w_mix": f(inputs["w_pool_mix"]), "pscale": f(inputs["pool_scale"]),
        "w_po": f(inputs["w_pool_out"]), "w_ao": f(inputs["w_attn_out"]), "w_o": f(inputs["w_o"]),
        "lng": f(inputs["ln_g"]), "lnb": f(inputs["ln_b"]),
    }
    shared.update(cst)
    maps = []
    for c in range(n_cores):
        m = dict(shared)
        m["xp"] = x_prompt[c % 4]
        m["xs"] = x_sample[c]
        m["stp"] = np.ascontiguousarray(state_pool[:, c])
        m["pt"] = np.ascontiguousarray(page_table[c].reshape(128, 1))
        maps.append(m)
    return maps, npool


def kernel(**inputs):
    maps, npool = make_in_maps(inputs)
    nc = _get_nc(npool)
    res = run_bass_kernel_spmd(nc, maps, core_ids=list(range(8)))
    r = res.results
    y_prompt = np.stack([r[b]["y_p"] for b in range(4)]).reshape(4, SEQ, D)
    y_sample = np.stack([r[c]["y_s"] for c in range(8)]).reshape(8, 4, D)
    k_prompt = np.stack([r[b]["k_p"] for b in range(4)], axis=1).reshape(DEPTH, 4, SEQ, 8, 128)
    v_prompt = np.stack([r[b]["v_p"] for b in range(4)], axis=1).reshape(DEPTH, 4, SEQ, 8, 128)
    ik_prompt = np.stack([r[b]["ik_p"] for b in range(4)], axis=1).reshape(DEPTH, 4, SEQ, 64)
    pl_prompt = np.stack([r[b]["pl_p"] for b in range(4)], axis=1).reshape(DEPTH, 4, 15, PW)
    k_sample = np.stack([r[c]["k_s"] for c in range(8)], axis=1).reshape(DEPTH, 8, 4, 8, 128)
    v_sample = np.stack([r[c]["v_s"] for c in range(8)], axis=1).reshape(DEPTH, 8, 4, 8, 128)
    ik_sample = np.stack([r[c]["ik_s"] for c in range(8)], axis=1).reshape(DEPTH, 8, 4, 64)
    pl_sample = np.stack([r[c]["pl_s"] for c in range(8)], axis=1).reshape(DEPTH, 8, 15, PW)
    return (y_prompt, y_sample, k_prompt, v_prompt, ik_prompt, pl_prompt, k_sample, v_sample, ik_sample, pl_sample)
```
